# Optimizing a Trainium2 kernel written in Bass

```python
import jax, jax.numpy as jnp
from jax import lax
import numpy as np

D_MODEL = 1024
BATCH = 8
SEQ = 4096
DEPTH = 1

A_HEADS = 8
A_HEAD_DIM = 64
A_WIDTH = A_HEADS * A_HEAD_DIM
MOBA_BLOCK = 256
MOBA_TOPK = 3
MOBA_QCHUNK = 32
ROPE_THETA = 10000.0
B_HEADS = 4
B_HEAD_DIM = 128
B_WIDTH = B_HEADS * B_HEAD_DIM
MLSTM_CHUNK = 64
CONV_WIDTH = 4
D_FF = ((8 * D_MODEL + 3 * 256 - 1) // (3 * 256)) * 256
NORM_EPS = 1e-6

IN_SIZES = [A_WIDTH, A_WIDTH, A_WIDTH,
            2 * B_WIDTH,
            B_WIDTH, B_WIDTH,
            B_HEADS, B_HEADS,
            D_MODEL, D_MODEL]
IN_WIDTH = sum(IN_SIZES)
IN_SPLITS = [int(s) for s in np.cumsum(IN_SIZES)[:-1]]

kernel_name = "moba_mlstm_gated_hybrid"


def rms_norm(x, g):
    xf = x.astype(jnp.float32)
    y = xf * lax.rsqrt(jnp.mean(xf * xf, axis=-1, keepdims=True) + NORM_EPS)
    return (y * g.astype(jnp.float32)).astype(x.dtype)


def rope(x, pos):
    half = x.shape[-1] // 2
    inv = ROPE_THETA ** (-jnp.arange(half, dtype=jnp.float32) / half)
    ang = pos.astype(jnp.float32)[:, None] * inv[None, :]
    cos, sin = jnp.cos(ang), jnp.sin(ang)
    xf = x.astype(jnp.float32)
    x1, x2 = xf[..., :half], xf[..., half:]
    return jnp.concatenate([x1 * cos - x2 * sin, x2 * cos + x1 * sin], axis=-1).astype(x.dtype)


def causal_depthwise_conv(x, w):
    width, chans = w.shape
    return lax.conv_general_dilated(
        x, w[:, None, :].astype(x.dtype), window_strides=(1,),
        padding=((width - 1, 0),), dimension_numbers=('NWC', 'WIO', 'NWC'),
        feature_group_count=chans)


def moba_attention(q, k, v):
    bn, nh, s, dh = q.shape
    nb = -(-s // MOBA_BLOCK)
    sp = nb * MOBA_BLOCK
    pad = ((0, 0), (0, 0), (0, sp - s), (0, 0))
    q, k, v = jnp.pad(q, pad), jnp.pad(k, pad), jnp.pad(v, pad)
    kb = k.reshape(bn, nh, nb, MOBA_BLOCK, dh)
    vb = v.reshape(bn, nh, nb, MOBA_BLOCK, dh)
    scale = dh ** -0.5
    topk = min(MOBA_TOPK, nb - 1)
    nq = sp // MOBA_QCHUNK
    qc = jnp.moveaxis(q.reshape(bn, nh, nq, MOBA_QCHUNK, dh), 2, 0)
    starts = jnp.arange(nq) * MOBA_QCHUNK
    if topk > 0:
        kmean = kb.astype(jnp.float32).mean(axis=3)
        gate = jnp.einsum('bhsd,bhnd->bhsn', q.astype(jnp.float32), kmean)
        q_blk = jnp.arange(sp) // MOBA_BLOCK
        past = jnp.arange(nb)[None, :] < q_blk[:, None]
        gate = jnp.where(past, gate, -jnp.inf)
        top_val, top_idx = lax.top_k(gate, topk)
        valid = jnp.isfinite(top_val)
        idx_c = jnp.moveaxis(top_idx.reshape(bn, nh, nq, MOBA_QCHUNK, topk), 2, 0)
        val_c = jnp.moveaxis(valid.reshape(bn, nh, nq, MOBA_QCHUNK, topk), 2, 0)
        xs = (qc, starts, idx_c, val_c)
    else:
        xs = (qc, starts)
    bi = jnp.arange(bn)[:, None, None, None]
    hi = jnp.arange(nh)[None, :, None, None]

    def attend_chunk(args):
        qi, start = args[0], args[1]
        blk = start // MOBA_BLOCK
        k_own = lax.dynamic_index_in_dim(kb, blk, axis=2, keepdims=False)
        v_own = lax.dynamic_index_in_dim(vb, blk, axis=2, keepdims=False)
        qpos = start + jnp.arange(MOBA_QCHUNK)
        kpos = blk * MOBA_BLOCK + jnp.arange(MOBA_BLOCK)
        causal = kpos[None, :] <= qpos[:, None]
        s_own = jnp.einsum('bhqd,bhkd->bhqk', qi, k_own).astype(jnp.float32) * scale
        s_own = jnp.where(causal, s_own, -jnp.inf)
        if topk > 0:
            idx_i, valid_i = args[2], args[3]
            k_sel = kb[bi, hi, idx_i]
            v_sel = vb[bi, hi, idx_i]
            s_sel = jnp.einsum('bhqd,bhqrkd->bhqrk', qi, k_sel).astype(jnp.float32) * scale
            s_sel = jnp.where(valid_i[..., None], s_sel, -jnp.inf)
            s_sel = s_sel.reshape(bn, nh, MOBA_QCHUNK, topk * MOBA_BLOCK)
            p = jax.nn.softmax(jnp.concatenate([s_sel, s_own], axis=-1), axis=-1).astype(v.dtype)
            p_sel = p[..., :topk * MOBA_BLOCK].reshape(bn, nh, MOBA_QCHUNK, topk, MOBA_BLOCK)
            return (jnp.einsum('bhqrk,bhqrkd->bhqd', p_sel, v_sel)
                    + jnp.einsum('bhqk,bhkd->bhqd', p[..., topk * MOBA_BLOCK:], v_own))
        p = jax.nn.softmax(s_own, axis=-1).astype(v.dtype)
        return jnp.einsum('bhqk,bhkd->bhqd', p, v_own)

    out = lax.map(attend_chunk, xs)
    return jnp.moveaxis(out, 0, 2).reshape(bn, nh, sp, dh)[:, :, :s]


def mlstm_chunkwise(q, k, v, i_pre, f_pre):
    bn, nh, s, d = q.shape
    L = MLSTM_CHUNK
    nc = s // L
    f32 = jnp.float32
    q = q.astype(f32)
    k = k.astype(f32) * (d ** -0.5)
    v = v.astype(f32)
    log_f = jax.nn.log_sigmoid(f_pre.astype(f32))
    log_i = i_pre.astype(f32)

    def chunks(a):
        return jnp.moveaxis(a.reshape(bn, nh, nc, L, *a.shape[3:]), 2, 0)

    tri = jnp.tril(jnp.ones((L, L), dtype=bool))

    def step(carry, xs):
        C, n, m = carry
        qc, kc, vc, ic, fc = xs
        b = jnp.cumsum(fc, axis=-1)
        dmat = jnp.where(tri, b[..., :, None] - b[..., None, :] + ic[..., None, :], -jnp.inf)
        m_inter = b + m[..., None]
        m_t = jnp.maximum(m_inter, dmat.max(axis=-1))
        sc = jnp.einsum('bhtd,bhsd->bhts', qc, kc) * jnp.exp(dmat - m_t[..., None])
        w_inter = jnp.exp(m_inter - m_t)
        num = (jnp.einsum('bhts,bhse->bhte', sc, vc)
               + w_inter[..., None] * jnp.einsum('bhtd,bhde->bhte', qc, C))
        den = sc.sum(axis=-1) + w_inter * jnp.einsum('bhtd,bhd->bht', qc, n)
        h = num / jnp.maximum(jnp.abs(den), jnp.exp(-m_t))[..., None]
        b_tot = b[..., -1]
        g = b_tot[..., None] - b + ic
        m_new = jnp.maximum(b_tot + m, g.max(axis=-1))
        w_c = jnp.exp(b_tot + m - m_new)
        w_s = jnp.exp(g - m_new[..., None])
        C = w_c[..., None, None] * C + jnp.einsum('bhs,bhsd,bhse->bhde', w_s, kc, vc)
        n = w_c[..., None] * n + jnp.einsum('bhs,bhsd->bhd', w_s, kc)
        return (C, n, m_new), h

    init = (jnp.zeros((bn, nh, d, d), f32), jnp.zeros((bn, nh, d), f32), jnp.zeros((bn, nh), f32))
    _, hs = lax.scan(step, init, (chunks(q), chunks(k), chunks(v), chunks(log_i), chunks(log_f)))
    return jnp.moveaxis(hs, 0, 2).reshape(bn, nh, s, d)


def setup_inputs(seed: int = 0) -> dict:
    key = jax.random.key(seed)
    ks = jax.random.split(key, 16)
    nrm = jax.random.normal
    f32 = jnp.float32
    return {
        "x": nrm(ks[0], (BATCH, SEQ, D_MODEL), f32),
        "norm_mix_g": 1.0 + 0.02 * nrm(ks[1], (DEPTH, D_MODEL), f32),
        "w_in": nrm(ks[2], (DEPTH, D_MODEL, IN_WIDTH), f32) * D_MODEL ** -0.5,
        "conv_w": nrm(ks[3], (DEPTH, CONV_WIDTH, 2 * B_WIDTH), f32) * CONV_WIDTH ** -0.5,
        "b_igate": 0.1 * nrm(ks[4], (DEPTH, B_HEADS), f32),
        "b_fgate": jnp.linspace(3.0, 6.0, B_HEADS, dtype=f32)[None, :] + 0.1 * nrm(ks[5], (DEPTH, B_HEADS), f32),
        "mlstm_norm_g": 1.0 + 0.02 * nrm(ks[6], (DEPTH, B_WIDTH), f32),
        "w_proj_a": nrm(ks[7], (DEPTH, A_WIDTH, D_MODEL), f32) * A_WIDTH ** -0.5,
        "w_proj_b": nrm(ks[8], (DEPTH, B_WIDTH, D_MODEL), f32) * B_WIDTH ** -0.5,
        "w_out": nrm(ks[9], (DEPTH, D_MODEL, D_MODEL), f32) * D_MODEL ** -0.5,
        "norm_ffn_g": 1.0 + 0.02 * nrm(ks[10], (DEPTH, D_MODEL), f32),
        "w_gate_up": nrm(ks[11], (DEPTH, D_MODEL, 2 * D_FF), f32) * D_MODEL ** -0.5,
        "w_down": nrm(ks[12], (DEPTH, D_FF, D_MODEL), f32) * D_FF ** -0.5,
        "norm_final_g": 1.0 + 0.02 * nrm(ks[13], (D_MODEL,), f32),
    }


def reference(x, norm_mix_g, w_in, conv_w, b_igate, b_fgate, mlstm_norm_g, w_proj_a, w_proj_b,
              w_out, norm_ffn_g, w_gate_up, w_down, norm_final_g):
    bn, s, _ = x.shape
    pos = jnp.arange(s)

    def to_heads(t, nh):
        return t.reshape(bn, s, nh, -1).transpose(0, 2, 1, 3)

    h = x
    for l in range(DEPTH):
        u = rms_norm(h, norm_mix_g[l])
        z = u @ w_in[l]
        qa, ka, va, qk_b, vb, ob, ib, fb, ga, gb = jnp.split(z, IN_SPLITS, axis=-1)

        qa = rope(to_heads(qa, A_HEADS), pos)
        ka = rope(to_heads(ka, A_HEADS), pos)
        ya = moba_attention(qa, ka, to_heads(va, A_HEADS))
        ya = ya.transpose(0, 2, 1, 3).reshape(bn, s, A_WIDTH)

        qk_b = jax.nn.silu(causal_depthwise_conv(qk_b, conv_w[l]))
        qb, kb_ = jnp.split(qk_b, 2, axis=-1)
        i_pre = (ib + b_igate[l].astype(ib.dtype)).transpose(0, 2, 1)
        f_pre = (fb + b_fgate[l].astype(fb.dtype)).transpose(0, 2, 1)
        hb = mlstm_chunkwise(to_heads(qb, B_HEADS), to_heads(kb_, B_HEADS), to_heads(vb, B_HEADS),
                             i_pre, f_pre)
        hb = hb * lax.rsqrt(jnp.mean(hb * hb, axis=-1, keepdims=True) + NORM_EPS)
        hb = hb.transpose(0, 2, 1, 3).reshape(bn, s, B_WIDTH) * mlstm_norm_g[l].astype(jnp.float32)
        yb = (jax.nn.sigmoid(ob.astype(jnp.float32)) * hb).astype(u.dtype)

        merged = (jax.nn.sigmoid(ga) * (ya @ w_proj_a[l])
                  + jax.nn.sigmoid(gb) * (yb @ w_proj_b[l]))
        h = h + merged @ w_out[l]

        u = rms_norm(h, norm_ffn_g[l])
        g, up = jnp.split(u @ w_gate_up[l], 2, axis=-1)
        h = h + (jax.nn.silu(g) * up) @ w_down[l]

    return rms_norm(h, norm_final_g)
```

```python
import numpy as np
import ml_dtypes
from contextlib import ExitStack

import concourse.bass as bass
import concourse.mybir as mybir
from concourse.bass_utils import run_bass_kernel_spmd

F32 = mybir.dt.float32
BF16 = mybir.dt.bfloat16
ALU = mybir.AluOpType
AF = mybir.ActivationFunctionType
AX = mybir.AxisListType

D = 1024
DFF = 2816
EPS = 1e-6
NCORES = 8


class Res:
    __slots__ = ("name", "w", "r")

    def __init__(self, name):
        self.name = name
        self.w = None
        self.r = {}


class Prog:
    COMPUTE = ("pe", "dve", "act", "pool")

    def __init__(self, nc, stack, n_sp=12, n_pool=6):
        self.nc = nc
        self.sem = {}
        for e in self.COMPUTE:
            self.sem[e] = stack.enter_context(nc.semaphore("s_" + e))
        self.dma_pool = {"sp": [], "pool": []}
        for i in range(n_sp):
            k = "dsp%d" % i
            self.sem[k] = stack.enter_context(nc.semaphore(k))
            self.dma_pool["sp"].append(k)
        for i in range(n_pool):
            k = "dpl%d" % i
            self.sem[k] = stack.enter_context(nc.semaphore(k))
            self.dma_pool["pool"].append(k)
        self.dma_rr = {"sp": 0, "pool": 0}
        self.cnt = {k: 0 for k in self.sem}
        self.ops = {e: [] for e in ("pe", "dve", "act", "pool", "sp")}
        self.know = {e: {} for e in self.ops}
        self.clock = {}
        self.pe_pending = False

    def _needs(self, reads, writes):
        need = {}
        for r in reads:
            if r.w is not None:
                k, v = r.w
                if need.get(k, 0) < v:
                    need[k] = v
        for w in writes:
            if w.w is not None:
                k, v = w.w
                if need.get(k, 0) < v:
                    need[k] = v
            for k, v in w.r.items():
                if need.get(k, 0) < v:
                    need[k] = v
        return need

    def _waits(self, eng, need):
        know = self.know[eng]
        waits = []
        for k, v in need.items():
            if eng == "pe" and k == "pe":
                continue
            if know.get(k, 0) >= v:
                continue
            waits.append((k, v))
        for k, v in waits:
            ck = self.clock.get((k, v))
            if ck:
                for kk, vv in ck.items():
                    if know.get(kk, 0) < vv:
                        know[kk] = vv
            if know.get(k, 0) < v:
                know[k] = v
        return waits

    def _record(self, point, reads, writes):
        k, v = point
        for r in reads:
            if r.r.get(k, 0) < v:
                r.r[k] = v
        for w in writes:
            w.w = point
            w.r = {}

    def op(self, eng, fn, reads=(), writes=(), inc=True):
        need = self._needs(reads, writes)
        waits = self._waits(eng, need)
        if eng == "pe" and not inc:
            point = ("pe", self.cnt["pe"] + 1)
            self.pe_pending = True
            self.ops[eng].append((waits, fn, None, 0))
        else:
            self.cnt[eng] += 1
            point = (eng, self.cnt[eng])
            if eng == "pe":
                self.pe_pending = False
            self.ops[eng].append((waits, fn, eng, 1))
            ck = dict(self.know[eng])
            ck[eng] = self.cnt[eng]
            self.clock[point] = ck
        self._record(point, reads, writes)
        return point

    def dma(self, q, fn, reads=(), writes=()):
        pool = self.dma_pool[q]
        sk = pool[self.dma_rr[q] % len(pool)]
        self.dma_rr[q] += 1
        need = self._needs(reads, writes)
        if self.cnt[sk] > 0 and need.get(sk, 0) < self.cnt[sk]:
            need[sk] = self.cnt[sk]
        waits = self._waits(q, need)
        self.cnt[sk] += 16
        point = (sk, self.cnt[sk])
        self.ops[q].append((waits, fn, sk, 16))
        ck = dict(self.know[q])
        ck[sk] = self.cnt[sk]
        self.clock[point] = ck
        self._record(point, reads, writes)
        return point

    def mm(self, out, lhsT, rhs, start, stop, reads, writes, inc=None):
        if inc is None:
            inc = stop
        return self.op("pe", lambda e: e.matmul(out, lhsT=lhsT, rhs=rhs, start=start, stop=stop),
                       reads, writes, inc)

    def tr(self, out, in_, ident, reads, writes, inc=True):
        return self.op("pe", lambda e: e.transpose(out=out, in_=in_, identity=ident), reads, writes, inc)

    def act(self, out, in_, func, reads, writes, bias=None, scale=None, accum=None):
        kw = {}
        if bias is not None:
            kw["bias"] = bias
        if scale is not None:
            kw["scale"] = scale
        if accum is not None:
            kw["accum_out"] = accum
        return self.op("act", lambda e: e.activation(out=out, in_=in_, func=func, **kw), reads, writes)

    def tt(self, eng, out, in0, in1, op, reads, writes):
        return self.op(eng, lambda e: e.tensor_tensor(out=out, in0=in0, in1=in1, op=op), reads, writes)

    def ts(self, eng, out, in0, s1, op0, reads, writes, s2=None, op1=None):
        if op1 is None:
            return self.op(eng, lambda e: e.tensor_scalar(out=out, in0=in0, scalar1=s1, scalar2=None, op0=op0),
                           reads, writes)
        return self.op(eng, lambda e: e.tensor_scalar(out=out, in0=in0, scalar1=s1, scalar2=s2, op0=op0, op1=op1),
                       reads, writes)

    def stt(self, eng, out, in0, scalar, in1, op0, op1, reads, writes):
        return self.op(eng, lambda e: e.scalar_tensor_tensor(out=out, in0=in0, scalar=scalar, in1=in1,
                                                             op0=op0, op1=op1), reads, writes)

    def cp(self, eng, out, in_, reads, writes):
        if eng == "act":
            return self.op("act", lambda e: e.copy(out=out, in_=in_), reads, writes)
        return self.op(eng, lambda e: e.tensor_copy(out=out, in_=in_), reads, writes)

    def recip(self, out, in_, reads, writes):
        return self.op("dve", lambda e: e.reciprocal(out=out, in_=in_), reads, writes)

    def memset(self, eng, ap, val, writes):
        return self.op(eng, lambda e: e.memset(ap, val), (), writes)

    def barrier(self):
        assert not self.pe_pending
        need = {k: v for k, v in self.cnt.items() if v > 0}
        for e in self.ops:
            waits = self._waits(e, dict(need))
            if waits:
                self.ops[e].append((waits, None, None, 0))

    def finish(self, final_points):
        need = {}
        for k, v in final_points:
            if need.get(k, 0) < v:
                need[k] = v
        waits = [(k, v) for k, v in need.items()]
        self.ops["sp"].append((waits, None, None, 0))

    def emit(self):
        nc = self.nc
        assert not self.pe_pending, "PE has trailing non-inc instructions"
        sem = self.sem
        ops = self.ops

        def run(e, lst):
            for waits, fn, sk, inc in lst:
                for k, v in waits:
                    e.wait_ge(sem[k], v)
                if fn is None:
                    continue
                ins = fn(e)
                if sk is not None:
                    ins.then_inc(sem[sk], inc)

        with nc.Block() as block:
            @block.tensor
            def _(e):
                run(e, ops["pe"])

            @block.vector
            def _(e):
                run(e, ops["dve"])

            @block.scalar
            def _(e):
                run(e, ops["act"])

            @block.gpsimd
            def _(e):
                run(e, ops["pool"])

            @block.sync
            def _(e):
                run(e, ops["sp"])


IN_W = 5640
A_QA, A_KA, A_VA = 0, 512, 1024
B_OFF = 1536
WB = IN_W - B_OFF
QKB, VBo, OBo, IBo, GAo, GBo = 0, 1024, 1536, 2048, 2056, 3080
LNC = float(np.log(128.0 ** -0.5))


def build_nc(S, phases=("A", "B", "C")):
    nc = bass.Bass("TRN2", target_bir_lowering=False)
    dr = {}

    def din(name, shape, dt=F32):
        dr[name] = nc.dram_tensor(name, shape, dt, kind="ExternalInput").ap()

    din("x", [S, D])
    din("norm_mix_g", [1, D])
    din("w_in", [D, IN_W])
    din("conv_wT", [D, 4])
    din("bias8", [1, 8])
    din("mlstm_norm_g", [1, 512])
    din("w_proj_a", [512, D])
    din("w_proj_b", [512, D])
    din("w_out", [D, D])
    din("norm_ffn_g", [1, D])
    din("norm_final_g", [1, D])
    din("w_gate_up", [D, 2 * DFF])
    din("w_down", [DFF, D])
    din("consts", [128, 4, 128])
    din("rope", [S, 2, 64])
    din("cmask", [128, 4, 512], BF16)
    din("blkc", [128, 16, 32])
    out_d = nc.dram_tensor("out", [S, D], F32, kind="ExternalOutput").ap()
    h1_d = nc.dram_tensor("h1_s", [S, D], F32, kind="Internal").ap()
    ya_d = nc.dram_tensor("ya_s", [8, 64, S], BF16, kind="Internal").ap()

    with ExitStack() as stack:
        P = Prog(nc, stack)
        ec = stack.enter_context
        identb = ec(nc.sbuf_tensor("identb", [128, 128], BF16))
        cst = ec(nc.sbuf_tensor("cst", [128, 4, 128], F32))
        r_c = Res("consts")
        P.dma("pool", lambda e: e.dma_start(out=identb[:], in_=dr["consts"][:, 0, :]), writes=[r_c])
        P.dma("sp", lambda e: e.dma_start(out=cst[:], in_=dr["consts"]), writes=[r_c])
        G = dict(identb=identb, cst=cst, r_c=r_c)

        r_ya = [[Res("ya_s%d_%d" % (i, h)) for h in range(8)] for i in range(S // 256)]
        r_h1 = [Res("h1_s%d" % i) for i in range(S // 256)]
        if "A" in phases:
            phase_A(nc, P, S, dr, ya_d, r_ya, G)
        else:
            with nc.sbuf_tensor("zt", [64, S], BF16) as zt:
                rz = Res("zt")
                P.memset("dve", zt[:], 0.0, [rz])
                for h in range(8):
                    P.dma("sp", lambda e, h=h: e.dma_start(out=ya_d[h], in_=zt[:]), reads=[rz],
                          writes=[r_ya[i][h] for i in range(S // 256)])
                P.barrier()
        P.barrier()
        if "B" in phases:
            phase_B(nc, P, S, dr, h1_d, ya_d, r_ya, r_h1, G)
            h_src = h1_d
        else:
            h_src = dr["x"]
        P.barrier()
        final_pts = phase_C(nc, P, S, h_src, r_h1, out_d, dr, G)
        P.finish(final_pts)
        P.emit()
    return nc


def load_norm_tile(nc, P, xt, r_x, s, gM, r_g, st8, r_st, col, epsb, r_eps, junk, r_junk, ub, r_ub):
    P.act(junk[:], xt[:, s, :], AF.Square, [r_x], [r_junk, r_st[col]], accum=st8[:, col:col + 1])
    P.act(st8[:, col:col + 1], st8[:, col:col + 1], AF.Sqrt, [r_st[col], r_eps], [r_st[col]],
          bias=epsb[:], scale=1.0 / D)
    P.recip(st8[:, col:col + 1], st8[:, col:col + 1], [r_st[col]], [r_st[col]])
    P.stt("dve", ub[:], xt[:, s, :], st8[:, col:col + 1], gM[:], ALU.mult, ALU.mult,
          [r_x, r_st[col], r_g], [r_ub])


def phase_A(nc, P, S, dr, ya_d, r_ya, G):
    T = 512
    NT = S // T
    NKT = S // 128
    identb, cst, r_c = G["identb"], G["cst"], G["r_c"]
    with ExitStack() as st:
        ec = st.enter_context

        def sbt(name, shape, dt):
            return ec(nc.sbuf_tensor(name, shape, dt))

        wA = sbt("wA", [128, 8, 1536], BF16)
        gM = sbt("gMa", [128, D], F32)
        cmk = sbt("cmk", [128, 4, 512], BF16)
        blkc = sbt("blkc_sb", [128, 16, 32], F32)
        epsb = sbt("epsb_a", [128, 1], F32)
        ropeT = [sbt("ropeT%d" % i, [128, 4, 2, 64], F32) for i in range(2)]
        kTa = sbt("kTa", [80, 8, S], BF16)
        vA = sbt("vA", [128, NKT, 8, 65], BF16)
        kmf = sbt("kmf", [64, 8], F32)
        kmT = sbt("kmT", [64, 8, 16], BF16)
        xs = [sbt("xs%d" % i, [128, 1, D], F32) for i in range(2)]
        ub = sbt("ub_a", [128, D], BF16)
        junk = sbt("junk_a", [128, D], BF16)
        st8 = sbt("st8_a", [128, 8], F32)
        uT = sbt("uTa", [128, 8, T], BF16)
        t1 = sbt("t1", [128, 8, 64], F32)
        t2 = sbt("t2", [128, 8, 64], F32)
        qtok = [sbt("qtok%d" % i, [128, 8, 80], BF16) for i in range(2)]
        ktok = [sbt("ktok%d" % i, [128, 8, 80], BF16) for i in range(2)]
        qTa = sbt("qTa", [80, 8, T], BF16)
        gsb = sbt("gsb_a", [128, 8, 16], F32)
        top8 = sbt("top8", [128, 8, 8], F32)
        PT = [sbt("PT%d" % i, [128, 512], BF16) for i in range(4)]
        oT = [sbt("oT%d" % i, [65, 512], F32) for i in range(2)]
        yo = [sbt("yo%d" % i, [64, 512], BF16) for i in range(2)]
        PB = [ec(nc.psum_tensor("PBa%d" % i, [128, 512], F32)) for i in range(8)]
        PBb = [PB[i].bitcast(BF16) for i in range(8)]

        R = lambda n: Res(n)
        r_wA = [R("wA%d" % c) for c in range(8)]
        r_k = R("constsA")
        r_rope = [R("rope%d" % i) for i in range(2)]
        r_kTa = [R("kTa%d" % i) for i in range(NKT)]
        r_vA = [R("vA%d" % i) for i in range(NKT)]
        r_kmf, r_kmT = R("kmf"), R("kmT")
        r_xs = [R("xs%d" % i) for i in range(2)]
        r_ub, r_junk = R("ub"), R("junk")
        r_st = [R("st%d" % i) for i in range(8)]
        r_uT = [R("uT%d" % s) for s in range(4)]
        r_t1, r_t2 = R("t1"), R("t2")
        r_qtok = [R("qtok%d" % i) for i in range(2)]
        r_ktok = [R("ktok%d" % i) for i in range(2)]
        r_qTa = [R("qTa%d" % s) for s in range(4)]
        r_gsb, r_top8 = R("gsb"), R("top8")
        r_PT = [R("PT%d" % i) for i in range(4)]
        r_oT = [R("oT%d" % i) for i in range(2)]
        r_yo = [R("yo%d" % i) for i in range(2)]
        r_pb = [R("pba%d" % i) for i in range(8)]

        wv = dr["w_in"].rearrange("(c p) f -> p c f", p=128)
        for c in range(8):
            P.dma("pool", lambda e, c=c: e.dma_start(out=wA[:, c, :], in_=wv[:, c, 0:1536]), writes=[r_wA[c]])
        P.dma("sp", lambda e: e.dma_start(out=gM[:], in_=dr["norm_mix_g"].to_broadcast([128, D])), writes=[r_k])
        P.dma("sp", lambda e: e.dma_start(out=cmk[:], in_=dr["cmask"]), writes=[r_k])
        P.dma("sp", lambda e: e.dma_start(out=blkc[:], in_=dr["blkc"]), writes=[r_k])
        P.memset("dve", epsb[:], EPS, [r_k])
        P.memset("dve", kmT[:], 0.0, [r_kmT])
        P.memset("dve", vA[:], 1.0, r_vA)

        xv = dr["x"].rearrange("(n p) d -> n p d", p=128)
        rope_v = dr["rope"].rearrange("(n s p) a d -> n p s a d", s=4, p=128)
        onesb = sbt("onesb", [128, 64], BF16)
        hiT = [sbt("hiT%d" % i, [65, 512], BF16) for i in range(2)]
        loT = [sbt("loT%d" % i, [65, 512], BF16) for i in range(2)]
        qTb = sbt("qTa2", [80, 8, T], BF16)
        qTas = [qTa, qTb]
        r_qTas = [r_qTa, [R("qTb%d" % s) for s in range(4)]]
        r_hl = [R("hl%d" % i) for i in range(2)]
        P.memset("dve", onesb[:], 1.0, [r_k])
        P.dma("sp", lambda e: e.dma_start(out=xs[0][:, 0, :], in_=xv[0]), writes=[r_xs[0]])

        def rope_ops(pz, rpz, rb, s, dst, rdst):
            z3 = pz.rearrange("p (h d) -> p h d", h=8)
            C2 = ropeT[rb][:, s, 0, :].unsqueeze(1).to_broadcast([128, 8, 64])
            Sa = ropeT[rb][:, s, 1, 0:32].unsqueeze(1).to_broadcast([128, 8, 32])
            Sb = ropeT[rb][:, s, 1, 32:64].unsqueeze(1).to_broadcast([128, 8, 32])
            P.tt("dve", t1[:], z3, C2, ALU.mult, [rpz, r_rope[rb]], [r_t1])
            P.tt("dve", t2[:, :, 0:32], z3[:, :, 32:64], Sa, ALU.mult, [rpz, r_rope[rb]], [r_t2])
            P.tt("dve", t2[:, :, 32:64], z3[:, :, 0:32], Sb, ALU.mult, [rpz, r_rope[rb]], [r_t2])
            P.tt("dve", dst[:, :, 0:64], t1[:], t2[:], ALU.add, [r_t1, r_t2], [rdst])

        def tr_group(src_fn, rsrc, rows, dst_fn, rdst, prow=None):
            for g4 in range(2):
                for j in range(4):
                    P.tr(PBb[g4][0:rows, j * 128:(j + 1) * 128], src_fn(g4 * 4 + j), identb[:], [rsrc, r_c],
                         [r_pb[g4]], inc=(j == 3))
            for g4 in range(2):
                lo_, hi_ = prow if prow else (0, rows)
                P.cp("dve", dst_fn(g4, lo_, hi_), PBb[g4][lo_:hi_, 0:512].rearrange("p (j t) -> p j t", j=4),
                     [r_pb[g4]], [rdst])

        def prologue(t):
            rb = t % 2
            qT_ = qTas[t % 2]
            rq_ = r_qTas[t % 2]
            P.dma("sp", lambda e, t=t, rb=rb: e.dma_start(out=ropeT[rb][:], in_=rope_v[t]), writes=[r_rope[rb]])
            for s in range(4):
                n = t * 4 + s
                sub = slice(s * 128, (s + 1) * 128)
                ksub = slice(n * 128, (n + 1) * 128)
                blk = n // 2
                if n + 1 < NKT:
                    P.dma("sp", lambda e, n=n: e.dma_start(out=xs[(n + 1) % 2][:, 0, :], in_=xv[n + 1]),
                          writes=[r_xs[(n + 1) % 2]])
                xt = xs[n % 2]
                load_norm_tile(nc, P, xt, r_xs[n % 2], 0, gM, r_k, st8, r_st, s, epsb, r_k, junk, r_junk, ub, r_ub)
                yield
                for g4 in range(2):
                    for j in range(4):
                        c = g4 * 4 + j
                        P.tr(PBb[g4][:, j * 128:(j + 1) * 128], ub[:, c * 128:(c + 1) * 128], identb[:],
                             [r_ub, r_c], [r_pb[g4]], inc=(j == 3))
                for g4 in range(2):
                    P.cp("dve", uT[:, g4 * 4:(g4 + 1) * 4, sub],
                         PBb[g4][:, 0:512].rearrange("p (j t) -> p j t", j=4), [r_pb[g4]], [r_uT[s]])
                yield
                kb, rkb = ktok[s % 2], r_ktok[s % 2]
                pz = PB[4][:]
                for c in range(8):
                    P.mm(pz, uT[:, c, sub], wA[:, c, A_KA:A_KA + 512], c == 0, c == 7, [r_uT[s], r_wA[c]], [r_pb[4]])
                rope_ops(pz, r_pb[4], rb, s, kb, rkb)
                P.cp("dve", kb[:, :, 64:80], blkc[:, blk, 16:32].unsqueeze(1).to_broadcast([128, 8, 16]),
                     [r_k], [rkb])
                yield
                tr_group(lambda h: kb[:, h, :], rkb, 80,
                         lambda g4, lo_, hi_: kTa[lo_:hi_, g4 * 4:(g4 + 1) * 4, ksub], r_kTa[n])
                if s % 2 == 1:
                    P.op("dve", lambda e, blk=blk: e.tensor_reduce(
                        out=kmf[:], in_=kTa[0:64, :, blk * 256:(blk + 1) * 256], axis=AX.X, op=ALU.add),
                        [r_kTa[n - 1], r_kTa[n]], [r_kmf])
                    P.ts("dve", kmT[:, :, blk], kmf[:], 1.0 / 256, ALU.mult, [r_kmf], [r_kmT])
                yield
                for c in range(8):
                    P.mm(pz, uT[:, c, sub], wA[:, c, A_VA:A_VA + 512], c == 0, c == 7, [r_uT[s], r_wA[c]], [r_pb[4]])
                P.cp("dve", vA[:, n, :, 0:64], pz.rearrange("p (h d) -> p h d", h=8), [r_pb[4]], [r_vA[n]])
                yield
                qb, rqb = qtok[s % 2], r_qtok[s % 2]
                for c in range(8):
                    P.mm(pz, uT[:, c, sub], wA[:, c, A_QA:A_QA + 512], c == 0, c == 7, [r_uT[s], r_wA[c]], [r_pb[4]])
                rope_ops(pz, r_pb[4], rb, s, qb, rqb)
                yield
                tr_group(lambda h: qb[:, h, 0:64], rqb, 64,
                         lambda g4, lo_, hi_: qT_[lo_:hi_, g4 * 4:(g4 + 1) * 4, sub], rq_[s])
                yield
                pgt = PB[5][:, 0:128].rearrange("p (h j) -> p h j", h=8)
                for h in range(8):
                    P.mm(pgt[:, h, :], qT_[0:64, h, sub], kmT[:, h, :], True, True, [rq_[s], r_kmT], [r_pb[5]],
                         inc=(h == 7))
                P.tt("dve", gsb[:], pgt, blkc[:, blk, 0:16].unsqueeze(1).to_broadcast([128, 8, 16]), ALU.add,
                     [r_pb[5], r_k], [r_gsb])
                for h in range(8):
                    P.op("dve", lambda e, h=h: e.max(out=top8[:, h, :], in_=gsb[:, h, :]), [r_gsb], [r_top8])
                P.tt("dve", qb[:, :, 64:80], gsb[:], top8[:, :, 3:4].to_broadcast([128, 8, 16]), ALU.is_lt,
                     [r_gsb, r_top8], [rqb])
                yield
                tr_group(lambda h: qb[:, h, :], rqb, 80,
                         lambda g4, lo_, hi_: qT_[lo_:hi_, g4 * 4:(g4 + 1) * 4, sub], rq_[s], prow=(64, 80))
                yield

        def attention(t):
            qT_ = qTas[t % 2]
            rq_ = r_qTas[t % 2]
            nkt = 4 * t + 4
            items = [(h, kt) for h in range(8) for kt in range(nkt)]
            LA = 1
            deferred = {}

            def emit_S(i):
                h, kt = items[i]
                bi = 2 + i % 2
                P.mm(PB[bi][:], kTa[:, h, kt * 128:(kt + 1) * 128], qT_[:, h, :], True, True,
                     [r_kTa[kt]] + rq_, [r_pb[bi]])

            def emit_PV(i):
                h, kt = items[i]
                bi = 2 + i % 2
                pi = i % 4
                po = PB[6 + h % 2]
                rpo = r_pb[6 + h % 2]
                P.act(PT[pi][:], PB[bi][:], AF.Exp, [r_pb[bi]], [r_PT[pi]], scale=0.125)
                if kt >= 4 * t:
                    P.tt("pool", PT[pi][:], PT[pi][:], cmk[:, kt - 4 * t, :], ALU.mult, [r_PT[pi], r_k],
                         [r_PT[pi]])
                P.mm(po[0:65, :], vA[:, kt, h, :], PT[pi][:], kt == 0, kt == nkt - 1, [r_vA[kt], r_PT[pi]], [rpo])
                if kt == nkt - 1:
                    ob, rob = oT[h % 2], r_oT[h % 2]
                    P.cp("act", ob[:], po[0:65, :], [rpo], [rob])
                    P.recip(ob[64:65, :], ob[64:65, :], [rob], [rob])
                    P.cp("dve", hiT[h % 2][64:65, :], ob[64:65, :], [rob], [r_hl[h % 2]])
                    P.tt("dve", loT[h % 2][64:65, :], ob[64:65, :], hiT[h % 2][64:65, :], ALU.subtract,
                         [rob, r_hl[h % 2]], [r_hl[h % 2]])
                    deferred.setdefault(min(i + 4, len(items) - 1), []).append(h)

            def emit_epi(h):
                ob, rob = oT[h % 2], r_oT[h % 2]
                P.mm(PB[5][0:64, :], onesb[64:65, :], hiT[h % 2][64:65, :], True, False, [r_k, r_hl[h % 2]],
                     [r_pb[5]], inc=False)
                P.mm(PB[5][0:64, :], onesb[64:65, :], loT[h % 2][64:65, :], False, True, [r_k, r_hl[h % 2]],
                     [r_pb[5]])
                yb_ = yo[h % 2]
                P.tt("dve", yb_[:], ob[0:64, :], PB[5][0:64, :], ALU.mult, [rob, r_pb[5]], [r_yo[h % 2]])
                P.dma("sp", lambda e, h=h, t=t, yb_=yb_: e.dma_start(out=ya_d[h, :, t * T:(t + 1) * T], in_=yb_[:]),
                      reads=[r_yo[h % 2]], writes=[r_ya[2 * t][h], r_ya[2 * t + 1][h]])

            for i in range(min(LA, len(items))):
                emit_S(i)
            for i in range(len(items)):
                if i + LA < len(items):
                    emit_S(i + LA)
                emit_PV(i)
                for hh in deferred.pop(i, []):
                    emit_epi(hh)
                yield
            assert not deferred

        for _ in prologue(0):
            pass
        for t in range(NT):
            n_main = 8 * (4 * t + 4)
            side = prologue(t + 1) if t + 1 < NT else iter(())
            n_side = 36
            done = 0
            for i, _ in enumerate(attention(t)):
                want = ((i + 1) * n_side) // n_main
                while done < want:
                    next(side, None)
                    done += 1
            for _ in side:
                pass


def phase_B(nc, P, S, dr, h1_d, ya_d, r_ya, r_h1, G):
    T = 256
    NT = S // T
    identb, cst, r_c = G["identb"], G["cst"], G["r_c"]
    idf, tri, trim, onesf = cst[:, 0, :], cst[:, 1, :], cst[:, 2, :], cst[:, 3, :]
    with ExitStack() as st:
        ec = st.enter_context

        def sbt(name, shape, dt):
            return ec(nc.sbuf_tensor(name, shape, dt))

        wB = sbt("wB", [128, 8, WB], BF16)
        wpa = sbt("wpa", [128, 4, D], BF16)
        wpb = sbt("wpb", [128, 4, D], BF16)
        wo = sbt("wo", [128, 8, D], BF16)
        gM = sbt("gMb", [128, D], F32)
        gN = sbt("gN", [128, 512], F32)
        bias8 = sbt("bias8_sb", [128, 8], F32)
        cw = sbt("cw", [128, 8, 4], F32)
        trib = sbt("trib", [128, 128], BF16)
        epsb = sbt("epsb_b", [128, 1], F32)
        oneb = sbt("oneb", [128, 1], F32)
        lncb = sbt("lncb", [128, 1], F32)
        xb = [sbt("xb%d" % i, [128, 2, D], F32) for i in range(3)]
        yaT = [sbt("yaT%d" % i, [128, 4, T], BF16) for i in range(3)]
        uT = [sbt("uTb%d" % i, [128, 8, T], BF16) for i in range(2)]
        ub = sbt("ub_b", [128, D], BF16)
        junk = sbt("junk_b", [128, D], BF16)
        st8 = sbt("st8_b", [128, 8], F32)
        gsb = sbt("gsb", [128, 2, 8], F32)
        e1 = sbt("e1", [128, 2, 4], F32)
        nlf = sbt("nlf", [128, 2, 4], F32)
        arg = sbt("arg", [128, 2, 4], F32)
        ksc = sbt("ksc", [128, 2, 4], F32)
        wc = sbt("wc", [128, 2, 4], F32)
        EBt = sbt("EBt", [128, 4, T], F32)
        EAt = sbt("EAt", [128, 4, T], F32)
        xc = sbt("xc", [128, 8, T + 3], F32)
        yc = [sbt("yc%d" % i, [128, T], F32) for i in range(2)]
        slq = [sbt("slq%d" % i, [128, T], BF16) for i in range(2)]
        kT0 = sbt("kT0", [128, 4, T], BF16)
        qT = sbt("qTb", [128, 4, T], BF16)
        kT = sbt("kTb", [128, 4, T], BF16)
        vaug = sbt("vaug", [128, 2, 4, 129], BF16)
        sgo = sbt("sgo", [128, 2, 512], F32)
        kS = sbt("kS", [128, 4, 128], BF16)
        scT = sbt("scT", [128, 4, 128], BF16)
        C32 = sbt("C32", [128, 4, 129], F32)
        Cbf = sbt("Cbf", [128, 4, 129], BF16)
        rr = sbt("rr", [128, 4], F32)
        ssq = sbt("ssq", [128, 4], F32)
        t4 = sbt("t4", [128, 4], F32)
        sc4 = sbt("sc4", [128, 4], F32)
        ybf = sbt("ybf", [128, 512], F32)
        ybb = sbt("ybb", [128, 512], BF16)
        ybT = sbt("ybT", [128, 4, T], BF16)
        sga = [sbt("sga%d" % i, [128, T], F32) for i in range(2)]
        sgb = [sbt("sgb%d" % i, [128, T], F32) for i in range(2)]
        m1 = [sbt("m1_%d" % i, [128, T], F32) for i in range(2)]
        m2 = [sbt("m2_%d" % i, [128, T], F32) for i in range(2)]
        mT = sbt("mT", [128, 8, T], BF16)
        PB = [ec(nc.psum_tensor("PBb%d" % i, [128, 512], F32)) for i in range(8)]

        R = lambda n: Res(n)
        r_wB = [R("wB%d" % c) for c in range(8)]
        r_wpa, r_wpb, r_wo = R("wpa"), R("wpb"), [R("wo%d" % c) for c in range(8)]
        r_k = R("constsB")
        r_xb = [[R("xb%d_%d" % (i, s)) for s in range(2)] for i in range(3)]
        r_yaT = [R("yaT%d" % i) for i in range(3)]
        r_uT = [[R("uTb%d_%d" % (i, s)) for s in range(2)] for i in range(2)]
        r_ub, r_junk = R("ub"), R("junk")
        r_st = [R("st%d" % i) for i in range(8)]
        r_gs = [R("gsb%d" % s) for s in range(2)]
        r_e1 = [R("e1%d" % s) for s in range(2)]
        r_nlf = [R("nlf%d" % s) for s in range(2)]
        r_arg = [R("arg%d" % s) for s in range(2)]
        r_ksc = [R("ksc%d" % s) for s in range(2)]
        r_wc = [R("wc%d" % s) for s in range(2)]
        r_EB = [[R("EB%d_%d" % (h, s)) for s in range(2)] for h in range(4)]
        r_EA = [[R("EA%d_%d" % (h, s)) for s in range(2)] for h in range(4)]
        r_xc = [R("xc%d" % m) for m in range(8)]
        r_yc = [R("yc%d" % i) for i in range(2)]
        r_slq = [R("slq%d" % i) for i in range(2)]
        r_kT0 = [R("kT0%d" % h) for h in range(4)]
        r_qT = [R("qT%d" % h) for h in range(4)]
        r_kT = [R("kT%d" % h) for h in range(4)]
        r_va = [R("vaug%d" % s) for s in range(2)]
        r_sgo = [R("sgo%d" % s) for s in range(2)]
        r_kS = [R("kS%d" % h) for h in range(4)]
        r_scT = [R("scT%d" % h) for h in range(4)]
        r_C32 = [R("C32%d" % h) for h in range(4)]
        r_Cbf = [R("Cbf%d" % h) for h in range(4)]
        r_rr, r_ssq, r_t4, r_sc4 = R("rr"), R("ssq"), R("t4"), R("sc4")
        r_ybf, r_ybb = R("ybf"), R("ybb")
        r_ybT = [R("ybT%d" % s) for s in range(2)]
        r_sga = [R("sga%d" % i) for i in range(2)]
        r_sgb = [R("sgb%d" % i) for i in range(2)]
        r_m1 = [R("m1%d" % i) for i in range(2)]
        r_m2 = [R("m2%d" % i) for i in range(2)]
        r_mT = [R("mT%d" % f) for f in range(8)]
        r_pb = [R("pb%d" % i) for i in range(8)]
        PBb = [PB[i].bitcast(BF16) for i in range(8)]

        wv = dr["w_in"].rearrange("(c p) f -> p c f", p=128)
        for c in range(8):
            P.dma("pool", lambda e, c=c: e.dma_start(out=wB[:, c, :], in_=wv[:, c, B_OFF:IN_W]), writes=[r_wB[c]])
        P.dma("pool", lambda e: e.dma_start(out=wpa[:], in_=dr["w_proj_a"].rearrange("(c p) f -> p c f", p=128)),
              writes=[r_wpa])
        P.dma("pool", lambda e: e.dma_start(out=wpb[:], in_=dr["w_proj_b"].rearrange("(c p) f -> p c f", p=128)),
              writes=[r_wpb])
        wov = dr["w_out"].rearrange("(c p) f -> p c f", p=128)
        for c in range(8):
            P.dma("pool", lambda e, c=c: e.dma_start(out=wo[:, c, :], in_=wov[:, c, :]), writes=[r_wo[c]])
        P.dma("pool", lambda e: e.dma_start(out=trib[:], in_=dr["consts"][:, 1, :]), writes=[r_k])
        P.dma("sp", lambda e: e.dma_start(out=gM[:], in_=dr["norm_mix_g"].to_broadcast([128, D])), writes=[r_k])
        P.dma("sp", lambda e: e.dma_start(out=gN[:], in_=dr["mlstm_norm_g"].to_broadcast([128, 512])), writes=[r_k])
        P.dma("sp", lambda e: e.dma_start(out=bias8[:], in_=dr["bias8"].to_broadcast([128, 8])), writes=[r_k])
        P.dma("sp", lambda e: e.dma_start(out=cw[:], in_=dr["conv_wT"].rearrange("(m p) j -> p m j", p=128)),
              writes=[r_k])
        P.memset("dve", epsb[:], EPS, [r_k])
        P.memset("dve", oneb[:], 1.0, [r_k])
        P.memset("dve", lncb[:], LNC, [r_k])
        P.memset("dve", C32[:], 0.0, r_C32)
        P.memset("dve", Cbf[:], 0.0, r_Cbf)
        P.memset("dve", vaug[:], 1.0, r_va)
        P.memset("dve", xc[:], 0.0, r_xc)

        xv = dr["x"].rearrange("(t s p) d -> t p s d", s=2, p=128)
        hv = h1_d.rearrange("(t s p) d -> t p s d", s=2, p=128)
        ya_v = ya_d.rearrange("h d s -> (h d) s").rearrange("(c p) s -> p c s", p=128)

        ybT2 = sbt("ybT2", [128, 4, T], BF16)
        ybTs = [ybT, ybT2]
        r_ybTs = [r_ybT, [R("ybTb%d" % s) for s in range(2)]]

        def stage_A(t):
            b = t % 2
            xt = xb[t % 3]
            uTt = uT[b]
            for s in range(2):
                sub = slice(s * 128, (s + 1) * 128)
                load_norm_tile(nc, P, xt, r_xb[t % 3][s], s, gM, r_k, st8, r_st, s, epsb, r_k, junk, r_junk, ub, r_ub)
                yield
                for g4 in range(2):
                    for j in range(4):
                        c = g4 * 4 + j
                        P.tr(PBb[2 + g4][:, j * 128:(j + 1) * 128], ub[:, c * 128:(c + 1) * 128],
                             identb[:], [r_ub, r_c], [r_pb[2 + g4]], inc=(j == 3))
                for g4 in range(2):
                    P.cp("act", uTt[:, g4 * 4:(g4 + 1) * 4, sub],
                         PBb[2 + g4][:, 0:512].rearrange("p (j t) -> p j t", j=4),
                         [r_pb[2 + g4]], [r_uT[b][s]])
                yield
                pg = PB[4][:, 0:8]
                for c in range(8):
                    P.mm(pg, uTt[:, c, sub], wB[:, c, IBo:IBo + 8], c == 0, c == 7,
                         [r_uT[b][s], r_wB[c]], [r_pb[4]])
                P.tt("dve", gsb[:, s, :], pg, bias8[:], ALU.add, [r_pb[4], r_k], [r_gs[s]])
                P.act(e1[:, s, :], gsb[:, s, 4:8], AF.Exp, [r_gs[s]], [r_e1[s]], scale=-1.0)
                P.act(nlf[:, s, :], e1[:, s, :], AF.Ln, [r_e1[s], r_k], [r_nlf[s]], bias=oneb[:])
                pv = PB[2][:]
                for c in range(8):
                    P.mm(pv, uTt[:, c, sub], wB[:, c, VBo:VBo + 512], c == 0, c == 7,
                         [r_uT[b][s], r_wB[c]], [r_pb[2]])
                P.cp("act", vaug[:, s, :, 0:128], pv.rearrange("p (h e) -> p h e", h=4), [r_pb[2]], [r_va[s]])
                yield
                po = PB[3][:]
                for c in range(8):
                    P.mm(po, uTt[:, c, sub], wB[:, c, OBo:OBo + 512], c == 0, c == 7,
                         [r_uT[b][s], r_wB[c]], [r_pb[3]])
                P.act(sgo[:, s, :], po, AF.Sigmoid, [r_pb[3]], [r_sgo[s]])
                P.mm(PB[4][:, 8:12], trim, nlf[:, s, :], True, True, [r_c, r_nlf[s]], [r_pb[4]], inc=False)
                P.mm(PB[4][:, 12:16], onesf, nlf[:, s, :], True, True, [r_c, r_nlf[s]], [r_pb[4]])
                P.tt("dve", arg[:, s, :], PB[4][:, 8:12], gsb[:, s, 0:4], ALU.add, [r_pb[4], r_gs[s]], [r_arg[s]])
                P.act(ksc[:, s, :], arg[:, s, :], AF.Exp, [r_arg[s], r_k], [r_ksc[s]], bias=lncb[:])
                P.act(wc[:, s, :], PB[4][:, 12:16], AF.Exp, [r_pb[4]], [r_wc[s]], scale=-1.0)
                yield
                for h in range(4):
                    i0 = (2 * h) % 3
                    i1 = (2 * h + 1) % 3
                    pe_b = PB[5 + i0][:, 0:128]
                    pe_a = PB[5 + i1][:, 0:128]
                    nb_l = nlf[:, s, h:h + 1].to_broadcast([128, 128])
                    ip_l = gsb[:, s, h:h + 1].to_broadcast([128, 128])
                    P.mm(pe_b, nb_l, tri, True, True, [r_nlf[s], r_c], [r_pb[5 + i0]])
                    P.mm(pe_a, ip_l, idf, True, False, [r_gs[s], r_c], [r_pb[5 + i1]], inc=False)
                    P.mm(pe_a, nb_l, tri, False, True, [r_nlf[s], r_c], [r_pb[5 + i1]])
                    P.act(EBt[:, h, sub], pe_b, AF.Exp, [r_pb[5 + i0]], [r_EB[h][s]], scale=-1.0)
                    P.act(EAt[:, h, sub], pe_a, AF.Exp, [r_pb[5 + i1], r_k], [r_EA[h][s]], bias=lncb[:])
                    if h % 2 == 1:
                        yield
            for m in range(8):
                hh = m % 4
                pq = PB[2 + m % 2][:, 0:256]
                for c in range(8):
                    P.mm(pq, wB[:, c, QKB + m * 128:QKB + (m + 1) * 128], uTt[:, c, :], c == 0, c == 7,
                         [r_wB[c]] + r_uT[b], [r_pb[2 + m % 2]])
                P.cp("act", xc[:, m, 3:3 + T], pq, [r_pb[2 + m % 2]], [r_xc[m]])
                y = yc[m % 2]
                ry = r_yc[m % 2]
                P.ts("dve", y[:], xc[:, m, 0:T], cw[:, m, 0:1], ALU.mult, [r_xc[m], r_k], [ry])
                for j in range(1, 4):
                    P.stt("dve", y[:], xc[:, m, j:j + T], cw[:, m, j:j + 1], y[:], ALU.mult, ALU.add,
                          [r_xc[m], r_k, ry], [ry])
                P.cp("dve", xc[:, m, 0:3], xc[:, m, T:T + 3], [r_xc[m]], [r_xc[m]])
                if m < 4:
                    sl = slq[m % 2]
                    P.act(sl[:], y[:], AF.Silu, [ry], [r_slq[m % 2]])
                    P.tt("dve", qT[:, hh, :], sl[:], EBt[:, hh, :], ALU.mult,
                         [r_slq[m % 2]] + r_EB[hh], [r_qT[hh]])
                else:
                    P.act(kT0[:, hh, :], y[:], AF.Silu, [ry], [r_kT0[hh]])
                    P.tt("dve", kT[:, hh, :], kT0[:, hh, :], EAt[:, hh, :], ALU.mult,
                         [r_kT0[hh]] + r_EA[hh], [r_kT[hh]])
                yield
            for s in range(2):
                sub = slice(s * 128, (s + 1) * 128)
                pnum = [PB[6][:, 0:129], PB[7][:, 0:129], PB[6][:, 256:385], PB[7][:, 256:385]]
                r_pn = [r_pb[6], r_pb[7], r_pb[6], r_pb[7]]
                pdc = [PB[2][:, 256:385], PB[3][:, 256:385], PB[2][:, 256:385], PB[3][:, 256:385]]
                r_pd = [r_pb[2], r_pb[3], r_pb[2], r_pb[3]]
                for h in range(4):
                    ptk = PBb[2 + h % 2][:, 0:128]
                    P.tr(ptk, kT0[:, h, sub], identb[:], [r_kT0[h], r_c], [r_pb[2 + h % 2]])
                    pS = PB[4 + h % 2][:, 0:128]
                    P.mm(pS, kT[:, h, sub], qT[:, h, sub], True, True, [r_kT[h], r_qT[h]], [r_pb[4 + h % 2]])
                    P.ts("dve", kS[:, h, :], ptk, ksc[:, s, h:h + 1], ALU.mult, [r_pb[2 + h % 2], r_ksc[s]],
                         [r_kS[h]])
                    P.tt("dve", scT[:, h, :], pS, trib[:], ALU.mult, [r_pb[4 + h % 2], r_k], [r_scT[h]])
                    yield
                    P.mm(pnum[h], qT[:, h, sub], Cbf[:, h, :], True, False, [r_qT[h], r_Cbf[h]], [r_pn[h]],
                         inc=False)
                    P.mm(pnum[h], scT[:, h, :], vaug[:, s, h, :], False, True, [r_scT[h], r_va[s]], [r_pn[h]])
                    P.mm(pdc[h], kS[:, h, :], vaug[:, s, h, :], True, True, [r_kS[h], r_va[s]], [r_pd[h]])
                    P.stt("dve", C32[:, h, :], C32[:, h, :], wc[:, s, h:h + 1], pdc[h], ALU.mult, ALU.add,
                          [r_C32[h], r_wc[s], r_pd[h]], [r_C32[h]])
                    P.cp("act", Cbf[:, h, :], C32[:, h, :], [r_C32[h]], [r_Cbf[h]])
                    yield
                for h in range(4):
                    P.act(junk[:, 0:128], pnum[h][:, 0:128], AF.Square, [r_pn[h]], [r_junk, r_ssq],
                          accum=ssq[:, h:h + 1])
                for h in range(4):
                    P.cp("dve", t4[:, h:h + 1], pnum[h][:, 128:129], [r_pn[h]], [r_t4])
                P.stt("dve", rr[:], t4[:], -1.0, t4[:], ALU.mult, ALU.max, [r_t4], [r_rr])
                P.ts("dve", rr[:], rr[:], 1.0, ALU.max, [r_rr], [r_rr])
                P.recip(rr[:], rr[:], [r_rr], [r_rr])
                P.tt("dve", t4[:], rr[:], rr[:], ALU.mult, [r_rr], [r_t4])
                P.tt("dve", t4[:], t4[:], ssq[:], ALU.mult, [r_t4, r_ssq], [r_t4])
                P.act(t4[:], t4[:], AF.Sqrt, [r_t4, r_k], [r_t4], bias=epsb[:], scale=1.0 / 128)
                P.recip(t4[:], t4[:], [r_t4], [r_t4])
                P.tt("dve", sc4[:], t4[:], rr[:], ALU.mult, [r_t4, r_rr], [r_sc4])
                yield
                for h in range(4):
                    P.stt("dve", ybf[:, h * 128:(h + 1) * 128], pnum[h][:, 0:128], sc4[:, h:h + 1],
                          gN[:, h * 128:(h + 1) * 128], ALU.mult, ALU.mult, [r_pn[h], r_sc4, r_k], [r_ybf])
                P.tt("dve", ybb[:], ybf[:], sgo[:, s, :], ALU.mult, [r_ybf, r_sgo[s]], [r_ybb])
                yield
                for h in range(4):
                    P.tr(PBb[3][:, h * 128:(h + 1) * 128], ybb[:, h * 128:(h + 1) * 128], identb[:],
                         [r_ybb, r_c], [r_pb[3]], inc=(h == 3))
                P.cp("act", ybTs[b][:, :, sub], PBb[3][:, 0:512].rearrange("p (j t) -> p j t", j=4),
                     [r_pb[3]], [r_ybTs[b][s]])
                yield

        def stage_M(t):
            b = t % 2
            xt = xb[t % 3]
            uTt = uT[b]
            for f in range(8):
                fs = slice(f * 128, (f + 1) * 128)
                pga = PB[0][:, 0:256]
                pgb = PB[0][:, 256:512]
                ppa = PB[1][:, 0:256]
                ppb = PB[1][:, 256:512]
                for c in range(8):
                    P.mm(pga, wB[:, c, GAo + f * 128:GAo + (f + 1) * 128], uTt[:, c, :], c == 0, c == 7,
                         [r_wB[c]] + r_uT[b], [r_pb[0]])
                for c in range(8):
                    P.mm(pgb, wB[:, c, GBo + f * 128:GBo + (f + 1) * 128], uTt[:, c, :], c == 0, c == 7,
                         [r_wB[c]] + r_uT[b], [r_pb[0]])
                for pr in range(4):
                    P.mm(ppa, wpa[:, pr, fs], yaT[t % 3][:, pr, :], pr == 0, pr == 3, [r_wpa, r_yaT[t % 3]], [r_pb[1]])
                for pr in range(4):
                    P.mm(ppb, wpb[:, pr, fs], ybTs[b][:, pr, :], pr == 0, pr == 3, [r_wpb] + r_ybTs[b], [r_pb[1]])
                k2 = f % 2
                P.act(sga[k2][:], pga, AF.Sigmoid, [r_pb[0]], [r_sga[k2]])
                P.act(sgb[k2][:], pgb, AF.Sigmoid, [r_pb[0]], [r_sgb[k2]])
                P.tt("dve", m1[k2][:], sga[k2][:], ppa, ALU.mult, [r_sga[k2], r_pb[1]], [r_m1[k2]])
                P.tt("dve", m2[k2][:], sgb[k2][:], ppb, ALU.mult, [r_sgb[k2], r_pb[1]], [r_m2[k2]])
                P.tt("dve", mT[:, f, :], m1[k2][:], m2[k2][:], ALU.add, [r_m1[k2], r_m2[k2]], [r_mT[f]])
                yield
            for s in range(2):
                sub = slice(s * 128, (s + 1) * 128)
                for n in range(2):
                    po2 = PB[n][:]
                    rpo = [r_pb[n]]
                    for c in range(8):
                        P.mm(po2, mT[:, c, sub], wo[:, c, n * 512:(n + 1) * 512], c == 0, c == 7,
                             [r_mT[c], r_wo[c]], rpo)
                    P.tt("dve", xt[:, s, n * 512:(n + 1) * 512], xt[:, s, n * 512:(n + 1) * 512], po2, ALU.add,
                         rpo + [r_xb[t % 3][s]], [r_xb[t % 3][s]])
                    yield
            P.dma("sp", lambda e, t=t, xt=xt: e.dma_start(out=hv[t], in_=xt[:]), reads=r_xb[t % 3], writes=[r_h1[t]])

        def prefetch(tt_):
            if tt_ < NT:
                P.dma("sp", lambda e: e.dma_start(out=xb[tt_ % 3][:], in_=xv[tt_]), writes=r_xb[tt_ % 3])
                P.dma("sp", lambda e: e.dma_start(out=yaT[tt_ % 3][:], in_=ya_v[:, :, tt_ * T:(tt_ + 1) * T]),
                      reads=r_ya[tt_], writes=[r_yaT[tt_ % 3]])

        prefetch(0)
        prefetch(1)
        for _ in stage_A(0):
            pass
        for t in range(NT):
            if t + 1 < NT:
                prefetch(t + 2)
                side = stage_M(t)
                n_side = 12
                done = 0
                for i, _ in enumerate(stage_A(t + 1)):
                    want = ((i + 1) * n_side) // 46
                    while done < want:
                        next(side, None)
                        done += 1
                for _ in side:
                    pass
            else:
                for _ in stage_M(t):
                    pass


def phase_C(nc, P, S, h_d, r_h1, out_d, dr, G):
    T = 256
    NT = S // T
    NF = DFF // 128
    identb, r_ident = G["identb"], G["r_c"]
    g_ffn_d, g_fin_d, wgu_d, wdn_d = dr["norm_ffn_g"], dr["norm_final_g"], dr["w_gate_up"], dr["w_down"]
    with ExitStack() as st:
        ec = st.enter_context
        wgu = ec(nc.sbuf_tensor("wgu", [128, 8, 2 * DFF], BF16))
        wdn = ec(nc.sbuf_tensor("wdn", [128, NF, D], BF16))
        gF = ec(nc.sbuf_tensor("gF", [128, D], F32))
        gL = ec(nc.sbuf_tensor("gL", [128, D], F32))
        hb = [ec(nc.sbuf_tensor("hb%d" % i, [128, 2, D], F32)) for i in range(2)]
        ub = ec(nc.sbuf_tensor("ub", [128, D], BF16))
        uT = [ec(nc.sbuf_tensor("uT%d" % i, [128, 8, T], BF16)) for i in range(2)]
        aT = ec(nc.sbuf_tensor("aT", [128, NF, T], BF16))
        sg = [ec(nc.sbuf_tensor("sg%d" % i, [128, T], BF16)) for i in range(2)]
        junk = ec(nc.sbuf_tensor("junk", [128, D], BF16))
        st8 = ec(nc.sbuf_tensor("st8", [128, 8], F32))
        epsb = ec(nc.sbuf_tensor("epsb", [128, 1], F32))
        r_eps = Res("eps")
        P.op("dve", lambda e: e.memset(epsb[:], EPS), writes=[r_eps])
        ptp = [ec(nc.psum_tensor("ptp%d" % i, [128, 4, 128], BF16)) for i in range(2)]
        pgu = [ec(nc.psum_tensor("pgu%d" % i, [128, 512], F32)) for i in range(4)]
        pdn = [ec(nc.psum_tensor("pdn%d" % i, [128, 512], F32)) for i in range(2)]

        r_wgu = [Res("wgu%d" % c) for c in range(8)]
        r_wdn = [Res("wdn%d" % c) for c in range(NF)]
        r_gF, r_gL = Res("gF"), Res("gL")
        r_hb = [[Res("hb%d_%d" % (i, s)) for s in range(2)] for i in range(2)]
        r_ub = Res("ub")
        r_uT = [Res("uT%d" % i) for i in range(2)]
        r_aT = [Res("aT%d" % j) for j in range(NF)]
        r_sg = [Res("sg%d" % i) for i in range(2)]
        r_junk = Res("junk")
        r_st = [Res("st%d" % i) for i in range(8)]
        r_ptp = [Res("ptp%d" % i) for i in range(2)]
        r_pgu = [Res("pgu%d" % i) for i in range(4)]
        r_pdn = [Res("pdn%d" % i) for i in range(2)]

        P.dma("pool", lambda e: e.dma_start(out=gF[:], in_=g_ffn_d.to_broadcast([128, D])), writes=[r_gF])
        P.dma("pool", lambda e: e.dma_start(out=gL[:], in_=g_fin_d.to_broadcast([128, D])), writes=[r_gL])
        wgu_v = wgu_d.rearrange("(c p) f -> p c f", p=128)
        wdn_v = wdn_d.rearrange("(c p) f -> p c f", p=128)
        for c in range(8):
            for hlf in range(2):
                P.dma("pool", lambda e, c=c, hlf=hlf: e.dma_start(
                    out=wgu[:, c, hlf * DFF:(hlf + 1) * DFF], in_=wgu_v[:, c, hlf * DFF:(hlf + 1) * DFF]),
                    writes=[r_wgu[c]])
        for c in range(NF):
            P.dma("pool", lambda e, c=c: e.dma_start(out=wdn[:, c, :], in_=wdn_v[:, c, :]),
                  writes=[r_wdn[c]])

        h_v = h_d.rearrange("(t s p) d -> t p s d", s=2, p=128)
        o_v = out_d.rearrange("(t s p) d -> t p s d", s=2, p=128)
        final_pts = []

        def rstd_from(h_ap, col, r_h):
            P.op("act", lambda e: e.activation(out=junk[:], in_=h_ap, func=AF.Square,
                                               accum_out=st8[:, col:col + 1]),
                 reads=[r_h], writes=[r_junk, r_st[col]])
            P.op("act", lambda e: e.activation(out=st8[:, col:col + 1], in_=st8[:, col:col + 1], func=AF.Sqrt,
                                               scale=1.0 / D, bias=epsb[:]),
                 reads=[r_st[col], r_eps], writes=[r_st[col]])
            P.op("dve", lambda e: e.reciprocal(out=st8[:, col:col + 1], in_=st8[:, col:col + 1]),
                 reads=[r_st[col]], writes=[r_st[col]])

        def prologue(t):
            b = t % 2
            hbt = hb[b]
            P.dma("sp", lambda e, t=t, hbt=hbt: e.dma_start(out=hbt[:], in_=h_v[t]),
                  reads=[r_h1[t]], writes=[r_hb[b][0], r_hb[b][1]])
            for s in range(2):
                rstd_from(hbt[:, s, :], s, r_hb[b][s])
                P.stt("dve", ub[:], hbt[:, s, :], st8[:, s:s + 1], gF[:], ALU.mult, ALU.mult,
                      [r_hb[b][s], r_st[s], r_gF], [r_ub])
                for g4 in range(2):
                    pt = ptp[g4]
                    for j in range(4):
                        c = g4 * 4 + j
                        P.tr(pt[:, j, :], ub[:, c * 128:(c + 1) * 128], identb[:], [r_ub, r_ident], [r_ptp[g4]],
                             inc=(j == 3))
                    P.cp("act", uT[b][:, g4 * 4:(g4 + 1) * 4, s * 128:(s + 1) * 128], pt[:], [r_ptp[g4]], [r_uT[b]])

        def gateup(t):
            b = t % 2
            for j in range(NF):
                pg = pgu[(2 * j) % 4]
                pu = pgu[(2 * j + 1) % 4]
                rg = r_pgu[(2 * j) % 4]
                ru = r_pgu[(2 * j + 1) % 4]
                for c in range(8):
                    P.mm(pg[:, 0:T], wgu[:, c, j * 128:(j + 1) * 128], uT[b][:, c, :], c == 0, c == 7,
                         [r_wgu[c], r_uT[b]], [rg])
                for c in range(8):
                    P.mm(pu[:, 0:T], wgu[:, c, DFF + j * 128:DFF + (j + 1) * 128], uT[b][:, c, :], c == 0, c == 7,
                         [r_wgu[c], r_uT[b]], [ru])
                sgt = sg[j % 2]
                P.act(sgt[:], pg[:, 0:T], AF.Silu, [rg], [r_sg[j % 2]])
                P.tt("dve", aT[:, j, :], sgt[:], pu[:, 0:T], ALU.mult, [ru, r_sg[j % 2]], [r_aT[j]])

        def down(t):
            b = t % 2
            hbt = hb[b]
            for s in range(2):
                for n in range(2):
                    pd = pdn[n]
                    for j in range(NF):
                        P.mm(pd[:], aT[:, j, s * 128:(s + 1) * 128], wdn[:, j, n * 512:(n + 1) * 512],
                             j == 0, j == NF - 1, [r_aT[j], r_wdn[j]], [r_pdn[n]])
                    P.tt("dve", hbt[:, s, n * 512:(n + 1) * 512], hbt[:, s, n * 512:(n + 1) * 512], pd[:], ALU.add,
                         [r_pdn[n], r_hb[b][s]], [r_hb[b][s]])
                rstd_from(hbt[:, s, :], 2 + s, r_hb[b][s])
                P.stt("dve", hbt[:, s, :], hbt[:, s, :], st8[:, 2 + s:3 + s], gL[:], ALU.mult, ALU.mult,
                      [r_hb[b][s], r_st[2 + s], r_gL], [r_hb[b][s]])
            final_pts.append(P.dma("sp", lambda e, t=t, hbt=hbt: e.dma_start(out=o_v[t], in_=hbt[:]),
                                   reads=[r_hb[b][0], r_hb[b][1]]))

        prologue(0)
        for t in range(NT):
            gateup(t)
            if t + 1 < NT:
                prologue(t + 1)
            down(t)
        return final_pts


def _host_consts(S):
    idx = np.arange(128)
    ident = np.eye(128, dtype=np.float32)
    tri = (idx[:, None] <= idx[None, :]).astype(np.float32)
    trim = -(idx[:, None] > idx[None, :]).astype(np.float32)
    ones = np.ones((128, 128), np.float32)
    consts = np.ascontiguousarray(np.stack([ident, tri, trim, ones], axis=1))
    inv = (np.float32(10000.0) ** (-np.arange(32, dtype=np.float32) / np.float32(32))).astype(np.float32)
    ang = np.arange(S, dtype=np.float32)[:, None] * inv[None, :]
    cos, sin = np.cos(ang).astype(np.float32), np.sin(ang).astype(np.float32)
    rope = np.ascontiguousarray(np.stack([np.concatenate([cos, cos], 1), np.concatenate([-sin, sin], 1)], axis=1))
    k = np.arange(128)[:, None, None]
    a = np.arange(4)[None, :, None]
    q = np.arange(512)[None, None, :]
    jq, ja = q // 256, a // 2
    cm = np.where(jq > ja, 1.0, np.where(jq < ja, 0.0, (q >= 128 * a + k).astype(np.float32)))
    cmask = np.ascontiguousarray(np.broadcast_to(cm, (128, 4, 512)).astype(ml_dtypes.bfloat16))
    blk = np.arange(16)[:, None]
    j = np.arange(16)[None, :]
    pastb = np.where(j < blk, 0.0, np.where(j == blk, 1e30, -1e30)).astype(np.float32)
    oneh = np.where(j == blk, -1024.0, 0.0).astype(np.float32)
    blkc = np.ascontiguousarray(np.broadcast_to(np.concatenate([pastb, oneh], 1)[None], (128, 16, 32)).astype(np.float32))
    return dict(consts=consts, rope=rope, cmask=cmask, blkc=blkc)


def make_in_maps(inp, ncores):
    f = lambda k: np.ascontiguousarray(np.asarray(inp[k], dtype=np.float32))
    x = np.asarray(inp["x"], dtype=np.float32)
    S = x.shape[1]
    common = {
        "norm_mix_g": f("norm_mix_g").reshape(1, D),
        "w_in": f("w_in")[0],
        "conv_wT": np.ascontiguousarray(f("conv_w")[0].T),
        "bias8": np.ascontiguousarray(np.concatenate([f("b_igate")[0], f("b_fgate")[0]]).reshape(1, 8)),
        "mlstm_norm_g": f("mlstm_norm_g").reshape(1, 512),
        "w_proj_a": f("w_proj_a")[0],
        "w_proj_b": f("w_proj_b")[0],
        "w_out": f("w_out")[0],
        "norm_ffn_g": f("norm_ffn_g").reshape(1, D),
        "norm_final_g": f("norm_final_g").reshape(1, D),
        "w_gate_up": f("w_gate_up")[0],
        "w_down": f("w_down")[0],
    }
    common.update(_host_consts(S))
    return [dict(common, x=np.ascontiguousarray(x[b])) for b in range(ncores)]


def kernel(x, norm_mix_g, w_in, conv_w, b_igate, b_fgate, mlstm_norm_g, w_proj_a, w_proj_b,
           w_out, norm_ffn_g, w_gate_up, w_down, norm_final_g):
    inp = dict(x=x, norm_mix_g=norm_mix_g, w_in=w_in, conv_w=conv_w, b_igate=b_igate, b_fgate=b_fgate,
               mlstm_norm_g=mlstm_norm_g, w_proj_a=w_proj_a, w_proj_b=w_proj_b, w_out=w_out,
               norm_ffn_g=norm_ffn_g, w_gate_up=w_gate_up, w_down=w_down, norm_final_g=norm_final_g)
    B, S, _ = np.asarray(x).shape
    nc = build_nc(S)
    in_maps = make_in_maps(inp, B)
    res = run_bass_kernel_spmd(nc, in_maps, core_ids=list(range(B)))
    return np.stack([np.asarray(r["out"]) for r in res.results], axis=0).astype(np.float32)
```

```python
import numpy as np
import ml_dtypes
from contextlib import ExitStack

import concourse.bass as bass
import concourse.mybir as mybir
from concourse.bass_utils import run_bass_kernel_spmd

F32 = mybir.dt.float32
BF16 = mybir.dt.bfloat16
ALU = mybir.AluOpType
AF = mybir.ActivationFunctionType
AX = mybir.AxisListType

D = 1024
DFF = 2816
EPS = 1e-6
NCORES = 8


class Res:
    __slots__ = ("name", "w", "r")

    def __init__(self, name):
        self.name = name
        self.w = None
        self.r = {}


class Prog:
    COMPUTE = ("pe", "dve", "act", "pool")

    def __init__(self, nc, stack, n_sp=12, n_pool=6):
        self.nc = nc
        self.sem = {}
        self.stack = stack
        self.epoch = 0
        self.ekey = {e: e for e in self.COMPUTE}
        for e in self.COMPUTE:
            self.sem[e] = stack.enter_context(nc.semaphore("s_" + e))
        self.dma_pool = {"sp": [], "pool": []}
        for i in range(n_sp):
            k = "dsp%d" % i
            self.sem[k] = stack.enter_context(nc.semaphore(k))
            self.dma_pool["sp"].append(k)
        for i in range(n_pool):
            k = "dpl%d" % i
            self.sem[k] = stack.enter_context(nc.semaphore(k))
            self.dma_pool["pool"].append(k)
        self.dma_rr = {"sp": 0, "pool": 0}
        self.cnt = {k: 0 for k in self.sem}
        self.ops = {e: [] for e in ("pe", "dve", "act", "pool", "sp")}
        self.know = {e: {} for e in self.ops}
        self.clock = {}
        self.pe_pending = False

    def _needs(self, reads, writes):
        need = {}
        for r in reads:
            if r.w is not None:
                k, v = r.w
                if need.get(k, 0) < v:
                    need[k] = v
        for w in writes:
            if w.w is not None:
                k, v = w.w
                if need.get(k, 0) < v:
                    need[k] = v
            for k, v in w.r.items():
                if need.get(k, 0) < v:
                    need[k] = v
        return need

    def _waits(self, eng, need):
        know = self.know[eng]
        waits = []
        for k, v in need.items():
            if eng == "pe" and k == self.ekey["pe"]:
                continue
            if know.get(k, 0) >= v:
                continue
            waits.append((k, v))
        for k, v in waits:
            ck = self.clock.get((k, v))
            if ck:
                for kk, vv in ck.items():
                    if know.get(kk, 0) < vv:
                        know[kk] = vv
            if know.get(k, 0) < v:
                know[k] = v
        return waits

    def _record(self, point, reads, writes):
        k, v = point
        for r in reads:
            if r.r.get(k, 0) < v:
                r.r[k] = v
        for w in writes:
            w.w = point
            w.r = {}

    def op(self, eng, fn, reads=(), writes=(), inc=True):
        need = self._needs(reads, writes)
        waits = self._waits(eng, need)
        sk = self.ekey[eng]
        if eng == "pe" and not inc:
            point = (sk, self.cnt[sk] + 1)
            self.pe_pending = True
            self.ops[eng].append((waits, fn, None, 0))
        else:
            self.cnt[sk] += 1
            point = (sk, self.cnt[sk])
            if eng == "pe":
                self.pe_pending = False
            self.ops[eng].append((waits, fn, sk, 1))
            ck = dict(self.know[eng])
            ck[sk] = self.cnt[sk]
            self.clock[point] = ck
        self._record(point, reads, writes)
        return point

    def new_epoch(self):
        self.epoch += 1
        for e in self.COMPUTE:
            k = "%s_%d" % (e, self.epoch)
            self.sem[k] = self.stack.enter_context(self.nc.semaphore("s_" + k))
            self.cnt[k] = 0
            self.ekey[e] = k

    def dma(self, q, fn, reads=(), writes=()):
        pool = self.dma_pool[q]
        sk = pool[self.dma_rr[q] % len(pool)]
        self.dma_rr[q] += 1
        need = self._needs(reads, writes)
        if self.cnt[sk] > 0 and need.get(sk, 0) < self.cnt[sk]:
            need[sk] = self.cnt[sk]
        waits = self._waits(q, need)
        self.cnt[sk] += 16
        point = (sk, self.cnt[sk])
        self.ops[q].append((waits, fn, sk, 16))
        ck = dict(self.know[q])
        ck[sk] = self.cnt[sk]
        self.clock[point] = ck
        self._record(point, reads, writes)
        return point

    def mm(self, out, lhsT, rhs, start, stop, reads, writes, inc=None):
        if inc is None:
            inc = stop
        return self.op("pe", lambda e: e.matmul(out, lhsT=lhsT, rhs=rhs, start=start, stop=stop),
                       reads, writes, inc)

    def tr(self, out, in_, ident, reads, writes, inc=True):
        return self.op("pe", lambda e: e.transpose(out=out, in_=in_, identity=ident), reads, writes, inc)

    def act(self, out, in_, func, reads, writes, bias=None, scale=None, accum=None):
        kw = {}
        if bias is not None:
            kw["bias"] = bias
        if scale is not None:
            kw["scale"] = scale
        if accum is not None:
            kw["accum_out"] = accum
        return self.op("act", lambda e: e.activation(out=out, in_=in_, func=func, **kw), reads, writes)

    def tt(self, eng, out, in0, in1, op, reads, writes):
        return self.op(eng, lambda e: e.tensor_tensor(out=out, in0=in0, in1=in1, op=op), reads, writes)

    def ts(self, eng, out, in0, s1, op0, reads, writes, s2=None, op1=None):
        if op1 is None:
            return self.op(eng, lambda e: e.tensor_scalar(out=out, in0=in0, scalar1=s1, scalar2=None, op0=op0),
                           reads, writes)
        return self.op(eng, lambda e: e.tensor_scalar(out=out, in0=in0, scalar1=s1, scalar2=s2, op0=op0, op1=op1),
                       reads, writes)

    def stt(self, eng, out, in0, scalar, in1, op0, op1, reads, writes):
        return self.op(eng, lambda e: e.scalar_tensor_tensor(out=out, in0=in0, scalar=scalar, in1=in1,
                                                             op0=op0, op1=op1), reads, writes)

    def cp(self, eng, out, in_, reads, writes):
        if eng == "act":
            return self.op("act", lambda e: e.copy(out=out, in_=in_), reads, writes)
        return self.op(eng, lambda e: e.tensor_copy(out=out, in_=in_), reads, writes)

    def recip(self, out, in_, reads, writes):
        return self.op("dve", lambda e: e.reciprocal(out=out, in_=in_), reads, writes)

    def memset(self, eng, ap, val, writes):
        return self.op(eng, lambda e: e.memset(ap, val), (), writes)

    def barrier(self):
        assert not self.pe_pending
        need = {k: v for k, v in self.cnt.items() if v > 0}
        for e in self.ops:
            waits = self._waits(e, dict(need))
            if waits:
                self.ops[e].append((waits, None, None, 0))

    def finish(self, final_points):
        need = {}
        for k, v in final_points:
            if need.get(k, 0) < v:
                need[k] = v
        waits = [(k, v) for k, v in need.items()]
        self.ops["sp"].append((waits, None, None, 0))

    def emit(self):
        nc = self.nc
        assert not self.pe_pending, "PE has trailing non-inc instructions"
        sem = self.sem
        ops = self.ops

        def run(e, lst):
            for waits, fn, sk, inc in lst:
                for k, v in waits:
                    e.wait_ge(sem[k], v)
                if fn is None:
                    continue
                ins = fn(e)
                if sk is not None:
                    ins.then_inc(sem[sk], inc)

        with nc.Block() as block:
            @block.tensor
            def _(e):
                run(e, ops["pe"])

            @block.vector
            def _(e):
                run(e, ops["dve"])

            @block.scalar
            def _(e):
                run(e, ops["act"])

            @block.gpsimd
            def _(e):
                run(e, ops["pool"])

            @block.sync
            def _(e):
                run(e, ops["sp"])


IN_W = 5640
A_QA, A_KA, A_VA = 0, 512, 1024
B_OFF = 1536
WB = IN_W - B_OFF
QKB, VBo, OBo, IBo, GAo, GBo = 0, 1024, 1536, 2048, 2056, 3080
LNC = float(np.log(128.0 ** -0.5))


def build_nc(S, phases=("A", "B", "C")):
    nc = bass.Bass("TRN2", target_bir_lowering=False)
    dr = {}

    def din(name, shape, dt=F32):
        dr[name] = nc.dram_tensor(name, shape, dt, kind="ExternalInput").ap()

    din("x", [S, D])
    din("norm_mix_g", [1, D])
    din("w_in", [D, IN_W])
    din("conv_wT", [D, 4])
    din("bias8", [1, 8])
    din("mlstm_norm_g", [1, 512])
    din("w_proj_a", [512, D])
    din("w_proj_b", [512, D])
    din("w_out", [D, D])
    din("norm_ffn_g", [1, D])
    din("norm_final_g", [1, D])
    din("w_gate_up", [D, 2 * DFF])
    din("w_down", [DFF, D])
    din("consts", [128, 4, 128])
    din("rope", [S, 2, 64])
    din("cmask", [128, 4, 512], BF16)
    din("blkc", [128, 16, 32])
    out_d = nc.dram_tensor("out", [S, D], F32, kind="ExternalOutput").ap()
    h1_d = nc.dram_tensor("h1_s", [S, D], F32, kind="Internal").ap()
    ya_d = nc.dram_tensor("ya_s", [8, 64, S], BF16, kind="Internal").ap()

    with ExitStack() as stack:
        P = Prog(nc, stack)
        ec = stack.enter_context
        identb = ec(nc.sbuf_tensor("identb", [128, 128], BF16))
        cst = ec(nc.sbuf_tensor("cst", [128, 4, 128], F32))
        r_c = Res("consts")
        P.dma("pool", lambda e: e.dma_start(out=identb[:], in_=dr["consts"][:, 0, :]), writes=[r_c])
        P.dma("sp", lambda e: e.dma_start(out=cst[:], in_=dr["consts"]), writes=[r_c])
        G = dict(identb=identb, cst=cst, r_c=r_c)

        r_ya = [[Res("ya_s%d_%d" % (i, h)) for h in range(8)] for i in range(S // 256)]
        r_h1 = [Res("h1_s%d" % i) for i in range(S // 256)]
        if "A" in phases:
            phase_A(nc, P, S, dr, ya_d, r_ya, G)
        else:
            with nc.sbuf_tensor("zt", [64, S], BF16) as zt:
                rz = Res("zt")
                P.memset("dve", zt[:], 0.0, [rz])
                for h in range(8):
                    P.dma("sp", lambda e, h=h: e.dma_start(out=ya_d[h], in_=zt[:]), reads=[rz],
                          writes=[r_ya[i][h] for i in range(S // 256)])
                P.barrier()
        P.barrier()
        P.new_epoch()
        if "B" in phases:
            phase_B(nc, P, S, dr, h1_d, ya_d, r_ya, r_h1, G)
            h_src = h1_d
        else:
            h_src = dr["x"]
        P.barrier()
        P.new_epoch()
        final_pts = phase_C(nc, P, S, h_src, r_h1, out_d, dr, G)
        P.finish(final_pts)
        P.emit()
    return nc


def load_norm_tile(nc, P, xt, r_x, s, gM, r_g, st8, r_st, col, epsb, r_eps, junk, r_junk, ub, r_ub):
    P.act(junk[:], xt[:, s, :], AF.Square, [r_x], [r_junk, r_st[col]], accum=st8[:, col:col + 1])
    P.act(st8[:, col:col + 1], st8[:, col:col + 1], AF.Sqrt, [r_st[col], r_eps], [r_st[col]],
          bias=epsb[:], scale=1.0 / D)
    P.recip(st8[:, col:col + 1], st8[:, col:col + 1], [r_st[col]], [r_st[col]])
    P.stt("dve", ub[:], xt[:, s, :], st8[:, col:col + 1], gM[:], ALU.mult, ALU.mult,
          [r_x, r_st[col], r_g], [r_ub])


def phase_A(nc, P, S, dr, ya_d, r_ya, G):
    T = 512
    NT = S // T
    NKT = S // 128
    identb, cst, r_c = G["identb"], G["cst"], G["r_c"]
    with ExitStack() as st:
        ec = st.enter_context

        def sbt(name, shape, dt):
            return ec(nc.sbuf_tensor(name, shape, dt))

        wA = sbt("wA", [128, 8, 1536], BF16)
        gM = sbt("gMa", [128, D], F32)
        cmk = sbt("cmk", [128, 4, 512], BF16)
        blkc = sbt("blkc_sb", [128, 16, 32], F32)
        epsb = sbt("epsb_a", [128, 1], F32)
        ropeT = [sbt("ropeT%d" % i, [128, 4, 2, 64], F32) for i in range(2)]
        kTa = sbt("kTa", [80, 8, S], BF16)
        vA = sbt("vA", [128, NKT, 8, 65], BF16)
        kmf = sbt("kmf", [64, 8], F32)
        kmT = sbt("kmT", [64, 8, 16], BF16)
        xs = [sbt("xs%d" % i, [128, 1, D], F32) for i in range(2)]
        ub = sbt("ub_a", [128, D], BF16)
        junk = sbt("junk_a", [128, D], BF16)
        st8 = sbt("st8_a", [128, 8], F32)
        uT = sbt("uTa", [128, 8, T], BF16)
        t1 = sbt("t1", [128, 8, 64], F32)
        t2 = sbt("t2", [128, 8, 64], F32)
        qtok = [sbt("qtok%d" % i, [128, 8, 80], BF16) for i in range(2)]
        ktok = [sbt("ktok%d" % i, [128, 8, 80], BF16) for i in range(2)]
        qTa = sbt("qTa", [80, 8, T], BF16)
        gsb = sbt("gsb_a", [128, 8, 16], F32)
        top8 = sbt("top8", [128, 8, 8], F32)
        PT = [sbt("PT%d" % i, [128, 512], BF16) for i in range(4)]
        oT = [sbt("oT%d" % i, [65, 512], F32) for i in range(2)]
        yo = [sbt("yo%d" % i, [64, 512], BF16) for i in range(2)]
        PB = [ec(nc.psum_tensor("PBa%d" % i, [128, 512], F32)) for i in range(8)]
        PBb = [PB[i].bitcast(BF16) for i in range(8)]

        R = lambda n: Res(n)
        r_wA = [R("wA%d" % c) for c in range(8)]
        r_k = R("constsA")
        r_rope = [R("rope%d" % i) for i in range(2)]
        r_kTa = [R("kTa%d" % i) for i in range(NKT)]
        r_vA = [R("vA%d" % i) for i in range(NKT)]
        r_kmf, r_kmT = R("kmf"), R("kmT")
        r_xs = [R("xs%d" % i) for i in range(2)]
        r_ub, r_junk = R("ub"), R("junk")
        r_st = [R("st%d" % i) for i in range(8)]
        r_uT = [R("uT%d" % s) for s in range(4)]
        r_t1, r_t2 = R("t1"), R("t2")
        r_qtok = [R("qtok%d" % i) for i in range(2)]
        r_ktok = [R("ktok%d" % i) for i in range(2)]
        r_qTa = [R("qTa%d" % s) for s in range(4)]
        r_gsb, r_top8 = R("gsb"), R("top8")
        r_PT = [R("PT%d" % i) for i in range(4)]
        r_oT = [R("oT%d" % i) for i in range(2)]
        r_yo = [R("yo%d" % i) for i in range(2)]
        r_pb = [R("pba%d" % i) for i in range(8)]

        wv = dr["w_in"].rearrange("(c p) f -> p c f", p=128)
        for c in range(8):
            P.dma("pool", lambda e, c=c: e.dma_start(out=wA[:, c, :], in_=wv[:, c, 0:1536]), writes=[r_wA[c]])
        P.dma("sp", lambda e: e.dma_start(out=gM[:], in_=dr["norm_mix_g"].to_broadcast([128, D])), writes=[r_k])
        P.dma("sp", lambda e: e.dma_start(out=cmk[:], in_=dr["cmask"]), writes=[r_k])
        P.dma("sp", lambda e: e.dma_start(out=blkc[:], in_=dr["blkc"]), writes=[r_k])
        P.memset("dve", epsb[:], EPS, [r_k])
        P.memset("dve", kmT[:], 0.0, [r_kmT])
        P.memset("dve", vA[:], 1.0, r_vA)

        xv = dr["x"].rearrange("(n p) d -> n p d", p=128)
        rope_v = dr["rope"].rearrange("(n s p) a d -> n p s a d", s=4, p=128)
        onesb = sbt("onesb", [128, 64], BF16)
        hiT = [sbt("hiT%d" % i, [65, 512], BF16) for i in range(2)]
        loT = [sbt("loT%d" % i, [65, 512], BF16) for i in range(2)]
        qTb = sbt("qTa2", [80, 8, T], BF16)
        qTas = [qTa, qTb]
        r_qTas = [r_qTa, [R("qTb%d" % s) for s in range(4)]]
        r_hl = [R("hl%d" % i) for i in range(2)]
        P.memset("dve", onesb[:], 1.0, [r_k])
        P.dma("sp", lambda e: e.dma_start(out=xs[0][:, 0, :], in_=xv[0]), writes=[r_xs[0]])

        def rope_ops(pz, rpz, rb, s, dst, rdst):
            z3 = pz.rearrange("p (h d) -> p h d", h=8)
            C2 = ropeT[rb][:, s, 0, :].unsqueeze(1).to_broadcast([128, 8, 64])
            Sa = ropeT[rb][:, s, 1, 0:32].unsqueeze(1).to_broadcast([128, 8, 32])
            Sb = ropeT[rb][:, s, 1, 32:64].unsqueeze(1).to_broadcast([128, 8, 32])
            P.tt("dve", t1[:], z3, C2, ALU.mult, [rpz, r_rope[rb]], [r_t1])
            P.tt("dve", t2[:, :, 0:32], z3[:, :, 32:64], Sa, ALU.mult, [rpz, r_rope[rb]], [r_t2])
            P.tt("dve", t2[:, :, 32:64], z3[:, :, 0:32], Sb, ALU.mult, [rpz, r_rope[rb]], [r_t2])
            P.tt("dve", dst[:, :, 0:64], t1[:], t2[:], ALU.add, [r_t1, r_t2], [rdst])

        def tr_group(src_fn, rsrc, rows, dst_fn, rdst, prow=None):
            for g4 in range(2):
                for j in range(4):
                    P.tr(PBb[g4][0:rows, j * 128:(j + 1) * 128], src_fn(g4 * 4 + j), identb[:], [rsrc, r_c],
                         [r_pb[g4]], inc=(j == 3))
            for g4 in range(2):
                lo_, hi_ = prow if prow else (0, rows)
                P.cp("dve", dst_fn(g4, lo_, hi_), PBb[g4][lo_:hi_, 0:512].rearrange("p (j t) -> p j t", j=4),
                     [r_pb[g4]], [rdst])

        def prologue(t):
            rb = t % 2
            qT_ = qTas[t % 2]
            rq_ = r_qTas[t % 2]
            P.dma("sp", lambda e, t=t, rb=rb: e.dma_start(out=ropeT[rb][:], in_=rope_v[t]), writes=[r_rope[rb]])
            for s in range(4):
                n = t * 4 + s
                sub = slice(s * 128, (s + 1) * 128)
                ksub = slice(n * 128, (n + 1) * 128)
                blk = n // 2
                if n + 1 < NKT:
                    P.dma("sp", lambda e, n=n: e.dma_start(out=xs[(n + 1) % 2][:, 0, :], in_=xv[n + 1]),
                          writes=[r_xs[(n + 1) % 2]])
                xt = xs[n % 2]
                load_norm_tile(nc, P, xt, r_xs[n % 2], 0, gM, r_k, st8, r_st, s, epsb, r_k, junk, r_junk, ub, r_ub)
                yield
                for g4 in range(2):
                    for j in range(4):
                        c = g4 * 4 + j
                        P.tr(PBb[g4][:, j * 128:(j + 1) * 128], ub[:, c * 128:(c + 1) * 128], identb[:],
                             [r_ub, r_c], [r_pb[g4]], inc=(j == 3))
                for g4 in range(2):
                    P.cp("dve", uT[:, g4 * 4:(g4 + 1) * 4, sub],
                         PBb[g4][:, 0:512].rearrange("p (j t) -> p j t", j=4), [r_pb[g4]], [r_uT[s]])
                yield
                kb, rkb = ktok[s % 2], r_ktok[s % 2]
                pz = PB[4][:]
                for c in range(8):
                    P.mm(pz, uT[:, c, sub], wA[:, c, A_KA:A_KA + 512], c == 0, c == 7, [r_uT[s], r_wA[c]], [r_pb[4]])
                rope_ops(pz, r_pb[4], rb, s, kb, rkb)
                P.cp("dve", kb[:, :, 64:80], blkc[:, blk, 16:32].unsqueeze(1).to_broadcast([128, 8, 16]),
                     [r_k], [rkb])
                yield
                tr_group(lambda h: kb[:, h, :], rkb, 80,
                         lambda g4, lo_, hi_: kTa[lo_:hi_, g4 * 4:(g4 + 1) * 4, ksub], r_kTa[n])
                if s % 2 == 1:
                    P.op("dve", lambda e, blk=blk: e.tensor_reduce(
                        out=kmf[:], in_=kTa[0:64, :, blk * 256:(blk + 1) * 256], axis=AX.X, op=ALU.add),
                        [r_kTa[n - 1], r_kTa[n]], [r_kmf])
                    P.ts("dve", kmT[:, :, blk], kmf[:], 1.0 / 256, ALU.mult, [r_kmf], [r_kmT])
                yield
                for c in range(8):
                    P.mm(pz, uT[:, c, sub], wA[:, c, A_VA:A_VA + 512], c == 0, c == 7, [r_uT[s], r_wA[c]], [r_pb[4]])
                P.cp("dve", vA[:, n, :, 0:64], pz.rearrange("p (h d) -> p h d", h=8), [r_pb[4]], [r_vA[n]])
                yield
                qb, rqb = qtok[s % 2], r_qtok[s % 2]
                for c in range(8):
                    P.mm(pz, uT[:, c, sub], wA[:, c, A_QA:A_QA + 512], c == 0, c == 7, [r_uT[s], r_wA[c]], [r_pb[4]])
                rope_ops(pz, r_pb[4], rb, s, qb, rqb)
                yield
                tr_group(lambda h: qb[:, h, 0:64], rqb, 64,
                         lambda g4, lo_, hi_: qT_[lo_:hi_, g4 * 4:(g4 + 1) * 4, sub], rq_[s])
                yield
                pgt = PB[5][:, 0:128].rearrange("p (h j) -> p h j", h=8)
                for h in range(8):
                    P.mm(pgt[:, h, :], qT_[0:64, h, sub], kmT[:, h, :], True, True, [rq_[s], r_kmT], [r_pb[5]],
                         inc=(h == 7))
                P.tt("dve", gsb[:], pgt, blkc[:, blk, 0:16].unsqueeze(1).to_broadcast([128, 8, 16]), ALU.add,
                     [r_pb[5], r_k], [r_gsb])
                for h in range(8):
                    P.op("dve", lambda e, h=h: e.max(out=top8[:, h, :], in_=gsb[:, h, :]), [r_gsb], [r_top8])
                P.tt("dve", qb[:, :, 64:80], gsb[:], top8[:, :, 3:4].to_broadcast([128, 8, 16]), ALU.is_lt,
                     [r_gsb, r_top8], [rqb])
                yield
                tr_group(lambda h: qb[:, h, :], rqb, 80,
                         lambda g4, lo_, hi_: qT_[lo_:hi_, g4 * 4:(g4 + 1) * 4, sub], rq_[s], prow=(64, 80))
                yield

        def attention(t):
            qT_ = qTas[t % 2]
            rq_ = r_qTas[t % 2]
            nkt = 4 * t + 4
            items = [(h, kt) for h in range(8) for kt in range(nkt)]
            LA = 1
            deferred = {}

            def emit_S(i):
                h, kt = items[i]
                bi = 2 + i % 2
                P.mm(PB[bi][:], kTa[:, h, kt * 128:(kt + 1) * 128], qT_[:, h, :], True, True,
                     [r_kTa[kt]] + rq_, [r_pb[bi]])

            def emit_PV(i):
                h, kt = items[i]
                bi = 2 + i % 2
                pi = i % 4
                po = PB[6 + h % 2]
                rpo = r_pb[6 + h % 2]
                P.act(PT[pi][:], PB[bi][:], AF.Exp, [r_pb[bi]], [r_PT[pi]], scale=0.125)
                if kt >= 4 * t:
                    P.tt("pool", PT[pi][:], PT[pi][:], cmk[:, kt - 4 * t, :], ALU.mult, [r_PT[pi], r_k],
                         [r_PT[pi]])
                P.mm(po[0:65, :], vA[:, kt, h, :], PT[pi][:], kt == 0, kt == nkt - 1, [r_vA[kt], r_PT[pi]], [rpo])
                if kt == nkt - 1:
                    ob, rob = oT[h % 2], r_oT[h % 2]
                    P.cp("act", ob[:], po[0:65, :], [rpo], [rob])
                    P.recip(ob[64:65, :], ob[64:65, :], [rob], [rob])
                    P.cp("dve", hiT[h % 2][64:65, :], ob[64:65, :], [rob], [r_hl[h % 2]])
                    P.tt("dve", loT[h % 2][64:65, :], ob[64:65, :], hiT[h % 2][64:65, :], ALU.subtract,
                         [rob, r_hl[h % 2]], [r_hl[h % 2]])
                    deferred.setdefault(min(i + 4, len(items) - 1), []).append(h)

            def emit_epi(h):
                ob, rob = oT[h % 2], r_oT[h % 2]
                P.mm(PB[5][0:64, :], onesb[64:65, :], hiT[h % 2][64:65, :], True, False, [r_k, r_hl[h % 2]],
                     [r_pb[5]], inc=False)
                P.mm(PB[5][0:64, :], onesb[64:65, :], loT[h % 2][64:65, :], False, True, [r_k, r_hl[h % 2]],
                     [r_pb[5]])
                yb_ = yo[h % 2]
                P.tt("dve", yb_[:], ob[0:64, :], PB[5][0:64, :], ALU.mult, [rob, r_pb[5]], [r_yo[h % 2]])
                P.dma("sp", lambda e, h=h, t=t, yb_=yb_: e.dma_start(out=ya_d[h, :, t * T:(t + 1) * T], in_=yb_[:]),
                      reads=[r_yo[h % 2]], writes=[r_ya[2 * t][h], r_ya[2 * t + 1][h]])

            for i in range(min(LA, len(items))):
                emit_S(i)
            for i in range(len(items)):
                if i + LA < len(items):
                    emit_S(i + LA)
                emit_PV(i)
                for hh in deferred.pop(i, []):
                    emit_epi(hh)
                yield
            assert not deferred

        for _ in prologue(0):
            pass
        for t in range(NT):
            n_main = 8 * (4 * t + 4)
            side = prologue(t + 1) if t + 1 < NT else iter(())
            n_side = 36
            done = 0
            for i, _ in enumerate(attention(t)):
                want = ((i + 1) * n_side) // n_main
                while done < want:
                    next(side, None)
                    done += 1
            for _ in side:
                pass


def phase_B(nc, P, S, dr, h1_d, ya_d, r_ya, r_h1, G):
    T = 256
    NT = S // T
    identb, cst, r_c = G["identb"], G["cst"], G["r_c"]
    idf, tri, trim, onesf = cst[:, 0, :], cst[:, 1, :], cst[:, 2, :], cst[:, 3, :]
    with ExitStack() as st:
        ec = st.enter_context

        def sbt(name, shape, dt):
            return ec(nc.sbuf_tensor(name, shape, dt))

        wB = sbt("wB", [128, 8, WB], BF16)
        wpa = sbt("wpa", [128, 4, D], BF16)
        wpb = sbt("wpb", [128, 4, D], BF16)
        wo = sbt("wo", [128, 8, D], BF16)
        gM = sbt("gMb", [128, D], F32)
        gN = sbt("gN", [128, 512], F32)
        bias8 = sbt("bias8_sb", [128, 8], F32)
        cw = sbt("cw", [128, 8, 4], F32)
        trib = sbt("trib", [128, 128], BF16)
        epsb = sbt("epsb_b", [128, 1], F32)
        oneb = sbt("oneb", [128, 1], F32)
        lncb = sbt("lncb", [128, 1], F32)
        lnhb = sbt("lnhb", [128, 1], F32)
        xb = [sbt("xb%d" % i, [128, 2, D], F32) for i in range(3)]
        yaT = [sbt("yaT%d" % i, [128, 4, T], BF16) for i in range(3)]
        uT = [sbt("uTb%d" % i, [128, 8, T], BF16) for i in range(2)]
        ub = sbt("ub_b", [128, D], BF16)
        junk = sbt("junk_b", [128, D], BF16)
        st8 = sbt("st8_b", [128, 8], F32)
        gsb = sbt("gsb", [128, 2, 8], F32)
        e1 = sbt("e1", [128, 2, 4], F32)
        nlf = sbt("nlf", [128, 2, 4], F32)
        arg = sbt("arg", [128, 2, 4], F32)
        ksc = sbt("ksc", [128, 2, 4], F32)
        wc = sbt("wc", [128, 2, 4], F32)
        EBt = sbt("EBt", [128, 4, T], F32)
        EAt = sbt("EAt", [128, 4, T], F32)
        xc = sbt("xc", [128, 8, T + 3], F32)
        yc = [sbt("yc%d" % i, [128, T], F32) for i in range(2)]
        slq = [sbt("slq%d" % i, [128, T], BF16) for i in range(2)]
        kT0 = sbt("kT0", [128, 4, T], BF16)
        qT = sbt("qTb", [128, 4, T], BF16)
        kT = sbt("kTb", [128, 4, T], BF16)
        vaug = sbt("vaug", [128, 2, 4, 129], BF16)
        sgo = sbt("sgo", [128, 2, 512], F32)
        kS = sbt("kS", [128, 4, 128], BF16)
        scT = sbt("scT", [128, 4, 128], BF16)
        C32 = sbt("C32", [128, 4, 129], F32)
        Cbf = sbt("Cbf", [128, 4, 129], BF16)
        rr = sbt("rr", [128, 4], F32)
        ssq = sbt("ssq", [128, 4], F32)
        t4 = sbt("t4", [128, 4], F32)
        sc4 = sbt("sc4", [128, 4], F32)
        ybf = sbt("ybf", [128, 512], F32)
        ybb = sbt("ybb", [128, 512], BF16)
        ybT = sbt("ybT", [128, 4, T], BF16)
        sga = [sbt("sga%d" % i, [128, T], F32) for i in range(2)]
        sgb = [sbt("sgb%d" % i, [128, T], F32) for i in range(2)]
        m1 = [sbt("m1_%d" % i, [128, T], F32) for i in range(2)]
        m2 = [sbt("m2_%d" % i, [128, T], F32) for i in range(2)]
        mT = sbt("mT", [128, 8, T], BF16)
        PB = [ec(nc.psum_tensor("PBb%d" % i, [128, 512], F32)) for i in range(8)]

        R = lambda n: Res(n)
        r_wB = [R("wB%d" % c) for c in range(8)]
        r_wpa, r_wpb, r_wo = R("wpa"), R("wpb"), [R("wo%d" % c) for c in range(8)]
        r_k = R("constsB")
        r_xb = [[R("xb%d_%d" % (i, s)) for s in range(2)] for i in range(3)]
        r_yaT = [R("yaT%d" % i) for i in range(3)]
        r_uT = [[R("uTb%d_%d" % (i, s)) for s in range(2)] for i in range(2)]
        r_ub, r_junk = R("ub"), R("junk")
        r_st = [R("st%d" % i) for i in range(8)]
        r_gs = [R("gsb%d" % s) for s in range(2)]
        r_e1 = [R("e1%d" % s) for s in range(2)]
        r_nlf = [R("nlf%d" % s) for s in range(2)]
        r_arg = [R("arg%d" % s) for s in range(2)]
        r_ksc = [R("ksc%d" % s) for s in range(2)]
        r_wc = [R("wc%d" % s) for s in range(2)]
        r_EB = [[R("EB%d_%d" % (h, s)) for s in range(2)] for h in range(4)]
        r_EA = [[R("EA%d_%d" % (h, s)) for s in range(2)] for h in range(4)]
        r_xc = [R("xc%d" % m) for m in range(8)]
        r_yc = [R("yc%d" % i) for i in range(2)]
        r_slq = [R("slq%d" % i) for i in range(2)]
        r_kT0 = [R("kT0%d" % h) for h in range(4)]
        r_qT = [R("qT%d" % h) for h in range(4)]
        r_kT = [R("kT%d" % h) for h in range(4)]
        r_va = [R("vaug%d" % s) for s in range(2)]
        r_sgo = [R("sgo%d" % s) for s in range(2)]
        r_kS = [R("kS%d" % h) for h in range(4)]
        r_scT = [R("scT%d" % h) for h in range(4)]
        r_C32 = [R("C32%d" % h) for h in range(4)]
        r_Cbf = [R("Cbf%d" % h) for h in range(4)]
        r_rr, r_ssq, r_t4, r_sc4 = R("rr"), R("ssq"), R("t4"), R("sc4")
        r_ybf, r_ybb = R("ybf"), R("ybb")
        r_ybT = [R("ybT%d" % s) for s in range(2)]
        r_sga = [R("sga%d" % i) for i in range(2)]
        r_sgb = [R("sgb%d" % i) for i in range(2)]
        r_m1 = [R("m1%d" % i) for i in range(2)]
        r_m2 = [R("m2%d" % i) for i in range(2)]
        r_mT = [R("mT%d" % f) for f in range(8)]
        r_pb = [R("pb%d" % i) for i in range(8)]
        PBb = [PB[i].bitcast(BF16) for i in range(8)]

        wv = dr["w_in"].rearrange("(c p) f -> p c f", p=128)
        r_wB2 = [R("wB2_%d" % c) for c in range(8)]
        for c in range(8):
            P.dma("pool", lambda e, c=c: e.dma_start(out=wB[:, c, 0:GAo], in_=wv[:, c, B_OFF:B_OFF + GAo]),
                  writes=[r_wB[c]])
        for c in range(8):
            P.dma("pool", lambda e, c=c: e.dma_start(out=wB[:, c, GAo:WB], in_=wv[:, c, B_OFF + GAo:IN_W]),
                  writes=[r_wB2[c]])
        P.dma("pool", lambda e: e.dma_start(out=wpa[:], in_=dr["w_proj_a"].rearrange("(c p) f -> p c f", p=128)),
              writes=[r_wpa])
        P.dma("pool", lambda e: e.dma_start(out=wpb[:], in_=dr["w_proj_b"].rearrange("(c p) f -> p c f", p=128)),
              writes=[r_wpb])
        wov = dr["w_out"].rearrange("(c p) f -> p c f", p=128)
        for c in range(8):
            P.dma("pool", lambda e, c=c: e.dma_start(out=wo[:, c, :], in_=wov[:, c, :]), writes=[r_wo[c]])
        P.dma("pool", lambda e: e.dma_start(out=trib[:], in_=dr["consts"][:, 1, :]), writes=[r_k])
        P.dma("sp", lambda e: e.dma_start(out=gM[:], in_=dr["norm_mix_g"].to_broadcast([128, D])), writes=[r_k])
        P.dma("sp", lambda e: e.dma_start(out=gN[:], in_=dr["mlstm_norm_g"].to_broadcast([128, 512])), writes=[r_k])
        P.dma("sp", lambda e: e.dma_start(out=bias8[:], in_=dr["bias8"].to_broadcast([128, 8])), writes=[r_k])
        P.dma("sp", lambda e: e.dma_start(out=cw[:], in_=dr["conv_wT"].rearrange("(m p) j -> p m j", p=128)),
              writes=[r_k])
        P.ts("dve", gN[:], gN[:], 0.5, ALU.mult, [r_k], [r_k])
        P.memset("dve", epsb[:], EPS, [r_k])
        P.memset("dve", oneb[:], 1.0, [r_k])
        P.memset("dve", lncb[:], LNC + float(np.log(0.5)), [r_k])
        P.memset("dve", lnhb[:], float(np.log(0.5)), [r_k])
        P.memset("dve", C32[:], 0.0, r_C32)
        P.memset("dve", Cbf[:], 0.0, r_Cbf)
        P.memset("dve", vaug[:], 1.0, r_va)
        P.memset("dve", xc[:], 0.0, r_xc)

        xv = dr["x"].rearrange("(t s p) d -> t p s d", s=2, p=128)
        hv = h1_d.rearrange("(t s p) d -> t p s d", s=2, p=128)
        ya_v = ya_d.rearrange("h d s -> (h d) s").rearrange("(c p) s -> p c s", p=128)

        ybT2 = sbt("ybT2", [128, 4, T], BF16)
        ybTs = [ybT, ybT2]
        r_ybTs = [r_ybT, [R("ybTb%d" % s) for s in range(2)]]

        def stage_A(t):
            b = t % 2
            xt = xb[t % 3]
            uTt = uT[b]
            for s in range(2):
                sub = slice(s * 128, (s + 1) * 128)
                load_norm_tile(nc, P, xt, r_xb[t % 3][s], s, gM, r_k, st8, r_st, s, epsb, r_k, junk, r_junk, ub, r_ub)
                yield
                for g4 in range(2):
                    for j in range(4):
                        c = g4 * 4 + j
                        P.tr(PBb[2 + g4][:, j * 128:(j + 1) * 128], ub[:, c * 128:(c + 1) * 128],
                             identb[:], [r_ub, r_c], [r_pb[2 + g4]], inc=(j == 3))
                for g4 in range(2):
                    P.cp("act", uTt[:, g4 * 4:(g4 + 1) * 4, sub],
                         PBb[2 + g4][:, 0:512].rearrange("p (j t) -> p j t", j=4),
                         [r_pb[2 + g4]], [r_uT[b][s]])
                yield
                pg = PB[4][:, 0:8]
                for c in range(8):
                    P.mm(pg, uTt[:, c, sub], wB[:, c, IBo:IBo + 8], c == 0, c == 7,
                         [r_uT[b][s], r_wB[c]], [r_pb[4]])
                P.tt("dve", gsb[:, s, :], pg, bias8[:], ALU.add, [r_pb[4], r_k], [r_gs[s]])
                P.act(e1[:, s, :], gsb[:, s, 4:8], AF.Exp, [r_gs[s]], [r_e1[s]], scale=-1.0)
                P.act(nlf[:, s, :], e1[:, s, :], AF.Ln, [r_e1[s], r_k], [r_nlf[s]], bias=oneb[:])
                pv = PB[2][:]
                for c in range(8):
                    P.mm(pv, uTt[:, c, sub], wB[:, c, VBo:VBo + 512], c == 0, c == 7,
                         [r_uT[b][s], r_wB[c]], [r_pb[2]])
                P.cp("act", vaug[:, s, :, 0:128], pv.rearrange("p (h e) -> p h e", h=4), [r_pb[2]], [r_va[s]])
                yield
                po = PB[3][:]
                for c in range(8):
                    P.mm(po, uTt[:, c, sub], wB[:, c, OBo:OBo + 512], c == 0, c == 7,
                         [r_uT[b][s], r_wB[c]], [r_pb[3]])
                P.act(sgo[:, s, :], po, AF.Tanh, [r_pb[3]], [r_sgo[s]], scale=0.5)
                P.mm(PB[4][:, 8:12], trim, nlf[:, s, :], True, True, [r_c, r_nlf[s]], [r_pb[4]], inc=False)
                P.mm(PB[4][:, 12:16], onesf, nlf[:, s, :], True, True, [r_c, r_nlf[s]], [r_pb[4]])
                P.tt("dve", arg[:, s, :], PB[4][:, 8:12], gsb[:, s, 0:4], ALU.add, [r_pb[4], r_gs[s]], [r_arg[s]])
                P.act(ksc[:, s, :], arg[:, s, :], AF.Exp, [r_arg[s], r_k], [r_ksc[s]], bias=lncb[:])
                P.act(wc[:, s, :], PB[4][:, 12:16], AF.Exp, [r_pb[4]], [r_wc[s]], scale=-1.0)
                yield
                for h in range(4):
                    i0 = (2 * h) % 3
                    i1 = (2 * h + 1) % 3
                    pe_b = PB[5 + i0][:, 0:128]
                    pe_a = PB[5 + i1][:, 0:128]
                    nb_l = nlf[:, s, h:h + 1].to_broadcast([128, 128])
                    ip_l = gsb[:, s, h:h + 1].to_broadcast([128, 128])
                    P.mm(pe_b, nb_l, tri, True, True, [r_nlf[s], r_c], [r_pb[5 + i0]])
                    P.mm(pe_a, ip_l, idf, True, False, [r_gs[s], r_c], [r_pb[5 + i1]], inc=False)
                    P.mm(pe_a, nb_l, tri, False, True, [r_nlf[s], r_c], [r_pb[5 + i1]])
                    P.act(EBt[:, h, sub], pe_b, AF.Exp, [r_pb[5 + i0], r_k], [r_EB[h][s]], scale=-1.0, bias=lnhb[:])
                    P.act(EAt[:, h, sub], pe_a, AF.Exp, [r_pb[5 + i1], r_k], [r_EA[h][s]], bias=lncb[:])
                    if h % 2 == 1:
                        yield
            for m in range(8):
                hh = m % 4
                pq = PB[2 + m % 2][:, 0:256]
                for c in range(8):
                    P.mm(pq, wB[:, c, QKB + m * 128:QKB + (m + 1) * 128], uTt[:, c, :], c == 0, c == 7,
                         [r_wB[c]] + r_uT[b], [r_pb[2 + m % 2]])
                P.cp("act", xc[:, m, 3:3 + T], pq, [r_pb[2 + m % 2]], [r_xc[m]])
                y = yc[m % 2]
                ry = r_yc[m % 2]
                P.ts("dve", y[:], xc[:, m, 0:T], cw[:, m, 0:1], ALU.mult, [r_xc[m], r_k], [ry])
                for j in range(1, 4):
                    P.stt("dve", y[:], xc[:, m, j:j + T], cw[:, m, j:j + 1], y[:], ALU.mult, ALU.add,
                          [r_xc[m], r_k, ry], [ry])
                P.cp("dve", xc[:, m, 0:3], xc[:, m, T:T + 3], [r_xc[m]], [r_xc[m]])
                if m < 4:
                    sl = slq[m % 2]
                    P.act(sl[:], y[:], AF.Tanh, [ry], [r_slq[m % 2]], scale=0.5)
                    P.stt("dve", y[:], sl[:], 1.0, y[:], ALU.add, ALU.mult, [r_slq[m % 2], ry], [ry])
                    P.tt("dve", qT[:, hh, :], y[:], EBt[:, hh, :], ALU.mult, [ry] + r_EB[hh], [r_qT[hh]])
                else:
                    sl = slq[m % 2]
                    P.act(sl[:], y[:], AF.Tanh, [ry], [r_slq[m % 2]], scale=0.5)
                    P.stt("dve", kT0[:, hh, :], sl[:], 1.0, y[:], ALU.add, ALU.mult, [r_slq[m % 2], ry], [r_kT0[hh]])
                    P.tt("dve", kT[:, hh, :], kT0[:, hh, :], EAt[:, hh, :], ALU.mult,
                         [r_kT0[hh]] + r_EA[hh], [r_kT[hh]])
                yield
            for s in range(2):
                sub = slice(s * 128, (s + 1) * 128)
                pnum = [PB[6][:, 0:129], PB[7][:, 0:129], PB[6][:, 256:385], PB[7][:, 256:385]]
                r_pn = [r_pb[6], r_pb[7], r_pb[6], r_pb[7]]
                pdc = [PB[2][:, 256:385], PB[3][:, 256:385], PB[2][:, 256:385], PB[3][:, 256:385]]
                r_pd = [r_pb[2], r_pb[3], r_pb[2], r_pb[3]]
                for h in range(4):
                    ptk = PBb[2 + h % 2][:, 0:128]
                    P.tr(ptk, kT0[:, h, sub], identb[:], [r_kT0[h], r_c], [r_pb[2 + h % 2]])
                    pS = PB[4 + h % 2][:, 0:128]
                    P.mm(pS, kT[:, h, sub], qT[:, h, sub], True, True, [r_kT[h], r_qT[h]], [r_pb[4 + h % 2]])
                    P.ts("dve", kS[:, h, :], ptk, ksc[:, s, h:h + 1], ALU.mult, [r_pb[2 + h % 2], r_ksc[s]],
                         [r_kS[h]])
                    P.tt("dve", scT[:, h, :], pS, trib[:], ALU.mult, [r_pb[4 + h % 2], r_k], [r_scT[h]])
                    yield
                    P.mm(pnum[h], qT[:, h, sub], Cbf[:, h, :], True, False, [r_qT[h], r_Cbf[h]], [r_pn[h]],
                         inc=False)
                    P.mm(pnum[h], scT[:, h, :], vaug[:, s, h, :], False, True, [r_scT[h], r_va[s]], [r_pn[h]])
                    P.mm(pdc[h], kS[:, h, :], vaug[:, s, h, :], True, True, [r_kS[h], r_va[s]], [r_pd[h]])
                    P.stt("dve", C32[:, h, :], C32[:, h, :], wc[:, s, h:h + 1], pdc[h], ALU.mult, ALU.add,
                          [r_C32[h], r_wc[s], r_pd[h]], [r_C32[h]])
                    P.cp("act", Cbf[:, h, :], C32[:, h, :], [r_C32[h]], [r_Cbf[h]])
                    yield
                for h in range(4):
                    P.act(junk[:, 0:128], pnum[h][:, 0:128], AF.Square, [r_pn[h]], [r_junk, r_ssq],
                          accum=ssq[:, h:h + 1])
                for h in range(4):
                    P.cp("dve", t4[:, h:h + 1], pnum[h][:, 128:129], [r_pn[h]], [r_t4])
                P.stt("dve", rr[:], t4[:], -1.0, t4[:], ALU.mult, ALU.max, [r_t4], [r_rr])
                P.ts("dve", rr[:], rr[:], 1.0, ALU.max, [r_rr], [r_rr])
                P.recip(rr[:], rr[:], [r_rr], [r_rr])
                P.tt("dve", t4[:], rr[:], rr[:], ALU.mult, [r_rr], [r_t4])
                P.tt("dve", t4[:], t4[:], ssq[:], ALU.mult, [r_t4, r_ssq], [r_t4])
                P.act(t4[:], t4[:], AF.Sqrt, [r_t4, r_k], [r_t4], bias=epsb[:], scale=1.0 / 128)
                P.recip(t4[:], t4[:], [r_t4], [r_t4])
                P.tt("dve", sc4[:], t4[:], rr[:], ALU.mult, [r_t4, r_rr], [r_sc4])
                yield
                for h in range(4):
                    P.stt("dve", ybf[:, h * 128:(h + 1) * 128], pnum[h][:, 0:128], sc4[:, h:h + 1],
                          gN[:, h * 128:(h + 1) * 128], ALU.mult, ALU.mult, [r_pn[h], r_sc4, r_k], [r_ybf])
                P.stt("dve", ybb[:], sgo[:, s, :], 1.0, ybf[:], ALU.add, ALU.mult, [r_ybf, r_sgo[s]], [r_ybb])
                yield
                for h in range(4):
                    P.tr(PBb[3][:, h * 128:(h + 1) * 128], ybb[:, h * 128:(h + 1) * 128], identb[:],
                         [r_ybb, r_c], [r_pb[3]], inc=(h == 3))
                P.cp("act", ybTs[b][:, :, sub], PBb[3][:, 0:512].rearrange("p (j t) -> p j t", j=4),
                     [r_pb[3]], [r_ybTs[b][s]])
                yield

        def stage_M(t):
            b = t % 2
            xt = xb[t % 3]
            uTt = uT[b]
            for f in range(8):
                fs = slice(f * 128, (f + 1) * 128)
                pga = PB[0][:, 0:256]
                pgb = PB[0][:, 256:512]
                ppa = PB[1][:, 0:256]
                ppb = PB[1][:, 256:512]
                for c in range(8):
                    P.mm(pga, wB[:, c, GAo + f * 128:GAo + (f + 1) * 128], uTt[:, c, :], c == 0, c == 7,
                         [r_wB2[c]] + r_uT[b], [r_pb[0]])
                for c in range(8):
                    P.mm(pgb, wB[:, c, GBo + f * 128:GBo + (f + 1) * 128], uTt[:, c, :], c == 0, c == 7,
                         [r_wB2[c]] + r_uT[b], [r_pb[0]])
                k2 = f % 2
                P.act(sga[k2][:], pga, AF.Tanh, [r_pb[0]], [r_sga[k2]], scale=0.5)
                P.act(sgb[k2][:], pgb, AF.Tanh, [r_pb[0]], [r_sgb[k2]], scale=0.5)
                for pr in range(4):
                    P.mm(ppa, wpa[:, pr, fs], yaT[t % 3][:, pr, :], pr == 0, pr == 3, [r_wpa, r_yaT[t % 3]], [r_pb[1]])
                for pr in range(4):
                    P.mm(ppb, wpb[:, pr, fs], ybTs[b][:, pr, :], pr == 0, pr == 3, [r_wpb] + r_ybTs[b], [r_pb[1]])
                P.stt("dve", m1[k2][:], sga[k2][:], 1.0, ppa, ALU.add, ALU.mult, [r_sga[k2], r_pb[1]], [r_m1[k2]])
                P.stt("dve", m2[k2][:], sgb[k2][:], 1.0, ppb, ALU.add, ALU.mult, [r_sgb[k2], r_pb[1]], [r_m2[k2]])
                P.tt("dve", mT[:, f, :], m1[k2][:], m2[k2][:], ALU.add, [r_m1[k2], r_m2[k2]], [r_mT[f]])
                yield
            for s in range(2):
                sub = slice(s * 128, (s + 1) * 128)
                for n in range(2):
                    po2 = PB[n][:]
                    rpo = [r_pb[n]]
                    for c in range(8):
                        P.mm(po2, mT[:, c, sub], wo[:, c, n * 512:(n + 1) * 512], c == 0, c == 7,
                             [r_mT[c], r_wo[c]], rpo)
                    P.stt("dve", xt[:, s, n * 512:(n + 1) * 512], po2, 0.5, xt[:, s, n * 512:(n + 1) * 512],
                          ALU.mult, ALU.add, rpo + [r_xb[t % 3][s]], [r_xb[t % 3][s]])
                    yield
            P.dma("sp", lambda e, t=t, xt=xt: e.dma_start(out=hv[t], in_=xt[:]), reads=r_xb[t % 3], writes=[r_h1[t]])

        def prefetch(tt_):
            if tt_ < NT:
                P.dma("sp", lambda e: e.dma_start(out=xb[tt_ % 3][:], in_=xv[tt_]), writes=r_xb[tt_ % 3])
                P.dma("sp", lambda e: e.dma_start(out=yaT[tt_ % 3][:], in_=ya_v[:, :, tt_ * T:(tt_ + 1) * T]),
                      reads=r_ya[tt_], writes=[r_yaT[tt_ % 3]])

        prefetch(0)
        prefetch(1)
        for _ in stage_A(0):
            pass
        for t in range(NT):
            if t + 1 < NT:
                prefetch(t + 2)
                side = stage_M(t)
                n_side = 12
                done = 0
                for i, _ in enumerate(stage_A(t + 1)):
                    want = ((i + 1) * n_side) // 46
                    while done < want:
                        next(side, None)
                        done += 1
                for _ in side:
                    pass
            else:
                for _ in stage_M(t):
                    pass


def phase_C(nc, P, S, h_d, r_h1, out_d, dr, G):
    T = 256
    NT = S // T
    NF = DFF // 128
    identb, r_ident = G["identb"], G["r_c"]
    g_ffn_d, g_fin_d, wgu_d, wdn_d = dr["norm_ffn_g"], dr["norm_final_g"], dr["w_gate_up"], dr["w_down"]
    with ExitStack() as st:
        ec = st.enter_context
        wgu = ec(nc.sbuf_tensor("wgu", [128, 8, 2 * DFF], BF16))
        wdn = ec(nc.sbuf_tensor("wdn", [128, NF, D], BF16))
        gF = ec(nc.sbuf_tensor("gF", [128, D], F32))
        gL = ec(nc.sbuf_tensor("gL", [128, D], F32))
        hb = [ec(nc.sbuf_tensor("hb%d" % i, [128, 2, D], F32)) for i in range(2)]
        ub = ec(nc.sbuf_tensor("ub", [128, D], BF16))
        uT = [ec(nc.sbuf_tensor("uT%d" % i, [128, 8, T], BF16)) for i in range(2)]
        aT = ec(nc.sbuf_tensor("aT", [128, NF, T], BF16))
        sg = [ec(nc.sbuf_tensor("sg%d" % i, [128, T], BF16)) for i in range(2)]
        junk = ec(nc.sbuf_tensor("junk", [128, D], BF16))
        st8 = ec(nc.sbuf_tensor("st8", [128, 8], F32))
        epsb = ec(nc.sbuf_tensor("epsb", [128, 1], F32))
        r_eps = Res("eps")
        P.op("dve", lambda e: e.memset(epsb[:], EPS), writes=[r_eps])
        ptp = [ec(nc.psum_tensor("ptp%d" % i, [128, 4, 128], BF16)) for i in range(2)]
        pgu = [ec(nc.psum_tensor("pgu%d" % i, [128, 512], F32)) for i in range(4)]
        pdn = [ec(nc.psum_tensor("pdn%d" % i, [128, 512], F32)) for i in range(2)]

        r_wdn = [Res("wdn%d" % c) for c in range(NF)]
        r_gF, r_gL = Res("gF"), Res("gL")
        r_hb = [[Res("hb%d_%d" % (i, s)) for s in range(2)] for i in range(2)]
        r_ub = Res("ub")
        r_uT = [Res("uT%d" % i) for i in range(2)]
        r_aT = [Res("aT%d" % j) for j in range(NF)]
        r_sg = [Res("sg%d" % i) for i in range(2)]
        r_junk = Res("junk")
        r_st = [Res("st%d" % i) for i in range(8)]
        r_ptp = [Res("ptp%d" % i) for i in range(2)]
        r_pgu = [Res("pgu%d" % i) for i in range(4)]
        r_pdn = [Res("pdn%d" % i) for i in range(2)]

        P.dma("pool", lambda e: e.dma_start(out=gF[:], in_=g_ffn_d.to_broadcast([128, D])), writes=[r_gF])
        P.dma("pool", lambda e: e.dma_start(out=gL[:], in_=g_fin_d.to_broadcast([128, D])), writes=[r_gL])
        wgu_v = wgu_d.rearrange("(c p) f -> p c f", p=128)
        wdn_v = wdn_d.rearrange("(c p) f -> p c f", p=128)
        NBLK = 11
        r_wgu = [Res("wgu%d" % i) for i in range(NBLK)]
        for i in range(NBLK):
            for hlf in range(2):
                lo = hlf * DFF + i * 256
                P.dma("pool", lambda e, lo=lo: e.dma_start(out=wgu[:, :, lo:lo + 256], in_=wgu_v[:, :, lo:lo + 256]),
                      writes=[r_wgu[i]])
        for c in range(NF):
            P.dma("pool", lambda e, c=c: e.dma_start(out=wdn[:, c, :], in_=wdn_v[:, c, :]),
                  writes=[r_wdn[c]])
        h_v = h_d.rearrange("(t s p) d -> t p s d", s=2, p=128)
        o_v = out_d.rearrange("(t s p) d -> t p s d", s=2, p=128)
        final_pts = []

        def rstd_from(h_ap, col, r_h):
            P.op("act", lambda e: e.activation(out=junk[:], in_=h_ap, func=AF.Square,
                                               accum_out=st8[:, col:col + 1]),
                 reads=[r_h], writes=[r_junk, r_st[col]])
            P.op("act", lambda e: e.activation(out=st8[:, col:col + 1], in_=st8[:, col:col + 1], func=AF.Sqrt,
                                               scale=1.0 / D, bias=epsb[:]),
                 reads=[r_st[col], r_eps], writes=[r_st[col]])
            P.op("dve", lambda e: e.reciprocal(out=st8[:, col:col + 1], in_=st8[:, col:col + 1]),
                 reads=[r_st[col]], writes=[r_st[col]])

        def prologue(t):
            b = t % 2
            hbt = hb[b]
            P.dma("sp", lambda e, t=t, hbt=hbt: e.dma_start(out=hbt[:], in_=h_v[t]),
                  reads=[r_h1[t]], writes=[r_hb[b][0], r_hb[b][1]])
            for s in range(2):
                rstd_from(hbt[:, s, :], s, r_hb[b][s])
                P.stt("dve", ub[:], hbt[:, s, :], st8[:, s:s + 1], gF[:], ALU.mult, ALU.mult,
                      [r_hb[b][s], r_st[s], r_gF], [r_ub])
                for g4 in range(2):
                    pt = ptp[g4]
                    for j in range(4):
                        c = g4 * 4 + j
                        P.tr(pt[:, j, :], ub[:, c * 128:(c + 1) * 128], identb[:], [r_ub, r_ident], [r_ptp[g4]],
                             inc=(j == 3))
                    P.cp("act", uT[b][:, g4 * 4:(g4 + 1) * 4, s * 128:(s + 1) * 128], pt[:], [r_ptp[g4]], [r_uT[b]])

        def gateup(t):
            b = t % 2
            for j in range(NF):
                pg = pgu[(2 * j) % 4]
                pu = pgu[(2 * j + 1) % 4]
                rg = r_pgu[(2 * j) % 4]
                ru = r_pgu[(2 * j + 1) % 4]
                for c in range(8):
                    P.mm(pg[:, 0:T], wgu[:, c, j * 128:(j + 1) * 128], uT[b][:, c, :], c == 0, c == 7,
                         [r_wgu[j // 2], r_uT[b]], [rg])
                for c in range(8):
                    P.mm(pu[:, 0:T], wgu[:, c, DFF + j * 128:DFF + (j + 1) * 128], uT[b][:, c, :], c == 0, c == 7,
                         [r_wgu[j // 2], r_uT[b]], [ru])
                sgt = sg[j % 2]
                P.act(sgt[:], pg[:, 0:T], AF.Silu, [rg], [r_sg[j % 2]])
                P.tt("dve", aT[:, j, :], sgt[:], pu[:, 0:T], ALU.mult, [ru, r_sg[j % 2]], [r_aT[j]])

        def down(t):
            b = t % 2
            hbt = hb[b]
            for s in range(2):
                for n in range(2):
                    pd = pdn[n]
                    for j in range(NF):
                        P.mm(pd[:], aT[:, j, s * 128:(s + 1) * 128], wdn[:, j, n * 512:(n + 1) * 512],
                             j == 0, j == NF - 1, [r_aT[j], r_wdn[j]], [r_pdn[n]])
                    P.tt("dve", hbt[:, s, n * 512:(n + 1) * 512], hbt[:, s, n * 512:(n + 1) * 512], pd[:], ALU.add,
                         [r_pdn[n], r_hb[b][s]], [r_hb[b][s]])
                rstd_from(hbt[:, s, :], 2 + s, r_hb[b][s])
                P.stt("dve", hbt[:, s, :], hbt[:, s, :], st8[:, 2 + s:3 + s], gL[:], ALU.mult, ALU.mult,
                      [r_hb[b][s], r_st[2 + s], r_gL], [r_hb[b][s]])
            final_pts.append(P.dma("sp", lambda e, t=t, hbt=hbt: e.dma_start(out=o_v[t], in_=hbt[:]),
                                   reads=[r_hb[b][0], r_hb[b][1]]))

        prologue(0)
        for t in range(NT):
            gateup(t)
            if t + 1 < NT:
                prologue(t + 1)
            down(t)
        return final_pts


def _host_consts(S):
    idx = np.arange(128)
    ident = np.eye(128, dtype=np.float32)
    tri = (idx[:, None] <= idx[None, :]).astype(np.float32)
    trim = -(idx[:, None] > idx[None, :]).astype(np.float32)
    ones = np.ones((128, 128), np.float32)
    consts = np.ascontiguousarray(np.stack([ident, tri, trim, ones], axis=1))
    inv = (np.float32(10000.0) ** (-np.arange(32, dtype=np.float32) / np.float32(32))).astype(np.float32)
    ang = np.arange(S, dtype=np.float32)[:, None] * inv[None, :]
    cos, sin = np.cos(ang).astype(np.float32), np.sin(ang).astype(np.float32)
    rope = np.ascontiguousarray(np.stack([np.concatenate([cos, cos], 1), np.concatenate([-sin, sin], 1)], axis=1))
    k = np.arange(128)[:, None, None]
    a = np.arange(4)[None, :, None]
    q = np.arange(512)[None, None, :]
    jq, ja = q // 256, a // 2
    cm = np.where(jq > ja, 1.0, np.where(jq < ja, 0.0, (q >= 128 * a + k).astype(np.float32)))
    cmask = np.ascontiguousarray(np.broadcast_to(cm, (128, 4, 512)).astype(ml_dtypes.bfloat16))
    blk = np.arange(16)[:, None]
    j = np.arange(16)[None, :]
    pastb = np.where(j < blk, 0.0, np.where(j == blk, 1e30, -1e30)).astype(np.float32)
    oneh = np.where(j == blk, -1024.0, 0.0).astype(np.float32)
    blkc = np.ascontiguousarray(np.broadcast_to(np.concatenate([pastb, oneh], 1)[None], (128, 16, 32)).astype(np.float32))
    return dict(consts=consts, rope=rope, cmask=cmask, blkc=blkc)


def make_in_maps(inp, ncores):
    f = lambda k: np.ascontiguousarray(np.asarray(inp[k], dtype=np.float32))
    x = np.asarray(inp["x"], dtype=np.float32)
    S = x.shape[1]
    common = {
        "norm_mix_g": f("norm_mix_g").reshape(1, D),
        "w_in": f("w_in")[0],
        "conv_wT": np.ascontiguousarray(f("conv_w")[0].T),
        "bias8": np.ascontiguousarray(np.concatenate([f("b_igate")[0], f("b_fgate")[0]]).reshape(1, 8)),
        "mlstm_norm_g": f("mlstm_norm_g").reshape(1, 512),
        "w_proj_a": f("w_proj_a")[0],
        "w_proj_b": f("w_proj_b")[0],
        "w_out": f("w_out")[0],
        "norm_ffn_g": f("norm_ffn_g").reshape(1, D),
        "norm_final_g": f("norm_final_g").reshape(1, D),
        "w_gate_up": f("w_gate_up")[0],
        "w_down": f("w_down")[0],
    }
    common.update(_host_consts(S))
    return [dict(common, x=np.ascontiguousarray(x[b])) for b in range(ncores)]


def kernel(x, norm_mix_g, w_in, conv_w, b_igate, b_fgate, mlstm_norm_g, w_proj_a, w_proj_b,
           w_out, norm_ffn_g, w_gate_up, w_down, norm_final_g):
    inp = dict(x=x, norm_mix_g=norm_mix_g, w_in=w_in, conv_w=conv_w, b_igate=b_igate, b_fgate=b_fgate,
               mlstm_norm_g=mlstm_norm_g, w_proj_a=w_proj_a, w_proj_b=w_proj_b, w_out=w_out,
               norm_ffn_g=norm_ffn_g, w_gate_up=w_gate_up, w_down=w_down, norm_final_g=norm_final_g)
    B, S, _ = np.asarray(x).shape
    nc = build_nc(S)
    in_maps = make_in_maps(inp, B)
    res = run_bass_kernel_spmd(nc, in_maps, core_ids=list(range(B)))
    return np.stack([np.asarray(r["out"]) for r in res.results], axis=0).astype(np.float32)
```

```python
import numpy as np
import ml_dtypes
from contextlib import ExitStack

import concourse.bass as bass
import concourse.mybir as mybir
from concourse.bass_utils import run_bass_kernel_spmd

F32 = mybir.dt.float32
BF16 = mybir.dt.bfloat16
ALU = mybir.AluOpType
AF = mybir.ActivationFunctionType
AX = mybir.AxisListType

D = 1024
DFF = 2816
EPS = 1e-6
NCORES = 8


class Res:
    __slots__ = ("name", "w", "r")

    def __init__(self, name):
        self.name = name
        self.w = None
        self.r = {}


class Prog:
    COMPUTE = ("pe", "dve", "act", "pool")

    def __init__(self, nc, stack, n_sp=12, n_pool=6):
        self.nc = nc
        self.sem = {}
        self.stack = stack
        self.epoch = 0
        self.ekey = {e: e for e in self.COMPUTE}
        for e in self.COMPUTE:
            self.sem[e] = stack.enter_context(nc.semaphore("s_" + e))
        self.dma_pool = {"sp": [], "pool": []}
        for i in range(n_sp):
            k = "dsp%d" % i
            self.sem[k] = stack.enter_context(nc.semaphore(k))
            self.dma_pool["sp"].append(k)
        for i in range(n_pool):
            k = "dpl%d" % i
            self.sem[k] = stack.enter_context(nc.semaphore(k))
            self.dma_pool["pool"].append(k)
        self.dma_rr = {"sp": 0, "pool": 0}
        self.cnt = {k: 0 for k in self.sem}
        self.ops = {e: [] for e in ("pe", "dve", "act", "pool", "sp")}
        self.know = {e: {} for e in self.ops}
        self.clock = {}
        self.pe_pending = False

    def _needs(self, reads, writes):
        need = {}
        for r in reads:
            if r.w is not None:
                k, v = r.w
                if need.get(k, 0) < v:
                    need[k] = v
        for w in writes:
            if w.w is not None:
                k, v = w.w
                if need.get(k, 0) < v:
                    need[k] = v
            for k, v in w.r.items():
                if need.get(k, 0) < v:
                    need[k] = v
        return need

    def _waits(self, eng, need):
        know = self.know[eng]
        waits = []
        for k, v in need.items():
            if eng == "pe" and k == self.ekey["pe"]:
                continue
            if know.get(k, 0) >= v:
                continue
            waits.append((k, v))
        for k, v in waits:
            ck = self.clock.get((k, v))
            if ck:
                for kk, vv in ck.items():
                    if know.get(kk, 0) < vv:
                        know[kk] = vv
            if know.get(k, 0) < v:
                know[k] = v
        return waits

    def _record(self, point, reads, writes):
        k, v = point
        for r in reads:
            if r.r.get(k, 0) < v:
                r.r[k] = v
        for w in writes:
            w.w = point
            w.r = {}

    def op(self, eng, fn, reads=(), writes=(), inc=True):
        need = self._needs(reads, writes)
        waits = self._waits(eng, need)
        sk = self.ekey[eng]
        if eng == "pe" and not inc:
            point = (sk, self.cnt[sk] + 1)
            self.pe_pending = True
            self.ops[eng].append((waits, fn, None, 0))
        else:
            self.cnt[sk] += 1
            point = (sk, self.cnt[sk])
            if eng == "pe":
                self.pe_pending = False
            self.ops[eng].append((waits, fn, sk, 1))
            ck = dict(self.know[eng])
            ck[sk] = self.cnt[sk]
            self.clock[point] = ck
        self._record(point, reads, writes)
        return point

    def new_epoch(self):
        self.epoch += 1
        for e in self.COMPUTE:
            k = "%s_%d" % (e, self.epoch)
            self.sem[k] = self.stack.enter_context(self.nc.semaphore("s_" + k))
            self.cnt[k] = 0
            self.ekey[e] = k

    def dma(self, q, fn, reads=(), writes=()):
        pool = self.dma_pool[q]
        sk = pool[self.dma_rr[q] % len(pool)]
        self.dma_rr[q] += 1
        need = self._needs(reads, writes)
        if self.cnt[sk] > 0 and need.get(sk, 0) < self.cnt[sk]:
            need[sk] = self.cnt[sk]
        waits = self._waits(q, need)
        self.cnt[sk] += 16
        point = (sk, self.cnt[sk])
        self.ops[q].append((waits, fn, sk, 16))
        ck = dict(self.know[q])
        ck[sk] = self.cnt[sk]
        self.clock[point] = ck
        self._record(point, reads, writes)
        return point

    def mm(self, out, lhsT, rhs, start, stop, reads, writes, inc=None):
        if inc is None:
            inc = stop
        return self.op("pe", lambda e: e.matmul(out, lhsT=lhsT, rhs=rhs, start=start, stop=stop),
                       reads, writes, inc)

    def tr(self, out, in_, ident, reads, writes, inc=True):
        return self.op("pe", lambda e: e.transpose(out=out, in_=in_, identity=ident), reads, writes, inc)

    def act(self, out, in_, func, reads, writes, bias=None, scale=None, accum=None):
        kw = {}
        if bias is not None:
            kw["bias"] = bias
        if scale is not None:
            kw["scale"] = scale
        if accum is not None:
            kw["accum_out"] = accum
        return self.op("act", lambda e: e.activation(out=out, in_=in_, func=func, **kw), reads, writes)

    def tt(self, eng, out, in0, in1, op, reads, writes):
        return self.op(eng, lambda e: e.tensor_tensor(out=out, in0=in0, in1=in1, op=op), reads, writes)

    def ts(self, eng, out, in0, s1, op0, reads, writes, s2=None, op1=None):
        if op1 is None:
            return self.op(eng, lambda e: e.tensor_scalar(out=out, in0=in0, scalar1=s1, scalar2=None, op0=op0),
                           reads, writes)
        return self.op(eng, lambda e: e.tensor_scalar(out=out, in0=in0, scalar1=s1, scalar2=s2, op0=op0, op1=op1),
                       reads, writes)

    def stt(self, eng, out, in0, scalar, in1, op0, op1, reads, writes):
        return self.op(eng, lambda e: e.scalar_tensor_tensor(out=out, in0=in0, scalar=scalar, in1=in1,
                                                             op0=op0, op1=op1), reads, writes)

    def cp(self, eng, out, in_, reads, writes):
        if eng == "act":
            return self.op("act", lambda e: e.copy(out=out, in_=in_), reads, writes)
        return self.op(eng, lambda e: e.tensor_copy(out=out, in_=in_), reads, writes)

    def recip(self, out, in_, reads, writes):
        return self.op("dve", lambda e: e.reciprocal(out=out, in_=in_), reads, writes)

    def memset(self, eng, ap, val, writes):
        return self.op(eng, lambda e: e.memset(ap, val), (), writes)

    def barrier(self):
        assert not self.pe_pending
        need = {k: v for k, v in self.cnt.items() if v > 0}
        for e in self.ops:
            waits = self._waits(e, dict(need))
            if waits:
                self.ops[e].append((waits, None, None, 0))

    def finish(self, final_points):
        need = {}
        for k, v in final_points:
            if need.get(k, 0) < v:
                need[k] = v
        waits = [(k, v) for k, v in need.items()]
        self.ops["sp"].append((waits, None, None, 0))

    def emit(self):
        nc = self.nc
        assert not self.pe_pending, "PE has trailing non-inc instructions"
        sem = self.sem
        ops = self.ops

        def run(e, lst):
            for waits, fn, sk, inc in lst:
                for k, v in waits:
                    e.wait_ge(sem[k], v)
                if fn is None:
                    continue
                ins = fn(e)
                if sk is not None:
                    ins.then_inc(sem[sk], inc)

        with nc.Block() as block:
            @block.tensor
            def _(e):
                run(e, ops["pe"])

            @block.vector
            def _(e):
                run(e, ops["dve"])

            @block.scalar
            def _(e):
                run(e, ops["act"])

            @block.gpsimd
            def _(e):
                run(e, ops["pool"])

            @block.sync
            def _(e):
                run(e, ops["sp"])


IN_W = 5640
A_QA, A_KA, A_VA = 0, 512, 1024
B_OFF = 1536
WB = IN_W - B_OFF
QKB, VBo, OBo, IBo, GAo, GBo = 0, 1024, 1536, 2048, 2056, 3080
LNC = float(np.log(128.0 ** -0.5))


def build_nc(S, phases=("A", "B", "C")):
    nc = bass.Bass("TRN2", target_bir_lowering=False)
    dr = {}

    def din(name, shape, dt=F32):
        dr[name] = nc.dram_tensor(name, shape, dt, kind="ExternalInput").ap()

    din("x", [S, D])
    din("norm_mix_g", [1, D])
    din("w_in", [D, IN_W])
    din("conv_wT", [D, 4])
    din("bias8", [1, 8])
    din("mlstm_norm_g", [1, 512])
    din("w_proj_a", [512, D])
    din("w_proj_b", [512, D])
    din("w_out", [D, D])
    din("norm_ffn_g", [1, D])
    din("norm_final_g", [1, D])
    din("w_gate_up", [D, 2 * DFF])
    din("w_down", [DFF, D])
    din("consts", [128, 4, 128])
    din("rope", [S, 2, 64])
    din("cmask", [128, 4, 512], BF16)
    din("blkc", [128, 16, 32])
    out_d = nc.dram_tensor("out", [S, D], F32, kind="ExternalOutput").ap()
    h1_d = nc.dram_tensor("h1_s", [S, D], F32, kind="Internal").ap()
    ya_d = nc.dram_tensor("ya_s", [8, 64, S], BF16, kind="Internal").ap()

    with ExitStack() as stack:
        P = Prog(nc, stack)
        ec = stack.enter_context
        identb = ec(nc.sbuf_tensor("identb", [128, 128], BF16))
        cst = ec(nc.sbuf_tensor("cst", [128, 4, 128], F32))
        r_c = Res("consts")
        P.dma("pool", lambda e: e.dma_start(out=identb[:], in_=dr["consts"][:, 0, :]), writes=[r_c])
        P.dma("sp", lambda e: e.dma_start(out=cst[:], in_=dr["consts"]), writes=[r_c])
        G = dict(identb=identb, cst=cst, r_c=r_c)

        r_ya = [[Res("ya_s%d_%d" % (i, h)) for h in range(8)] for i in range(S // 256)]
        r_h1 = [Res("h1_s%d" % i) for i in range(S // 256)]
        if "A" in phases:
            phase_A(nc, P, S, dr, ya_d, r_ya, G)
        else:
            with nc.sbuf_tensor("zt", [64, S], BF16) as zt:
                rz = Res("zt")
                P.memset("dve", zt[:], 0.0, [rz])
                for h in range(8):
                    P.dma("sp", lambda e, h=h: e.dma_start(out=ya_d[h], in_=zt[:]), reads=[rz],
                          writes=[r_ya[i][h] for i in range(S // 256)])
                P.barrier()
        P.barrier()
        P.new_epoch()
        if "B" in phases:
            phase_B(nc, P, S, dr, h1_d, ya_d, r_ya, r_h1, G)
            h_src = h1_d
        else:
            h_src = dr["x"]
        P.barrier()
        P.new_epoch()
        final_pts = phase_C(nc, P, S, h_src, r_h1, out_d, dr, G)
        P.finish(final_pts)
        P.emit()
    return nc


def load_norm_tile(nc, P, xt, r_x, s, gM, r_g, st8, r_st, col, epsb, r_eps, junk, r_junk, ub, r_ub):
    P.act(junk[:], xt[:, s, :], AF.Square, [r_x], [r_junk, r_st[col]], accum=st8[:, col:col + 1])
    P.act(st8[:, col:col + 1], st8[:, col:col + 1], AF.Sqrt, [r_st[col], r_eps], [r_st[col]],
          bias=epsb[:], scale=1.0 / D)
    P.recip(st8[:, col:col + 1], st8[:, col:col + 1], [r_st[col]], [r_st[col]])
    P.stt("dve", ub[:], xt[:, s, :], st8[:, col:col + 1], gM[:], ALU.mult, ALU.mult,
          [r_x, r_st[col], r_g], [r_ub])


def phase_A(nc, P, S, dr, ya_d, r_ya, G):
    T = 512
    NT = S // T
    NKT = S // 128
    identb, cst, r_c = G["identb"], G["cst"], G["r_c"]
    with ExitStack() as st:
        ec = st.enter_context

        def sbt(name, shape, dt):
            return ec(nc.sbuf_tensor(name, shape, dt))

        wA = sbt("wA", [128, 8, 1536], BF16)
        gM = sbt("gMa", [128, D], F32)
        cmk = sbt("cmk", [128, 4, 512], BF16)
        blkc = sbt("blkc_sb", [128, 16, 32], F32)
        epsb = sbt("epsb_a", [128, 1], F32)
        ropeT = [sbt("ropeT%d" % i, [128, 4, 2, 64], F32) for i in range(2)]
        kTa = sbt("kTa", [80, 8, S], BF16)
        vA = sbt("vA", [128, NKT, 8, 65], BF16)
        kmf = sbt("kmf", [64, 8], F32)
        kmT = sbt("kmT", [64, 8, 16], BF16)
        xs = [sbt("xs%d" % i, [128, 1, D], F32) for i in range(2)]
        ub = sbt("ub_a", [128, D], BF16)
        junk = sbt("junk_a", [128, D], BF16)
        st8 = sbt("st8_a", [128, 8], F32)
        uT = sbt("uTa", [128, 8, T], BF16)
        t1 = sbt("t1", [128, 8, 64], F32)
        t2 = sbt("t2", [128, 8, 64], F32)
        qtok = [sbt("qtok%d" % i, [128, 8, 80], BF16) for i in range(2)]
        ktok = [sbt("ktok%d" % i, [128, 8, 80], BF16) for i in range(2)]
        qTa = sbt("qTa", [80, 8, T], BF16)
        gsb = sbt("gsb_a", [128, 8, 16], F32)
        top8 = sbt("top8", [128, 8, 8], F32)
        PT = [sbt("PT%d" % i, [128, 512], BF16) for i in range(4)]
        oT = [sbt("oT%d" % i, [65, 512], F32) for i in range(2)]
        yo = [sbt("yo%d" % i, [64, 512], BF16) for i in range(2)]
        PB = [ec(nc.psum_tensor("PBa%d" % i, [128, 512], F32)) for i in range(8)]
        PBb = [PB[i].bitcast(BF16) for i in range(8)]

        R = lambda n: Res(n)
        r_wA = [R("wA%d" % c) for c in range(8)]
        r_k = R("constsA")
        r_rope = [R("rope%d" % i) for i in range(2)]
        r_kTa = [R("kTa%d" % i) for i in range(NKT)]
        r_vA = [R("vA%d" % i) for i in range(NKT)]
        r_kmf, r_kmT = R("kmf"), R("kmT")
        r_xs = [R("xs%d" % i) for i in range(2)]
        r_ub, r_junk = R("ub"), R("junk")
        r_st = [R("st%d" % i) for i in range(8)]
        r_uT = [R("uT%d" % s) for s in range(4)]
        r_t1, r_t2 = R("t1"), R("t2")
        r_qtok = [R("qtok%d" % i) for i in range(2)]
        r_ktok = [R("ktok%d" % i) for i in range(2)]
        r_qTa = [R("qTa%d" % s) for s in range(4)]
        r_gsb, r_top8 = R("gsb"), R("top8")
        r_PT = [R("PT%d" % i) for i in range(4)]
        r_oT = [R("oT%d" % i) for i in range(2)]
        r_yo = [R("yo%d" % i) for i in range(2)]
        r_pb = [R("pba%d" % i) for i in range(8)]

        wv = dr["w_in"].rearrange("(c p) f -> p c f", p=128)
        for c in range(8):
            P.dma("pool", lambda e, c=c: e.dma_start(out=wA[:, c, :], in_=wv[:, c, 0:1536]), writes=[r_wA[c]])
        P.dma("sp", lambda e: e.dma_start(out=gM[:], in_=dr["norm_mix_g"].to_broadcast([128, D])), writes=[r_k])
        P.dma("sp", lambda e: e.dma_start(out=cmk[:], in_=dr["cmask"]), writes=[r_k])
        P.dma("sp", lambda e: e.dma_start(out=blkc[:], in_=dr["blkc"]), writes=[r_k])
        P.memset("dve", epsb[:], EPS, [r_k])
        P.memset("dve", kmT[:], 0.0, [r_kmT])
        P.memset("dve", vA[:], 1.0, r_vA)

        xv = dr["x"].rearrange("(n p) d -> n p d", p=128)
        rope_v = dr["rope"].rearrange("(n s p) a d -> n p s a d", s=4, p=128)
        onesb = sbt("onesb", [128, 64], BF16)
        hiT = [sbt("hiT%d" % i, [65, 512], BF16) for i in range(2)]
        loT = [sbt("loT%d" % i, [65, 512], BF16) for i in range(2)]
        qTb = sbt("qTa2", [80, 8, T], BF16)
        qTas = [qTa, qTb]
        r_qTas = [r_qTa, [R("qTb%d" % s) for s in range(4)]]
        r_hl = [R("hl%d" % i) for i in range(2)]
        P.memset("dve", onesb[:], 1.0, [r_k])
        P.dma("sp", lambda e: e.dma_start(out=xs[0][:, 0, :], in_=xv[0]), writes=[r_xs[0]])

        def rope_ops(pz, rpz, rb, s, dst, rdst):
            z3 = pz.rearrange("p (h d) -> p h d", h=8)
            C2 = ropeT[rb][:, s, 0, :].unsqueeze(1).to_broadcast([128, 8, 64])
            Sa = ropeT[rb][:, s, 1, 0:32].unsqueeze(1).to_broadcast([128, 8, 32])
            Sb = ropeT[rb][:, s, 1, 32:64].unsqueeze(1).to_broadcast([128, 8, 32])
            P.tt("dve", t1[:], z3, C2, ALU.mult, [rpz, r_rope[rb]], [r_t1])
            P.tt("dve", t2[:, :, 0:32], z3[:, :, 32:64], Sa, ALU.mult, [rpz, r_rope[rb]], [r_t2])
            P.tt("dve", t2[:, :, 32:64], z3[:, :, 0:32], Sb, ALU.mult, [rpz, r_rope[rb]], [r_t2])
            P.tt("dve", dst[:, :, 0:64], t1[:], t2[:], ALU.add, [r_t1, r_t2], [rdst])

        def tr_group(src_fn, rsrc, rows, dst_fn, rdst, prow=None):
            for g4 in range(2):
                for j in range(4):
                    P.tr(PBb[g4][0:rows, j * 128:(j + 1) * 128], src_fn(g4 * 4 + j), identb[:], [rsrc, r_c],
                         [r_pb[g4]], inc=(j == 3))
            for g4 in range(2):
                lo_, hi_ = prow if prow else (0, rows)
                P.cp("dve", dst_fn(g4, lo_, hi_), PBb[g4][lo_:hi_, 0:512].rearrange("p (j t) -> p j t", j=4),
                     [r_pb[g4]], [rdst])

        def prologue(t):
            rb = t % 2
            qT_ = qTas[t % 2]
            rq_ = r_qTas[t % 2]
            P.dma("sp", lambda e, t=t, rb=rb: e.dma_start(out=ropeT[rb][:], in_=rope_v[t]), writes=[r_rope[rb]])
            for s in range(4):
                n = t * 4 + s
                sub = slice(s * 128, (s + 1) * 128)
                ksub = slice(n * 128, (n + 1) * 128)
                blk = n // 2
                if n + 1 < NKT:
                    P.dma("sp", lambda e, n=n: e.dma_start(out=xs[(n + 1) % 2][:, 0, :], in_=xv[n + 1]),
                          writes=[r_xs[(n + 1) % 2]])
                xt = xs[n % 2]
                load_norm_tile(nc, P, xt, r_xs[n % 2], 0, gM, r_k, st8, r_st, s, epsb, r_k, junk, r_junk, ub, r_ub)
                yield
                for g4 in range(2):
                    for j in range(4):
                        c = g4 * 4 + j
                        P.tr(PBb[g4][:, j * 128:(j + 1) * 128], ub[:, c * 128:(c + 1) * 128], identb[:],
                             [r_ub, r_c], [r_pb[g4]], inc=(j == 3))
                for g4 in range(2):
                    P.cp("dve", uT[:, g4 * 4:(g4 + 1) * 4, sub],
                         PBb[g4][:, 0:512].rearrange("p (j t) -> p j t", j=4), [r_pb[g4]], [r_uT[s]])
                yield
                kb, rkb = ktok[s % 2], r_ktok[s % 2]
                pz = PB[5][:]
                for c in range(8):
                    P.mm(pz, uT[:, c, sub], wA[:, c, A_KA:A_KA + 512], c == 0, c == 7, [r_uT[s], r_wA[c]], [r_pb[5]])
                rope_ops(pz, r_pb[5], rb, s, kb, rkb)
                P.cp("dve", kb[:, :, 64:80], blkc[:, blk, 16:32].unsqueeze(1).to_broadcast([128, 8, 16]),
                     [r_k], [rkb])
                yield
                tr_group(lambda h: kb[:, h, :], rkb, 80,
                         lambda g4, lo_, hi_: kTa[lo_:hi_, g4 * 4:(g4 + 1) * 4, ksub], r_kTa[n])
                if s % 2 == 1:
                    P.op("dve", lambda e, blk=blk: e.tensor_reduce(
                        out=kmf[:], in_=kTa[0:64, :, blk * 256:(blk + 1) * 256], axis=AX.X, op=ALU.add),
                        [r_kTa[n - 1], r_kTa[n]], [r_kmf])
                    P.ts("dve", kmT[:, :, blk], kmf[:], 1.0 / 256, ALU.mult, [r_kmf], [r_kmT])
                yield
                for c in range(8):
                    P.mm(pz, uT[:, c, sub], wA[:, c, A_VA:A_VA + 512], c == 0, c == 7, [r_uT[s], r_wA[c]], [r_pb[5]])
                P.cp("dve", vA[:, n, :, 0:64], pz.rearrange("p (h d) -> p h d", h=8), [r_pb[5]], [r_vA[n]])
                yield
                qb, rqb = qtok[s % 2], r_qtok[s % 2]
                for c in range(8):
                    P.mm(pz, uT[:, c, sub], wA[:, c, A_QA:A_QA + 512], c == 0, c == 7, [r_uT[s], r_wA[c]], [r_pb[5]])
                rope_ops(pz, r_pb[5], rb, s, qb, rqb)
                yield
                tr_group(lambda h: qb[:, h, 0:64], rqb, 64,
                         lambda g4, lo_, hi_: qT_[lo_:hi_, g4 * 4:(g4 + 1) * 4, sub], rq_[s])
                yield
                pgt = PB[5][:, 0:128].rearrange("p (h j) -> p h j", h=8)
                for h in range(8):
                    P.mm(pgt[:, h, :], qT_[0:64, h, sub], kmT[:, h, :], True, True, [rq_[s], r_kmT], [r_pb[5]],
                         inc=(h == 7))
                P.tt("dve", gsb[:], pgt, blkc[:, blk, 0:16].unsqueeze(1).to_broadcast([128, 8, 16]), ALU.add,
                     [r_pb[5], r_k], [r_gsb])
                for h in range(8):
                    P.op("dve", lambda e, h=h: e.max(out=top8[:, h, :], in_=gsb[:, h, :]), [r_gsb], [r_top8])
                P.tt("dve", qb[:, :, 64:80], gsb[:], top8[:, :, 3:4].to_broadcast([128, 8, 16]), ALU.is_lt,
                     [r_gsb, r_top8], [rqb])
                yield
                tr_group(lambda h: qb[:, h, :], rqb, 80,
                         lambda g4, lo_, hi_: qT_[lo_:hi_, g4 * 4:(g4 + 1) * 4, sub], rq_[s], prow=(64, 80))
                yield

        def attention(t):
            qT_ = qTas[t % 2]
            rq_ = r_qTas[t % 2]
            nkt = 4 * t + 4
            items = [(h, kt) for h in range(8) for kt in range(nkt)]
            LA = 2
            deferred = {}

            def emit_S(i):
                h, kt = items[i]
                bi = 2 + i % 3
                P.mm(PB[bi][:], kTa[:, h, kt * 128:(kt + 1) * 128], qT_[:, h, :], True, True,
                     [r_kTa[kt]] + rq_, [r_pb[bi]])

            def emit_PV(i):
                h, kt = items[i]
                bi = 2 + i % 3
                pi = i % 4
                po = PB[6 + h % 2]
                rpo = r_pb[6 + h % 2]
                P.act(PT[pi][:], PB[bi][:], AF.Exp, [r_pb[bi]], [r_PT[pi]], scale=0.125)
                if kt >= 4 * t:
                    P.tt("pool", PT[pi][:], PT[pi][:], cmk[:, kt - 4 * t, :], ALU.mult, [r_PT[pi], r_k],
                         [r_PT[pi]])
                P.mm(po[0:65, :], vA[:, kt, h, :], PT[pi][:], kt == 0, kt == nkt - 1, [r_vA[kt], r_PT[pi]], [rpo])
                if kt == nkt - 1:
                    ob, rob = oT[h % 2], r_oT[h % 2]
                    P.cp("dve", ob[:], po[0:65, :], [rpo], [rob])
                    P.recip(ob[64:65, :], ob[64:65, :], [rob], [rob])
                    P.cp("dve", hiT[h % 2][64:65, :], ob[64:65, :], [rob], [r_hl[h % 2]])
                    P.tt("dve", loT[h % 2][64:65, :], ob[64:65, :], hiT[h % 2][64:65, :], ALU.subtract,
                         [rob, r_hl[h % 2]], [r_hl[h % 2]])
                    deferred.setdefault(min(i + 4, len(items) - 1), []).append(h)

            def emit_epi(h):
                ob, rob = oT[h % 2], r_oT[h % 2]
                P.mm(PB[5][0:64, :], onesb[64:65, :], hiT[h % 2][64:65, :], True, False, [r_k, r_hl[h % 2]],
                     [r_pb[5]], inc=False)
                P.mm(PB[5][0:64, :], onesb[64:65, :], loT[h % 2][64:65, :], False, True, [r_k, r_hl[h % 2]],
                     [r_pb[5]])
                yb_ = yo[h % 2]
                P.tt("dve", yb_[:], ob[0:64, :], PB[5][0:64, :], ALU.mult, [rob, r_pb[5]], [r_yo[h % 2]])
                P.dma("sp", lambda e, h=h, t=t, yb_=yb_: e.dma_start(out=ya_d[h, :, t * T:(t + 1) * T], in_=yb_[:]),
                      reads=[r_yo[h % 2]], writes=[r_ya[2 * t][h], r_ya[2 * t + 1][h]])

            for i in range(min(LA, len(items))):
                emit_S(i)
            for i in range(len(items)):
                if i + LA < len(items):
                    emit_S(i + LA)
                emit_PV(i)
                for hh in deferred.pop(i, []):
                    emit_epi(hh)
                yield
            assert not deferred

        for _ in prologue(0):
            pass
        for t in range(NT):
            n_main = 8 * (4 * t + 4)
            side = prologue(t + 1) if t + 1 < NT else iter(())
            n_side = 36
            done = 0
            for i, _ in enumerate(attention(t)):
                want = ((i + 1) * n_side) // n_main
                while done < want:
                    next(side, None)
                    done += 1
            for _ in side:
                pass


def phase_B(nc, P, S, dr, h1_d, ya_d, r_ya, r_h1, G):
    T = 256
    NT = S // T
    identb, cst, r_c = G["identb"], G["cst"], G["r_c"]
    idf, tri, trim, onesf = cst[:, 0, :], cst[:, 1, :], cst[:, 2, :], cst[:, 3, :]
    with ExitStack() as st:
        ec = st.enter_context

        def sbt(name, shape, dt):
            return ec(nc.sbuf_tensor(name, shape, dt))

        wB = sbt("wB", [128, 8, WB], BF16)
        wpa = sbt("wpa", [128, 4, D], BF16)
        wpb = sbt("wpb", [128, 4, D], BF16)
        wo = sbt("wo", [128, 8, D], BF16)
        gM = sbt("gMb", [128, D], F32)
        gN = sbt("gN", [128, 512], F32)
        bias8 = sbt("bias8_sb", [128, 8], F32)
        cw = sbt("cw", [128, 8, 4], F32)
        trib = sbt("trib", [128, 128], BF16)
        epsb = sbt("epsb_b", [128, 1], F32)
        oneb = sbt("oneb", [128, 1], F32)
        lncb = sbt("lncb", [128, 1], F32)
        lnhb = sbt("lnhb", [128, 1], F32)
        xb = [sbt("xb%d" % i, [128, 2, D], F32) for i in range(3)]
        yaT = [sbt("yaT%d" % i, [128, 4, T], BF16) for i in range(3)]
        uT = [sbt("uTb%d" % i, [128, 8, T], BF16) for i in range(2)]
        ub = sbt("ub_b", [128, D], BF16)
        junk = sbt("junk_b", [128, D], BF16)
        st8 = sbt("st8_b", [128, 8], F32)
        gsb = sbt("gsb", [128, 2, 8], F32)
        e1 = sbt("e1", [128, 2, 4], F32)
        nlf = sbt("nlf", [128, 2, 4], F32)
        arg = sbt("arg", [128, 2, 4], F32)
        ksc = sbt("ksc", [128, 2, 4], F32)
        wc = sbt("wc", [128, 2, 4], F32)
        EBt = sbt("EBt", [128, 4, T], F32)
        EAt = sbt("EAt", [128, 4, T], F32)
        xc = sbt("xc", [128, 8, T + 3], F32)
        yc = [sbt("yc%d" % i, [128, T], F32) for i in range(2)]
        slq = [sbt("slq%d" % i, [128, T], BF16) for i in range(2)]
        kT0 = sbt("kT0", [128, 4, T], BF16)
        qT = sbt("qTb", [128, 4, T], BF16)
        kT = sbt("kTb", [128, 4, T], BF16)
        vaug = sbt("vaug", [128, 2, 4, 129], BF16)
        sgo = sbt("sgo", [128, 2, 512], F32)
        kS = sbt("kS", [128, 4, 128], BF16)
        scT = sbt("scT", [128, 4, 128], BF16)
        C32 = sbt("C32", [128, 4, 129], F32)
        Cbf = sbt("Cbf", [128, 4, 129], BF16)
        rr = sbt("rr", [128, 4], F32)
        ssq = sbt("ssq", [128, 4], F32)
        t4 = sbt("t4", [128, 4], F32)
        sc4 = sbt("sc4", [128, 4], F32)
        ybf = sbt("ybf", [128, 512], F32)
        ybb = sbt("ybb", [128, 512], BF16)
        ybT = sbt("ybT", [128, 4, T], BF16)
        sga = [sbt("sga%d" % i, [128, T], F32) for i in range(2)]
        sgb = [sbt("sgb%d" % i, [128, T], F32) for i in range(2)]
        m1 = [sbt("m1_%d" % i, [128, T], F32) for i in range(2)]
        m2 = [sbt("m2_%d" % i, [128, T], F32) for i in range(2)]
        mT = sbt("mT", [128, 8, T], BF16)
        PB = [ec(nc.psum_tensor("PBb%d" % i, [128, 512], F32)) for i in range(8)]

        R = lambda n: Res(n)
        r_wB = [R("wB%d" % c) for c in range(8)]
        r_wpa, r_wpb, r_wo = R("wpa"), R("wpb"), [R("wo%d" % c) for c in range(8)]
        r_k = R("constsB")
        r_xb = [[R("xb%d_%d" % (i, s)) for s in range(2)] for i in range(3)]
        r_yaT = [R("yaT%d" % i) for i in range(3)]
        r_uT = [[R("uTb%d_%d" % (i, s)) for s in range(2)] for i in range(2)]
        r_ub, r_junk = R("ub"), R("junk")
        r_st = [R("st%d" % i) for i in range(8)]
        r_gs = [R("gsb%d" % s) for s in range(2)]
        r_e1 = [R("e1%d" % s) for s in range(2)]
        r_nlf = [R("nlf%d" % s) for s in range(2)]
        r_arg = [R("arg%d" % s) for s in range(2)]
        r_ksc = [R("ksc%d" % s) for s in range(2)]
        r_wc = [R("wc%d" % s) for s in range(2)]
        r_EB = [[R("EB%d_%d" % (h, s)) for s in range(2)] for h in range(4)]
        r_EA = [[R("EA%d_%d" % (h, s)) for s in range(2)] for h in range(4)]
        r_xc = [R("xc%d" % m) for m in range(8)]
        r_yc = [R("yc%d" % i) for i in range(2)]
        r_slq = [R("slq%d" % i) for i in range(2)]
        r_kT0 = [R("kT0%d" % h) for h in range(4)]
        r_qT = [R("qT%d" % h) for h in range(4)]
        r_kT = [R("kT%d" % h) for h in range(4)]
        r_va = [R("vaug%d" % s) for s in range(2)]
        r_sgo = [R("sgo%d" % s) for s in range(2)]
        r_kS = [R("kS%d" % h) for h in range(4)]
        r_scT = [R("scT%d" % h) for h in range(4)]
        r_C32 = [R("C32%d" % h) for h in range(4)]
        r_Cbf = [R("Cbf%d" % h) for h in range(4)]
        r_rr, r_ssq, r_t4, r_sc4 = R("rr"), R("ssq"), R("t4"), R("sc4")
        r_ybf, r_ybb = R("ybf"), R("ybb")
        r_ybT = [R("ybT%d" % s) for s in range(2)]
        r_sga = [R("sga%d" % i) for i in range(2)]
        r_sgb = [R("sgb%d" % i) for i in range(2)]
        r_m1 = [R("m1%d" % i) for i in range(2)]
        r_m2 = [R("m2%d" % i) for i in range(2)]
        r_mT = [R("mT%d" % f) for f in range(8)]
        r_pb = [R("pb%d" % i) for i in range(8)]
        PBb = [PB[i].bitcast(BF16) for i in range(8)]

        wv = dr["w_in"].rearrange("(c p) f -> p c f", p=128)
        r_wB2 = [R("wB2_%d" % c) for c in range(8)]
        for c in range(8):
            P.dma("pool", lambda e, c=c: e.dma_start(out=wB[:, c, 0:GAo], in_=wv[:, c, B_OFF:B_OFF + GAo]),
                  writes=[r_wB[c]])
        for c in range(8):
            P.dma("pool", lambda e, c=c: e.dma_start(out=wB[:, c, GAo:WB], in_=wv[:, c, B_OFF + GAo:IN_W]),
                  writes=[r_wB2[c]])
        P.dma("pool", lambda e: e.dma_start(out=wpa[:], in_=dr["w_proj_a"].rearrange("(c p) f -> p c f", p=128)),
              writes=[r_wpa])
        P.dma("pool", lambda e: e.dma_start(out=wpb[:], in_=dr["w_proj_b"].rearrange("(c p) f -> p c f", p=128)),
              writes=[r_wpb])
        wov = dr["w_out"].rearrange("(c p) f -> p c f", p=128)
        for c in range(8):
            P.dma("pool", lambda e, c=c: e.dma_start(out=wo[:, c, :], in_=wov[:, c, :]), writes=[r_wo[c]])
        P.dma("pool", lambda e: e.dma_start(out=trib[:], in_=dr["consts"][:, 1, :]), writes=[r_k])
        P.dma("sp", lambda e: e.dma_start(out=gM[:], in_=dr["norm_mix_g"].to_broadcast([128, D])), writes=[r_k])
        P.dma("sp", lambda e: e.dma_start(out=gN[:], in_=dr["mlstm_norm_g"].to_broadcast([128, 512])), writes=[r_k])
        P.dma("sp", lambda e: e.dma_start(out=bias8[:], in_=dr["bias8"].to_broadcast([128, 8])), writes=[r_k])
        P.dma("sp", lambda e: e.dma_start(out=cw[:], in_=dr["conv_wT"].rearrange("(m p) j -> p m j", p=128)),
              writes=[r_k])
        P.ts("dve", gN[:], gN[:], 0.5, ALU.mult, [r_k], [r_k])
        P.memset("dve", epsb[:], EPS, [r_k])
        P.memset("dve", oneb[:], 1.0, [r_k])
        P.memset("dve", lncb[:], LNC + float(np.log(0.5)), [r_k])
        P.memset("dve", lnhb[:], float(np.log(0.5)), [r_k])
        P.memset("dve", C32[:], 0.0, r_C32)
        P.memset("dve", Cbf[:], 0.0, r_Cbf)
        P.memset("dve", vaug[:], 1.0, r_va)
        P.memset("dve", xc[:], 0.0, r_xc)

        xv = dr["x"].rearrange("(t s p) d -> t p s d", s=2, p=128)
        hv = h1_d.rearrange("(t s p) d -> t p s d", s=2, p=128)
        ya_v = ya_d.rearrange("h d s -> (h d) s").rearrange("(c p) s -> p c s", p=128)

        ybT2 = sbt("ybT2", [128, 4, T], BF16)
        ybTs = [ybT, ybT2]
        r_ybTs = [r_ybT, [R("ybTb%d" % s) for s in range(2)]]

        def stage_A(t):
            b = t % 2
            xt = xb[t % 3]
            uTt = uT[b]
            for s in range(2):
                sub = slice(s * 128, (s + 1) * 128)
                load_norm_tile(nc, P, xt, r_xb[t % 3][s], s, gM, r_k, st8, r_st, s, epsb, r_k, junk, r_junk, ub, r_ub)
                yield
                for g4 in range(2):
                    for j in range(4):
                        c = g4 * 4 + j
                        P.tr(PBb[2 + g4][:, j * 128:(j + 1) * 128], ub[:, c * 128:(c + 1) * 128],
                             identb[:], [r_ub, r_c], [r_pb[2 + g4]], inc=(j == 3))
                for g4 in range(2):
                    P.cp("act", uTt[:, g4 * 4:(g4 + 1) * 4, sub],
                         PBb[2 + g4][:, 0:512].rearrange("p (j t) -> p j t", j=4),
                         [r_pb[2 + g4]], [r_uT[b][s]])
                yield
                pg = PB[4][:, 0:8]
                for c in range(8):
                    P.mm(pg, uTt[:, c, sub], wB[:, c, IBo:IBo + 8], c == 0, c == 7,
                         [r_uT[b][s], r_wB[c]], [r_pb[4]])
                P.tt("dve", gsb[:, s, :], pg, bias8[:], ALU.add, [r_pb[4], r_k], [r_gs[s]])
                P.act(e1[:, s, :], gsb[:, s, 4:8], AF.Exp, [r_gs[s]], [r_e1[s]], scale=-1.0)
                P.act(nlf[:, s, :], e1[:, s, :], AF.Ln, [r_e1[s], r_k], [r_nlf[s]], bias=oneb[:])
                pv = PB[2][:]
                for c in range(8):
                    P.mm(pv, uTt[:, c, sub], wB[:, c, VBo:VBo + 512], c == 0, c == 7,
                         [r_uT[b][s], r_wB[c]], [r_pb[2]])
                P.cp("act", vaug[:, s, :, 0:128], pv.rearrange("p (h e) -> p h e", h=4), [r_pb[2]], [r_va[s]])
                yield
                po = PB[3][:]
                for c in range(8):
                    P.mm(po, uTt[:, c, sub], wB[:, c, OBo:OBo + 512], c == 0, c == 7,
                         [r_uT[b][s], r_wB[c]], [r_pb[3]])
                P.act(sgo[:, s, :], po, AF.Tanh, [r_pb[3]], [r_sgo[s]], scale=0.5)
                P.mm(PB[4][:, 8:12], trim, nlf[:, s, :], True, True, [r_c, r_nlf[s]], [r_pb[4]], inc=False)
                P.mm(PB[4][:, 12:16], onesf, nlf[:, s, :], True, True, [r_c, r_nlf[s]], [r_pb[4]])
                P.tt("dve", arg[:, s, :], PB[4][:, 8:12], gsb[:, s, 0:4], ALU.add, [r_pb[4], r_gs[s]], [r_arg[s]])
                P.act(ksc[:, s, :], arg[:, s, :], AF.Exp, [r_arg[s], r_k], [r_ksc[s]], bias=lncb[:])
                P.act(wc[:, s, :], PB[4][:, 12:16], AF.Exp, [r_pb[4]], [r_wc[s]], scale=-1.0)
                yield
                for h in range(4):
                    i0 = (2 * h) % 3
                    i1 = (2 * h + 1) % 3
                    pe_b = PB[5 + i0][:, 0:128]
                    pe_a = PB[5 + i1][:, 0:128]
                    nb_l = nlf[:, s, h:h + 1].to_broadcast([128, 128])
                    ip_l = gsb[:, s, h:h + 1].to_broadcast([128, 128])
                    P.mm(pe_b, nb_l, tri, True, True, [r_nlf[s], r_c], [r_pb[5 + i0]])
                    P.mm(pe_a, ip_l, idf, True, False, [r_gs[s], r_c], [r_pb[5 + i1]], inc=False)
                    P.mm(pe_a, nb_l, tri, False, True, [r_nlf[s], r_c], [r_pb[5 + i1]])
                    P.act(EBt[:, h, sub], pe_b, AF.Exp, [r_pb[5 + i0], r_k], [r_EB[h][s]], scale=-1.0, bias=lnhb[:])
                    P.act(EAt[:, h, sub], pe_a, AF.Exp, [r_pb[5 + i1], r_k], [r_EA[h][s]], bias=lncb[:])
                    if h % 2 == 1:
                        yield
            for m in range(8):
                hh = m % 4
                pq = PB[2 + m % 2][:, 0:256]
                for c in range(8):
                    P.mm(pq, wB[:, c, QKB + m * 128:QKB + (m + 1) * 128], uTt[:, c, :], c == 0, c == 7,
                         [r_wB[c]] + r_uT[b], [r_pb[2 + m % 2]])
                P.cp("act", xc[:, m, 3:3 + T], pq, [r_pb[2 + m % 2]], [r_xc[m]])
                y = yc[m % 2]
                ry = r_yc[m % 2]
                P.ts("dve", y[:], xc[:, m, 0:T], cw[:, m, 0:1], ALU.mult, [r_xc[m], r_k], [ry])
                for j in range(1, 4):
                    P.stt("dve", y[:], xc[:, m, j:j + T], cw[:, m, j:j + 1], y[:], ALU.mult, ALU.add,
                          [r_xc[m], r_k, ry], [ry])
                P.cp("dve", xc[:, m, 0:3], xc[:, m, T:T + 3], [r_xc[m]], [r_xc[m]])
                if m < 4:
                    sl = slq[m % 2]
                    P.act(sl[:], y[:], AF.Tanh, [ry], [r_slq[m % 2]], scale=0.5)
                    P.stt("dve", y[:], sl[:], 1.0, y[:], ALU.add, ALU.mult, [r_slq[m % 2], ry], [ry])
                    P.tt("dve", qT[:, hh, :], y[:], EBt[:, hh, :], ALU.mult, [ry] + r_EB[hh], [r_qT[hh]])
                else:
                    sl = slq[m % 2]
                    P.act(sl[:], y[:], AF.Tanh, [ry], [r_slq[m % 2]], scale=0.5)
                    P.stt("dve", kT0[:, hh, :], sl[:], 1.0, y[:], ALU.add, ALU.mult, [r_slq[m % 2], ry], [r_kT0[hh]])
                    P.tt("dve", kT[:, hh, :], kT0[:, hh, :], EAt[:, hh, :], ALU.mult,
                         [r_kT0[hh]] + r_EA[hh], [r_kT[hh]])
                yield
            for s in range(2):
                sub = slice(s * 128, (s + 1) * 128)
                pnum = [PB[6][:, 0:129], PB[7][:, 0:129], PB[6][:, 256:385], PB[7][:, 256:385]]
                r_pn = [r_pb[6], r_pb[7], r_pb[6], r_pb[7]]
                pdc = [PB[2][:, 256:385], PB[3][:, 256:385], PB[2][:, 256:385], PB[3][:, 256:385]]
                r_pd = [r_pb[2], r_pb[3], r_pb[2], r_pb[3]]
                for h in range(4):
                    ptk = PBb[2 + h % 2][:, 0:128]
                    P.tr(ptk, kT0[:, h, sub], identb[:], [r_kT0[h], r_c], [r_pb[2 + h % 2]])
                    pS = PB[4 + h % 2][:, 0:128]
                    P.mm(pS, kT[:, h, sub], qT[:, h, sub], True, True, [r_kT[h], r_qT[h]], [r_pb[4 + h % 2]])
                    P.ts("dve", kS[:, h, :], ptk, ksc[:, s, h:h + 1], ALU.mult, [r_pb[2 + h % 2], r_ksc[s]],
                         [r_kS[h]])
                    P.tt("dve", scT[:, h, :], pS, trib[:], ALU.mult, [r_pb[4 + h % 2], r_k], [r_scT[h]])
                    yield
                    P.mm(pnum[h], qT[:, h, sub], Cbf[:, h, :], True, False, [r_qT[h], r_Cbf[h]], [r_pn[h]],
                         inc=False)
                    P.mm(pnum[h], scT[:, h, :], vaug[:, s, h, :], False, True, [r_scT[h], r_va[s]], [r_pn[h]])
                    P.mm(pdc[h], kS[:, h, :], vaug[:, s, h, :], True, True, [r_kS[h], r_va[s]], [r_pd[h]])
                    P.stt("dve", C32[:, h, :], C32[:, h, :], wc[:, s, h:h + 1], pdc[h], ALU.mult, ALU.add,
                          [r_C32[h], r_wc[s], r_pd[h]], [r_C32[h]])
                    P.cp("act", Cbf[:, h, :], C32[:, h, :], [r_C32[h]], [r_Cbf[h]])
                    yield
                for h in range(4):
                    P.act(junk[:, 0:128], pnum[h][:, 0:128], AF.Square, [r_pn[h]], [r_junk, r_ssq],
                          accum=ssq[:, h:h + 1])
                for h in range(4):
                    P.cp("dve", t4[:, h:h + 1], pnum[h][:, 128:129], [r_pn[h]], [r_t4])
                P.stt("dve", rr[:], t4[:], -1.0, t4[:], ALU.mult, ALU.max, [r_t4], [r_rr])
                P.ts("dve", rr[:], rr[:], 1.0, ALU.max, [r_rr], [r_rr])
                P.recip(rr[:], rr[:], [r_rr], [r_rr])
                P.tt("dve", t4[:], rr[:], rr[:], ALU.mult, [r_rr], [r_t4])
                P.tt("dve", t4[:], t4[:], ssq[:], ALU.mult, [r_t4, r_ssq], [r_t4])
                P.act(t4[:], t4[:], AF.Sqrt, [r_t4, r_k], [r_t4], bias=epsb[:], scale=1.0 / 128)
                P.recip(t4[:], t4[:], [r_t4], [r_t4])
                P.tt("dve", sc4[:], t4[:], rr[:], ALU.mult, [r_t4, r_rr], [r_sc4])
                yield
                for h in range(4):
                    P.stt("dve", ybf[:, h * 128:(h + 1) * 128], pnum[h][:, 0:128], sc4[:, h:h + 1],
                          gN[:, h * 128:(h + 1) * 128], ALU.mult, ALU.mult, [r_pn[h], r_sc4, r_k], [r_ybf])
                P.stt("dve", ybb[:], sgo[:, s, :], 1.0, ybf[:], ALU.add, ALU.mult, [r_ybf, r_sgo[s]], [r_ybb])
                yield
                for h in range(4):
                    P.tr(PBb[3][:, h * 128:(h + 1) * 128], ybb[:, h * 128:(h + 1) * 128], identb[:],
                         [r_ybb, r_c], [r_pb[3]], inc=(h == 3))
                P.cp("act", ybTs[b][:, :, sub], PBb[3][:, 0:512].rearrange("p (j t) -> p j t", j=4),
                     [r_pb[3]], [r_ybTs[b][s]])
                yield

        def stage_M(t):
            b = t % 2
            xt = xb[t % 3]
            uTt = uT[b]
            for f in range(8):
                fs = slice(f * 128, (f + 1) * 128)
                pga = PB[0][:, 0:256]
                pgb = PB[0][:, 256:512]
                ppa = PB[1][:, 0:256]
                ppb = PB[1][:, 256:512]
                for c in range(8):
                    P.mm(pga, wB[:, c, GAo + f * 128:GAo + (f + 1) * 128], uTt[:, c, :], c == 0, c == 7,
                         [r_wB2[c]] + r_uT[b], [r_pb[0]])
                for c in range(8):
                    P.mm(pgb, wB[:, c, GBo + f * 128:GBo + (f + 1) * 128], uTt[:, c, :], c == 0, c == 7,
                         [r_wB2[c]] + r_uT[b], [r_pb[0]])
                k2 = f % 2
                P.act(sga[k2][:], pga, AF.Tanh, [r_pb[0]], [r_sga[k2]], scale=0.5)
                P.act(sgb[k2][:], pgb, AF.Tanh, [r_pb[0]], [r_sgb[k2]], scale=0.5)
                for pr in range(4):
                    P.mm(ppa, wpa[:, pr, fs], yaT[t % 3][:, pr, :], pr == 0, pr == 3, [r_wpa, r_yaT[t % 3]], [r_pb[1]])
                for pr in range(4):
                    P.mm(ppb, wpb[:, pr, fs], ybTs[b][:, pr, :], pr == 0, pr == 3, [r_wpb] + r_ybTs[b], [r_pb[1]])
                P.stt("dve", m1[k2][:], sga[k2][:], 1.0, ppa, ALU.add, ALU.mult, [r_sga[k2], r_pb[1]], [r_m1[k2]])
                P.stt("dve", m2[k2][:], sgb[k2][:], 1.0, ppb, ALU.add, ALU.mult, [r_sgb[k2], r_pb[1]], [r_m2[k2]])
                P.tt("dve", mT[:, f, :], m1[k2][:], m2[k2][:], ALU.add, [r_m1[k2], r_m2[k2]], [r_mT[f]])
                yield
            for s in range(2):
                sub = slice(s * 128, (s + 1) * 128)
                for n in range(2):
                    po2 = PB[n][:]
                    rpo = [r_pb[n]]
                    for c in range(8):
                        P.mm(po2, mT[:, c, sub], wo[:, c, n * 512:(n + 1) * 512], c == 0, c == 7,
                             [r_mT[c], r_wo[c]], rpo)
                    P.stt("dve", xt[:, s, n * 512:(n + 1) * 512], po2, 0.5, xt[:, s, n * 512:(n + 1) * 512],
                          ALU.mult, ALU.add, rpo + [r_xb[t % 3][s]], [r_xb[t % 3][s]])
                    yield
            P.dma("sp", lambda e, t=t, xt=xt: e.dma_start(out=hv[t], in_=xt[:]), reads=r_xb[t % 3], writes=[r_h1[t]])

        def prefetch(tt_):
            if tt_ < NT:
                P.dma("sp", lambda e: e.dma_start(out=xb[tt_ % 3][:], in_=xv[tt_]), writes=r_xb[tt_ % 3])
                P.dma("sp", lambda e: e.dma_start(out=yaT[tt_ % 3][:], in_=ya_v[:, :, tt_ * T:(tt_ + 1) * T]),
                      reads=r_ya[tt_], writes=[r_yaT[tt_ % 3]])

        prefetch(0)
        prefetch(1)
        for _ in stage_A(0):
            pass
        for t in range(NT):
            if t + 1 < NT:
                prefetch(t + 2)
                side = stage_M(t)
                n_side = 12
                done = 0
                for i, _ in enumerate(stage_A(t + 1)):
                    want = ((i + 1) * n_side) // 46
                    while done < want:
                        next(side, None)
                        done += 1
                for _ in side:
                    pass
            else:
                for _ in stage_M(t):
                    pass


def phase_C(nc, P, S, h_d, r_h1, out_d, dr, G):
    T = 256
    NT = S // T
    NF = DFF // 128
    identb, r_ident = G["identb"], G["r_c"]
    g_ffn_d, g_fin_d, wgu_d, wdn_d = dr["norm_ffn_g"], dr["norm_final_g"], dr["w_gate_up"], dr["w_down"]
    with ExitStack() as st:
        ec = st.enter_context
        wgu = ec(nc.sbuf_tensor("wgu", [128, 8, 2 * DFF], BF16))
        wdn = ec(nc.sbuf_tensor("wdn", [128, NF, D], BF16))
        gF = ec(nc.sbuf_tensor("gF", [128, D], F32))
        gL = ec(nc.sbuf_tensor("gL", [128, D], F32))
        hb = [ec(nc.sbuf_tensor("hb%d" % i, [128, 2, D], F32)) for i in range(2)]
        ub = ec(nc.sbuf_tensor("ub", [128, D], BF16))
        uT = [ec(nc.sbuf_tensor("uT%d" % i, [128, 8, T], BF16)) for i in range(2)]
        aT = ec(nc.sbuf_tensor("aT", [128, NF, T], BF16))
        sg = [ec(nc.sbuf_tensor("sg%d" % i, [128, T], BF16)) for i in range(2)]
        junk = ec(nc.sbuf_tensor("junk", [128, D], BF16))
        st8 = ec(nc.sbuf_tensor("st8", [128, 8], F32))
        epsb = ec(nc.sbuf_tensor("epsb", [128, 1], F32))
        r_eps = Res("eps")
        P.op("dve", lambda e: e.memset(epsb[:], EPS), writes=[r_eps])
        ptp = [ec(nc.psum_tensor("ptp%d" % i, [128, 4, 128], BF16)) for i in range(2)]
        pgu = [ec(nc.psum_tensor("pgu%d" % i, [128, 512], F32)) for i in range(4)]
        pdn = [ec(nc.psum_tensor("pdn%d" % i, [128, 512], F32)) for i in range(2)]

        r_wdn = [Res("wdn%d" % c) for c in range(NF)]
        r_gF, r_gL = Res("gF"), Res("gL")
        r_hb = [[Res("hb%d_%d" % (i, s)) for s in range(2)] for i in range(2)]
        r_ub = Res("ub")
        r_uT = [Res("uT%d" % i) for i in range(2)]
        r_aT = [Res("aT%d" % j) for j in range(NF)]
        r_sg = [Res("sg%d" % i) for i in range(2)]
        r_junk = Res("junk")
        r_st = [Res("st%d" % i) for i in range(8)]
        r_ptp = [Res("ptp%d" % i) for i in range(2)]
        r_pgu = [Res("pgu%d" % i) for i in range(4)]
        r_pdn = [Res("pdn%d" % i) for i in range(2)]

        P.dma("pool", lambda e: e.dma_start(out=gF[:], in_=g_ffn_d.to_broadcast([128, D])), writes=[r_gF])
        P.dma("pool", lambda e: e.dma_start(out=gL[:], in_=g_fin_d.to_broadcast([128, D])), writes=[r_gL])
        wgu_v = wgu_d.rearrange("(c p) f -> p c f", p=128)
        wdn_v = wdn_d.rearrange("(c p) f -> p c f", p=128)
        NBLK = 11
        r_wgu = [Res("wgu%d" % i) for i in range(NBLK)]
        for i in range(NBLK):
            for hlf in range(2):
                lo = hlf * DFF + i * 256
                P.dma("pool", lambda e, lo=lo: e.dma_start(out=wgu[:, :, lo:lo + 256], in_=wgu_v[:, :, lo:lo + 256]),
                      writes=[r_wgu[i]])
        for c in range(NF):
            P.dma("pool", lambda e, c=c: e.dma_start(out=wdn[:, c, :], in_=wdn_v[:, c, :]),
                  writes=[r_wdn[c]])
        h_v = h_d.rearrange("(t s p) d -> t p s d", s=2, p=128)
        o_v = out_d.rearrange("(t s p) d -> t p s d", s=2, p=128)
        final_pts = []

        def rstd_from(h_ap, col, r_h):
            P.op("act", lambda e: e.activation(out=junk[:], in_=h_ap, func=AF.Square,
                                               accum_out=st8[:, col:col + 1]),
                 reads=[r_h], writes=[r_junk, r_st[col]])
            P.op("act", lambda e: e.activation(out=st8[:, col:col + 1], in_=st8[:, col:col + 1], func=AF.Sqrt,
                                               scale=1.0 / D, bias=epsb[:]),
                 reads=[r_st[col], r_eps], writes=[r_st[col]])
            P.op("dve", lambda e: e.reciprocal(out=st8[:, col:col + 1], in_=st8[:, col:col + 1]),
                 reads=[r_st[col]], writes=[r_st[col]])

        def prologue(t):
            b = t % 2
            hbt = hb[b]
            P.dma("sp", lambda e, t=t, hbt=hbt: e.dma_start(out=hbt[:], in_=h_v[t]),
                  reads=[r_h1[t]], writes=[r_hb[b][0], r_hb[b][1]])
            for s in range(2):
                rstd_from(hbt[:, s, :], s, r_hb[b][s])
                P.stt("dve", ub[:], hbt[:, s, :], st8[:, s:s + 1], gF[:], ALU.mult, ALU.mult,
                      [r_hb[b][s], r_st[s], r_gF], [r_ub])
                for g4 in range(2):
                    pt = ptp[g4]
                    for j in range(4):
                        c = g4 * 4 + j
                        P.tr(pt[:, j, :], ub[:, c * 128:(c + 1) * 128], identb[:], [r_ub, r_ident], [r_ptp[g4]],
                             inc=(j == 3))
                    P.cp("act", uT[b][:, g4 * 4:(g4 + 1) * 4, s * 128:(s + 1) * 128], pt[:], [r_ptp[g4]], [r_uT[b]])

        def gateup(t):
            b = t % 2
            for j in range(NF):
                pg = pgu[(2 * j) % 4]
                pu = pgu[(2 * j + 1) % 4]
                rg = r_pgu[(2 * j) % 4]
                ru = r_pgu[(2 * j + 1) % 4]
                for c in range(8):
                    P.mm(pg[:, 0:T], wgu[:, c, j * 128:(j + 1) * 128], uT[b][:, c, :], c == 0, c == 7,
                         [r_wgu[j // 2], r_uT[b]], [rg])
                for c in range(8):
                    P.mm(pu[:, 0:T], wgu[:, c, DFF + j * 128:DFF + (j + 1) * 128], uT[b][:, c, :], c == 0, c == 7,
                         [r_wgu[j // 2], r_uT[b]], [ru])
                sgt = sg[j % 2]
                P.act(sgt[:], pg[:, 0:T], AF.Silu, [rg], [r_sg[j % 2]])
                P.tt("dve", aT[:, j, :], sgt[:], pu[:, 0:T], ALU.mult, [ru, r_sg[j % 2]], [r_aT[j]])

        def down(t):
            b = t % 2
            hbt = hb[b]
            for s in range(2):
                for n in range(2):
                    pd = pdn[n]
                    for j in range(NF):
                        P.mm(pd[:], aT[:, j, s * 128:(s + 1) * 128], wdn[:, j, n * 512:(n + 1) * 512],
                             j == 0, j == NF - 1, [r_aT[j], r_wdn[j]], [r_pdn[n]])
                    P.tt("dve", hbt[:, s, n * 512:(n + 1) * 512], hbt[:, s, n * 512:(n + 1) * 512], pd[:], ALU.add,
                         [r_pdn[n], r_hb[b][s]], [r_hb[b][s]])
                rstd_from(hbt[:, s, :], 2 + s, r_hb[b][s])
                P.stt("dve", hbt[:, s, :], hbt[:, s, :], st8[:, 2 + s:3 + s], gL[:], ALU.mult, ALU.mult,
                      [r_hb[b][s], r_st[2 + s], r_gL], [r_hb[b][s]])
            final_pts.append(P.dma("sp", lambda e, t=t, hbt=hbt: e.dma_start(out=o_v[t], in_=hbt[:]),
                                   reads=[r_hb[b][0], r_hb[b][1]]))

        prologue(0)
        for t in range(NT):
            gateup(t)
            if t + 1 < NT:
                prologue(t + 1)
            down(t)
        return final_pts


def _host_consts(S):
    idx = np.arange(128)
    ident = np.eye(128, dtype=np.float32)
    tri = (idx[:, None] <= idx[None, :]).astype(np.float32)
    trim = -(idx[:, None] > idx[None, :]).astype(np.float32)
    ones = np.ones((128, 128), np.float32)
    consts = np.ascontiguousarray(np.stack([ident, tri, trim, ones], axis=1))
    inv = (np.float32(10000.0) ** (-np.arange(32, dtype=np.float32) / np.float32(32))).astype(np.float32)
    ang = np.arange(S, dtype=np.float32)[:, None] * inv[None, :]
    cos, sin = np.cos(ang).astype(np.float32), np.sin(ang).astype(np.float32)
    rope = np.ascontiguousarray(np.stack([np.concatenate([cos, cos], 1), np.concatenate([-sin, sin], 1)], axis=1))
    k = np.arange(128)[:, None, None]
    a = np.arange(4)[None, :, None]
    q = np.arange(512)[None, None, :]
    jq, ja = q // 256, a // 2
    cm = np.where(jq > ja, 1.0, np.where(jq < ja, 0.0, (q >= 128 * a + k).astype(np.float32)))
    cmask = np.ascontiguousarray(np.broadcast_to(cm, (128, 4, 512)).astype(ml_dtypes.bfloat16))
    blk = np.arange(16)[:, None]
    j = np.arange(16)[None, :]
    pastb = np.where(j < blk, 0.0, np.where(j == blk, 1e30, -1e30)).astype(np.float32)
    oneh = np.where(j == blk, -1024.0, 0.0).astype(np.float32)
    blkc = np.ascontiguousarray(np.broadcast_to(np.concatenate([pastb, oneh], 1)[None], (128, 16, 32)).astype(np.float32))
    return dict(consts=consts, rope=rope, cmask=cmask, blkc=blkc)


def make_in_maps(inp, ncores):
    f = lambda k: np.ascontiguousarray(np.asarray(inp[k], dtype=np.float32))
    x = np.asarray(inp["x"], dtype=np.float32)
    S = x.shape[1]
    common = {
        "norm_mix_g": f("norm_mix_g").reshape(1, D),
        "w_in": f("w_in")[0],
        "conv_wT": np.ascontiguousarray(f("conv_w")[0].T),
        "bias8": np.ascontiguousarray(np.concatenate([f("b_igate")[0], f("b_fgate")[0]]).reshape(1, 8)),
        "mlstm_norm_g": f("mlstm_norm_g").reshape(1, 512),
        "w_proj_a": f("w_proj_a")[0],
        "w_proj_b": f("w_proj_b")[0],
        "w_out": f("w_out")[0],
        "norm_ffn_g": f("norm_ffn_g").reshape(1, D),
        "norm_final_g": f("norm_final_g").reshape(1, D),
        "w_gate_up": f("w_gate_up")[0],
        "w_down": f("w_down")[0],
    }
    common.update(_host_consts(S))
    return [dict(common, x=np.ascontiguousarray(x[b])) for b in range(ncores)]


def kernel(x, norm_mix_g, w_in, conv_w, b_igate, b_fgate, mlstm_norm_g, w_proj_a, w_proj_b,
           w_out, norm_ffn_g, w_gate_up, w_down, norm_final_g):
    inp = dict(x=x, norm_mix_g=norm_mix_g, w_in=w_in, conv_w=conv_w, b_igate=b_igate, b_fgate=b_fgate,
               mlstm_norm_g=mlstm_norm_g, w_proj_a=w_proj_a, w_proj_b=w_proj_b, w_out=w_out,
               norm_ffn_g=norm_ffn_g, w_gate_up=w_gate_up, w_down=w_down, norm_final_g=norm_final_g)
    B, S, _ = np.asarray(x).shape
    nc = build_nc(S)
    in_maps = make_in_maps(inp, B)
    res = run_bass_kernel_spmd(nc, in_maps, core_ids=list(range(B)))
    return np.stack([np.asarray(r["out"]) for r in res.results], axis=0).astype(np.float32)
```

```python
import numpy as np
import ml_dtypes
from contextlib import ExitStack

import concourse.bass as bass
import concourse.mybir as mybir
from concourse.bass_utils import run_bass_kernel_spmd

F32 = mybir.dt.float32
BF16 = mybir.dt.bfloat16
ALU = mybir.AluOpType
AF = mybir.ActivationFunctionType
AX = mybir.AxisListType

D = 1024
DFF = 2816
EPS = 1e-6
NCORES = 8


class Res:
    __slots__ = ("name", "w", "r")

    def __init__(self, name):
        self.name = name
        self.w = None
        self.r = {}


class Prog:
    COMPUTE = ("pe", "dve", "act", "pool")

    def __init__(self, nc, stack, n_sp=12, n_pool=6):
        self.nc = nc
        self.sem = {}
        self.stack = stack
        self.epoch = 0
        self.ekey = {e: e for e in self.COMPUTE}
        for e in self.COMPUTE:
            self.sem[e] = stack.enter_context(nc.semaphore("s_" + e))
        self.dma_pool = {"sp": [], "pool": []}
        for i in range(n_sp):
            k = "dsp%d" % i
            self.sem[k] = stack.enter_context(nc.semaphore(k))
            self.dma_pool["sp"].append(k)
        for i in range(n_pool):
            k = "dpl%d" % i
            self.sem[k] = stack.enter_context(nc.semaphore(k))
            self.dma_pool["pool"].append(k)
        self.dma_rr = {"sp": 0, "pool": 0}
        self.cnt = {k: 0 for k in self.sem}
        self.ops = {e: [] for e in ("pe", "dve", "act", "pool", "sp")}
        self.know = {e: {} for e in self.ops}
        self.clock = {}
        self.pe_pending = False

    def _needs(self, reads, writes):
        need = {}
        for r in reads:
            if r.w is not None:
                k, v = r.w
                if need.get(k, 0) < v:
                    need[k] = v
        for w in writes:
            if w.w is not None:
                k, v = w.w
                if need.get(k, 0) < v:
                    need[k] = v
            for k, v in w.r.items():
                if need.get(k, 0) < v:
                    need[k] = v
        return need

    def _waits(self, eng, need):
        know = self.know[eng]
        waits = []
        for k, v in need.items():
            if eng == "pe" and k == self.ekey["pe"]:
                continue
            if know.get(k, 0) >= v:
                continue
            waits.append((k, v))
        for k, v in waits:
            ck = self.clock.get((k, v))
            if ck:
                for kk, vv in ck.items():
                    if know.get(kk, 0) < vv:
                        know[kk] = vv
            if know.get(k, 0) < v:
                know[k] = v
        return waits

    def _record(self, point, reads, writes):
        k, v = point
        for r in reads:
            if r.r.get(k, 0) < v:
                r.r[k] = v
        for w in writes:
            w.w = point
            w.r = {}

    def op(self, eng, fn, reads=(), writes=(), inc=True):
        need = self._needs(reads, writes)
        waits = self._waits(eng, need)
        sk = self.ekey[eng]
        if eng == "pe" and not inc:
            point = (sk, self.cnt[sk] + 1)
            self.pe_pending = True
            self.ops[eng].append((waits, fn, None, 0))
        else:
            self.cnt[sk] += 1
            point = (sk, self.cnt[sk])
            if eng == "pe":
                self.pe_pending = False
            self.ops[eng].append((waits, fn, sk, 1))
            ck = dict(self.know[eng])
            ck[sk] = self.cnt[sk]
            self.clock[point] = ck
        self._record(point, reads, writes)
        return point

    def new_epoch(self):
        self.epoch += 1
        for e in self.COMPUTE:
            k = "%s_%d" % (e, self.epoch)
            self.sem[k] = self.stack.enter_context(self.nc.semaphore("s_" + k))
            self.cnt[k] = 0
            self.ekey[e] = k

    def dma(self, q, fn, reads=(), writes=()):
        pool = self.dma_pool[q]
        sk = pool[self.dma_rr[q] % len(pool)]
        self.dma_rr[q] += 1
        need = self._needs(reads, writes)
        if self.cnt[sk] > 0 and need.get(sk, 0) < self.cnt[sk]:
            need[sk] = self.cnt[sk]
        waits = self._waits(q, need)
        self.cnt[sk] += 16
        point = (sk, self.cnt[sk])
        self.ops[q].append((waits, fn, sk, 16))
        ck = dict(self.know[q])
        ck[sk] = self.cnt[sk]
        self.clock[point] = ck
        self._record(point, reads, writes)
        return point

    def mm(self, out, lhsT, rhs, start, stop, reads, writes, inc=None):
        if inc is None:
            inc = stop
        return self.op("pe", lambda e: e.matmul(out, lhsT=lhsT, rhs=rhs, start=start, stop=stop),
                       reads, writes, inc)

    def tr(self, out, in_, ident, reads, writes, inc=True):
        return self.op("pe", lambda e: e.transpose(out=out, in_=in_, identity=ident), reads, writes, inc)

    def act(self, out, in_, func, reads, writes, bias=None, scale=None, accum=None):
        kw = {}
        if bias is not None:
            kw["bias"] = bias
        if scale is not None:
            kw["scale"] = scale
        if accum is not None:
            kw["accum_out"] = accum
        return self.op("act", lambda e: e.activation(out=out, in_=in_, func=func, **kw), reads, writes)

    def tt(self, eng, out, in0, in1, op, reads, writes):
        return self.op(eng, lambda e: e.tensor_tensor(out=out, in0=in0, in1=in1, op=op), reads, writes)

    def ts(self, eng, out, in0, s1, op0, reads, writes, s2=None, op1=None):
        if op1 is None:
            return self.op(eng, lambda e: e.tensor_scalar(out=out, in0=in0, scalar1=s1, scalar2=None, op0=op0),
                           reads, writes)
        return self.op(eng, lambda e: e.tensor_scalar(out=out, in0=in0, scalar1=s1, scalar2=s2, op0=op0, op1=op1),
                       reads, writes)

    def stt(self, eng, out, in0, scalar, in1, op0, op1, reads, writes):
        return self.op(eng, lambda e: e.scalar_tensor_tensor(out=out, in0=in0, scalar=scalar, in1=in1,
                                                             op0=op0, op1=op1), reads, writes)

    def cp(self, eng, out, in_, reads, writes):
        if eng == "act":
            return self.op("act", lambda e: e.copy(out=out, in_=in_), reads, writes)
        return self.op(eng, lambda e: e.tensor_copy(out=out, in_=in_), reads, writes)

    def recip(self, out, in_, reads, writes):
        return self.op("dve", lambda e: e.reciprocal(out=out, in_=in_), reads, writes)

    def memset(self, eng, ap, val, writes):
        return self.op(eng, lambda e: e.memset(ap, val), (), writes)

    def barrier(self):
        assert not self.pe_pending
        need = {k: v for k, v in self.cnt.items() if v > 0}
        for e in self.ops:
            waits = self._waits(e, dict(need))
            if waits:
                self.ops[e].append((waits, None, None, 0))

    def finish(self, final_points):
        need = {}
        for k, v in final_points:
            if need.get(k, 0) < v:
                need[k] = v
        waits = [(k, v) for k, v in need.items()]
        self.ops["sp"].append((waits, None, None, 0))

    def emit(self):
        nc = self.nc
        assert not self.pe_pending, "PE has trailing non-inc instructions"
        sem = self.sem
        ops = self.ops

        def run(e, lst):
            for waits, fn, sk, inc in lst:
                for k, v in waits:
                    e.wait_ge(sem[k], v)
                if fn is None:
                    continue
                ins = fn(e)
                if sk is not None:
                    ins.then_inc(sem[sk], inc)

        with nc.Block() as block:
            @block.tensor
            def _(e):
                run(e, ops["pe"])

            @block.vector
            def _(e):
                run(e, ops["dve"])

            @block.scalar
            def _(e):
                run(e, ops["act"])

            @block.gpsimd
            def _(e):
                run(e, ops["pool"])

            @block.sync
            def _(e):
                run(e, ops["sp"])


IN_W = 5640
A_QA, A_KA, A_VA = 0, 512, 1024
B_OFF = 1536
WB = IN_W - B_OFF
QKB, VBo, OBo, IBo, GAo, GBo = 0, 1024, 1536, 2048, 2056, 3080
LNC = float(np.log(128.0 ** -0.5))


def build_nc(S, phases=("A", "B", "C")):
    nc = bass.Bass("TRN2", target_bir_lowering=False)
    dr = {}

    def din(name, shape, dt=F32):
        dr[name] = nc.dram_tensor(name, shape, dt, kind="ExternalInput").ap()

    din("x", [S, D])
    din("norm_mix_g", [1, D])
    din("w_in", [D, IN_W])
    din("conv_wT", [D, 4])
    din("bias8", [1, 8])
    din("mlstm_norm_g", [1, 512])
    din("w_proj_a", [512, D])
    din("w_proj_b", [512, D])
    din("w_out", [D, D])
    din("norm_ffn_g", [1, D])
    din("norm_final_g", [1, D])
    din("w_gate_up", [D, 2 * DFF])
    din("w_down", [DFF, D])
    din("consts", [128, 4, 128])
    din("rope", [S, 2, 64])
    din("cmask", [128, 4, 512], BF16)
    din("blkc", [128, 16, 32])
    out_d = nc.dram_tensor("out", [S, D], F32, kind="ExternalOutput").ap()
    h1_d = nc.dram_tensor("h1_s", [S, D], F32, kind="Internal").ap()
    ya_d = nc.dram_tensor("ya_s", [8, 64, S], BF16, kind="Internal").ap()

    with ExitStack() as stack:
        P = Prog(nc, stack)
        ec = stack.enter_context
        identb = ec(nc.sbuf_tensor("identb", [128, 128], BF16))
        cst = ec(nc.sbuf_tensor("cst", [128, 4, 128], F32))
        r_c = Res("consts")
        P.dma("pool", lambda e: e.dma_start(out=identb[:], in_=dr["consts"][:, 0, :]), writes=[r_c])
        P.dma("sp", lambda e: e.dma_start(out=cst[:], in_=dr["consts"]), writes=[r_c])
        G = dict(identb=identb, cst=cst, r_c=r_c)

        r_ya = [[Res("ya_s%d_%d" % (i, h)) for h in range(8)] for i in range(S // 256)]
        r_h1 = [Res("h1_s%d" % i) for i in range(S // 256)]
        if "A" in phases:
            phase_A(nc, P, S, dr, ya_d, r_ya, G)
        else:
            with nc.sbuf_tensor("zt", [64, S], BF16) as zt:
                rz = Res("zt")
                P.memset("dve", zt[:], 0.0, [rz])
                for h in range(8):
                    P.dma("sp", lambda e, h=h: e.dma_start(out=ya_d[h], in_=zt[:]), reads=[rz],
                          writes=[r_ya[i][h] for i in range(S // 256)])
                P.barrier()
        P.barrier()
        P.new_epoch()
        if "B" in phases:
            phase_B(nc, P, S, dr, h1_d, ya_d, r_ya, r_h1, G)
            h_src = h1_d
        else:
            h_src = dr["x"]
        P.barrier()
        P.new_epoch()
        final_pts = phase_C(nc, P, S, h_src, r_h1, out_d, dr, G)
        P.finish(final_pts)
        P.emit()
    return nc


def load_norm_tile(nc, P, xt, r_x, s, gM, r_g, st8, r_st, col, epsb, r_eps, junk, r_junk, ub, r_ub):
    P.act(junk[:], xt[:, s, :], AF.Square, [r_x], [r_junk, r_st[col]], accum=st8[:, col:col + 1])
    P.act(st8[:, col:col + 1], st8[:, col:col + 1], AF.Sqrt, [r_st[col], r_eps], [r_st[col]],
          bias=epsb[:], scale=1.0 / D)
    P.recip(st8[:, col:col + 1], st8[:, col:col + 1], [r_st[col]], [r_st[col]])
    P.stt("dve", ub[:], xt[:, s, :], st8[:, col:col + 1], gM[:], ALU.mult, ALU.mult,
          [r_x, r_st[col], r_g], [r_ub])


def phase_A(nc, P, S, dr, ya_d, r_ya, G):
    T = 512
    NT = S // T
    NKT = S // 128
    identb, cst, r_c = G["identb"], G["cst"], G["r_c"]
    with ExitStack() as st:
        ec = st.enter_context

        def sbt(name, shape, dt):
            return ec(nc.sbuf_tensor(name, shape, dt))

        wA = sbt("wA", [128, 8, 1536], BF16)
        gM = sbt("gMa", [128, D], F32)
        cmk = sbt("cmk", [128, 4, 512], BF16)
        blkc = sbt("blkc_sb", [128, 16, 32], F32)
        epsb = sbt("epsb_a", [128, 1], F32)
        ropeT = [sbt("ropeT%d" % i, [128, 4, 2, 64], F32) for i in range(2)]
        kTa = sbt("kTa", [80, 8, S], BF16)
        vA = sbt("vA", [128, NKT, 8, 65], BF16)
        kmf = sbt("kmf", [64, 8], F32)
        kmT = sbt("kmT", [64, 8, 16], BF16)
        xs = [sbt("xs%d" % i, [128, 1, D], F32) for i in range(2)]
        ub = sbt("ub_a", [128, D], BF16)
        junk = sbt("junk_a", [128, D], BF16)
        st8 = sbt("st8_a", [128, 8], F32)
        uT = sbt("uTa", [128, 8, T], BF16)
        t1 = sbt("t1", [128, 8, 64], F32)
        t2 = sbt("t2", [128, 8, 64], F32)
        qtok = [sbt("qtok%d" % i, [128, 8, 80], BF16) for i in range(2)]
        ktok = [sbt("ktok%d" % i, [128, 8, 80], BF16) for i in range(2)]
        qTa = sbt("qTa", [80, 8, T], BF16)
        gsb = sbt("gsb_a", [128, 8, 16], F32)
        top8 = sbt("top8", [128, 8, 8], F32)
        PT = [sbt("PT%d" % i, [128, 512], BF16) for i in range(4)]
        oT = [sbt("oT%d" % i, [65, 512], F32) for i in range(2)]
        yo = [sbt("yo%d" % i, [64, 512], BF16) for i in range(2)]
        PB = [ec(nc.psum_tensor("PBa%d" % i, [128, 512], F32)) for i in range(8)]
        PBb = [PB[i].bitcast(BF16) for i in range(8)]

        R = lambda n: Res(n)
        r_wA = [R("wA%d" % c) for c in range(8)]
        r_k = R("constsA")
        r_rope = [R("rope%d" % i) for i in range(2)]
        r_kTa = [R("kTa%d" % i) for i in range(NKT)]
        r_vA = [R("vA%d" % i) for i in range(NKT)]
        r_kmf, r_kmT = R("kmf"), R("kmT")
        r_xs = [R("xs%d" % i) for i in range(2)]
        r_ub, r_junk = R("ub"), R("junk")
        r_st = [R("st%d" % i) for i in range(8)]
        r_uT = [R("uT%d" % s) for s in range(4)]
        r_t1, r_t2 = R("t1"), R("t2")
        r_qtok = [R("qtok%d" % i) for i in range(2)]
        r_ktok = [R("ktok%d" % i) for i in range(2)]
        r_qTa = [R("qTa%d" % s) for s in range(4)]
        r_gsb, r_top8 = R("gsb"), R("top8")
        r_PT = [R("PT%d" % i) for i in range(4)]
        r_oT = [R("oT%d" % i) for i in range(2)]
        r_yo = [R("yo%d" % i) for i in range(2)]
        r_pb = [R("pba%d" % i) for i in range(8)]

        wv = dr["w_in"].rearrange("(c p) f -> p c f", p=128)
        for c in range(8):
            P.dma("pool", lambda e, c=c: e.dma_start(out=wA[:, c, :], in_=wv[:, c, 0:1536]), writes=[r_wA[c]])
        P.dma("sp", lambda e: e.dma_start(out=gM[:], in_=dr["norm_mix_g"].to_broadcast([128, D])), writes=[r_k])
        P.dma("sp", lambda e: e.dma_start(out=cmk[:], in_=dr["cmask"]), writes=[r_k])
        P.dma("sp", lambda e: e.dma_start(out=blkc[:], in_=dr["blkc"]), writes=[r_k])
        P.memset("dve", epsb[:], EPS, [r_k])
        P.memset("dve", kmT[:], 0.0, [r_kmT])
        P.memset("dve", vA[:], 1.0, r_vA)

        xv = dr["x"].rearrange("(n p) d -> n p d", p=128)
        rope_v = dr["rope"].rearrange("(n s p) a d -> n p s a d", s=4, p=128)
        onesb = sbt("onesb", [128, 64], BF16)
        hiT = [sbt("hiT%d" % i, [65, 512], BF16) for i in range(2)]
        loT = [sbt("loT%d" % i, [65, 512], BF16) for i in range(2)]
        qTb = sbt("qTa2", [80, 8, T], BF16)
        qTas = [qTa, qTb]
        r_qTas = [r_qTa, [R("qTb%d" % s) for s in range(4)]]
        r_hl = [R("hl%d" % i) for i in range(2)]
        P.memset("dve", onesb[:], 1.0, [r_k])
        P.dma("sp", lambda e: e.dma_start(out=xs[0][:, 0, :], in_=xv[0]), writes=[r_xs[0]])

        def rope_ops(pz, rpz, rb, s, dst, rdst):
            z3 = pz.rearrange("p (h d) -> p h d", h=8)
            C2 = ropeT[rb][:, s, 0, :].unsqueeze(1).to_broadcast([128, 8, 64])
            Sa = ropeT[rb][:, s, 1, 0:32].unsqueeze(1).to_broadcast([128, 8, 32])
            Sb = ropeT[rb][:, s, 1, 32:64].unsqueeze(1).to_broadcast([128, 8, 32])
            P.tt("dve", t1[:], z3, C2, ALU.mult, [rpz, r_rope[rb]], [r_t1])
            P.tt("dve", t2[:, :, 0:32], z3[:, :, 32:64], Sa, ALU.mult, [rpz, r_rope[rb]], [r_t2])
            P.tt("dve", t2[:, :, 32:64], z3[:, :, 0:32], Sb, ALU.mult, [rpz, r_rope[rb]], [r_t2])
            P.tt("dve", dst[:, :, 0:64], t1[:], t2[:], ALU.add, [r_t1, r_t2], [rdst])

        def tr_group(src_fn, rsrc, rows, dst_fn, rdst, prow=None):
            for g4 in range(2):
                for j in range(4):
                    P.tr(PBb[g4][0:rows, j * 128:(j + 1) * 128], src_fn(g4 * 4 + j), identb[:], [rsrc, r_c],
                         [r_pb[g4]], inc=(j == 3))
            for g4 in range(2):
                lo_, hi_ = prow if prow else (0, rows)
                P.cp("dve", dst_fn(g4, lo_, hi_), PBb[g4][lo_:hi_, 0:512].rearrange("p (j t) -> p j t", j=4),
                     [r_pb[g4]], [rdst])

        def prologue(t):
            rb = t % 2
            qT_ = qTas[t % 2]
            rq_ = r_qTas[t % 2]
            P.dma("sp", lambda e, t=t, rb=rb: e.dma_start(out=ropeT[rb][:], in_=rope_v[t]), writes=[r_rope[rb]])
            for s in range(4):
                n = t * 4 + s
                sub = slice(s * 128, (s + 1) * 128)
                ksub = slice(n * 128, (n + 1) * 128)
                blk = n // 2
                if n + 1 < NKT:
                    P.dma("sp", lambda e, n=n: e.dma_start(out=xs[(n + 1) % 2][:, 0, :], in_=xv[n + 1]),
                          writes=[r_xs[(n + 1) % 2]])
                xt = xs[n % 2]
                load_norm_tile(nc, P, xt, r_xs[n % 2], 0, gM, r_k, st8, r_st, s, epsb, r_k, junk, r_junk, ub, r_ub)
                yield
                for g4 in range(2):
                    for j in range(4):
                        c = g4 * 4 + j
                        P.tr(PBb[g4][:, j * 128:(j + 1) * 128], ub[:, c * 128:(c + 1) * 128], identb[:],
                             [r_ub, r_c], [r_pb[g4]], inc=(j == 3))
                for g4 in range(2):
                    P.cp("dve", uT[:, g4 * 4:(g4 + 1) * 4, sub],
                         PBb[g4][:, 0:512].rearrange("p (j t) -> p j t", j=4), [r_pb[g4]], [r_uT[s]])
                yield
                kb, rkb = ktok[s % 2], r_ktok[s % 2]
                pz = PB[5][:]
                for c in range(8):
                    P.mm(pz, uT[:, c, sub], wA[:, c, A_KA:A_KA + 512], c == 0, c == 7, [r_uT[s], r_wA[c]], [r_pb[5]])
                rope_ops(pz, r_pb[5], rb, s, kb, rkb)
                P.cp("dve", kb[:, :, 64:80], blkc[:, blk, 16:32].unsqueeze(1).to_broadcast([128, 8, 16]),
                     [r_k], [rkb])
                yield
                tr_group(lambda h: kb[:, h, :], rkb, 80,
                         lambda g4, lo_, hi_: kTa[lo_:hi_, g4 * 4:(g4 + 1) * 4, ksub], r_kTa[n])
                if s % 2 == 1:
                    P.op("dve", lambda e, blk=blk: e.tensor_reduce(
                        out=kmf[:], in_=kTa[0:64, :, blk * 256:(blk + 1) * 256], axis=AX.X, op=ALU.add),
                        [r_kTa[n - 1], r_kTa[n]], [r_kmf])
                    P.ts("dve", kmT[:, :, blk], kmf[:], 1.0 / 256, ALU.mult, [r_kmf], [r_kmT])
                yield
                for c in range(8):
                    P.mm(pz, uT[:, c, sub], wA[:, c, A_VA:A_VA + 512], c == 0, c == 7, [r_uT[s], r_wA[c]], [r_pb[5]])
                P.cp("dve", vA[:, n, :, 0:64], pz.rearrange("p (h d) -> p h d", h=8), [r_pb[5]], [r_vA[n]])
                yield
                qb, rqb = qtok[s % 2], r_qtok[s % 2]
                for c in range(8):
                    P.mm(pz, uT[:, c, sub], wA[:, c, A_QA:A_QA + 512], c == 0, c == 7, [r_uT[s], r_wA[c]], [r_pb[5]])
                rope_ops(pz, r_pb[5], rb, s, qb, rqb)
                yield
                tr_group(lambda h: qb[:, h, 0:64], rqb, 64,
                         lambda g4, lo_, hi_: qT_[lo_:hi_, g4 * 4:(g4 + 1) * 4, sub], rq_[s])
                yield
                pgt = PB[5][:, 0:128].rearrange("p (h j) -> p h j", h=8)
                for h in range(8):
                    P.mm(pgt[:, h, :], qT_[0:64, h, sub], kmT[:, h, :], True, True, [rq_[s], r_kmT], [r_pb[5]],
                         inc=(h == 7))
                P.tt("dve", gsb[:], pgt, blkc[:, blk, 0:16].unsqueeze(1).to_broadcast([128, 8, 16]), ALU.add,
                     [r_pb[5], r_k], [r_gsb])
                for h in range(8):
                    P.op("dve", lambda e, h=h: e.max(out=top8[:, h, :], in_=gsb[:, h, :]), [r_gsb], [r_top8])
                P.tt("dve", qb[:, :, 64:80], gsb[:], top8[:, :, 3:4].to_broadcast([128, 8, 16]), ALU.is_lt,
                     [r_gsb, r_top8], [rqb])
                yield
                tr_group(lambda h: qb[:, h, :], rqb, 80,
                         lambda g4, lo_, hi_: qT_[lo_:hi_, g4 * 4:(g4 + 1) * 4, sub], rq_[s], prow=(64, 80))
                yield

        def attention(t):
            qT_ = qTas[t % 2]
            rq_ = r_qTas[t % 2]
            nkt = 4 * t + 4
            items = [(h, kt) for h in range(8) for kt in range(nkt)]
            LA = 2
            deferred = {}

            def emit_S(i):
                h, kt = items[i]
                bi = 2 + i % 3
                q0 = 128 * max(0, kt - 4 * t)
                P.mm(PB[bi][:, q0:512], kTa[:, h, kt * 128:(kt + 1) * 128], qT_[:, h, q0:512], True, True,
                     [r_kTa[kt]] + rq_, [r_pb[bi]])

            def emit_PV(i):
                h, kt = items[i]
                bi = 2 + i % 3
                pi = i % 4
                po = PB[6 + h % 2]
                rpo = r_pb[6 + h % 2]
                q0 = 128 * max(0, kt - 4 * t)
                P.act(PT[pi][:, q0:512], PB[bi][:, q0:512], AF.Exp, [r_pb[bi]], [r_PT[pi]], scale=0.125)
                if kt >= 4 * t:
                    P.tt("pool", PT[pi][:, q0:512], PT[pi][:, q0:512], cmk[:, kt - 4 * t, q0:512], ALU.mult,
                         [r_PT[pi], r_k], [r_PT[pi]])
                P.mm(po[0:65, q0:512], vA[:, kt, h, :], PT[pi][:, q0:512], kt == 0, kt == nkt - 1,
                     [r_vA[kt], r_PT[pi]], [rpo])
                if kt == nkt - 1:
                    ob, rob = oT[h % 2], r_oT[h % 2]
                    P.cp("dve", ob[:], po[0:65, :], [rpo], [rob])
                    P.recip(ob[64:65, :], ob[64:65, :], [rob], [rob])
                    P.cp("dve", hiT[h % 2][64:65, :], ob[64:65, :], [rob], [r_hl[h % 2]])
                    P.tt("dve", loT[h % 2][64:65, :], ob[64:65, :], hiT[h % 2][64:65, :], ALU.subtract,
                         [rob, r_hl[h % 2]], [r_hl[h % 2]])
                    deferred.setdefault(min(i + 4, len(items) - 1), []).append(h)

            def emit_epi(h):
                ob, rob = oT[h % 2], r_oT[h % 2]
                P.mm(PB[5][0:64, :], onesb[64:65, :], hiT[h % 2][64:65, :], True, False, [r_k, r_hl[h % 2]],
                     [r_pb[5]], inc=False)
                P.mm(PB[5][0:64, :], onesb[64:65, :], loT[h % 2][64:65, :], False, True, [r_k, r_hl[h % 2]],
                     [r_pb[5]])
                yb_ = yo[h % 2]
                P.tt("dve", yb_[:], ob[0:64, :], PB[5][0:64, :], ALU.mult, [rob, r_pb[5]], [r_yo[h % 2]])
                P.dma("sp", lambda e, h=h, t=t, yb_=yb_: e.dma_start(out=ya_d[h, :, t * T:(t + 1) * T], in_=yb_[:]),
                      reads=[r_yo[h % 2]], writes=[r_ya[2 * t][h], r_ya[2 * t + 1][h]])

            for i in range(min(LA, len(items))):
                emit_S(i)
            for i in range(len(items)):
                if i + LA < len(items):
                    emit_S(i + LA)
                emit_PV(i)
                for hh in deferred.pop(i, []):
                    emit_epi(hh)
                yield
            assert not deferred

        for _ in prologue(0):
            pass
        for t in range(NT):
            n_main = 8 * (4 * t + 4)
            side = prologue(t + 1) if t + 1 < NT else iter(())
            n_side = 36
            done = 0
            for i, _ in enumerate(attention(t)):
                want = ((i + 1) * n_side) // n_main
                while done < want:
                    next(side, None)
                    done += 1
            for _ in side:
                pass


def phase_B(nc, P, S, dr, h1_d, ya_d, r_ya, r_h1, G):
    T = 256
    NT = S // T
    identb, cst, r_c = G["identb"], G["cst"], G["r_c"]
    idf, tri, trim, onesf = cst[:, 0, :], cst[:, 1, :], cst[:, 2, :], cst[:, 3, :]
    with ExitStack() as st:
        ec = st.enter_context

        def sbt(name, shape, dt):
            return ec(nc.sbuf_tensor(name, shape, dt))

        wB = sbt("wB", [128, 8, WB], BF16)
        wpa = sbt("wpa", [128, 4, D], BF16)
        wpb = sbt("wpb", [128, 4, D], BF16)
        wo = sbt("wo", [128, 8, D], BF16)
        gM = sbt("gMb", [128, D], F32)
        gN = sbt("gN", [128, 512], F32)
        bias8 = sbt("bias8_sb", [128, 8], F32)
        cw = sbt("cw", [128, 8, 4], F32)
        trib = sbt("trib", [128, 128], BF16)
        epsb = sbt("epsb_b", [128, 1], F32)
        oneb = sbt("oneb", [128, 1], F32)
        lncb = sbt("lncb", [128, 1], F32)
        lnhb = sbt("lnhb", [128, 1], F32)
        xb = [sbt("xb%d" % i, [128, 2, D], F32) for i in range(3)]
        yaT = [sbt("yaT%d" % i, [128, 4, T], BF16) for i in range(3)]
        uT = [sbt("uTb%d" % i, [128, 8, T], BF16) for i in range(2)]
        ub = sbt("ub_b", [128, D], BF16)
        junk = sbt("junk_b", [128, D], BF16)
        st8 = sbt("st8_b", [128, 8], F32)
        gsb = sbt("gsb", [128, 2, 8], F32)
        e1 = sbt("e1", [128, 2, 4], F32)
        nlf = sbt("nlf", [128, 2, 4], F32)
        arg = sbt("arg", [128, 2, 4], F32)
        ksc = sbt("ksc", [128, 2, 4], F32)
        wc = sbt("wc", [128, 2, 4], F32)
        EBt = sbt("EBt", [128, 4, T], F32)
        EAt = sbt("EAt", [128, 4, T], F32)
        xc = sbt("xc", [128, 8, T + 3], F32)
        yc = [sbt("yc%d" % i, [128, T], F32) for i in range(2)]
        slq = [sbt("slq%d" % i, [128, T], BF16) for i in range(2)]
        kT0 = sbt("kT0", [128, 4, T], BF16)
        qT = sbt("qTb", [128, 4, T], BF16)
        kT = sbt("kTb", [128, 4, T], BF16)
        vaug = sbt("vaug", [128, 2, 4, 129], BF16)
        sgo = sbt("sgo", [128, 2, 512], F32)
        kS = sbt("kS", [128, 4, 128], BF16)
        scT = sbt("scT", [128, 4, 128], BF16)
        C32 = sbt("C32", [128, 4, 129], F32)
        Cbf = sbt("Cbf", [128, 4, 129], BF16)
        rr = sbt("rr", [128, 4], F32)
        ssq = sbt("ssq", [128, 4], F32)
        t4 = sbt("t4", [128, 4], F32)
        sc4 = sbt("sc4", [128, 4], F32)
        ybf = sbt("ybf", [128, 512], F32)
        ybb = sbt("ybb", [128, 512], BF16)
        ybT = sbt("ybT", [128, 4, T], BF16)
        sga = [sbt("sga%d" % i, [128, T], F32) for i in range(2)]
        sgb = [sbt("sgb%d" % i, [128, T], F32) for i in range(2)]
        m1 = [sbt("m1_%d" % i, [128, T], F32) for i in range(2)]
        m2 = [sbt("m2_%d" % i, [128, T], F32) for i in range(2)]
        mT = sbt("mT", [128, 8, T], BF16)
        PB = [ec(nc.psum_tensor("PBb%d" % i, [128, 512], F32)) for i in range(8)]

        R = lambda n: Res(n)
        r_wB = [R("wB%d" % c) for c in range(8)]
        r_wpa, r_wpb, r_wo = R("wpa"), R("wpb"), [R("wo%d" % c) for c in range(8)]
        r_k = R("constsB")
        r_xb = [[R("xb%d_%d" % (i, s)) for s in range(2)] for i in range(3)]
        r_yaT = [R("yaT%d" % i) for i in range(3)]
        r_uT = [[R("uTb%d_%d" % (i, s)) for s in range(2)] for i in range(2)]
        r_ub, r_junk = R("ub"), R("junk")
        r_st = [R("st%d" % i) for i in range(8)]
        r_gs = [R("gsb%d" % s) for s in range(2)]
        r_e1 = [R("e1%d" % s) for s in range(2)]
        r_nlf = [R("nlf%d" % s) for s in range(2)]
        r_arg = [R("arg%d" % s) for s in range(2)]
        r_ksc = [R("ksc%d" % s) for s in range(2)]
        r_wc = [R("wc%d" % s) for s in range(2)]
        r_EB = [[R("EB%d_%d" % (h, s)) for s in range(2)] for h in range(4)]
        r_EA = [[R("EA%d_%d" % (h, s)) for s in range(2)] for h in range(4)]
        r_xc = [R("xc%d" % m) for m in range(8)]
        r_yc = [R("yc%d" % i) for i in range(2)]
        r_slq = [R("slq%d" % i) for i in range(2)]
        r_kT0 = [R("kT0%d" % h) for h in range(4)]
        r_qT = [R("qT%d" % h) for h in range(4)]
        r_kT = [R("kT%d" % h) for h in range(4)]
        r_va = [R("vaug%d" % s) for s in range(2)]
        r_sgo = [R("sgo%d" % s) for s in range(2)]
        r_kS = [R("kS%d" % h) for h in range(4)]
        r_scT = [R("scT%d" % h) for h in range(4)]
        r_C32 = [R("C32%d" % h) for h in range(4)]
        r_Cbf = [R("Cbf%d" % h) for h in range(4)]
        r_rr, r_ssq, r_t4, r_sc4 = R("rr"), R("ssq"), R("t4"), R("sc4")
        r_ybf, r_ybb = R("ybf"), R("ybb")
        r_ybT = [R("ybT%d" % s) for s in range(2)]
        r_sga = [R("sga%d" % i) for i in range(2)]
        r_sgb = [R("sgb%d" % i) for i in range(2)]
        r_m1 = [R("m1%d" % i) for i in range(2)]
        r_m2 = [R("m2%d" % i) for i in range(2)]
        r_mT = [R("mT%d" % f) for f in range(8)]
        r_pb = [R("pb%d" % i) for i in range(8)]
        PBb = [PB[i].bitcast(BF16) for i in range(8)]

        wv = dr["w_in"].rearrange("(c p) f -> p c f", p=128)
        r_wB2 = [R("wB2_%d" % c) for c in range(8)]
        for c in range(8):
            P.dma("pool", lambda e, c=c: e.dma_start(out=wB[:, c, 0:GAo], in_=wv[:, c, B_OFF:B_OFF + GAo]),
                  writes=[r_wB[c]])
        for c in range(8):
            P.dma("pool", lambda e, c=c: e.dma_start(out=wB[:, c, GAo:WB], in_=wv[:, c, B_OFF + GAo:IN_W]),
                  writes=[r_wB2[c]])
        P.dma("pool", lambda e: e.dma_start(out=wpa[:], in_=dr["w_proj_a"].rearrange("(c p) f -> p c f", p=128)),
              writes=[r_wpa])
        P.dma("pool", lambda e: e.dma_start(out=wpb[:], in_=dr["w_proj_b"].rearrange("(c p) f -> p c f", p=128)),
              writes=[r_wpb])
        wov = dr["w_out"].rearrange("(c p) f -> p c f", p=128)
        for c in range(8):
            P.dma("pool", lambda e, c=c: e.dma_start(out=wo[:, c, :], in_=wov[:, c, :]), writes=[r_wo[c]])
        P.dma("pool", lambda e: e.dma_start(out=trib[:], in_=dr["consts"][:, 1, :]), writes=[r_k])
        P.dma("sp", lambda e: e.dma_start(out=gM[:], in_=dr["norm_mix_g"].to_broadcast([128, D])), writes=[r_k])
        P.dma("sp", lambda e: e.dma_start(out=gN[:], in_=dr["mlstm_norm_g"].to_broadcast([128, 512])), writes=[r_k])
        P.dma("sp", lambda e: e.dma_start(out=bias8[:], in_=dr["bias8"].to_broadcast([128, 8])), writes=[r_k])
        P.dma("sp", lambda e: e.dma_start(out=cw[:], in_=dr["conv_wT"].rearrange("(m p) j -> p m j", p=128)),
              writes=[r_k])
        P.ts("dve", gN[:], gN[:], 0.5, ALU.mult, [r_k], [r_k])
        P.memset("dve", epsb[:], EPS, [r_k])
        P.memset("dve", oneb[:], 1.0, [r_k])
        P.memset("dve", lncb[:], LNC + float(np.log(0.5)), [r_k])
        P.memset("dve", lnhb[:], float(np.log(0.5)), [r_k])
        P.memset("dve", C32[:], 0.0, r_C32)
        P.memset("dve", Cbf[:], 0.0, r_Cbf)
        P.memset("dve", vaug[:], 1.0, r_va)
        P.memset("dve", xc[:], 0.0, r_xc)

        xv = dr["x"].rearrange("(t s p) d -> t p s d", s=2, p=128)
        hv = h1_d.rearrange("(t s p) d -> t p s d", s=2, p=128)
        ya_v = ya_d.rearrange("h d s -> (h d) s").rearrange("(c p) s -> p c s", p=128)

        ybT2 = sbt("ybT2", [128, 4, T], BF16)
        ybTs = [ybT, ybT2]
        r_ybTs = [r_ybT, [R("ybTb%d" % s) for s in range(2)]]

        def stage_A(t):
            b = t % 2
            xt = xb[t % 3]
            uTt = uT[b]
            for s in range(2):
                sub = slice(s * 128, (s + 1) * 128)
                load_norm_tile(nc, P, xt, r_xb[t % 3][s], s, gM, r_k, st8, r_st, s, epsb, r_k, junk, r_junk, ub, r_ub)
                yield
                for g4 in range(2):
                    for j in range(4):
                        c = g4 * 4 + j
                        P.tr(PBb[2 + g4][:, j * 128:(j + 1) * 128], ub[:, c * 128:(c + 1) * 128],
                             identb[:], [r_ub, r_c], [r_pb[2 + g4]], inc=(j == 3))
                for g4 in range(2):
                    P.cp("act", uTt[:, g4 * 4:(g4 + 1) * 4, sub],
                         PBb[2 + g4][:, 0:512].rearrange("p (j t) -> p j t", j=4),
                         [r_pb[2 + g4]], [r_uT[b][s]])
                yield
                pg = PB[4][:, 0:8]
                for c in range(8):
                    P.mm(pg, uTt[:, c, sub], wB[:, c, IBo:IBo + 8], c == 0, c == 7,
                         [r_uT[b][s], r_wB[c]], [r_pb[4]])
                P.tt("dve", gsb[:, s, :], pg, bias8[:], ALU.add, [r_pb[4], r_k], [r_gs[s]])
                P.act(e1[:, s, :], gsb[:, s, 4:8], AF.Exp, [r_gs[s]], [r_e1[s]], scale=-1.0)
                P.act(nlf[:, s, :], e1[:, s, :], AF.Ln, [r_e1[s], r_k], [r_nlf[s]], bias=oneb[:])
                pv = PB[2][:]
                for c in range(8):
                    P.mm(pv, uTt[:, c, sub], wB[:, c, VBo:VBo + 512], c == 0, c == 7,
                         [r_uT[b][s], r_wB[c]], [r_pb[2]])
                P.cp("act", vaug[:, s, :, 0:128], pv.rearrange("p (h e) -> p h e", h=4), [r_pb[2]], [r_va[s]])
                yield
                po = PB[3][:]
                for c in range(8):
                    P.mm(po, uTt[:, c, sub], wB[:, c, OBo:OBo + 512], c == 0, c == 7,
                         [r_uT[b][s], r_wB[c]], [r_pb[3]])
                P.act(sgo[:, s, :], po, AF.Tanh, [r_pb[3]], [r_sgo[s]], scale=0.5)
                P.mm(PB[4][:, 8:12], trim, nlf[:, s, :], True, True, [r_c, r_nlf[s]], [r_pb[4]], inc=False)
                P.mm(PB[4][:, 12:16], onesf, nlf[:, s, :], True, True, [r_c, r_nlf[s]], [r_pb[4]])
                P.tt("dve", arg[:, s, :], PB[4][:, 8:12], gsb[:, s, 0:4], ALU.add, [r_pb[4], r_gs[s]], [r_arg[s]])
                P.act(ksc[:, s, :], arg[:, s, :], AF.Exp, [r_arg[s], r_k], [r_ksc[s]], bias=lncb[:])
                P.act(wc[:, s, :], PB[4][:, 12:16], AF.Exp, [r_pb[4]], [r_wc[s]], scale=-1.0)
                yield
                for h in range(4):
                    i0 = (2 * h) % 3
                    i1 = (2 * h + 1) % 3
                    pe_b = PB[5 + i0][:, 0:128]
                    pe_a = PB[5 + i1][:, 0:128]
                    nb_l = nlf[:, s, h:h + 1].to_broadcast([128, 128])
                    ip_l = gsb[:, s, h:h + 1].to_broadcast([128, 128])
                    P.mm(pe_b, nb_l, tri, True, True, [r_nlf[s], r_c], [r_pb[5 + i0]])
                    P.mm(pe_a, ip_l, idf, True, False, [r_gs[s], r_c], [r_pb[5 + i1]], inc=False)
                    P.mm(pe_a, nb_l, tri, False, True, [r_nlf[s], r_c], [r_pb[5 + i1]])
                    P.act(EBt[:, h, sub], pe_b, AF.Exp, [r_pb[5 + i0], r_k], [r_EB[h][s]], scale=-1.0, bias=lnhb[:])
                    P.act(EAt[:, h, sub], pe_a, AF.Exp, [r_pb[5 + i1], r_k], [r_EA[h][s]], bias=lncb[:])
                    if h % 2 == 1:
                        yield
            for m in range(8):
                hh = m % 4
                pq = PB[2 + m % 2][:, 0:256]
                for c in range(8):
                    P.mm(pq, wB[:, c, QKB + m * 128:QKB + (m + 1) * 128], uTt[:, c, :], c == 0, c == 7,
                         [r_wB[c]] + r_uT[b], [r_pb[2 + m % 2]])
                P.cp("act", xc[:, m, 3:3 + T], pq, [r_pb[2 + m % 2]], [r_xc[m]])
                y = yc[m % 2]
                ry = r_yc[m % 2]
                P.ts("dve", y[:], xc[:, m, 0:T], cw[:, m, 0:1], ALU.mult, [r_xc[m], r_k], [ry])
                for j in range(1, 4):
                    P.stt("dve", y[:], xc[:, m, j:j + T], cw[:, m, j:j + 1], y[:], ALU.mult, ALU.add,
                          [r_xc[m], r_k, ry], [ry])
                P.cp("dve", xc[:, m, 0:3], xc[:, m, T:T + 3], [r_xc[m]], [r_xc[m]])
                if m < 4:
                    sl = slq[m % 2]
                    P.act(sl[:], y[:], AF.Tanh, [ry], [r_slq[m % 2]], scale=0.5)
                    P.stt("dve", y[:], sl[:], 1.0, y[:], ALU.add, ALU.mult, [r_slq[m % 2], ry], [ry])
                    P.tt("dve", qT[:, hh, :], y[:], EBt[:, hh, :], ALU.mult, [ry] + r_EB[hh], [r_qT[hh]])
                else:
                    sl = slq[m % 2]
                    P.act(sl[:], y[:], AF.Tanh, [ry], [r_slq[m % 2]], scale=0.5)
                    P.stt("dve", kT0[:, hh, :], sl[:], 1.0, y[:], ALU.add, ALU.mult, [r_slq[m % 2], ry], [r_kT0[hh]])
                    P.tt("dve", kT[:, hh, :], kT0[:, hh, :], EAt[:, hh, :], ALU.mult,
                         [r_kT0[hh]] + r_EA[hh], [r_kT[hh]])
                yield
            for s in range(2):
                sub = slice(s * 128, (s + 1) * 128)
                pnum = [PB[6][:, 0:129], PB[7][:, 0:129], PB[6][:, 256:385], PB[7][:, 256:385]]
                r_pn = [r_pb[6], r_pb[7], r_pb[6], r_pb[7]]
                pdc = [PB[2][:, 256:385], PB[3][:, 256:385], PB[2][:, 256:385], PB[3][:, 256:385]]
                r_pd = [r_pb[2], r_pb[3], r_pb[2], r_pb[3]]
                for h in range(4):
                    ptk = PBb[2 + h % 2][:, 0:128]
                    P.tr(ptk, kT0[:, h, sub], identb[:], [r_kT0[h], r_c], [r_pb[2 + h % 2]])
                    pS = PB[4 + h % 2][:, 0:128]
                    P.mm(pS, kT[:, h, sub], qT[:, h, sub], True, True, [r_kT[h], r_qT[h]], [r_pb[4 + h % 2]])
                    P.ts("dve", kS[:, h, :], ptk, ksc[:, s, h:h + 1], ALU.mult, [r_pb[2 + h % 2], r_ksc[s]],
                         [r_kS[h]])
                    P.tt("dve", scT[:, h, :], pS, trib[:], ALU.mult, [r_pb[4 + h % 2], r_k], [r_scT[h]])
                    yield
                    P.mm(pnum[h], qT[:, h, sub], Cbf[:, h, :], True, False, [r_qT[h], r_Cbf[h]], [r_pn[h]],
                         inc=False)
                    P.mm(pnum[h], scT[:, h, :], vaug[:, s, h, :], False, True, [r_scT[h], r_va[s]], [r_pn[h]])
                    P.mm(pdc[h], kS[:, h, :], vaug[:, s, h, :], True, True, [r_kS[h], r_va[s]], [r_pd[h]])
                    P.stt("dve", C32[:, h, :], C32[:, h, :], wc[:, s, h:h + 1], pdc[h], ALU.mult, ALU.add,
                          [r_C32[h], r_wc[s], r_pd[h]], [r_C32[h]])
                    P.cp("act", Cbf[:, h, :], C32[:, h, :], [r_C32[h]], [r_Cbf[h]])
                    yield
                for h in range(4):
                    P.act(junk[:, 0:128], pnum[h][:, 0:128], AF.Square, [r_pn[h]], [r_junk, r_ssq],
                          accum=ssq[:, h:h + 1])
                for h in range(4):
                    P.cp("dve", t4[:, h:h + 1], pnum[h][:, 128:129], [r_pn[h]], [r_t4])
                P.stt("dve", rr[:], t4[:], -1.0, t4[:], ALU.mult, ALU.max, [r_t4], [r_rr])
                P.ts("dve", rr[:], rr[:], 1.0, ALU.max, [r_rr], [r_rr])
                P.recip(rr[:], rr[:], [r_rr], [r_rr])
                P.tt("dve", t4[:], rr[:], rr[:], ALU.mult, [r_rr], [r_t4])
                P.tt("dve", t4[:], t4[:], ssq[:], ALU.mult, [r_t4, r_ssq], [r_t4])
                P.act(t4[:], t4[:], AF.Sqrt, [r_t4, r_k], [r_t4], bias=epsb[:], scale=1.0 / 128)
                P.recip(t4[:], t4[:], [r_t4], [r_t4])
                P.tt("dve", sc4[:], t4[:], rr[:], ALU.mult, [r_t4, r_rr], [r_sc4])
                yield
                for h in range(4):
                    P.stt("dve", ybf[:, h * 128:(h + 1) * 128], pnum[h][:, 0:128], sc4[:, h:h + 1],
                          gN[:, h * 128:(h + 1) * 128], ALU.mult, ALU.mult, [r_pn[h], r_sc4, r_k], [r_ybf])
                P.stt("dve", ybb[:], sgo[:, s, :], 1.0, ybf[:], ALU.add, ALU.mult, [r_ybf, r_sgo[s]], [r_ybb])
                yield
                for h in range(4):
                    P.tr(PBb[3][:, h * 128:(h + 1) * 128], ybb[:, h * 128:(h + 1) * 128], identb[:],
                         [r_ybb, r_c], [r_pb[3]], inc=(h == 3))
                P.cp("act", ybTs[b][:, :, sub], PBb[3][:, 0:512].rearrange("p (j t) -> p j t", j=4),
                     [r_pb[3]], [r_ybTs[b][s]])
                yield

        def stage_M(t):
            b = t % 2
            xt = xb[t % 3]
            uTt = uT[b]
            for f in range(8):
                fs = slice(f * 128, (f + 1) * 128)
                pga = PB[0][:, 0:256]
                pgb = PB[0][:, 256:512]
                ppa = PB[1][:, 0:256]
                ppb = PB[1][:, 256:512]
                for c in range(8):
                    P.mm(pga, wB[:, c, GAo + f * 128:GAo + (f + 1) * 128], uTt[:, c, :], c == 0, c == 7,
                         [r_wB2[c]] + r_uT[b], [r_pb[0]])
                for c in range(8):
                    P.mm(pgb, wB[:, c, GBo + f * 128:GBo + (f + 1) * 128], uTt[:, c, :], c == 0, c == 7,
                         [r_wB2[c]] + r_uT[b], [r_pb[0]])
                k2 = f % 2
                P.act(sga[k2][:], pga, AF.Tanh, [r_pb[0]], [r_sga[k2]], scale=0.5)
                P.act(sgb[k2][:], pgb, AF.Tanh, [r_pb[0]], [r_sgb[k2]], scale=0.5)
                for pr in range(4):
                    P.mm(ppa, wpa[:, pr, fs], yaT[t % 3][:, pr, :], pr == 0, pr == 3, [r_wpa, r_yaT[t % 3]], [r_pb[1]])
                for pr in range(4):
                    P.mm(ppb, wpb[:, pr, fs], ybTs[b][:, pr, :], pr == 0, pr == 3, [r_wpb] + r_ybTs[b], [r_pb[1]])
                P.stt("dve", m1[k2][:], sga[k2][:], 1.0, ppa, ALU.add, ALU.mult, [r_sga[k2], r_pb[1]], [r_m1[k2]])
                P.stt("dve", m2[k2][:], sgb[k2][:], 1.0, ppb, ALU.add, ALU.mult, [r_sgb[k2], r_pb[1]], [r_m2[k2]])
                P.tt("dve", mT[:, f, :], m1[k2][:], m2[k2][:], ALU.add, [r_m1[k2], r_m2[k2]], [r_mT[f]])
                yield
            for s in range(2):
                sub = slice(s * 128, (s + 1) * 128)
                for n in range(2):
                    po2 = PB[n][:]
                    rpo = [r_pb[n]]
                    for c in range(8):
                        P.mm(po2, mT[:, c, sub], wo[:, c, n * 512:(n + 1) * 512], c == 0, c == 7,
                             [r_mT[c], r_wo[c]], rpo)
                    P.stt("dve", xt[:, s, n * 512:(n + 1) * 512], po2, 0.5, xt[:, s, n * 512:(n + 1) * 512],
                          ALU.mult, ALU.add, rpo + [r_xb[t % 3][s]], [r_xb[t % 3][s]])
                    yield
            P.dma("sp", lambda e, t=t, xt=xt: e.dma_start(out=hv[t], in_=xt[:]), reads=r_xb[t % 3], writes=[r_h1[t]])

        def prefetch(tt_):
            if tt_ < NT:
                P.dma("sp", lambda e: e.dma_start(out=xb[tt_ % 3][:], in_=xv[tt_]), writes=r_xb[tt_ % 3])
                P.dma("sp", lambda e: e.dma_start(out=yaT[tt_ % 3][:], in_=ya_v[:, :, tt_ * T:(tt_ + 1) * T]),
                      reads=r_ya[tt_], writes=[r_yaT[tt_ % 3]])

        prefetch(0)
        prefetch(1)
        for _ in stage_A(0):
            pass
        for t in range(NT):
            if t + 1 < NT:
                prefetch(t + 2)
                side = stage_M(t)
                n_side = 12
                done = 0
                for i, _ in enumerate(stage_A(t + 1)):
                    want = ((i + 1) * n_side) // 46
                    while done < want:
                        next(side, None)
                        done += 1
                for _ in side:
                    pass
            else:
                for _ in stage_M(t):
                    pass


def phase_C(nc, P, S, h_d, r_h1, out_d, dr, G):
    T = 256
    NT = S // T
    NF = DFF // 128
    identb, r_ident = G["identb"], G["r_c"]
    g_ffn_d, g_fin_d, wgu_d, wdn_d = dr["norm_ffn_g"], dr["norm_final_g"], dr["w_gate_up"], dr["w_down"]
    with ExitStack() as st:
        ec = st.enter_context
        wgu = ec(nc.sbuf_tensor("wgu", [128, 8, 2 * DFF], BF16))
        wdn = ec(nc.sbuf_tensor("wdn", [128, NF, D], BF16))
        gF = ec(nc.sbuf_tensor("gF", [128, D], F32))
        gL = ec(nc.sbuf_tensor("gL", [128, D], F32))
        hb = [ec(nc.sbuf_tensor("hb%d" % i, [128, 2, D], F32)) for i in range(2)]
        ub = ec(nc.sbuf_tensor("ub", [128, D], BF16))
        uT = [ec(nc.sbuf_tensor("uT%d" % i, [128, 8, T], BF16)) for i in range(2)]
        aT = ec(nc.sbuf_tensor("aT", [128, NF, T], BF16))
        sg = [ec(nc.sbuf_tensor("sg%d" % i, [128, T], BF16)) for i in range(2)]
        junk = ec(nc.sbuf_tensor("junk", [128, D], BF16))
        st8 = ec(nc.sbuf_tensor("st8", [128, 8], F32))
        epsb = ec(nc.sbuf_tensor("epsb", [128, 1], F32))
        r_eps = Res("eps")
        P.op("dve", lambda e: e.memset(epsb[:], EPS), writes=[r_eps])
        ptp = [ec(nc.psum_tensor("ptp%d" % i, [128, 4, 128], BF16)) for i in range(2)]
        pgu = [ec(nc.psum_tensor("pgu%d" % i, [128, 512], F32)) for i in range(4)]
        pdn = [ec(nc.psum_tensor("pdn%d" % i, [128, 512], F32)) for i in range(2)]

        r_wdn = [Res("wdn%d" % c) for c in range(NF)]
        r_gF, r_gL = Res("gF"), Res("gL")
        r_hb = [[Res("hb%d_%d" % (i, s)) for s in range(2)] for i in range(2)]
        r_ub = Res("ub")
        r_uT = [Res("uT%d" % i) for i in range(2)]
        r_aT = [Res("aT%d" % j) for j in range(NF)]
        r_sg = [Res("sg%d" % i) for i in range(2)]
        r_junk = Res("junk")
        r_st = [Res("st%d" % i) for i in range(8)]
        r_ptp = [Res("ptp%d" % i) for i in range(2)]
        r_pgu = [Res("pgu%d" % i) for i in range(4)]
        r_pdn = [Res("pdn%d" % i) for i in range(2)]

        P.dma("pool", lambda e: e.dma_start(out=gF[:], in_=g_ffn_d.to_broadcast([128, D])), writes=[r_gF])
        P.dma("pool", lambda e: e.dma_start(out=gL[:], in_=g_fin_d.to_broadcast([128, D])), writes=[r_gL])
        wgu_v = wgu_d.rearrange("(c p) f -> p c f", p=128)
        wdn_v = wdn_d.rearrange("(c p) f -> p c f", p=128)
        NBLK = 11
        r_wgu = [Res("wgu%d" % i) for i in range(NBLK)]
        for i in range(NBLK):
            for hlf in range(2):
                lo = hlf * DFF + i * 256
                P.dma("pool", lambda e, lo=lo: e.dma_start(out=wgu[:, :, lo:lo + 256], in_=wgu_v[:, :, lo:lo + 256]),
                      writes=[r_wgu[i]])
        for c in range(NF):
            P.dma("pool", lambda e, c=c: e.dma_start(out=wdn[:, c, :], in_=wdn_v[:, c, :]),
                  writes=[r_wdn[c]])
        h_v = h_d.rearrange("(t s p) d -> t p s d", s=2, p=128)
        o_v = out_d.rearrange("(t s p) d -> t p s d", s=2, p=128)
        final_pts = []

        def rstd_from(h_ap, col, r_h):
            P.op("act", lambda e: e.activation(out=junk[:], in_=h_ap, func=AF.Square,
                                               accum_out=st8[:, col:col + 1]),
                 reads=[r_h], writes=[r_junk, r_st[col]])
            P.op("act", lambda e: e.activation(out=st8[:, col:col + 1], in_=st8[:, col:col + 1], func=AF.Sqrt,
                                               scale=1.0 / D, bias=epsb[:]),
                 reads=[r_st[col], r_eps], writes=[r_st[col]])
            P.op("dve", lambda e: e.reciprocal(out=st8[:, col:col + 1], in_=st8[:, col:col + 1]),
                 reads=[r_st[col]], writes=[r_st[col]])

        def prologue(t):
            b = t % 2
            hbt = hb[b]
            P.dma("sp", lambda e, t=t, hbt=hbt: e.dma_start(out=hbt[:], in_=h_v[t]),
                  reads=[r_h1[t]], writes=[r_hb[b][0], r_hb[b][1]])
            for s in range(2):
                rstd_from(hbt[:, s, :], s, r_hb[b][s])
                P.stt("dve", ub[:], hbt[:, s, :], st8[:, s:s + 1], gF[:], ALU.mult, ALU.mult,
                      [r_hb[b][s], r_st[s], r_gF], [r_ub])
                for g4 in range(2):
                    pt = ptp[g4]
                    for j in range(4):
                        c = g4 * 4 + j
                        P.tr(pt[:, j, :], ub[:, c * 128:(c + 1) * 128], identb[:], [r_ub, r_ident], [r_ptp[g4]],
                             inc=(j == 3))
                    P.cp("act", uT[b][:, g4 * 4:(g4 + 1) * 4, s * 128:(s + 1) * 128], pt[:], [r_ptp[g4]], [r_uT[b]])

        def gateup(t):
            b = t % 2
            for j in range(NF):
                pg = pgu[(2 * j) % 4]
                pu = pgu[(2 * j + 1) % 4]
                rg = r_pgu[(2 * j) % 4]
                ru = r_pgu[(2 * j + 1) % 4]
                for c in range(8):
                    P.mm(pg[:, 0:T], wgu[:, c, j * 128:(j + 1) * 128], uT[b][:, c, :], c == 0, c == 7,
                         [r_wgu[j // 2], r_uT[b]], [rg])
                for c in range(8):
                    P.mm(pu[:, 0:T], wgu[:, c, DFF + j * 128:DFF + (j + 1) * 128], uT[b][:, c, :], c == 0, c == 7,
                         [r_wgu[j // 2], r_uT[b]], [ru])
                sgt = sg[j % 2]
                P.act(sgt[:], pg[:, 0:T], AF.Silu, [rg], [r_sg[j % 2]])
                P.tt("dve", aT[:, j, :], sgt[:], pu[:, 0:T], ALU.mult, [ru, r_sg[j % 2]], [r_aT[j]])

        def down(t):
            b = t % 2
            hbt = hb[b]
            for s in range(2):
                for n in range(2):
                    pd = pdn[n]
                    for j in range(NF):
                        P.mm(pd[:], aT[:, j, s * 128:(s + 1) * 128], wdn[:, j, n * 512:(n + 1) * 512],
                             j == 0, j == NF - 1, [r_aT[j], r_wdn[j]], [r_pdn[n]])
                    P.tt("dve", hbt[:, s, n * 512:(n + 1) * 512], hbt[:, s, n * 512:(n + 1) * 512], pd[:], ALU.add,
                         [r_pdn[n], r_hb[b][s]], [r_hb[b][s]])
                rstd_from(hbt[:, s, :], 2 + s, r_hb[b][s])
                P.stt("dve", hbt[:, s, :], hbt[:, s, :], st8[:, 2 + s:3 + s], gL[:], ALU.mult, ALU.mult,
                      [r_hb[b][s], r_st[2 + s], r_gL], [r_hb[b][s]])
            final_pts.append(P.dma("sp", lambda e, t=t, hbt=hbt: e.dma_start(out=o_v[t], in_=hbt[:]),
                                   reads=[r_hb[b][0], r_hb[b][1]]))

        prologue(0)
        for t in range(NT):
            gateup(t)
            if t + 1 < NT:
                prologue(t + 1)
            down(t)
        return final_pts


def _host_consts(S):
    idx = np.arange(128)
    ident = np.eye(128, dtype=np.float32)
    tri = (idx[:, None] <= idx[None, :]).astype(np.float32)
    trim = -(idx[:, None] > idx[None, :]).astype(np.float32)
    ones = np.ones((128, 128), np.float32)
    consts = np.ascontiguousarray(np.stack([ident, tri, trim, ones], axis=1))
    inv = (np.float32(10000.0) ** (-np.arange(32, dtype=np.float32) / np.float32(32))).astype(np.float32)
    ang = np.arange(S, dtype=np.float32)[:, None] * inv[None, :]
    cos, sin = np.cos(ang).astype(np.float32), np.sin(ang).astype(np.float32)
    rope = np.ascontiguousarray(np.stack([np.concatenate([cos, cos], 1), np.concatenate([-sin, sin], 1)], axis=1))
    k = np.arange(128)[:, None, None]
    a = np.arange(4)[None, :, None]
    q = np.arange(512)[None, None, :]
    jq, ja = q // 256, a // 2
    cm = np.where(jq > ja, 1.0, np.where(jq < ja, 0.0, (q >= 128 * a + k).astype(np.float32)))
    cmask = np.ascontiguousarray(np.broadcast_to(cm, (128, 4, 512)).astype(ml_dtypes.bfloat16))
    blk = np.arange(16)[:, None]
    j = np.arange(16)[None, :]
    pastb = np.where(j < blk, 0.0, np.where(j == blk, 1e30, -1e30)).astype(np.float32)
    oneh = np.where(j == blk, -1024.0, 0.0).astype(np.float32)
    blkc = np.ascontiguousarray(np.broadcast_to(np.concatenate([pastb, oneh], 1)[None], (128, 16, 32)).astype(np.float32))
    return dict(consts=consts, rope=rope, cmask=cmask, blkc=blkc)


def make_in_maps(inp, ncores):
    f = lambda k: np.ascontiguousarray(np.asarray(inp[k], dtype=np.float32))
    x = np.asarray(inp["x"], dtype=np.float32)
    S = x.shape[1]
    common = {
        "norm_mix_g": f("norm_mix_g").reshape(1, D),
        "w_in": f("w_in")[0],
        "conv_wT": np.ascontiguousarray(f("conv_w")[0].T),
        "bias8": np.ascontiguousarray(np.concatenate([f("b_igate")[0], f("b_fgate")[0]]).reshape(1, 8)),
        "mlstm_norm_g": f("mlstm_norm_g").reshape(1, 512),
        "w_proj_a": f("w_proj_a")[0],
        "w_proj_b": f("w_proj_b")[0],
        "w_out": f("w_out")[0],
        "norm_ffn_g": f("norm_ffn_g").reshape(1, D),
        "norm_final_g": f("norm_final_g").reshape(1, D),
        "w_gate_up": f("w_gate_up")[0],
        "w_down": f("w_down")[0],
    }
    common.update(_host_consts(S))
    return [dict(common, x=np.ascontiguousarray(x[b])) for b in range(ncores)]


def kernel(x, norm_mix_g, w_in, conv_w, b_igate, b_fgate, mlstm_norm_g, w_proj_a, w_proj_b,
           w_out, norm_ffn_g, w_gate_up, w_down, norm_final_g):
    inp = dict(x=x, norm_mix_g=norm_mix_g, w_in=w_in, conv_w=conv_w, b_igate=b_igate, b_fgate=b_fgate,
               mlstm_norm_g=mlstm_norm_g, w_proj_a=w_proj_a, w_proj_b=w_proj_b, w_out=w_out,
               norm_ffn_g=norm_ffn_g, w_gate_up=w_gate_up, w_down=w_down, norm_final_g=norm_final_g)
    B, S, _ = np.asarray(x).shape
    nc = build_nc(S)
    in_maps = make_in_maps(inp, B)
    res = run_bass_kernel_spmd(nc, in_maps, core_ids=list(range(B)))
    return np.stack([np.asarray(r["out"]) for r in res.results], axis=0).astype(np.float32)
```

```python
import numpy as np
import ml_dtypes
from contextlib import ExitStack

import concourse.bass as bass
import concourse.mybir as mybir
from concourse.bass_utils import run_bass_kernel_spmd

F32 = mybir.dt.float32
BF16 = mybir.dt.bfloat16
ALU = mybir.AluOpType
AF = mybir.ActivationFunctionType
AX = mybir.AxisListType

D = 1024
DFF = 2816
EPS = 1e-6
NCORES = 8


class Res:
    __slots__ = ("name", "w", "r")

    def __init__(self, name):
        self.name = name
        self.w = None
        self.r = {}


class Prog:
    COMPUTE = ("pe", "dve", "act", "pool")

    def __init__(self, nc, stack, n_sp=12, n_pool=6):
        self.nc = nc
        self.sem = {}
        self.stack = stack
        self.epoch = 0
        self.ekey = {e: e for e in self.COMPUTE}
        for e in self.COMPUTE:
            self.sem[e] = stack.enter_context(nc.semaphore("s_" + e))
        self.dma_pool = {"sp": [], "pool": []}
        for i in range(n_sp):
            k = "dsp%d" % i
            self.sem[k] = stack.enter_context(nc.semaphore(k))
            self.dma_pool["sp"].append(k)
        for i in range(n_pool):
            k = "dpl%d" % i
            self.sem[k] = stack.enter_context(nc.semaphore(k))
            self.dma_pool["pool"].append(k)
        self.dma_rr = {"sp": 0, "pool": 0}
        self.cnt = {k: 0 for k in self.sem}
        self.ops = {e: [] for e in ("pe", "dve", "act", "pool", "sp")}
        self.know = {e: {} for e in self.ops}
        self.clock = {}
        self.pe_pending = False

    def _needs(self, reads, writes):
        need = {}
        for r in reads:
            if r.w is not None:
                k, v = r.w
                if need.get(k, 0) < v:
                    need[k] = v
        for w in writes:
            if w.w is not None:
                k, v = w.w
                if need.get(k, 0) < v:
                    need[k] = v
            for k, v in w.r.items():
                if need.get(k, 0) < v:
                    need[k] = v
        return need

    def _waits(self, eng, need):
        know = self.know[eng]
        waits = []
        for k, v in need.items():
            if eng == "pe" and k == self.ekey["pe"]:
                continue
            if know.get(k, 0) >= v:
                continue
            waits.append((k, v))
        for k, v in waits:
            ck = self.clock.get((k, v))
            if ck:
                for kk, vv in ck.items():
                    if know.get(kk, 0) < vv:
                        know[kk] = vv
            if know.get(k, 0) < v:
                know[k] = v
        return waits

    def _record(self, point, reads, writes):
        k, v = point
        for r in reads:
            if r.r.get(k, 0) < v:
                r.r[k] = v
        for w in writes:
            w.w = point
            w.r = {}

    def op(self, eng, fn, reads=(), writes=(), inc=True):
        need = self._needs(reads, writes)
        waits = self._waits(eng, need)
        sk = self.ekey[eng]
        if eng == "pe" and not inc:
            point = (sk, self.cnt[sk] + 1)
            self.pe_pending = True
            self.ops[eng].append((waits, fn, None, 0))
        else:
            self.cnt[sk] += 1
            point = (sk, self.cnt[sk])
            if eng == "pe":
                self.pe_pending = False
            self.ops[eng].append((waits, fn, sk, 1))
            ck = dict(self.know[eng])
            ck[sk] = self.cnt[sk]
            self.clock[point] = ck
        self._record(point, reads, writes)
        return point

    def new_epoch(self):
        self.epoch += 1
        for e in self.COMPUTE:
            k = "%s_%d" % (e, self.epoch)
            self.sem[k] = self.stack.enter_context(self.nc.semaphore("s_" + k))
            self.cnt[k] = 0
            self.ekey[e] = k

    def dma(self, q, fn, reads=(), writes=()):
        pool = self.dma_pool[q]
        sk = pool[self.dma_rr[q] % len(pool)]
        self.dma_rr[q] += 1
        need = self._needs(reads, writes)
        if self.cnt[sk] > 0 and need.get(sk, 0) < self.cnt[sk]:
            need[sk] = self.cnt[sk]
        waits = self._waits(q, need)
        self.cnt[sk] += 16
        point = (sk, self.cnt[sk])
        self.ops[q].append((waits, fn, sk, 16))
        ck = dict(self.know[q])
        ck[sk] = self.cnt[sk]
        self.clock[point] = ck
        self._record(point, reads, writes)
        return point

    def mm(self, out, lhsT, rhs, start, stop, reads, writes, inc=None):
        if inc is None:
            inc = stop
        return self.op("pe", lambda e: e.matmul(out, lhsT=lhsT, rhs=rhs, start=start, stop=stop),
                       reads, writes, inc)

    def tr(self, out, in_, ident, reads, writes, inc=True):
        return self.op("pe", lambda e: e.transpose(out=out, in_=in_, identity=ident), reads, writes, inc)

    def act(self, out, in_, func, reads, writes, bias=None, scale=None, accum=None):
        kw = {}
        if bias is not None:
            kw["bias"] = bias
        if scale is not None:
            kw["scale"] = scale
        if accum is not None:
            kw["accum_out"] = accum
        return self.op("act", lambda e: e.activation(out=out, in_=in_, func=func, **kw), reads, writes)

    def tt(self, eng, out, in0, in1, op, reads, writes):
        return self.op(eng, lambda e: e.tensor_tensor(out=out, in0=in0, in1=in1, op=op), reads, writes)

    def ts(self, eng, out, in0, s1, op0, reads, writes, s2=None, op1=None):
        if op1 is None:
            return self.op(eng, lambda e: e.tensor_scalar(out=out, in0=in0, scalar1=s1, scalar2=None, op0=op0),
                           reads, writes)
        return self.op(eng, lambda e: e.tensor_scalar(out=out, in0=in0, scalar1=s1, scalar2=s2, op0=op0, op1=op1),
                       reads, writes)

    def stt(self, eng, out, in0, scalar, in1, op0, op1, reads, writes):
        return self.op(eng, lambda e: e.scalar_tensor_tensor(out=out, in0=in0, scalar=scalar, in1=in1,
                                                             op0=op0, op1=op1), reads, writes)

    def cp(self, eng, out, in_, reads, writes):
        if eng == "act":
            return self.op("act", lambda e: e.copy(out=out, in_=in_), reads, writes)
        return self.op(eng, lambda e: e.tensor_copy(out=out, in_=in_), reads, writes)

    def recip(self, out, in_, reads, writes):
        return self.op("dve", lambda e: e.reciprocal(out=out, in_=in_), reads, writes)

    def memset(self, eng, ap, val, writes):
        return self.op(eng, lambda e: e.memset(ap, val), (), writes)

    def barrier(self):
        assert not self.pe_pending
        need = {k: v for k, v in self.cnt.items() if v > 0}
        for e in self.ops:
            waits = self._waits(e, dict(need))
            if waits:
                self.ops[e].append((waits, None, None, 0))

    def finish(self, final_points):
        need = {}
        for k, v in final_points:
            if need.get(k, 0) < v:
                need[k] = v
        waits = [(k, v) for k, v in need.items()]
        self.ops["sp"].append((waits, None, None, 0))

    def emit(self):
        nc = self.nc
        assert not self.pe_pending, "PE has trailing non-inc instructions"
        sem = self.sem
        ops = self.ops

        def run(e, lst):
            for waits, fn, sk, inc in lst:
                for k, v in waits:
                    e.wait_ge(sem[k], v)
                if fn is None:
                    continue
                ins = fn(e)
                if sk is not None:
                    ins.then_inc(sem[sk], inc)

        with nc.Block() as block:
            @block.tensor
            def _(e):
                run(e, ops["pe"])

            @block.vector
            def _(e):
                run(e, ops["dve"])

            @block.scalar
            def _(e):
                run(e, ops["act"])

            @block.gpsimd
            def _(e):
                run(e, ops["pool"])

            @block.sync
            def _(e):
                run(e, ops["sp"])


IN_W = 5640
A_QA, A_KA, A_VA = 0, 512, 1024
B_OFF = 1536
WB = IN_W - B_OFF
QKB, VBo, OBo, IBo, GAo, GBo = 0, 1024, 1536, 2048, 2056, 3080
LNC = float(np.log(128.0 ** -0.5))


def build_nc(S, phases=("A", "B", "C")):
    nc = bass.Bass("TRN2", target_bir_lowering=False)
    dr = {}

    def din(name, shape, dt=F32):
        dr[name] = nc.dram_tensor(name, shape, dt, kind="ExternalInput").ap()

    din("x", [S, D])
    din("norm_mix_g", [1, D])
    din("w_in", [D, IN_W])
    din("conv_wT", [D, 4])
    din("bias8", [1, 8])
    din("mlstm_norm_g", [1, 512])
    din("w_proj_a", [512, D])
    din("w_proj_b", [512, D])
    din("w_out", [D, D])
    din("norm_ffn_g", [1, D])
    din("norm_final_g", [1, D])
    din("w_gate_up", [D, 2 * DFF])
    din("w_down", [DFF, D])
    din("consts", [128, 4, 128])
    din("rope", [S, 2, 64])
    din("cmask", [128, 4, 512], BF16)
    din("blkc", [128, 16, 32])
    out_d = nc.dram_tensor("out", [S, D], F32, kind="ExternalOutput").ap()
    h1_d = nc.dram_tensor("h1_s", [S, D], F32, kind="Internal").ap()
    ya_d = nc.dram_tensor("ya_s", [8, 64, S], BF16, kind="Internal").ap()

    with ExitStack() as stack:
        P = Prog(nc, stack)
        ec = stack.enter_context
        identb = ec(nc.sbuf_tensor("identb", [128, 128], BF16))
        cst = ec(nc.sbuf_tensor("cst", [128, 4, 128], F32))
        r_c = Res("consts")
        P.dma("pool", lambda e: e.dma_start(out=identb[:], in_=dr["consts"][:, 0, :]), writes=[r_c])
        P.dma("sp", lambda e: e.dma_start(out=cst[:], in_=dr["consts"]), writes=[r_c])
        G = dict(identb=identb, cst=cst, r_c=r_c)

        r_ya = [[Res("ya_s%d_%d" % (i, h)) for h in range(8)] for i in range(S // 256)]
        r_h1 = [Res("h1_s%d" % i) for i in range(S // 256)]
        if "A" in phases:
            phase_A(nc, P, S, dr, ya_d, r_ya, G)
        else:
            with nc.sbuf_tensor("zt", [64, S], BF16) as zt:
                rz = Res("zt")
                P.memset("dve", zt[:], 0.0, [rz])
                for h in range(8):
                    P.dma("sp", lambda e, h=h: e.dma_start(out=ya_d[h], in_=zt[:]), reads=[rz],
                          writes=[r_ya[i][h] for i in range(S // 256)])
                P.barrier()
        P.barrier()
        P.new_epoch()
        if "B" in phases:
            phase_B(nc, P, S, dr, h1_d, ya_d, r_ya, r_h1, G)
            h_src = h1_d
        else:
            h_src = dr["x"]
        P.barrier()
        P.new_epoch()
        final_pts = phase_C(nc, P, S, h_src, r_h1, out_d, dr, G)
        P.finish(final_pts)
        P.emit()
    return nc


def load_norm_tile(nc, P, xt, r_x, s, gM, r_g, st8, r_st, col, epsb, r_eps, junk, r_junk, ub, r_ub):
    P.act(junk[:], xt[:, s, :], AF.Square, [r_x], [r_junk, r_st[col]], accum=st8[:, col:col + 1])
    P.act(st8[:, col:col + 1], st8[:, col:col + 1], AF.Sqrt, [r_st[col], r_eps], [r_st[col]],
          bias=epsb[:], scale=1.0 / D)
    P.recip(st8[:, col:col + 1], st8[:, col:col + 1], [r_st[col]], [r_st[col]])
    P.stt("dve", ub[:], xt[:, s, :], st8[:, col:col + 1], gM[:], ALU.mult, ALU.mult,
          [r_x, r_st[col], r_g], [r_ub])


def phase_A(nc, P, S, dr, ya_d, r_ya, G):
    T = 512
    NT = S // T
    NKT = S // 128
    identb, cst, r_c = G["identb"], G["cst"], G["r_c"]
    with ExitStack() as st:
        ec = st.enter_context

        def sbt(name, shape, dt):
            return ec(nc.sbuf_tensor(name, shape, dt))

        wA = sbt("wA", [128, 8, 1536], BF16)
        gM = sbt("gMa", [128, D], F32)
        cmk = sbt("cmk", [128, 4, 512], BF16)
        blkc = sbt("blkc_sb", [128, 16, 32], F32)
        epsb = sbt("epsb_a", [128, 1], F32)
        ropeT = [sbt("ropeT%d" % i, [128, 4, 2, 64], F32) for i in range(2)]
        kTa = sbt("kTa", [80, 8, S], BF16)
        vA = sbt("vA", [128, NKT, 8, 65], BF16)
        kmf = sbt("kmf", [64, 8], F32)
        kmT = sbt("kmT", [64, 8, 16], BF16)
        xs = [sbt("xs%d" % i, [128, 1, D], F32) for i in range(2)]
        ub = sbt("ub_a", [128, D], BF16)
        junk = sbt("junk_a", [128, D], BF16)
        st8 = sbt("st8_a", [128, 8], F32)
        uT = sbt("uTa", [128, 8, T], BF16)
        t1 = sbt("t1", [128, 8, 64], F32)
        t2 = sbt("t2", [128, 8, 64], F32)
        qtok = [sbt("qtok%d" % i, [128, 8, 80], BF16) for i in range(2)]
        ktok = [sbt("ktok%d" % i, [128, 8, 80], BF16) for i in range(2)]
        qTa = sbt("qTa", [80, 8, T], BF16)
        gsb = sbt("gsb_a", [128, 8, 16], F32)
        top8 = sbt("top8", [128, 8, 8], F32)
        PT = [sbt("PT%d" % i, [128, 512], BF16) for i in range(4)]
        oT = [sbt("oT%d" % i, [65, 512], F32) for i in range(2)]
        yo = [sbt("yo%d" % i, [64, 512], BF16) for i in range(2)]
        PB = [ec(nc.psum_tensor("PBa%d" % i, [128, 512], F32)) for i in range(8)]
        PBb = [PB[i].bitcast(BF16) for i in range(8)]

        R = lambda n: Res(n)
        r_wA = [R("wA%d" % c) for c in range(8)]
        r_k = R("constsA")
        r_rope = [R("rope%d" % i) for i in range(2)]
        r_kTa = [R("kTa%d" % i) for i in range(NKT)]
        r_vA = [R("vA%d" % i) for i in range(NKT)]
        r_kmf, r_kmT = R("kmf"), R("kmT")
        r_xs = [R("xs%d" % i) for i in range(2)]
        r_ub, r_junk = R("ub"), R("junk")
        r_st = [R("st%d" % i) for i in range(8)]
        r_uT = [R("uT%d" % s) for s in range(4)]
        r_t1, r_t2 = R("t1"), R("t2")
        r_qtok = [R("qtok%d" % i) for i in range(2)]
        r_ktok = [R("ktok%d" % i) for i in range(2)]
        r_qTa = [R("qTa%d" % s) for s in range(4)]
        r_gsb, r_top8 = R("gsb"), R("top8")
        r_PT = [R("PT%d" % i) for i in range(4)]
        r_oT = [R("oT%d" % i) for i in range(2)]
        r_yo = [R("yo%d" % i) for i in range(2)]
        r_pb = [R("pba%d" % i) for i in range(8)]

        wv = dr["w_in"].rearrange("(c p) f -> p c f", p=128)
        for c in range(8):
            P.dma("pool", lambda e, c=c: e.dma_start(out=wA[:, c, :], in_=wv[:, c, 0:1536]), writes=[r_wA[c]])
        P.dma("sp", lambda e: e.dma_start(out=gM[:], in_=dr["norm_mix_g"].to_broadcast([128, D])), writes=[r_k])
        P.dma("sp", lambda e: e.dma_start(out=cmk[:], in_=dr["cmask"]), writes=[r_k])
        P.dma("sp", lambda e: e.dma_start(out=blkc[:], in_=dr["blkc"]), writes=[r_k])
        P.memset("dve", epsb[:], EPS, [r_k])
        P.memset("dve", kmT[:], 0.0, [r_kmT])
        P.memset("dve", vA[:], 1.0, r_vA)

        xv = dr["x"].rearrange("(n p) d -> n p d", p=128)
        rope_v = dr["rope"].rearrange("(n s p) a d -> n p s a d", s=4, p=128)
        onesb = sbt("onesb", [128, 64], BF16)
        hiT = [sbt("hiT%d" % i, [65, 512], BF16) for i in range(2)]
        loT = [sbt("loT%d" % i, [65, 512], BF16) for i in range(2)]
        qTb = sbt("qTa2", [80, 8, T], BF16)
        qTas = [qTa, qTb]
        r_qTas = [r_qTa, [R("qTb%d" % s) for s in range(4)]]
        r_hl = [R("hl%d" % i) for i in range(2)]
        P.memset("dve", onesb[:], 1.0, [r_k])
        P.dma("sp", lambda e: e.dma_start(out=xs[0][:, 0, :], in_=xv[0]), writes=[r_xs[0]])

        def rope_ops(pz, rpz, rb, s, dst, rdst):
            z3 = pz.rearrange("p (h d) -> p h d", h=8)
            C2 = ropeT[rb][:, s, 0, :].unsqueeze(1).to_broadcast([128, 8, 64])
            Sa = ropeT[rb][:, s, 1, 0:32].unsqueeze(1).to_broadcast([128, 8, 32])
            Sb = ropeT[rb][:, s, 1, 32:64].unsqueeze(1).to_broadcast([128, 8, 32])
            P.tt("dve", t1[:], z3, C2, ALU.mult, [rpz, r_rope[rb]], [r_t1])
            P.tt("dve", t2[:, :, 0:32], z3[:, :, 32:64], Sa, ALU.mult, [rpz, r_rope[rb]], [r_t2])
            P.tt("dve", t2[:, :, 32:64], z3[:, :, 0:32], Sb, ALU.mult, [rpz, r_rope[rb]], [r_t2])
            P.tt("dve", dst[:, :, 0:64], t1[:], t2[:], ALU.add, [r_t1, r_t2], [rdst])

        def tr_group(src_fn, rsrc, rows, dst_fn, rdst, prow=None):
            for g4 in range(2):
                for j in range(4):
                    P.tr(PBb[g4][0:rows, j * 128:(j + 1) * 128], src_fn(g4 * 4 + j), identb[:], [rsrc, r_c],
                         [r_pb[g4]], inc=(j == 3))
            for g4 in range(2):
                lo_, hi_ = prow if prow else (0, rows)
                P.cp("dve", dst_fn(g4, lo_, hi_), PBb[g4][lo_:hi_, 0:512].rearrange("p (j t) -> p j t", j=4),
                     [r_pb[g4]], [rdst])

        def prologue(t):
            rb = t % 2
            qT_ = qTas[t % 2]
            rq_ = r_qTas[t % 2]
            P.dma("sp", lambda e, t=t, rb=rb: e.dma_start(out=ropeT[rb][:], in_=rope_v[t]), writes=[r_rope[rb]])
            for s in range(4):
                n = t * 4 + s
                sub = slice(s * 128, (s + 1) * 128)
                ksub = slice(n * 128, (n + 1) * 128)
                blk = n // 2
                if n + 1 < NKT:
                    P.dma("sp", lambda e, n=n: e.dma_start(out=xs[(n + 1) % 2][:, 0, :], in_=xv[n + 1]),
                          writes=[r_xs[(n + 1) % 2]])
                xt = xs[n % 2]
                load_norm_tile(nc, P, xt, r_xs[n % 2], 0, gM, r_k, st8, r_st, s, epsb, r_k, junk, r_junk, ub, r_ub)
                yield
                for g4 in range(2):
                    for j in range(4):
                        c = g4 * 4 + j
                        P.tr(PBb[g4][:, j * 128:(j + 1) * 128], ub[:, c * 128:(c + 1) * 128], identb[:],
                             [r_ub, r_c], [r_pb[g4]], inc=(j == 3))
                for g4 in range(2):
                    P.cp("dve", uT[:, g4 * 4:(g4 + 1) * 4, sub],
                         PBb[g4][:, 0:512].rearrange("p (j t) -> p j t", j=4), [r_pb[g4]], [r_uT[s]])
                yield
                kb, rkb = ktok[s % 2], r_ktok[s % 2]
                pz = PB[5][:]
                for c in range(8):
                    P.mm(pz, uT[:, c, sub], wA[:, c, A_KA:A_KA + 512], c == 0, c == 7, [r_uT[s], r_wA[c]], [r_pb[5]])
                rope_ops(pz, r_pb[5], rb, s, kb, rkb)
                P.cp("dve", kb[:, :, 64:80], blkc[:, blk, 16:32].unsqueeze(1).to_broadcast([128, 8, 16]),
                     [r_k], [rkb])
                yield
                tr_group(lambda h: kb[:, h, :], rkb, 80,
                         lambda g4, lo_, hi_: kTa[lo_:hi_, g4 * 4:(g4 + 1) * 4, ksub], r_kTa[n])
                if s % 2 == 1:
                    P.op("dve", lambda e, blk=blk: e.tensor_reduce(
                        out=kmf[:], in_=kTa[0:64, :, blk * 256:(blk + 1) * 256], axis=AX.X, op=ALU.add),
                        [r_kTa[n - 1], r_kTa[n]], [r_kmf])
                    P.ts("dve", kmT[:, :, blk], kmf[:], 1.0 / 256, ALU.mult, [r_kmf], [r_kmT])
                yield
                for c in range(8):
                    P.mm(pz, uT[:, c, sub], wA[:, c, A_VA:A_VA + 512], c == 0, c == 7, [r_uT[s], r_wA[c]], [r_pb[5]])
                P.cp("dve", vA[:, n, :, 0:64], pz.rearrange("p (h d) -> p h d", h=8), [r_pb[5]], [r_vA[n]])
                yield
                qb, rqb = qtok[s % 2], r_qtok[s % 2]
                for c in range(8):
                    P.mm(pz, uT[:, c, sub], wA[:, c, A_QA:A_QA + 512], c == 0, c == 7, [r_uT[s], r_wA[c]], [r_pb[5]])
                rope_ops(pz, r_pb[5], rb, s, qb, rqb)
                yield
                tr_group(lambda h: qb[:, h, 0:64], rqb, 64,
                         lambda g4, lo_, hi_: qT_[lo_:hi_, g4 * 4:(g4 + 1) * 4, sub], rq_[s])
                yield
                pgt = PB[5][:, 0:128].rearrange("p (h j) -> p h j", h=8)
                for h in range(8):
                    P.mm(pgt[:, h, :], qT_[0:64, h, sub], kmT[:, h, :], True, True, [rq_[s], r_kmT], [r_pb[5]],
                         inc=(h == 7))
                P.tt("dve", gsb[:], pgt, blkc[:, blk, 0:16].unsqueeze(1).to_broadcast([128, 8, 16]), ALU.add,
                     [r_pb[5], r_k], [r_gsb])
                for h in range(8):
                    P.op("dve", lambda e, h=h: e.max(out=top8[:, h, :], in_=gsb[:, h, :]), [r_gsb], [r_top8])
                P.tt("dve", qb[:, :, 64:80], gsb[:], top8[:, :, 3:4].to_broadcast([128, 8, 16]), ALU.is_lt,
                     [r_gsb, r_top8], [rqb])
                yield
                tr_group(lambda h: qb[:, h, :], rqb, 80,
                         lambda g4, lo_, hi_: qT_[lo_:hi_, g4 * 4:(g4 + 1) * 4, sub], rq_[s], prow=(64, 80))
                yield

        def attention(t):
            qT_ = qTas[t % 2]
            rq_ = r_qTas[t % 2]
            nkt = 4 * t + 4
            items = [(h, kt) for h in range(8) for kt in range(nkt)]
            LA = 2
            deferred = {}

            def emit_S(i):
                h, kt = items[i]
                bi = 2 + i % 3
                q0 = 128 * max(0, kt - 4 * t)
                P.mm(PB[bi][:, q0:512], kTa[:, h, kt * 128:(kt + 1) * 128], qT_[:, h, q0:512], True, True,
                     [r_kTa[kt]] + rq_, [r_pb[bi]])

            def emit_PV(i):
                h, kt = items[i]
                bi = 2 + i % 3
                pi = i % 4
                po = PB[6 + h % 2]
                rpo = r_pb[6 + h % 2]
                q0 = 128 * max(0, kt - 4 * t)
                P.act(PT[pi][:, q0:512], PB[bi][:, q0:512], AF.Exp, [r_pb[bi]], [r_PT[pi]], scale=0.125)
                if kt >= 4 * t:
                    P.tt("pool", PT[pi][:, q0:512], PT[pi][:, q0:512], cmk[:, kt - 4 * t, q0:512], ALU.mult,
                         [r_PT[pi], r_k], [r_PT[pi]])
                P.mm(po[0:65, q0:512], vA[:, kt, h, :], PT[pi][:, q0:512], kt == 0, kt == nkt - 1,
                     [r_vA[kt], r_PT[pi]], [rpo])
                if kt == nkt - 1:
                    ob, rob = oT[h % 2], r_oT[h % 2]
                    P.cp("dve", ob[:], po[0:65, :], [rpo], [rob])
                    P.recip(ob[64:65, :], ob[64:65, :], [rob], [rob])
                    P.cp("dve", hiT[h % 2][64:65, :], ob[64:65, :], [rob], [r_hl[h % 2]])
                    P.tt("dve", loT[h % 2][64:65, :], ob[64:65, :], hiT[h % 2][64:65, :], ALU.subtract,
                         [rob, r_hl[h % 2]], [r_hl[h % 2]])
                    deferred.setdefault(min(i + 4, len(items) - 1), []).append(h)

            def emit_epi(h):
                ob, rob = oT[h % 2], r_oT[h % 2]
                P.mm(PB[5][0:64, :], onesb[64:65, :], hiT[h % 2][64:65, :], True, False, [r_k, r_hl[h % 2]],
                     [r_pb[5]], inc=False)
                P.mm(PB[5][0:64, :], onesb[64:65, :], loT[h % 2][64:65, :], False, True, [r_k, r_hl[h % 2]],
                     [r_pb[5]])
                yb_ = yo[h % 2]
                P.tt("dve", yb_[:], ob[0:64, :], PB[5][0:64, :], ALU.mult, [rob, r_pb[5]], [r_yo[h % 2]])
                P.dma("sp", lambda e, h=h, t=t, yb_=yb_: e.dma_start(out=ya_d[h, :, t * T:(t + 1) * T], in_=yb_[:]),
                      reads=[r_yo[h % 2]], writes=[r_ya[2 * t][h], r_ya[2 * t + 1][h]])

            for i in range(min(LA, len(items))):
                emit_S(i)
            for i in range(len(items)):
                if i + LA < len(items):
                    emit_S(i + LA)
                emit_PV(i)
                for hh in deferred.pop(i, []):
                    emit_epi(hh)
                yield
            assert not deferred

        for _ in prologue(0):
            pass
        for t in range(NT):
            n_main = 8 * (4 * t + 4)
            side = prologue(t + 1) if t + 1 < NT else iter(())
            n_side = 36
            done = 0
            for i, _ in enumerate(attention(t)):
                want = ((i + 1) * n_side) // n_main
                while done < want:
                    next(side, None)
                    done += 1
            for _ in side:
                pass


def phase_B(nc, P, S, dr, h1_d, ya_d, r_ya, r_h1, G):
    T = 256
    NT = S // T
    identb, cst, r_c = G["identb"], G["cst"], G["r_c"]
    idf, tri, trim, onesf = cst[:, 0, :], cst[:, 1, :], cst[:, 2, :], cst[:, 3, :]
    with ExitStack() as st:
        ec = st.enter_context

        def sbt(name, shape, dt):
            return ec(nc.sbuf_tensor(name, shape, dt))

        wB = sbt("wB", [128, 8, WB], BF16)
        wpa = sbt("wpa", [128, 4, D], BF16)
        wpb = sbt("wpb", [128, 4, D], BF16)
        wo = sbt("wo", [128, 8, D], BF16)
        gM = sbt("gMb", [128, D], F32)
        gN = sbt("gN", [128, 512], F32)
        bias8 = sbt("bias8_sb", [128, 8], F32)
        cw = sbt("cw", [128, 8, 4], F32)
        trib = sbt("trib", [128, 128], BF16)
        epsb = sbt("epsb_b", [128, 1], F32)
        oneb = sbt("oneb", [128, 1], F32)
        lncb = sbt("lncb", [128, 1], F32)
        lnhb = sbt("lnhb", [128, 1], F32)
        xb = [sbt("xb%d" % i, [128, 2, D], F32) for i in range(3)]
        yaT = [sbt("yaT%d" % i, [128, 4, T], BF16) for i in range(3)]
        uT = [sbt("uTb%d" % i, [128, 8, T], BF16) for i in range(2)]
        ub2 = [sbt("ub_b%d" % i, [128, D], BF16) for i in range(2)]
        junk = sbt("junk_b", [128, D], BF16)
        st8 = sbt("st8_b", [128, 8], F32)
        gsb = sbt("gsb", [128, 2, 8], F32)
        e1 = sbt("e1", [128, 2, 4], F32)
        nlf = sbt("nlf", [128, 2, 4], F32)
        arg = sbt("arg", [128, 2, 4], F32)
        ksc = sbt("ksc", [128, 2, 4], F32)
        wc = sbt("wc", [128, 2, 4], F32)
        EBt = sbt("EBt", [128, 4, T], F32)
        EAt = sbt("EAt", [128, 4, T], F32)
        xc = sbt("xc", [128, 8, T + 3], F32)
        yc = [sbt("yc%d" % i, [128, T], F32) for i in range(2)]
        slq = [sbt("slq%d" % i, [128, T], BF16) for i in range(2)]
        kT0 = sbt("kT0", [128, 4, T], BF16)
        qT = sbt("qTb", [128, 4, T], BF16)
        kT = sbt("kTb", [128, 4, T], BF16)
        vaug = sbt("vaug", [128, 2, 4, 129], BF16)
        sgo = sbt("sgo", [128, 2, 512], F32)
        kS = sbt("kS", [128, 4, 128], BF16)
        scT = sbt("scT", [128, 4, 128], BF16)
        C32 = sbt("C32", [128, 4, 129], F32)
        Cbf = sbt("Cbf", [128, 4, 129], BF16)
        rr = sbt("rr", [128, 4], F32)
        ssq = sbt("ssq", [128, 4], F32)
        t4 = sbt("t4", [128, 4], F32)
        sc4 = sbt("sc4", [128, 4], F32)
        ybf = sbt("ybf", [128, 512], F32)
        ybb2 = [sbt("ybb%d" % i, [128, 512], BF16) for i in range(2)]
        ybT = sbt("ybT", [128, 4, T], BF16)
        sga = [sbt("sga%d" % i, [128, T], F32) for i in range(2)]
        sgb = [sbt("sgb%d" % i, [128, T], F32) for i in range(2)]
        m1 = [sbt("m1_%d" % i, [128, T], F32) for i in range(2)]
        m2 = [sbt("m2_%d" % i, [128, T], F32) for i in range(2)]
        mT = sbt("mT", [128, 8, T], BF16)
        PB = [ec(nc.psum_tensor("PBb%d" % i, [128, 512], F32)) for i in range(8)]

        R = lambda n: Res(n)
        r_wB = [R("wB%d" % c) for c in range(8)]
        r_wpa, r_wpb, r_wo = R("wpa"), R("wpb"), [R("wo%d" % c) for c in range(8)]
        r_k = R("constsB")
        r_xb = [[R("xb%d_%d" % (i, s)) for s in range(2)] for i in range(3)]
        r_yaT = [R("yaT%d" % i) for i in range(3)]
        r_uT = [[R("uTb%d_%d" % (i, s)) for s in range(2)] for i in range(2)]
        r_ub2, r_junk = [R("ub0"), R("ub1")], R("junk")
        r_st = [R("st%d" % i) for i in range(8)]
        r_gs = [R("gsb%d" % s) for s in range(2)]
        r_e1 = [R("e1%d" % s) for s in range(2)]
        r_nlf = [R("nlf%d" % s) for s in range(2)]
        r_arg = [R("arg%d" % s) for s in range(2)]
        r_ksc = [R("ksc%d" % s) for s in range(2)]
        r_wc = [R("wc%d" % s) for s in range(2)]
        r_EB = [[R("EB%d_%d" % (h, s)) for s in range(2)] for h in range(4)]
        r_EA = [[R("EA%d_%d" % (h, s)) for s in range(2)] for h in range(4)]
        r_xc = [R("xc%d" % m) for m in range(8)]
        r_yc = [R("yc%d" % i) for i in range(2)]
        r_slq = [R("slq%d" % i) for i in range(2)]
        r_kT0 = [R("kT0%d" % h) for h in range(4)]
        r_qT = [R("qT%d" % h) for h in range(4)]
        r_kT = [R("kT%d" % h) for h in range(4)]
        r_va = [R("vaug%d" % s) for s in range(2)]
        r_sgo = [R("sgo%d" % s) for s in range(2)]
        r_kS = [R("kS%d" % h) for h in range(4)]
        r_scT = [R("scT%d" % h) for h in range(4)]
        r_C32 = [R("C32%d" % h) for h in range(4)]
        r_Cbf = [R("Cbf%d" % h) for h in range(4)]
        r_rr, r_ssq, r_t4, r_sc4 = R("rr"), R("ssq"), R("t4"), R("sc4")
        r_ybf, r_ybb2 = R("ybf"), [R("ybb0"), R("ybb1")]
        r_ybT = [R("ybT%d" % s) for s in range(2)]
        r_sga = [R("sga%d" % i) for i in range(2)]
        r_sgb = [R("sgb%d" % i) for i in range(2)]
        r_m1 = [R("m1%d" % i) for i in range(2)]
        r_m2 = [R("m2%d" % i) for i in range(2)]
        r_mT = [R("mT%d" % f) for f in range(8)]
        r_pb = [R("pb%d" % i) for i in range(8)]
        PBb = [PB[i].bitcast(BF16) for i in range(8)]

        wv = dr["w_in"].rearrange("(c p) f -> p c f", p=128)
        r_wB2 = [R("wB2_%d" % c) for c in range(8)]
        for c in range(8):
            P.dma("pool", lambda e, c=c: e.dma_start(out=wB[:, c, 0:GAo], in_=wv[:, c, B_OFF:B_OFF + GAo]),
                  writes=[r_wB[c]])
        for c in range(8):
            P.dma("pool", lambda e, c=c: e.dma_start(out=wB[:, c, GAo:WB], in_=wv[:, c, B_OFF + GAo:IN_W]),
                  writes=[r_wB2[c]])
        P.dma("pool", lambda e: e.dma_start(out=wpa[:], in_=dr["w_proj_a"].rearrange("(c p) f -> p c f", p=128)),
              writes=[r_wpa])
        P.dma("pool", lambda e: e.dma_start(out=wpb[:], in_=dr["w_proj_b"].rearrange("(c p) f -> p c f", p=128)),
              writes=[r_wpb])
        wov = dr["w_out"].rearrange("(c p) f -> p c f", p=128)
        for c in range(8):
            P.dma("pool", lambda e, c=c: e.dma_start(out=wo[:, c, :], in_=wov[:, c, :]), writes=[r_wo[c]])
        P.dma("pool", lambda e: e.dma_start(out=trib[:], in_=dr["consts"][:, 1, :]), writes=[r_k])
        P.dma("sp", lambda e: e.dma_start(out=gM[:], in_=dr["norm_mix_g"].to_broadcast([128, D])), writes=[r_k])
        P.dma("sp", lambda e: e.dma_start(out=gN[:], in_=dr["mlstm_norm_g"].to_broadcast([128, 512])), writes=[r_k])
        P.dma("sp", lambda e: e.dma_start(out=bias8[:], in_=dr["bias8"].to_broadcast([128, 8])), writes=[r_k])
        P.dma("sp", lambda e: e.dma_start(out=cw[:], in_=dr["conv_wT"].rearrange("(m p) j -> p m j", p=128)),
              writes=[r_k])
        P.ts("dve", gN[:], gN[:], 0.5, ALU.mult, [r_k], [r_k])
        P.memset("dve", epsb[:], EPS, [r_k])
        P.memset("dve", oneb[:], 1.0, [r_k])
        P.memset("dve", lncb[:], LNC + float(np.log(0.5)), [r_k])
        P.memset("dve", lnhb[:], float(np.log(0.5)), [r_k])
        P.memset("dve", C32[:], 0.0, r_C32)
        P.memset("dve", Cbf[:], 0.0, r_Cbf)
        P.memset("dve", vaug[:], 1.0, r_va)
        P.memset("dve", xc[:], 0.0, r_xc)

        xv = dr["x"].rearrange("(t s p) d -> t p s d", s=2, p=128)
        hv = h1_d.rearrange("(t s p) d -> t p s d", s=2, p=128)
        ya_v = ya_d.rearrange("h d s -> (h d) s").rearrange("(c p) s -> p c s", p=128)

        ybT2 = sbt("ybT2", [128, 4, T], BF16)
        ybTs = [ybT, ybT2]
        r_ybTs = [r_ybT, [R("ybTb%d" % s) for s in range(2)]]

        def stage_A(t):
            b = t % 2
            xt = xb[t % 3]
            uTt = uT[b]
            for s in range(2):
                load_norm_tile(nc, P, xt, r_xb[t % 3][s], s, gM, r_k, st8, r_st, s, epsb, r_k, junk, r_junk,
                               ub2[s], r_ub2[s])
            yield
            for s in range(2):
                sub = slice(s * 128, (s + 1) * 128)
                ub, r_ub = ub2[s], r_ub2[s]
                for g4 in range(2):
                    for j in range(4):
                        c = g4 * 4 + j
                        P.tr(PBb[2 + g4][:, j * 128:(j + 1) * 128], ub[:, c * 128:(c + 1) * 128],
                             identb[:], [r_ub, r_c], [r_pb[2 + g4]], inc=(j == 3))
                for g4 in range(2):
                    P.cp("act", uTt[:, g4 * 4:(g4 + 1) * 4, sub],
                         PBb[2 + g4][:, 0:512].rearrange("p (j t) -> p j t", j=4),
                         [r_pb[2 + g4]], [r_uT[b][s]])
                yield
                pg = PB[4][:, 0:8]
                for c in range(8):
                    P.mm(pg, uTt[:, c, sub], wB[:, c, IBo:IBo + 8], c == 0, c == 7,
                         [r_uT[b][s], r_wB[c]], [r_pb[4]])
                P.tt("dve", gsb[:, s, :], pg, bias8[:], ALU.add, [r_pb[4], r_k], [r_gs[s]])
                P.act(e1[:, s, :], gsb[:, s, 4:8], AF.Exp, [r_gs[s]], [r_e1[s]], scale=-1.0)
                P.act(nlf[:, s, :], e1[:, s, :], AF.Ln, [r_e1[s], r_k], [r_nlf[s]], bias=oneb[:])
                pv = PB[2][:]
                for c in range(8):
                    P.mm(pv, uTt[:, c, sub], wB[:, c, VBo:VBo + 512], c == 0, c == 7,
                         [r_uT[b][s], r_wB[c]], [r_pb[2]])
                P.cp("act", vaug[:, s, :, 0:128], pv.rearrange("p (h e) -> p h e", h=4), [r_pb[2]], [r_va[s]])
                yield
                po = PB[3][:]
                for c in range(8):
                    P.mm(po, uTt[:, c, sub], wB[:, c, OBo:OBo + 512], c == 0, c == 7,
                         [r_uT[b][s], r_wB[c]], [r_pb[3]])
                P.act(sgo[:, s, :], po, AF.Tanh, [r_pb[3]], [r_sgo[s]], scale=0.5)
                P.mm(PB[4][:, 8:12], trim, nlf[:, s, :], True, True, [r_c, r_nlf[s]], [r_pb[4]], inc=False)
                P.mm(PB[4][:, 12:16], onesf, nlf[:, s, :], True, True, [r_c, r_nlf[s]], [r_pb[4]])
                P.tt("dve", arg[:, s, :], PB[4][:, 8:12], gsb[:, s, 0:4], ALU.add, [r_pb[4], r_gs[s]], [r_arg[s]])
                P.act(ksc[:, s, :], arg[:, s, :], AF.Exp, [r_arg[s], r_k], [r_ksc[s]], bias=lncb[:])
                P.act(wc[:, s, :], PB[4][:, 12:16], AF.Exp, [r_pb[4]], [r_wc[s]], scale=-1.0)
                yield
                for h in range(4):
                    i0 = (2 * h) % 3
                    i1 = (2 * h + 1) % 3
                    pe_b = PB[5 + i0][:, 0:128]
                    pe_a = PB[5 + i1][:, 0:128]
                    nb_l = nlf[:, s, h:h + 1].to_broadcast([128, 128])
                    ip_l = gsb[:, s, h:h + 1].to_broadcast([128, 128])
                    P.mm(pe_b, nb_l, tri, True, True, [r_nlf[s], r_c], [r_pb[5 + i0]])
                    P.mm(pe_a, ip_l, idf, True, False, [r_gs[s], r_c], [r_pb[5 + i1]], inc=False)
                    P.mm(pe_a, nb_l, tri, False, True, [r_nlf[s], r_c], [r_pb[5 + i1]])
                    P.act(EBt[:, h, sub], pe_b, AF.Exp, [r_pb[5 + i0], r_k], [r_EB[h][s]], scale=-1.0, bias=lnhb[:])
                    P.act(EAt[:, h, sub], pe_a, AF.Exp, [r_pb[5 + i1], r_k], [r_EA[h][s]], bias=lncb[:])
                    if h % 2 == 1:
                        yield
            for m in range(8):
                hh = m % 4
                pq = PB[2 + m % 2][:, 0:256]
                for c in range(8):
                    P.mm(pq, wB[:, c, QKB + m * 128:QKB + (m + 1) * 128], uTt[:, c, :], c == 0, c == 7,
                         [r_wB[c]] + r_uT[b], [r_pb[2 + m % 2]])
                P.cp("act", xc[:, m, 3:3 + T], pq, [r_pb[2 + m % 2]], [r_xc[m]])
                y = yc[m % 2]
                ry = r_yc[m % 2]
                P.ts("dve", y[:], xc[:, m, 0:T], cw[:, m, 0:1], ALU.mult, [r_xc[m], r_k], [ry])
                for j in range(1, 4):
                    P.stt("dve", y[:], xc[:, m, j:j + T], cw[:, m, j:j + 1], y[:], ALU.mult, ALU.add,
                          [r_xc[m], r_k, ry], [ry])
                P.cp("dve", xc[:, m, 0:3], xc[:, m, T:T + 3], [r_xc[m]], [r_xc[m]])
                if m < 4:
                    sl = slq[m % 2]
                    P.act(sl[:], y[:], AF.Tanh, [ry], [r_slq[m % 2]], scale=0.5)
                    P.stt("dve", y[:], sl[:], 1.0, y[:], ALU.add, ALU.mult, [r_slq[m % 2], ry], [ry])
                    P.tt("dve", qT[:, hh, :], y[:], EBt[:, hh, :], ALU.mult, [ry] + r_EB[hh], [r_qT[hh]])
                else:
                    sl = slq[m % 2]
                    P.act(sl[:], y[:], AF.Tanh, [ry], [r_slq[m % 2]], scale=0.5)
                    P.stt("dve", kT0[:, hh, :], sl[:], 1.0, y[:], ALU.add, ALU.mult, [r_slq[m % 2], ry], [r_kT0[hh]])
                    P.tt("dve", kT[:, hh, :], kT0[:, hh, :], EAt[:, hh, :], ALU.mult,
                         [r_kT0[hh]] + r_EA[hh], [r_kT[hh]])
                yield
            for s in range(2):
                sub = slice(s * 128, (s + 1) * 128)
                pnum = [PB[6][:, 0:129], PB[7][:, 0:129], PB[6][:, 256:385], PB[7][:, 256:385]]
                r_pn = [r_pb[6], r_pb[7], r_pb[6], r_pb[7]]
                pdc = [PB[2][:, 256:385], PB[3][:, 256:385], PB[2][:, 256:385], PB[3][:, 256:385]]
                r_pd = [r_pb[2], r_pb[3], r_pb[2], r_pb[3]]
                for h in range(4):
                    ptk = PBb[2 + h % 2][:, 0:128]
                    P.tr(ptk, kT0[:, h, sub], identb[:], [r_kT0[h], r_c], [r_pb[2 + h % 2]])
                    pS = PB[4 + h % 2][:, 0:128]
                    P.mm(pS, kT[:, h, sub], qT[:, h, sub], True, True, [r_kT[h], r_qT[h]], [r_pb[4 + h % 2]])
                    P.ts("dve", kS[:, h, :], ptk, ksc[:, s, h:h + 1], ALU.mult, [r_pb[2 + h % 2], r_ksc[s]],
                         [r_kS[h]])
                    P.tt("dve", scT[:, h, :], pS, trib[:], ALU.mult, [r_pb[4 + h % 2], r_k], [r_scT[h]])
                    yield
                    P.mm(pnum[h], qT[:, h, sub], Cbf[:, h, :], True, False, [r_qT[h], r_Cbf[h]], [r_pn[h]],
                         inc=False)
                    P.mm(pnum[h], scT[:, h, :], vaug[:, s, h, :], False, True, [r_scT[h], r_va[s]], [r_pn[h]])
                    P.mm(pdc[h], kS[:, h, :], vaug[:, s, h, :], True, True, [r_kS[h], r_va[s]], [r_pd[h]])
                    P.stt("dve", C32[:, h, :], C32[:, h, :], wc[:, s, h:h + 1], pdc[h], ALU.mult, ALU.add,
                          [r_C32[h], r_wc[s], r_pd[h]], [r_C32[h]])
                    P.cp("act", Cbf[:, h, :], C32[:, h, :], [r_C32[h]], [r_Cbf[h]])
                    yield
                for h in range(4):
                    P.act(junk[:, 0:128], pnum[h][:, 0:128], AF.Square, [r_pn[h]], [r_junk, r_ssq],
                          accum=ssq[:, h:h + 1])
                for h in range(4):
                    P.cp("dve", t4[:, h:h + 1], pnum[h][:, 128:129], [r_pn[h]], [r_t4])
                P.stt("dve", rr[:], t4[:], -1.0, t4[:], ALU.mult, ALU.max, [r_t4], [r_rr])
                P.ts("dve", rr[:], rr[:], 1.0, ALU.max, [r_rr], [r_rr])
                P.recip(rr[:], rr[:], [r_rr], [r_rr])
                P.tt("dve", t4[:], rr[:], rr[:], ALU.mult, [r_rr], [r_t4])
                P.tt("dve", t4[:], t4[:], ssq[:], ALU.mult, [r_t4, r_ssq], [r_t4])
                P.act(t4[:], t4[:], AF.Sqrt, [r_t4, r_k], [r_t4], bias=epsb[:], scale=1.0 / 128)
                P.recip(t4[:], t4[:], [r_t4], [r_t4])
                P.tt("dve", sc4[:], t4[:], rr[:], ALU.mult, [r_t4, r_rr], [r_sc4])
                yield
                for h in range(4):
                    P.stt("dve", ybf[:, h * 128:(h + 1) * 128], pnum[h][:, 0:128], sc4[:, h:h + 1],
                          gN[:, h * 128:(h + 1) * 128], ALU.mult, ALU.mult, [r_pn[h], r_sc4, r_k], [r_ybf])
                P.stt("dve", ybb2[s][:], sgo[:, s, :], 1.0, ybf[:], ALU.add, ALU.mult, [r_ybf, r_sgo[s]], [r_ybb2[s]])
                yield
                if s == 1:
                    yb_transposes(t, 0)
                    yield

        def yb_transposes(t, s):
            b = t % 2
            sub = slice(s * 128, (s + 1) * 128)
            for h in range(4):
                P.tr(PBb[3][:, h * 128:(h + 1) * 128], ybb2[s][:, h * 128:(h + 1) * 128], identb[:],
                     [r_ybb2[s], r_c], [r_pb[3]], inc=(h == 3))
            P.cp("act", ybTs[b][:, :, sub], PBb[3][:, 0:512].rearrange("p (j t) -> p j t", j=4),
                 [r_pb[3]], [r_ybTs[b][s]])

        def stage_M(t):
            b = t % 2
            xt = xb[t % 3]
            uTt = uT[b]
            for f in range(8):
                fs = slice(f * 128, (f + 1) * 128)
                pga = PB[0][:, 0:256]
                pgb = PB[0][:, 256:512]
                ppa = PB[1][:, 0:256]
                ppb = PB[1][:, 256:512]
                for c in range(8):
                    P.mm(pga, wB[:, c, GAo + f * 128:GAo + (f + 1) * 128], uTt[:, c, :], c == 0, c == 7,
                         [r_wB2[c]] + r_uT[b], [r_pb[0]])
                for c in range(8):
                    P.mm(pgb, wB[:, c, GBo + f * 128:GBo + (f + 1) * 128], uTt[:, c, :], c == 0, c == 7,
                         [r_wB2[c]] + r_uT[b], [r_pb[0]])
                k2 = f % 2
                P.act(sga[k2][:], pga, AF.Tanh, [r_pb[0]], [r_sga[k2]], scale=0.5)
                P.act(sgb[k2][:], pgb, AF.Tanh, [r_pb[0]], [r_sgb[k2]], scale=0.5)
                for pr in range(4):
                    P.mm(ppa, wpa[:, pr, fs], yaT[t % 3][:, pr, :], pr == 0, pr == 3, [r_wpa, r_yaT[t % 3]], [r_pb[1]])
                if f == 0:
                    yb_transposes(t, 1)
                for pr in range(4):
                    P.mm(ppb, wpb[:, pr, fs], ybTs[b][:, pr, :], pr == 0, pr == 3, [r_wpb] + r_ybTs[b], [r_pb[1]])
                P.stt("dve", m1[k2][:], sga[k2][:], 1.0, ppa, ALU.add, ALU.mult, [r_sga[k2], r_pb[1]], [r_m1[k2]])
                P.stt("dve", m2[k2][:], sgb[k2][:], 1.0, ppb, ALU.add, ALU.mult, [r_sgb[k2], r_pb[1]], [r_m2[k2]])
                P.tt("dve", mT[:, f, :], m1[k2][:], m2[k2][:], ALU.add, [r_m1[k2], r_m2[k2]], [r_mT[f]])
                yield
            for s in range(2):
                sub = slice(s * 128, (s + 1) * 128)
                for n in range(2):
                    po2 = PB[n][:]
                    rpo = [r_pb[n]]
                    for c in range(8):
                        P.mm(po2, mT[:, c, sub], wo[:, c, n * 512:(n + 1) * 512], c == 0, c == 7,
                             [r_mT[c], r_wo[c]], rpo)
                    P.stt("dve", xt[:, s, n * 512:(n + 1) * 512], po2, 0.5, xt[:, s, n * 512:(n + 1) * 512],
                          ALU.mult, ALU.add, rpo + [r_xb[t % 3][s]], [r_xb[t % 3][s]])
                    yield
            P.dma("sp", lambda e, t=t, xt=xt: e.dma_start(out=hv[t], in_=xt[:]), reads=r_xb[t % 3], writes=[r_h1[t]])

        def prefetch(tt_):
            if tt_ < NT:
                P.dma("sp", lambda e: e.dma_start(out=xb[tt_ % 3][:], in_=xv[tt_]), writes=r_xb[tt_ % 3])
                P.dma("sp", lambda e: e.dma_start(out=yaT[tt_ % 3][:], in_=ya_v[:, :, tt_ * T:(tt_ + 1) * T]),
                      reads=r_ya[tt_], writes=[r_yaT[tt_ % 3]])

        prefetch(0)
        prefetch(1)
        for _ in stage_A(0):
            pass
        for t in range(NT):
            if t + 1 < NT:
                prefetch(t + 2)
                side = stage_M(t)
                n_side = 12
                done = 0
                for i, _ in enumerate(stage_A(t + 1)):
                    want = ((i + 1) * n_side) // 46
                    while done < want:
                        next(side, None)
                        done += 1
                for _ in side:
                    pass
            else:
                for _ in stage_M(t):
                    pass


def phase_C(nc, P, S, h_d, r_h1, out_d, dr, G):
    T = 256
    NT = S // T
    NF = DFF // 128
    identb, r_ident = G["identb"], G["r_c"]
    g_ffn_d, g_fin_d, wgu_d, wdn_d = dr["norm_ffn_g"], dr["norm_final_g"], dr["w_gate_up"], dr["w_down"]
    with ExitStack() as st:
        ec = st.enter_context
        wgu = ec(nc.sbuf_tensor("wgu", [128, 8, 2 * DFF], BF16))
        wdn = ec(nc.sbuf_tensor("wdn", [128, NF, D], BF16))
        gF = ec(nc.sbuf_tensor("gF", [128, D], F32))
        gL = ec(nc.sbuf_tensor("gL", [128, D], F32))
        hb = [ec(nc.sbuf_tensor("hb%d" % i, [128, 2, D], F32)) for i in range(2)]
        ub = ec(nc.sbuf_tensor("ub", [128, D], BF16))
        uT = [ec(nc.sbuf_tensor("uT%d" % i, [128, 8, T], BF16)) for i in range(2)]
        aT = ec(nc.sbuf_tensor("aT", [128, NF, T], BF16))
        sg = [ec(nc.sbuf_tensor("sg%d" % i, [128, T], BF16)) for i in range(2)]
        junk = ec(nc.sbuf_tensor("junk", [128, D], BF16))
        st8 = ec(nc.sbuf_tensor("st8", [128, 8], F32))
        epsb = ec(nc.sbuf_tensor("epsb", [128, 1], F32))
        r_eps = Res("eps")
        P.op("dve", lambda e: e.memset(epsb[:], EPS), writes=[r_eps])
        ptp = [ec(nc.psum_tensor("ptp%d" % i, [128, 4, 128], BF16)) for i in range(2)]
        pgu = [ec(nc.psum_tensor("pgu%d" % i, [128, 512], F32)) for i in range(4)]
        pdn = [ec(nc.psum_tensor("pdn%d" % i, [128, 512], F32)) for i in range(2)]

        r_wdn = [Res("wdn%d" % c) for c in range(NF)]
        r_gF, r_gL = Res("gF"), Res("gL")
        r_hb = [[Res("hb%d_%d" % (i, s)) for s in range(2)] for i in range(2)]
        r_ub = Res("ub")
        r_uT = [Res("uT%d" % i) for i in range(2)]
        r_aT = [Res("aT%d" % j) for j in range(NF)]
        r_sg = [Res("sg%d" % i) for i in range(2)]
        r_junk = Res("junk")
        r_st = [Res("st%d" % i) for i in range(8)]
        r_ptp = [Res("ptp%d" % i) for i in range(2)]
        r_pgu = [Res("pgu%d" % i) for i in range(4)]
        r_pdn = [Res("pdn%d" % i) for i in range(2)]

        P.dma("pool", lambda e: e.dma_start(out=gF[:], in_=g_ffn_d.to_broadcast([128, D])), writes=[r_gF])
        P.dma("pool", lambda e: e.dma_start(out=gL[:], in_=g_fin_d.to_broadcast([128, D])), writes=[r_gL])
        wgu_v = wgu_d.rearrange("(c p) f -> p c f", p=128)
        wdn_v = wdn_d.rearrange("(c p) f -> p c f", p=128)
        NBLK = 11
        r_wgu = [Res("wgu%d" % i) for i in range(NBLK)]
        for i in range(NBLK):
            for hlf in range(2):
                lo = hlf * DFF + i * 256
                P.dma("pool", lambda e, lo=lo: e.dma_start(out=wgu[:, :, lo:lo + 256], in_=wgu_v[:, :, lo:lo + 256]),
                      writes=[r_wgu[i]])
        for c in range(NF):
            P.dma("pool", lambda e, c=c: e.dma_start(out=wdn[:, c, :], in_=wdn_v[:, c, :]),
                  writes=[r_wdn[c]])
        h_v = h_d.rearrange("(t s p) d -> t p s d", s=2, p=128)
        o_v = out_d.rearrange("(t s p) d -> t p s d", s=2, p=128)
        final_pts = []

        def rstd_from(h_ap, col, r_h):
            P.op("act", lambda e: e.activation(out=junk[:], in_=h_ap, func=AF.Square,
                                               accum_out=st8[:, col:col + 1]),
                 reads=[r_h], writes=[r_junk, r_st[col]])
            P.op("act", lambda e: e.activation(out=st8[:, col:col + 1], in_=st8[:, col:col + 1], func=AF.Sqrt,
                                               scale=1.0 / D, bias=epsb[:]),
                 reads=[r_st[col], r_eps], writes=[r_st[col]])
            P.op("dve", lambda e: e.reciprocal(out=st8[:, col:col + 1], in_=st8[:, col:col + 1]),
                 reads=[r_st[col]], writes=[r_st[col]])

        def prologue(t):
            b = t % 2
            hbt = hb[b]
            P.dma("sp", lambda e, t=t, hbt=hbt: e.dma_start(out=hbt[:], in_=h_v[t]),
                  reads=[r_h1[t]], writes=[r_hb[b][0], r_hb[b][1]])
            for s in range(2):
                rstd_from(hbt[:, s, :], s, r_hb[b][s])
                P.stt("dve", ub[:], hbt[:, s, :], st8[:, s:s + 1], gF[:], ALU.mult, ALU.mult,
                      [r_hb[b][s], r_st[s], r_gF], [r_ub])
                for g4 in range(2):
                    pt = ptp[g4]
                    for j in range(4):
                        c = g4 * 4 + j
                        P.tr(pt[:, j, :], ub[:, c * 128:(c + 1) * 128], identb[:], [r_ub, r_ident], [r_ptp[g4]],
                             inc=(j == 3))
                    P.cp("act", uT[b][:, g4 * 4:(g4 + 1) * 4, s * 128:(s + 1) * 128], pt[:], [r_ptp[g4]], [r_uT[b]])

        def gateup(t):
            b = t % 2
            for j in range(NF):
                pg = pgu[(2 * j) % 4]
                pu = pgu[(2 * j + 1) % 4]
                rg = r_pgu[(2 * j) % 4]
                ru = r_pgu[(2 * j + 1) % 4]
                for c in range(8):
                    P.mm(pg[:, 0:T], wgu[:, c, j * 128:(j + 1) * 128], uT[b][:, c, :], c == 0, c == 7,
                         [r_wgu[j // 2], r_uT[b]], [rg])
                for c in range(8):
                    P.mm(pu[:, 0:T], wgu[:, c, DFF + j * 128:DFF + (j + 1) * 128], uT[b][:, c, :], c == 0, c == 7,
                         [r_wgu[j // 2], r_uT[b]], [ru])
                sgt = sg[j % 2]
                P.act(sgt[:], pg[:, 0:T], AF.Silu, [rg], [r_sg[j % 2]])
                P.tt("dve", aT[:, j, :], sgt[:], pu[:, 0:T], ALU.mult, [ru, r_sg[j % 2]], [r_aT[j]])

        def down(t):
            b = t % 2
            hbt = hb[b]
            for s in range(2):
                for n in range(2):
                    pd = pdn[n]
                    for j in range(NF):
                        P.mm(pd[:], aT[:, j, s * 128:(s + 1) * 128], wdn[:, j, n * 512:(n + 1) * 512],
                             j == 0, j == NF - 1, [r_aT[j], r_wdn[j]], [r_pdn[n]])
                    P.tt("dve", hbt[:, s, n * 512:(n + 1) * 512], hbt[:, s, n * 512:(n + 1) * 512], pd[:], ALU.add,
                         [r_pdn[n], r_hb[b][s]], [r_hb[b][s]])
                rstd_from(hbt[:, s, :], 2 + s, r_hb[b][s])
                P.stt("dve", hbt[:, s, :], hbt[:, s, :], st8[:, 2 + s:3 + s], gL[:], ALU.mult, ALU.mult,
                      [r_hb[b][s], r_st[2 + s], r_gL], [r_hb[b][s]])
            final_pts.append(P.dma("sp", lambda e, t=t, hbt=hbt: e.dma_start(out=o_v[t], in_=hbt[:]),
                                   reads=[r_hb[b][0], r_hb[b][1]]))

        prologue(0)
        for t in range(NT):
            gateup(t)
            if t + 1 < NT:
                prologue(t + 1)
            down(t)
        return final_pts


def _host_consts(S):
    idx = np.arange(128)
    ident = np.eye(128, dtype=np.float32)
    tri = (idx[:, None] <= idx[None, :]).astype(np.float32)
    trim = -(idx[:, None] > idx[None, :]).astype(np.float32)
    ones = np.ones((128, 128), np.float32)
    consts = np.ascontiguousarray(np.stack([ident, tri, trim, ones], axis=1))
    inv = (np.float32(10000.0) ** (-np.arange(32, dtype=np.float32) / np.float32(32))).astype(np.float32)
    ang = np.arange(S, dtype=np.float32)[:, None] * inv[None, :]
    cos, sin = np.cos(ang).astype(np.float32), np.sin(ang).astype(np.float32)
    rope = np.ascontiguousarray(np.stack([np.concatenate([cos, cos], 1), np.concatenate([-sin, sin], 1)], axis=1))
    k = np.arange(128)[:, None, None]
    a = np.arange(4)[None, :, None]
    q = np.arange(512)[None, None, :]
    jq, ja = q // 256, a // 2
    cm = np.where(jq > ja, 1.0, np.where(jq < ja, 0.0, (q >= 128 * a + k).astype(np.float32)))
    cmask = np.ascontiguousarray(np.broadcast_to(cm, (128, 4, 512)).astype(ml_dtypes.bfloat16))
    blk = np.arange(16)[:, None]
    j = np.arange(16)[None, :]
    pastb = np.where(j < blk, 0.0, np.where(j == blk, 1e30, -1e30)).astype(np.float32)
    oneh = np.where(j == blk, -1024.0, 0.0).astype(np.float32)
    blkc = np.ascontiguousarray(np.broadcast_to(np.concatenate([pastb, oneh], 1)[None], (128, 16, 32)).astype(np.float32))
    return dict(consts=consts, rope=rope, cmask=cmask, blkc=blkc)


def make_in_maps(inp, ncores):
    f = lambda k: np.ascontiguousarray(np.asarray(inp[k], dtype=np.float32))
    x = np.asarray(inp["x"], dtype=np.float32)
    S = x.shape[1]
    common = {
        "norm_mix_g": f("norm_mix_g").reshape(1, D),
        "w_in": f("w_in")[0],
        "conv_wT": np.ascontiguousarray(f("conv_w")[0].T),
        "bias8": np.ascontiguousarray(np.concatenate([f("b_igate")[0], f("b_fgate")[0]]).reshape(1, 8)),
        "mlstm_norm_g": f("mlstm_norm_g").reshape(1, 512),
        "w_proj_a": f("w_proj_a")[0],
        "w_proj_b": f("w_proj_b")[0],
        "w_out": f("w_out")[0],
        "norm_ffn_g": f("norm_ffn_g").reshape(1, D),
        "norm_final_g": f("norm_final_g").reshape(1, D),
        "w_gate_up": f("w_gate_up")[0],
        "w_down": f("w_down")[0],
    }
    common.update(_host_consts(S))
    return [dict(common, x=np.ascontiguousarray(x[b])) for b in range(ncores)]


def kernel(x, norm_mix_g, w_in, conv_w, b_igate, b_fgate, mlstm_norm_g, w_proj_a, w_proj_b,
           w_out, norm_ffn_g, w_gate_up, w_down, norm_final_g):
    inp = dict(x=x, norm_mix_g=norm_mix_g, w_in=w_in, conv_w=conv_w, b_igate=b_igate, b_fgate=b_fgate,
               mlstm_norm_g=mlstm_norm_g, w_proj_a=w_proj_a, w_proj_b=w_proj_b, w_out=w_out,
               norm_ffn_g=norm_ffn_g, w_gate_up=w_gate_up, w_down=w_down, norm_final_g=norm_final_g)
    B, S, _ = np.asarray(x).shape
    nc = build_nc(S)
    in_maps = make_in_maps(inp, B)
    res = run_bass_kernel_spmd(nc, in_maps, core_ids=list(range(B)))
    return np.stack([np.asarray(r["out"]) for r in res.results], axis=0).astype(np.float32)
```

```python
import numpy as np
import ml_dtypes
from contextlib import ExitStack

import concourse.bass as bass
import concourse.mybir as mybir
from concourse.bass_utils import run_bass_kernel_spmd

F32 = mybir.dt.float32
BF16 = mybir.dt.bfloat16
ALU = mybir.AluOpType
AF = mybir.ActivationFunctionType
AX = mybir.AxisListType

D = 1024
DFF = 2816
EPS = 1e-6
NCORES = 8


class Res:
    __slots__ = ("name", "w", "r")

    def __init__(self, name):
        self.name = name
        self.w = None
        self.r = {}


class Prog:
    COMPUTE = ("pe", "dve", "act", "pool")

    def __init__(self, nc, stack, n_sp=12, n_pool=6):
        self.nc = nc
        self.sem = {}
        self.stack = stack
        self.epoch = 0
        self.ekey = {e: e for e in self.COMPUTE}
        for e in self.COMPUTE:
            self.sem[e] = stack.enter_context(nc.semaphore("s_" + e))
        self.dma_pool = {"sp": [], "pool": []}
        for i in range(n_sp):
            k = "dsp%d" % i
            self.sem[k] = stack.enter_context(nc.semaphore(k))
            self.dma_pool["sp"].append(k)
        for i in range(n_pool):
            k = "dpl%d" % i
            self.sem[k] = stack.enter_context(nc.semaphore(k))
            self.dma_pool["pool"].append(k)
        self.dma_rr = {"sp": 0, "pool": 0}
        self.cnt = {k: 0 for k in self.sem}
        self.ops = {e: [] for e in ("pe", "dve", "act", "pool", "sp")}
        self.know = {e: {} for e in self.ops}
        self.clock = {}
        self.pe_pending = False

    def _needs(self, reads, writes):
        need = {}
        for r in reads:
            if r.w is not None:
                k, v = r.w
                if need.get(k, 0) < v:
                    need[k] = v
        for w in writes:
            if w.w is not None:
                k, v = w.w
                if need.get(k, 0) < v:
                    need[k] = v
            for k, v in w.r.items():
                if need.get(k, 0) < v:
                    need[k] = v
        return need

    def _waits(self, eng, need):
        know = self.know[eng]
        waits = []
        for k, v in need.items():
            if eng == "pe" and k == self.ekey["pe"]:
                continue
            if know.get(k, 0) >= v:
                continue
            waits.append((k, v))
        for k, v in waits:
            ck = self.clock.get((k, v))
            if ck:
                for kk, vv in ck.items():
                    if know.get(kk, 0) < vv:
                        know[kk] = vv
            if know.get(k, 0) < v:
                know[k] = v
        return waits

    def _record(self, point, reads, writes):
        k, v = point
        for r in reads:
            if r.r.get(k, 0) < v:
                r.r[k] = v
        for w in writes:
            w.w = point
            w.r = {}

    def op(self, eng, fn, reads=(), writes=(), inc=True):
        need = self._needs(reads, writes)
        waits = self._waits(eng, need)
        sk = self.ekey[eng]
        if eng == "pe" and not inc:
            point = (sk, self.cnt[sk] + 1)
            self.pe_pending = True
            self.ops[eng].append((waits, fn, None, 0))
        else:
            self.cnt[sk] += 1
            point = (sk, self.cnt[sk])
            if eng == "pe":
                self.pe_pending = False
            self.ops[eng].append((waits, fn, sk, 1))
            ck = dict(self.know[eng])
            ck[sk] = self.cnt[sk]
            self.clock[point] = ck
        self._record(point, reads, writes)
        return point

    def new_epoch(self):
        self.epoch += 1
        for e in self.COMPUTE:
            k = "%s_%d" % (e, self.epoch)
            self.sem[k] = self.stack.enter_context(self.nc.semaphore("s_" + k))
            self.cnt[k] = 0
            self.ekey[e] = k

    def dma(self, q, fn, reads=(), writes=()):
        pool = self.dma_pool[q]
        sk = pool[self.dma_rr[q] % len(pool)]
        self.dma_rr[q] += 1
        need = self._needs(reads, writes)
        if self.cnt[sk] > 0 and need.get(sk, 0) < self.cnt[sk]:
            need[sk] = self.cnt[sk]
        waits = self._waits(q, need)
        self.cnt[sk] += 16
        point = (sk, self.cnt[sk])
        self.ops[q].append((waits, fn, sk, 16))
        ck = dict(self.know[q])
        ck[sk] = self.cnt[sk]
        self.clock[point] = ck
        self._record(point, reads, writes)
        return point

    def mm(self, out, lhsT, rhs, start, stop, reads, writes, inc=None):
        if inc is None:
            inc = stop
        return self.op("pe", lambda e: e.matmul(out, lhsT=lhsT, rhs=rhs, start=start, stop=stop),
                       reads, writes, inc)

    def tr(self, out, in_, ident, reads, writes, inc=True):
        return self.op("pe", lambda e: e.transpose(out=out, in_=in_, identity=ident), reads, writes, inc)

    def act(self, out, in_, func, reads, writes, bias=None, scale=None, accum=None):
        kw = {}
        if bias is not None:
            kw["bias"] = bias
        if scale is not None:
            kw["scale"] = scale
        if accum is not None:
            kw["accum_out"] = accum
        return self.op("act", lambda e: e.activation(out=out, in_=in_, func=func, **kw), reads, writes)

    def tt(self, eng, out, in0, in1, op, reads, writes):
        return self.op(eng, lambda e: e.tensor_tensor(out=out, in0=in0, in1=in1, op=op), reads, writes)

    def ts(self, eng, out, in0, s1, op0, reads, writes, s2=None, op1=None):
        if op1 is None:
            return self.op(eng, lambda e: e.tensor_scalar(out=out, in0=in0, scalar1=s1, scalar2=None, op0=op0),
                           reads, writes)
        return self.op(eng, lambda e: e.tensor_scalar(out=out, in0=in0, scalar1=s1, scalar2=s2, op0=op0, op1=op1),
                       reads, writes)

    def stt(self, eng, out, in0, scalar, in1, op0, op1, reads, writes):
        return self.op(eng, lambda e: e.scalar_tensor_tensor(out=out, in0=in0, scalar=scalar, in1=in1,
                                                             op0=op0, op1=op1), reads, writes)

    def cp(self, eng, out, in_, reads, writes):
        if eng == "act":
            return self.op("act", lambda e: e.copy(out=out, in_=in_), reads, writes)
        return self.op(eng, lambda e: e.tensor_copy(out=out, in_=in_), reads, writes)

    def recip(self, out, in_, reads, writes):
        return self.op("dve", lambda e: e.reciprocal(out=out, in_=in_), reads, writes)

    def memset(self, eng, ap, val, writes):
        return self.op(eng, lambda e: e.memset(ap, val), (), writes)

    def barrier(self):
        assert not self.pe_pending
        need = {k: v for k, v in self.cnt.items() if v > 0}
        for e in self.ops:
            waits = self._waits(e, dict(need))
            if waits:
                self.ops[e].append((waits, None, None, 0))

    def finish(self, final_points):
        need = {}
        for k, v in final_points:
            if need.get(k, 0) < v:
                need[k] = v
        waits = [(k, v) for k, v in need.items()]
        self.ops["sp"].append((waits, None, None, 0))

    def emit(self):
        nc = self.nc
        assert not self.pe_pending, "PE has trailing non-inc instructions"
        sem = self.sem
        ops = self.ops

        def run(e, lst):
            for waits, fn, sk, inc in lst:
                for k, v in waits:
                    e.wait_ge(sem[k], v)
                if fn is None:
                    continue
                ins = fn(e)
                if sk is not None:
                    ins.then_inc(sem[sk], inc)

        with nc.Block() as block:
            @block.tensor
            def _(e):
                run(e, ops["pe"])

            @block.vector
            def _(e):
                run(e, ops["dve"])

            @block.scalar
            def _(e):
                run(e, ops["act"])

            @block.gpsimd
            def _(e):
                run(e, ops["pool"])

            @block.sync
            def _(e):
                run(e, ops["sp"])


IN_W = 5640
A_QA, A_KA, A_VA = 0, 512, 1024
B_OFF = 1536
WB = IN_W - B_OFF
QKB, VBo, OBo, IBo, GAo, GBo = 0, 1024, 1536, 2048, 2056, 3080
LNC = float(np.log(128.0 ** -0.5))


def build_nc(S, phases=("A", "B", "C")):
    nc = bass.Bass("TRN2", target_bir_lowering=False)
    dr = {}

    def din(name, shape, dt=F32):
        dr[name] = nc.dram_tensor(name, shape, dt, kind="ExternalInput").ap()

    din("x", [S, D])
    din("norm_mix_g", [1, D])
    din("w_in", [D, IN_W])
    din("conv_wT", [D, 4])
    din("bias8", [1, 8])
    din("mlstm_norm_g", [1, 512])
    din("w_proj_a", [512, D])
    din("w_proj_b", [512, D])
    din("w_out", [D, D])
    din("norm_ffn_g", [1, D])
    din("norm_final_g", [1, D])
    din("w_gate_up", [D, 2 * DFF])
    din("w_down", [DFF, D])
    din("consts", [128, 4, 128])
    din("rope", [S, 2, 64])
    din("cmask", [128, 4, 512], BF16)
    din("blkc", [128, 16, 32])
    out_d = nc.dram_tensor("out", [S, D], F32, kind="ExternalOutput").ap()
    h1_d = nc.dram_tensor("h1_s", [S, D], F32, kind="Internal").ap()
    ya_d = nc.dram_tensor("ya_s", [8, 64, S], BF16, kind="Internal").ap()

    with ExitStack() as stack:
        P = Prog(nc, stack)
        ec = stack.enter_context
        identb = ec(nc.sbuf_tensor("identb", [128, 128], BF16))
        cst = ec(nc.sbuf_tensor("cst", [128, 4, 128], F32))
        r_c = Res("consts")
        P.dma("pool", lambda e: e.dma_start(out=identb[:], in_=dr["consts"][:, 0, :]), writes=[r_c])
        P.dma("sp", lambda e: e.dma_start(out=cst[:], in_=dr["consts"]), writes=[r_c])
        G = dict(identb=identb, cst=cst, r_c=r_c)

        r_ya = [[Res("ya_s%d_%d" % (i, h)) for h in range(8)] for i in range(S // 256)]
        r_h1 = [Res("h1_s%d" % i) for i in range(S // 256)]
        if "A" in phases:
            phase_A(nc, P, S, dr, ya_d, r_ya, G)
        else:
            with nc.sbuf_tensor("zt", [64, S], BF16) as zt:
                rz = Res("zt")
                P.memset("dve", zt[:], 0.0, [rz])
                for h in range(8):
                    P.dma("sp", lambda e, h=h: e.dma_start(out=ya_d[h], in_=zt[:]), reads=[rz],
                          writes=[r_ya[i][h] for i in range(S // 256)])
                P.barrier()
        P.barrier()
        P.new_epoch()
        if "B" in phases:
            phase_B(nc, P, S, dr, h1_d, ya_d, r_ya, r_h1, G)
            h_src = h1_d
        else:
            h_src = dr["x"]
        P.barrier()
        P.new_epoch()
        final_pts = phase_C(nc, P, S, h_src, r_h1, out_d, dr, G)
        P.finish(final_pts)
        P.emit()
    return nc


def load_norm_tile(nc, P, xt, r_x, s, gM, r_g, st8, r_st, col, epsb, r_eps, junk, r_junk, ub, r_ub):
    P.act(junk[:], xt[:, s, :], AF.Square, [r_x], [r_junk, r_st[col]], accum=st8[:, col:col + 1])
    P.act(st8[:, col:col + 1], st8[:, col:col + 1], AF.Sqrt, [r_st[col], r_eps], [r_st[col]],
          bias=epsb[:], scale=1.0 / D)
    P.recip(st8[:, col:col + 1], st8[:, col:col + 1], [r_st[col]], [r_st[col]])
    P.stt("dve", ub[:], xt[:, s, :], st8[:, col:col + 1], gM[:], ALU.mult, ALU.mult,
          [r_x, r_st[col], r_g], [r_ub])


def phase_A(nc, P, S, dr, ya_d, r_ya, G):
    T = 512
    NT = S // T
    NKT = S // 128
    identb, cst, r_c = G["identb"], G["cst"], G["r_c"]
    with ExitStack() as st:
        ec = st.enter_context

        def sbt(name, shape, dt):
            return ec(nc.sbuf_tensor(name, shape, dt))

        wA = sbt("wA", [128, 8, 1536], BF16)
        gM = sbt("gMa", [128, D], F32)
        cmk = sbt("cmk", [128, 4, 512], BF16)
        blkc = sbt("blkc_sb", [128, 16, 32], F32)
        epsb = sbt("epsb_a", [128, 1], F32)
        ropeT = [sbt("ropeT%d" % i, [128, 4, 2, 64], F32) for i in range(2)]
        kTa = sbt("kTa", [80, 8, S], BF16)
        vA = sbt("vA", [128, NKT, 8, 65], BF16)
        kmf = sbt("kmf", [64, 8], F32)
        kmT = sbt("kmT", [64, 8, 16], BF16)
        xs = [sbt("xs%d" % i, [128, 1, D], F32) for i in range(2)]
        ub = sbt("ub_a", [128, D], BF16)
        junk = sbt("junk_a", [128, D], BF16)
        st8 = sbt("st8_a", [128, 8], F32)
        uT = sbt("uTa", [128, 8, T], BF16)
        t1 = sbt("t1", [128, 8, 64], F32)
        t2 = sbt("t2", [128, 8, 64], F32)
        qtok = [sbt("qtok%d" % i, [128, 8, 80], BF16) for i in range(2)]
        ktok = [sbt("ktok%d" % i, [128, 8, 80], BF16) for i in range(2)]
        qTa = sbt("qTa", [80, 8, T], BF16)
        gsb = sbt("gsb_a", [128, 8, 16], F32)
        top8 = sbt("top8", [128, 8, 8], F32)
        PT = [sbt("PT%d" % i, [128, 512], BF16) for i in range(4)]
        oT = [sbt("oT%d" % i, [65, 512], F32) for i in range(2)]
        yo = [sbt("yo%d" % i, [64, 512], BF16) for i in range(2)]
        PB = [ec(nc.psum_tensor("PBa%d" % i, [128, 512], F32)) for i in range(8)]
        PBb = [PB[i].bitcast(BF16) for i in range(8)]

        R = lambda n: Res(n)
        r_wA = [R("wA%d" % c) for c in range(8)]
        r_k = R("constsA")
        r_rope = [R("rope%d" % i) for i in range(2)]
        r_kTa = [R("kTa%d" % i) for i in range(NKT)]
        r_vA = [R("vA%d" % i) for i in range(NKT)]
        r_kmf, r_kmT = R("kmf"), R("kmT")
        r_xs = [R("xs%d" % i) for i in range(2)]
        r_ub, r_junk = R("ub"), R("junk")
        r_st = [R("st%d" % i) for i in range(8)]
        r_uT = [R("uT%d" % s) for s in range(4)]
        r_t1, r_t2 = R("t1"), R("t2")
        r_qtok = [R("qtok%d" % i) for i in range(2)]
        r_ktok = [R("ktok%d" % i) for i in range(2)]
        r_qTa = [R("qTa%d" % s) for s in range(4)]
        r_gsb, r_top8 = R("gsb"), R("top8")
        r_PT = [R("PT%d" % i) for i in range(4)]
        r_oT = [R("oT%d" % i) for i in range(2)]
        r_yo = [R("yo%d" % i) for i in range(2)]
        r_pb = [R("pba%d" % i) for i in range(8)]

        wv = dr["w_in"].rearrange("(c p) f -> p c f", p=128)
        for c in range(8):
            P.dma("pool", lambda e, c=c: e.dma_start(out=wA[:, c, :], in_=wv[:, c, 0:1536]), writes=[r_wA[c]])
        P.dma("sp", lambda e: e.dma_start(out=gM[:], in_=dr["norm_mix_g"].to_broadcast([128, D])), writes=[r_k])
        P.dma("sp", lambda e: e.dma_start(out=cmk[:], in_=dr["cmask"]), writes=[r_k])
        P.dma("sp", lambda e: e.dma_start(out=blkc[:], in_=dr["blkc"]), writes=[r_k])
        P.memset("dve", epsb[:], EPS, [r_k])
        P.memset("dve", kmT[:], 0.0, [r_kmT])
        P.memset("dve", vA[:], 1.0, r_vA)

        xv = dr["x"].rearrange("(n p) d -> n p d", p=128)
        rope_v = dr["rope"].rearrange("(n s p) a d -> n p s a d", s=4, p=128)
        onesb = sbt("onesb", [128, 64], BF16)
        hiT = [sbt("hiT%d" % i, [65, 512], BF16) for i in range(2)]
        loT = [sbt("loT%d" % i, [65, 512], BF16) for i in range(2)]
        qTb = sbt("qTa2", [80, 8, T], BF16)
        qTas = [qTa, qTb]
        r_qTas = [r_qTa, [R("qTb%d" % s) for s in range(4)]]
        r_hl = [R("hl%d" % i) for i in range(2)]
        P.memset("dve", onesb[:], 1.0, [r_k])
        P.dma("sp", lambda e: e.dma_start(out=xs[0][:, 0, :], in_=xv[0]), writes=[r_xs[0]])

        def rope_ops(pz, rpz, rb, s, dst, rdst):
            z3 = pz.rearrange("p (h d) -> p h d", h=8)
            C2 = ropeT[rb][:, s, 0, :].unsqueeze(1).to_broadcast([128, 8, 64])
            Sa = ropeT[rb][:, s, 1, 0:32].unsqueeze(1).to_broadcast([128, 8, 32])
            Sb = ropeT[rb][:, s, 1, 32:64].unsqueeze(1).to_broadcast([128, 8, 32])
            P.tt("dve", t1[:], z3, C2, ALU.mult, [rpz, r_rope[rb]], [r_t1])
            P.tt("dve", t2[:, :, 0:32], z3[:, :, 32:64], Sa, ALU.mult, [rpz, r_rope[rb]], [r_t2])
            P.tt("dve", t2[:, :, 32:64], z3[:, :, 0:32], Sb, ALU.mult, [rpz, r_rope[rb]], [r_t2])
            P.tt("dve", dst[:, :, 0:64], t1[:], t2[:], ALU.add, [r_t1, r_t2], [rdst])

        def tr_group(src_fn, rsrc, rows, dst_fn, rdst, prow=None):
            for g4 in range(2):
                for j in range(4):
                    P.tr(PBb[g4][0:rows, j * 128:(j + 1) * 128], src_fn(g4 * 4 + j), identb[:], [rsrc, r_c],
                         [r_pb[g4]], inc=(j == 3))
            for g4 in range(2):
                lo_, hi_ = prow if prow else (0, rows)
                P.cp("dve", dst_fn(g4, lo_, hi_), PBb[g4][lo_:hi_, 0:512].rearrange("p (j t) -> p j t", j=4),
                     [r_pb[g4]], [rdst])

        def prologue(t):
            rb = t % 2
            qT_ = qTas[t % 2]
            rq_ = r_qTas[t % 2]
            P.dma("sp", lambda e, t=t, rb=rb: e.dma_start(out=ropeT[rb][:], in_=rope_v[t]), writes=[r_rope[rb]])
            for s in range(4):
                n = t * 4 + s
                sub = slice(s * 128, (s + 1) * 128)
                ksub = slice(n * 128, (n + 1) * 128)
                blk = n // 2
                if n + 1 < NKT:
                    P.dma("sp", lambda e, n=n: e.dma_start(out=xs[(n + 1) % 2][:, 0, :], in_=xv[n + 1]),
                          writes=[r_xs[(n + 1) % 2]])
                xt = xs[n % 2]
                load_norm_tile(nc, P, xt, r_xs[n % 2], 0, gM, r_k, st8, r_st, s, epsb, r_k, junk, r_junk, ub, r_ub)
                yield
                for g4 in range(2):
                    for j in range(4):
                        c = g4 * 4 + j
                        P.tr(PBb[g4][:, j * 128:(j + 1) * 128], ub[:, c * 128:(c + 1) * 128], identb[:],
                             [r_ub, r_c], [r_pb[g4]], inc=(j == 3))
                for g4 in range(2):
                    P.cp("dve", uT[:, g4 * 4:(g4 + 1) * 4, sub],
                         PBb[g4][:, 0:512].rearrange("p (j t) -> p j t", j=4), [r_pb[g4]], [r_uT[s]])
                yield
                kb, rkb = ktok[s % 2], r_ktok[s % 2]
                pz = PB[5][:]
                for c in range(8):
                    P.mm(pz, uT[:, c, sub], wA[:, c, A_KA:A_KA + 512], c == 0, c == 7, [r_uT[s], r_wA[c]], [r_pb[5]])
                rope_ops(pz, r_pb[5], rb, s, kb, rkb)
                P.cp("dve", kb[:, :, 64:80], blkc[:, blk, 16:32].unsqueeze(1).to_broadcast([128, 8, 16]),
                     [r_k], [rkb])
                yield
                tr_group(lambda h: kb[:, h, :], rkb, 80,
                         lambda g4, lo_, hi_: kTa[lo_:hi_, g4 * 4:(g4 + 1) * 4, ksub], r_kTa[n])
                if s % 2 == 1:
                    P.op("dve", lambda e, blk=blk: e.tensor_reduce(
                        out=kmf[:], in_=kTa[0:64, :, blk * 256:(blk + 1) * 256], axis=AX.X, op=ALU.add),
                        [r_kTa[n - 1], r_kTa[n]], [r_kmf])
                    P.ts("dve", kmT[:, :, blk], kmf[:], 1.0 / 256, ALU.mult, [r_kmf], [r_kmT])
                yield
                for c in range(8):
                    P.mm(pz, uT[:, c, sub], wA[:, c, A_VA:A_VA + 512], c == 0, c == 7, [r_uT[s], r_wA[c]], [r_pb[5]])
                P.cp("dve", vA[:, n, :, 0:64], pz.rearrange("p (h d) -> p h d", h=8), [r_pb[5]], [r_vA[n]])
                yield
                qb, rqb = qtok[s % 2], r_qtok[s % 2]
                for c in range(8):
                    P.mm(pz, uT[:, c, sub], wA[:, c, A_QA:A_QA + 512], c == 0, c == 7, [r_uT[s], r_wA[c]], [r_pb[5]])
                rope_ops(pz, r_pb[5], rb, s, qb, rqb)
                yield
                tr_group(lambda h: qb[:, h, 0:64], rqb, 64,
                         lambda g4, lo_, hi_: qT_[lo_:hi_, g4 * 4:(g4 + 1) * 4, sub], rq_[s])
                yield
                pgt = PB[5][:, 0:128].rearrange("p (h j) -> p h j", h=8)
                for h in range(8):
                    P.mm(pgt[:, h, :], qT_[0:64, h, sub], kmT[:, h, :], True, True, [rq_[s], r_kmT], [r_pb[5]],
                         inc=(h == 7))
                P.tt("dve", gsb[:], pgt, blkc[:, blk, 0:16].unsqueeze(1).to_broadcast([128, 8, 16]), ALU.add,
                     [r_pb[5], r_k], [r_gsb])
                for h in range(8):
                    P.op("dve", lambda e, h=h: e.max(out=top8[:, h, :], in_=gsb[:, h, :]), [r_gsb], [r_top8])
                P.tt("dve", qb[:, :, 64:80], gsb[:], top8[:, :, 3:4].to_broadcast([128, 8, 16]), ALU.is_lt,
                     [r_gsb, r_top8], [rqb])
                yield
                tr_group(lambda h: qb[:, h, :], rqb, 80,
                         lambda g4, lo_, hi_: qT_[lo_:hi_, g4 * 4:(g4 + 1) * 4, sub], rq_[s], prow=(64, 80))
                yield

        def attention(t):
            qT_ = qTas[t % 2]
            rq_ = r_qTas[t % 2]
            nkt = 4 * t + 4
            items = [(h, kt) for h in range(8) for kt in range(nkt)]
            LA = 2
            deferred = {}

            def emit_S(i):
                h, kt = items[i]
                bi = 2 + i % 3
                q0 = 128 * max(0, kt - 4 * t)
                P.mm(PB[bi][:, q0:512], kTa[:, h, kt * 128:(kt + 1) * 128], qT_[:, h, q0:512], True, True,
                     [r_kTa[kt]] + rq_, [r_pb[bi]])

            def emit_PV(i):
                h, kt = items[i]
                bi = 2 + i % 3
                pi = i % 4
                po = PB[6 + h % 2]
                rpo = r_pb[6 + h % 2]
                q0 = 128 * max(0, kt - 4 * t)
                P.act(PT[pi][:, q0:512], PB[bi][:, q0:512], AF.Exp, [r_pb[bi]], [r_PT[pi]], scale=0.125)
                if kt >= 4 * t:
                    P.tt("pool", PT[pi][:, q0:512], PT[pi][:, q0:512], cmk[:, kt - 4 * t, q0:512], ALU.mult,
                         [r_PT[pi], r_k], [r_PT[pi]])
                P.mm(po[0:65, q0:512], vA[:, kt, h, :], PT[pi][:, q0:512], kt == 0, kt == nkt - 1,
                     [r_vA[kt], r_PT[pi]], [rpo])
                if kt == nkt - 1:
                    ob, rob = oT[h % 2], r_oT[h % 2]
                    P.cp("dve", ob[:], po[0:65, :], [rpo], [rob])
                    P.recip(ob[64:65, :], ob[64:65, :], [rob], [rob])
                    P.cp("dve", hiT[h % 2][64:65, :], ob[64:65, :], [rob], [r_hl[h % 2]])
                    P.tt("dve", loT[h % 2][64:65, :], ob[64:65, :], hiT[h % 2][64:65, :], ALU.subtract,
                         [rob, r_hl[h % 2]], [r_hl[h % 2]])
                    deferred.setdefault(min(i + 4, len(items) - 1), []).append(h)

            def emit_epi(h):
                ob, rob = oT[h % 2], r_oT[h % 2]
                P.mm(PB[5][0:64, :], onesb[64:65, :], hiT[h % 2][64:65, :], True, False, [r_k, r_hl[h % 2]],
                     [r_pb[5]], inc=False)
                P.mm(PB[5][0:64, :], onesb[64:65, :], loT[h % 2][64:65, :], False, True, [r_k, r_hl[h % 2]],
                     [r_pb[5]])
                yb_ = yo[h % 2]
                P.tt("dve", yb_[:], ob[0:64, :], PB[5][0:64, :], ALU.mult, [rob, r_pb[5]], [r_yo[h % 2]])
                P.dma("sp", lambda e, h=h, t=t, yb_=yb_: e.dma_start(out=ya_d[h, :, t * T:(t + 1) * T], in_=yb_[:]),
                      reads=[r_yo[h % 2]], writes=[r_ya[2 * t][h], r_ya[2 * t + 1][h]])

            for i in range(min(LA, len(items))):
                emit_S(i)
            for i in range(len(items)):
                if i + LA < len(items):
                    emit_S(i + LA)
                emit_PV(i)
                for hh in deferred.pop(i, []):
                    emit_epi(hh)
                yield
            assert not deferred

        for _ in prologue(0):
            pass
        for t in range(NT):
            n_main = 8 * (4 * t + 4)
            side = prologue(t + 1) if t + 1 < NT else iter(())
            n_side = 36
            done = 0
            for i, _ in enumerate(attention(t)):
                want = ((i + 1) * n_side) // n_main
                while done < want:
                    next(side, None)
                    done += 1
            for _ in side:
                pass


def phase_B(nc, P, S, dr, h1_d, ya_d, r_ya, r_h1, G):
    T = 256
    NT = S // T
    identb, cst, r_c = G["identb"], G["cst"], G["r_c"]
    idf, tri, trim, onesf = cst[:, 0, :], cst[:, 1, :], cst[:, 2, :], cst[:, 3, :]
    with ExitStack() as st:
        ec = st.enter_context

        def sbt(name, shape, dt):
            return ec(nc.sbuf_tensor(name, shape, dt))

        wB = sbt("wB", [128, 8, WB], BF16)
        wpa = sbt("wpa", [128, 4, D], BF16)
        wpb = sbt("wpb", [128, 4, D], BF16)
        wo = sbt("wo", [128, 8, D], BF16)
        gM = sbt("gMb", [128, D], F32)
        gN = sbt("gN", [128, 512], F32)
        bias8 = sbt("bias8_sb", [128, 8], F32)
        cw = sbt("cw", [128, 8, 4], F32)
        trib = sbt("trib", [128, 128], BF16)
        epsb = sbt("epsb_b", [128, 1], F32)
        oneb = sbt("oneb", [128, 1], F32)
        lncb = sbt("lncb", [128, 1], F32)
        lnhb = sbt("lnhb", [128, 1], F32)
        xb = [sbt("xb%d" % i, [128, 2, D], F32) for i in range(3)]
        yaT = [sbt("yaT%d" % i, [128, 4, T], BF16) for i in range(3)]
        uT = [sbt("uTb%d" % i, [128, 8, T], BF16) for i in range(2)]
        ub2 = [sbt("ub_b%d" % i, [128, D], BF16) for i in range(2)]
        junk = sbt("junk_b", [128, D], BF16)
        st8 = sbt("st8_b", [128, 8], F32)
        gsb = sbt("gsb", [128, 2, 8], F32)
        e1 = sbt("e1", [128, 2, 4], F32)
        nlf = sbt("nlf", [128, 2, 4], F32)
        arg = sbt("arg", [128, 2, 4], F32)
        ksc = sbt("ksc", [128, 2, 4], F32)
        wc = sbt("wc", [128, 2, 4], F32)
        EBt = sbt("EBt", [128, 4, T], F32)
        EAt = sbt("EAt", [128, 4, T], F32)
        xc = sbt("xc", [128, 8, T + 3], F32)
        yc = [sbt("yc%d" % i, [128, T], F32) for i in range(2)]
        slq = [sbt("slq%d" % i, [128, T], BF16) for i in range(2)]
        kT0 = sbt("kT0", [128, 4, T], BF16)
        qT = sbt("qTb", [128, 4, T], BF16)
        kT = sbt("kTb", [128, 4, T], BF16)
        vaug = sbt("vaug", [128, 2, 4, 129], BF16)
        sgo = sbt("sgo", [128, 2, 512], F32)
        kS = sbt("kS", [128, 4, 128], BF16)
        scT = sbt("scT", [128, 4, 128], BF16)
        C32 = sbt("C32", [128, 4, 129], F32)
        Cbf = sbt("Cbf", [128, 4, 129], BF16)
        numS = [sbt("numS%d" % i, [128, 4, 129], F32) for i in range(2)]
        r_numS = [[Res("numS%d_%d" % (i, h)) for h in range(4)] for i in range(2)]
        rr = sbt("rr", [128, 4], F32)
        ssq = sbt("ssq", [128, 4], F32)
        t4 = sbt("t4", [128, 4], F32)
        sc4 = sbt("sc4", [128, 4], F32)
        ybf = sbt("ybf", [128, 512], F32)
        ybb2 = [sbt("ybb%d" % i, [128, 512], BF16) for i in range(2)]
        ybT = sbt("ybT", [128, 4, T], BF16)
        sga = [sbt("sga%d" % i, [128, T], F32) for i in range(1)]
        sgb = [sbt("sgb%d" % i, [128, T], F32) for i in range(1)]
        m1 = [sbt("m1_%d" % i, [128, T], F32) for i in range(1)]
        m2 = [sbt("m2_%d" % i, [128, T], F32) for i in range(1)]
        mT = sbt("mT", [128, 8, T], BF16)
        PB = [ec(nc.psum_tensor("PBb%d" % i, [128, 512], F32)) for i in range(8)]

        R = lambda n: Res(n)
        r_wB = [R("wB%d" % c) for c in range(8)]
        r_wpa, r_wpb, r_wo = R("wpa"), R("wpb"), [R("wo%d" % c) for c in range(8)]
        r_k = R("constsB")
        r_xb = [[R("xb%d_%d" % (i, s)) for s in range(2)] for i in range(3)]
        r_yaT = [R("yaT%d" % i) for i in range(3)]
        r_uT = [[R("uTb%d_%d" % (i, s)) for s in range(2)] for i in range(2)]
        r_ub2, r_junk = [R("ub0"), R("ub1")], R("junk")
        r_st = [R("st%d" % i) for i in range(8)]
        r_gs = [R("gsb%d" % s) for s in range(2)]
        r_e1 = [R("e1%d" % s) for s in range(2)]
        r_nlf = [R("nlf%d" % s) for s in range(2)]
        r_arg = [R("arg%d" % s) for s in range(2)]
        r_ksc = [R("ksc%d" % s) for s in range(2)]
        r_wc = [R("wc%d" % s) for s in range(2)]
        r_EB = [[R("EB%d_%d" % (h, s)) for s in range(2)] for h in range(4)]
        r_EA = [[R("EA%d_%d" % (h, s)) for s in range(2)] for h in range(4)]
        r_xc = [R("xc%d" % m) for m in range(8)]
        r_yc = [R("yc%d" % i) for i in range(2)]
        r_slq = [R("slq%d" % i) for i in range(2)]
        r_kT0 = [R("kT0%d" % h) for h in range(4)]
        r_qT = [R("qT%d" % h) for h in range(4)]
        r_kT = [R("kT%d" % h) for h in range(4)]
        r_va = [R("vaug%d" % s) for s in range(2)]
        r_sgo = [R("sgo%d" % s) for s in range(2)]
        r_kS = [R("kS%d" % h) for h in range(4)]
        r_scT = [R("scT%d" % h) for h in range(4)]
        r_C32 = [R("C32%d" % h) for h in range(4)]
        r_Cbf = [R("Cbf%d" % h) for h in range(4)]
        r_rr, r_ssq, r_t4, r_sc4 = R("rr"), R("ssq"), R("t4"), R("sc4")
        r_ybf, r_ybb2 = R("ybf"), [R("ybb0"), R("ybb1")]
        r_ybT = [R("ybT%d" % s) for s in range(2)]
        r_sga = [R("sga%d" % i) for i in range(2)]
        r_sgb = [R("sgb%d" % i) for i in range(2)]
        r_m1 = [R("m1%d" % i) for i in range(2)]
        r_m2 = [R("m2%d" % i) for i in range(2)]
        r_mT = [R("mT%d" % f) for f in range(8)]
        r_pb = [R("pb%d" % i) for i in range(8)]
        PBb = [PB[i].bitcast(BF16) for i in range(8)]

        wv = dr["w_in"].rearrange("(c p) f -> p c f", p=128)
        r_wB2 = [R("wB2_%d" % c) for c in range(8)]
        for c in range(8):
            P.dma("pool", lambda e, c=c: e.dma_start(out=wB[:, c, 0:GAo], in_=wv[:, c, B_OFF:B_OFF + GAo]),
                  writes=[r_wB[c]])
        for c in range(8):
            P.dma("pool", lambda e, c=c: e.dma_start(out=wB[:, c, GAo:WB], in_=wv[:, c, B_OFF + GAo:IN_W]),
                  writes=[r_wB2[c]])
        P.dma("pool", lambda e: e.dma_start(out=wpa[:], in_=dr["w_proj_a"].rearrange("(c p) f -> p c f", p=128)),
              writes=[r_wpa])
        P.dma("pool", lambda e: e.dma_start(out=wpb[:], in_=dr["w_proj_b"].rearrange("(c p) f -> p c f", p=128)),
              writes=[r_wpb])
        wov = dr["w_out"].rearrange("(c p) f -> p c f", p=128)
        for c in range(8):
            P.dma("pool", lambda e, c=c: e.dma_start(out=wo[:, c, :], in_=wov[:, c, :]), writes=[r_wo[c]])
        P.dma("pool", lambda e: e.dma_start(out=trib[:], in_=dr["consts"][:, 1, :]), writes=[r_k])
        P.dma("sp", lambda e: e.dma_start(out=gM[:], in_=dr["norm_mix_g"].to_broadcast([128, D])), writes=[r_k])
        P.dma("sp", lambda e: e.dma_start(out=gN[:], in_=dr["mlstm_norm_g"].to_broadcast([128, 512])), writes=[r_k])
        P.dma("sp", lambda e: e.dma_start(out=bias8[:], in_=dr["bias8"].to_broadcast([128, 8])), writes=[r_k])
        P.dma("sp", lambda e: e.dma_start(out=cw[:], in_=dr["conv_wT"].rearrange("(m p) j -> p m j", p=128)),
              writes=[r_k])
        P.ts("dve", gN[:], gN[:], 0.5, ALU.mult, [r_k], [r_k])
        P.memset("dve", epsb[:], EPS, [r_k])
        P.memset("dve", oneb[:], 1.0, [r_k])
        P.memset("dve", lncb[:], LNC + float(np.log(0.5)), [r_k])
        P.memset("dve", lnhb[:], float(np.log(0.5)), [r_k])
        P.memset("dve", C32[:], 0.0, r_C32)
        P.memset("dve", Cbf[:], 0.0, r_Cbf)
        P.memset("dve", vaug[:], 1.0, r_va)
        P.memset("dve", xc[:], 0.0, r_xc)

        xv = dr["x"].rearrange("(t s p) d -> t p s d", s=2, p=128)
        hv = h1_d.rearrange("(t s p) d -> t p s d", s=2, p=128)
        ya_v = ya_d.rearrange("h d s -> (h d) s").rearrange("(c p) s -> p c s", p=128)

        ybT2 = sbt("ybT2", [128, 4, T], BF16)
        ybTs = [ybT, ybT2]
        r_ybTs = [r_ybT, [R("ybTb%d" % s) for s in range(2)]]

        def stage_A(t):
            b = t % 2
            xt = xb[t % 3]
            uTt = uT[b]
            for s in range(2):
                load_norm_tile(nc, P, xt, r_xb[t % 3][s], s, gM, r_k, st8, r_st, s, epsb, r_k, junk, r_junk,
                               ub2[s], r_ub2[s])
            yield
            for s in range(2):
                sub = slice(s * 128, (s + 1) * 128)
                ub, r_ub = ub2[s], r_ub2[s]
                for g4 in range(2):
                    for j in range(4):
                        c = g4 * 4 + j
                        P.tr(PBb[2 + g4][:, j * 128:(j + 1) * 128], ub[:, c * 128:(c + 1) * 128],
                             identb[:], [r_ub, r_c], [r_pb[2 + g4]], inc=(j == 3))
                for g4 in range(2):
                    P.cp("act", uTt[:, g4 * 4:(g4 + 1) * 4, sub],
                         PBb[2 + g4][:, 0:512].rearrange("p (j t) -> p j t", j=4),
                         [r_pb[2 + g4]], [r_uT[b][s]])
                yield
                pg = PB[4][:, 0:8]
                for c in range(8):
                    P.mm(pg, uTt[:, c, sub], wB[:, c, IBo:IBo + 8], c == 0, c == 7,
                         [r_uT[b][s], r_wB[c]], [r_pb[4]])
                P.tt("dve", gsb[:, s, :], pg, bias8[:], ALU.add, [r_pb[4], r_k], [r_gs[s]])
                P.act(e1[:, s, :], gsb[:, s, 4:8], AF.Exp, [r_gs[s]], [r_e1[s]], scale=-1.0)
                P.act(nlf[:, s, :], e1[:, s, :], AF.Ln, [r_e1[s], r_k], [r_nlf[s]], bias=oneb[:])
                pv = PB[2][:]
                for c in range(8):
                    P.mm(pv, uTt[:, c, sub], wB[:, c, VBo:VBo + 512], c == 0, c == 7,
                         [r_uT[b][s], r_wB[c]], [r_pb[2]])
                P.cp("act", vaug[:, s, :, 0:128], pv.rearrange("p (h e) -> p h e", h=4), [r_pb[2]], [r_va[s]])
                yield
                po = PB[3][:]
                for c in range(8):
                    P.mm(po, uTt[:, c, sub], wB[:, c, OBo:OBo + 512], c == 0, c == 7,
                         [r_uT[b][s], r_wB[c]], [r_pb[3]])
                P.act(sgo[:, s, :], po, AF.Tanh, [r_pb[3]], [r_sgo[s]], scale=0.5)
                P.mm(PB[4][:, 8:12], trim, nlf[:, s, :], True, True, [r_c, r_nlf[s]], [r_pb[4]], inc=False)
                P.mm(PB[4][:, 12:16], onesf, nlf[:, s, :], True, True, [r_c, r_nlf[s]], [r_pb[4]])
                P.tt("dve", arg[:, s, :], PB[4][:, 8:12], gsb[:, s, 0:4], ALU.add, [r_pb[4], r_gs[s]], [r_arg[s]])
                P.act(ksc[:, s, :], arg[:, s, :], AF.Exp, [r_arg[s], r_k], [r_ksc[s]], bias=lncb[:])
                P.act(wc[:, s, :], PB[4][:, 12:16], AF.Exp, [r_pb[4]], [r_wc[s]], scale=-1.0)
                yield
                for h in range(4):
                    i0 = (2 * h) % 3
                    i1 = (2 * h + 1) % 3
                    pe_b = PB[5 + i0][:, 0:128]
                    pe_a = PB[5 + i1][:, 0:128]
                    nb_l = nlf[:, s, h:h + 1].to_broadcast([128, 128])
                    ip_l = gsb[:, s, h:h + 1].to_broadcast([128, 128])
                    P.mm(pe_b, nb_l, tri, True, True, [r_nlf[s], r_c], [r_pb[5 + i0]])
                    P.mm(pe_a, ip_l, idf, True, False, [r_gs[s], r_c], [r_pb[5 + i1]], inc=False)
                    P.mm(pe_a, nb_l, tri, False, True, [r_nlf[s], r_c], [r_pb[5 + i1]])
                    P.act(EBt[:, h, sub], pe_b, AF.Exp, [r_pb[5 + i0], r_k], [r_EB[h][s]], scale=-1.0, bias=lnhb[:])
                    P.act(EAt[:, h, sub], pe_a, AF.Exp, [r_pb[5 + i1], r_k], [r_EA[h][s]], bias=lncb[:])
                    if h % 2 == 1:
                        yield
            for m in range(8):
                hh = m % 4
                pq = PB[2 + m % 2][:, 0:256]
                for c in range(8):
                    P.mm(pq, wB[:, c, QKB + m * 128:QKB + (m + 1) * 128], uTt[:, c, :], c == 0, c == 7,
                         [r_wB[c]] + r_uT[b], [r_pb[2 + m % 2]])
                P.cp("act", xc[:, m, 3:3 + T], pq, [r_pb[2 + m % 2]], [r_xc[m]])
                y = yc[m % 2]
                ry = r_yc[m % 2]
                P.ts("dve", y[:], xc[:, m, 0:T], cw[:, m, 0:1], ALU.mult, [r_xc[m], r_k], [ry])
                for j in range(1, 4):
                    P.stt("dve", y[:], xc[:, m, j:j + T], cw[:, m, j:j + 1], y[:], ALU.mult, ALU.add,
                          [r_xc[m], r_k, ry], [ry])
                P.cp("dve", xc[:, m, 0:3], xc[:, m, T:T + 3], [r_xc[m]], [r_xc[m]])
                if m < 4:
                    sl = slq[m % 2]
                    P.act(sl[:], y[:], AF.Tanh, [ry], [r_slq[m % 2]], scale=0.5)
                    P.stt("dve", y[:], sl[:], 1.0, y[:], ALU.add, ALU.mult, [r_slq[m % 2], ry], [ry])
                    P.tt("dve", qT[:, hh, :], y[:], EBt[:, hh, :], ALU.mult, [ry] + r_EB[hh], [r_qT[hh]])
                else:
                    sl = slq[m % 2]
                    P.act(sl[:], y[:], AF.Tanh, [ry], [r_slq[m % 2]], scale=0.5)
                    P.stt("dve", kT0[:, hh, :], sl[:], 1.0, y[:], ALU.add, ALU.mult, [r_slq[m % 2], ry], [r_kT0[hh]])
                    P.tt("dve", kT[:, hh, :], kT0[:, hh, :], EAt[:, hh, :], ALU.mult,
                         [r_kT0[hh]] + r_EA[hh], [r_kT[hh]])
                yield
            for s in range(2):
                sub = slice(s * 128, (s + 1) * 128)
                pnum = [PB[6][:, 0:129], PB[7][:, 0:129], PB[6][:, 256:385], PB[7][:, 256:385]]
                r_pn = [r_pb[6], r_pb[7], r_pb[6], r_pb[7]]
                pdc = [PB[2][:, 256:385], PB[3][:, 256:385], PB[2][:, 256:385], PB[3][:, 256:385]]
                r_pd = [r_pb[2], r_pb[3], r_pb[2], r_pb[3]]
                for h in range(4):
                    ptk = PBb[2 + h % 2][:, 0:128]
                    P.tr(ptk, kT0[:, h, sub], identb[:], [r_kT0[h], r_c], [r_pb[2 + h % 2]])
                    pS = PB[4 + h % 2][:, 0:128]
                    P.mm(pS, kT[:, h, sub], qT[:, h, sub], True, True, [r_kT[h], r_qT[h]], [r_pb[4 + h % 2]])
                    P.ts("dve", kS[:, h, :], ptk, ksc[:, s, h:h + 1], ALU.mult, [r_pb[2 + h % 2], r_ksc[s]],
                         [r_kS[h]])
                    P.tt("dve", scT[:, h, :], pS, trib[:], ALU.mult, [r_pb[4 + h % 2], r_k], [r_scT[h]])
                    yield
                    P.mm(pnum[h], qT[:, h, sub], Cbf[:, h, :], True, False, [r_qT[h], r_Cbf[h]], [r_pn[h]],
                         inc=False)
                    P.mm(pnum[h], scT[:, h, :], vaug[:, s, h, :], False, True, [r_scT[h], r_va[s]], [r_pn[h]])
                    P.cp("act", numS[s][:, h, :], pnum[h], [r_pn[h]], [r_numS[s][h]])
                    P.mm(pdc[h], kS[:, h, :], vaug[:, s, h, :], True, True, [r_kS[h], r_va[s]], [r_pd[h]])
                    P.stt("dve", C32[:, h, :], C32[:, h, :], wc[:, s, h:h + 1], pdc[h], ALU.mult, ALU.add,
                          [r_C32[h], r_wc[s], r_pd[h]], [r_C32[h]])
                    P.cp("act", Cbf[:, h, :], C32[:, h, :], [r_C32[h]], [r_Cbf[h]])
                    yield
                for h in range(4):
                    P.act(junk[:, 0:128], numS[s][:, h, 0:128], AF.Square, [r_numS[s][h]], [r_junk, r_ssq],
                          accum=ssq[:, h:h + 1])
                P.stt("dve", rr[:], numS[s][:, :, 128], -1.0, numS[s][:, :, 128], ALU.mult, ALU.max,
                      r_numS[s], [r_rr])
                P.ts("dve", rr[:], rr[:], 1.0, ALU.max, [r_rr], [r_rr])
                P.recip(rr[:], rr[:], [r_rr], [r_rr])
                P.tt("dve", t4[:], rr[:], rr[:], ALU.mult, [r_rr], [r_t4])
                P.tt("dve", t4[:], t4[:], ssq[:], ALU.mult, [r_t4, r_ssq], [r_t4])
                P.act(t4[:], t4[:], AF.Sqrt, [r_t4, r_k], [r_t4], bias=epsb[:], scale=1.0 / 128)
                P.recip(t4[:], t4[:], [r_t4], [r_t4])
                P.tt("dve", sc4[:], t4[:], rr[:], ALU.mult, [r_t4, r_rr], [r_sc4])
                yield
                for h in range(4):
                    P.stt("dve", ybf[:, h * 128:(h + 1) * 128], numS[s][:, h, 0:128], sc4[:, h:h + 1],
                          gN[:, h * 128:(h + 1) * 128], ALU.mult, ALU.mult, [r_numS[s][h], r_sc4, r_k], [r_ybf])
                P.stt("dve", ybb2[s][:], sgo[:, s, :], 1.0, ybf[:], ALU.add, ALU.mult, [r_ybf, r_sgo[s]], [r_ybb2[s]])
                yield
                if s == 1:
                    yb_transposes(t, 0)
                    yield

        def yb_transposes(t, s):
            b = t % 2
            sub = slice(s * 128, (s + 1) * 128)
            for h in range(4):
                P.tr(PBb[3][:, h * 128:(h + 1) * 128], ybb2[s][:, h * 128:(h + 1) * 128], identb[:],
                     [r_ybb2[s], r_c], [r_pb[3]], inc=(h == 3))
            P.cp("act", ybTs[b][:, :, sub], PBb[3][:, 0:512].rearrange("p (j t) -> p j t", j=4),
                 [r_pb[3]], [r_ybTs[b][s]])

        def stage_M(t):
            b = t % 2
            xt = xb[t % 3]
            uTt = uT[b]
            for f in range(8):
                fs = slice(f * 128, (f + 1) * 128)
                pga = PB[0][:, 0:256]
                pgb = PB[0][:, 256:512]
                ppa = PB[1][:, 0:256]
                ppb = PB[1][:, 256:512]
                for c in range(8):
                    P.mm(pga, wB[:, c, GAo + f * 128:GAo + (f + 1) * 128], uTt[:, c, :], c == 0, c == 7,
                         [r_wB2[c]] + r_uT[b], [r_pb[0]])
                for c in range(8):
                    P.mm(pgb, wB[:, c, GBo + f * 128:GBo + (f + 1) * 128], uTt[:, c, :], c == 0, c == 7,
                         [r_wB2[c]] + r_uT[b], [r_pb[0]])
                k2 = 0
                P.act(sga[k2][:], pga, AF.Tanh, [r_pb[0]], [r_sga[k2]], scale=0.5)
                P.act(sgb[k2][:], pgb, AF.Tanh, [r_pb[0]], [r_sgb[k2]], scale=0.5)
                for pr in range(4):
                    P.mm(ppa, wpa[:, pr, fs], yaT[t % 3][:, pr, :], pr == 0, pr == 3, [r_wpa, r_yaT[t % 3]], [r_pb[1]])
                if f == 0:
                    yb_transposes(t, 1)
                for pr in range(4):
                    P.mm(ppb, wpb[:, pr, fs], ybTs[b][:, pr, :], pr == 0, pr == 3, [r_wpb] + r_ybTs[b], [r_pb[1]])
                P.stt("dve", m1[k2][:], sga[k2][:], 1.0, ppa, ALU.add, ALU.mult, [r_sga[k2], r_pb[1]], [r_m1[k2]])
                P.stt("dve", m2[k2][:], sgb[k2][:], 1.0, ppb, ALU.add, ALU.mult, [r_sgb[k2], r_pb[1]], [r_m2[k2]])
                P.tt("dve", mT[:, f, :], m1[k2][:], m2[k2][:], ALU.add, [r_m1[k2], r_m2[k2]], [r_mT[f]])
                yield
            for s in range(2):
                sub = slice(s * 128, (s + 1) * 128)
                for n in range(2):
                    po2 = PB[n][:]
                    rpo = [r_pb[n]]
                    for c in range(8):
                        P.mm(po2, mT[:, c, sub], wo[:, c, n * 512:(n + 1) * 512], c == 0, c == 7,
                             [r_mT[c], r_wo[c]], rpo)
                    P.stt("dve", xt[:, s, n * 512:(n + 1) * 512], po2, 0.5, xt[:, s, n * 512:(n + 1) * 512],
                          ALU.mult, ALU.add, rpo + [r_xb[t % 3][s]], [r_xb[t % 3][s]])
                    yield
            P.dma("sp", lambda e, t=t, xt=xt: e.dma_start(out=hv[t], in_=xt[:]), reads=r_xb[t % 3], writes=[r_h1[t]])

        def prefetch(tt_):
            if tt_ < NT:
                P.dma("sp", lambda e: e.dma_start(out=xb[tt_ % 3][:], in_=xv[tt_]), writes=r_xb[tt_ % 3])
                P.dma("sp", lambda e: e.dma_start(out=yaT[tt_ % 3][:], in_=ya_v[:, :, tt_ * T:(tt_ + 1) * T]),
                      reads=r_ya[tt_], writes=[r_yaT[tt_ % 3]])

        prefetch(0)
        prefetch(1)
        for _ in stage_A(0):
            pass
        for t in range(NT):
            if t + 1 < NT:
                prefetch(t + 2)
                side = stage_M(t)
                n_side = 12
                done = 0
                for i, _ in enumerate(stage_A(t + 1)):
                    want = ((i + 1) * n_side) // 46
                    while done < want:
                        next(side, None)
                        done += 1
                for _ in side:
                    pass
            else:
                for _ in stage_M(t):
                    pass


def phase_C(nc, P, S, h_d, r_h1, out_d, dr, G):
    T = 256
    NT = S // T
    NF = DFF // 128
    identb, r_ident = G["identb"], G["r_c"]
    g_ffn_d, g_fin_d, wgu_d, wdn_d = dr["norm_ffn_g"], dr["norm_final_g"], dr["w_gate_up"], dr["w_down"]
    with ExitStack() as st:
        ec = st.enter_context
        wgu = ec(nc.sbuf_tensor("wgu", [128, 8, 2 * DFF], BF16))
        wdn = ec(nc.sbuf_tensor("wdn", [128, NF, D], BF16))
        gF = ec(nc.sbuf_tensor("gF", [128, D], F32))
        gL = ec(nc.sbuf_tensor("gL", [128, D], F32))
        hb = [ec(nc.sbuf_tensor("hb%d" % i, [128, 2, D], F32)) for i in range(2)]
        ub = ec(nc.sbuf_tensor("ub", [128, D], BF16))
        uT = [ec(nc.sbuf_tensor("uT%d" % i, [128, 8, T], BF16)) for i in range(2)]
        aT = ec(nc.sbuf_tensor("aT", [128, NF, T], BF16))
        sg = [ec(nc.sbuf_tensor("sg%d" % i, [128, T], BF16)) for i in range(2)]
        junk = ec(nc.sbuf_tensor("junk", [128, D], BF16))
        st8 = ec(nc.sbuf_tensor("st8", [128, 8], F32))
        epsb = ec(nc.sbuf_tensor("epsb", [128, 1], F32))
        r_eps = Res("eps")
        P.op("dve", lambda e: e.memset(epsb[:], EPS), writes=[r_eps])
        ptp = [ec(nc.psum_tensor("ptp%d" % i, [128, 4, 128], BF16)) for i in range(2)]
        pgu = [ec(nc.psum_tensor("pgu%d" % i, [128, 512], F32)) for i in range(4)]
        pdn = [ec(nc.psum_tensor("pdn%d" % i, [128, 512], F32)) for i in range(2)]

        r_wdn = [Res("wdn%d" % c) for c in range(NF)]
        r_gF, r_gL = Res("gF"), Res("gL")
        r_hb = [[Res("hb%d_%d" % (i, s)) for s in range(2)] for i in range(2)]
        r_ub = Res("ub")
        r_uT = [Res("uT%d" % i) for i in range(2)]
        r_aT = [Res("aT%d" % j) for j in range(NF)]
        r_sg = [Res("sg%d" % i) for i in range(2)]
        r_junk = Res("junk")
        r_st = [Res("st%d" % i) for i in range(8)]
        r_ptp = [Res("ptp%d" % i) for i in range(2)]
        r_pgu = [Res("pgu%d" % i) for i in range(4)]
        r_pdn = [Res("pdn%d" % i) for i in range(2)]

        P.dma("pool", lambda e: e.dma_start(out=gF[:], in_=g_ffn_d.to_broadcast([128, D])), writes=[r_gF])
        P.dma("pool", lambda e: e.dma_start(out=gL[:], in_=g_fin_d.to_broadcast([128, D])), writes=[r_gL])
        wgu_v = wgu_d.rearrange("(c p) f -> p c f", p=128)
        wdn_v = wdn_d.rearrange("(c p) f -> p c f", p=128)
        NBLK = 11
        r_wgu = [Res("wgu%d" % i) for i in range(NBLK)]
        for i in range(NBLK):
            for hlf in range(2):
                lo = hlf * DFF + i * 256
                P.dma("pool", lambda e, lo=lo: e.dma_start(out=wgu[:, :, lo:lo + 256], in_=wgu_v[:, :, lo:lo + 256]),
                      writes=[r_wgu[i]])
        for c in range(NF):
            P.dma("pool", lambda e, c=c: e.dma_start(out=wdn[:, c, :], in_=wdn_v[:, c, :]),
                  writes=[r_wdn[c]])
        h_v = h_d.rearrange("(t s p) d -> t p s d", s=2, p=128)
        o_v = out_d.rearrange("(t s p) d -> t p s d", s=2, p=128)
        final_pts = []

        def rstd_from(h_ap, col, r_h):
            P.op("act", lambda e: e.activation(out=junk[:], in_=h_ap, func=AF.Square,
                                               accum_out=st8[:, col:col + 1]),
                 reads=[r_h], writes=[r_junk, r_st[col]])
            P.op("act", lambda e: e.activation(out=st8[:, col:col + 1], in_=st8[:, col:col + 1], func=AF.Sqrt,
                                               scale=1.0 / D, bias=epsb[:]),
                 reads=[r_st[col], r_eps], writes=[r_st[col]])
            P.op("dve", lambda e: e.reciprocal(out=st8[:, col:col + 1], in_=st8[:, col:col + 1]),
                 reads=[r_st[col]], writes=[r_st[col]])

        def prologue(t):
            b = t % 2
            hbt = hb[b]
            P.dma("sp", lambda e, t=t, hbt=hbt: e.dma_start(out=hbt[:], in_=h_v[t]),
                  reads=[r_h1[t]], writes=[r_hb[b][0], r_hb[b][1]])
            for s in range(2):
                rstd_from(hbt[:, s, :], s, r_hb[b][s])
                P.stt("dve", ub[:], hbt[:, s, :], st8[:, s:s + 1], gF[:], ALU.mult, ALU.mult,
                      [r_hb[b][s], r_st[s], r_gF], [r_ub])
                for g4 in range(2):
                    pt = ptp[g4]
                    for j in range(4):
                        c = g4 * 4 + j
                        P.tr(pt[:, j, :], ub[:, c * 128:(c + 1) * 128], identb[:], [r_ub, r_ident], [r_ptp[g4]],
                             inc=(j == 3))
                    P.cp("act", uT[b][:, g4 * 4:(g4 + 1) * 4, s * 128:(s + 1) * 128], pt[:], [r_ptp[g4]], [r_uT[b]])

        def gateup(t):
            b = t % 2
            for j in range(NF):
                pg = pgu[(2 * j) % 4]
                pu = pgu[(2 * j + 1) % 4]
                rg = r_pgu[(2 * j) % 4]
                ru = r_pgu[(2 * j + 1) % 4]
                for c in range(8):
                    P.mm(pg[:, 0:T], wgu[:, c, j * 128:(j + 1) * 128], uT[b][:, c, :], c == 0, c == 7,
                         [r_wgu[j // 2], r_uT[b]], [rg])
                for c in range(8):
                    P.mm(pu[:, 0:T], wgu[:, c, DFF + j * 128:DFF + (j + 1) * 128], uT[b][:, c, :], c == 0, c == 7,
                         [r_wgu[j // 2], r_uT[b]], [ru])
                sgt = sg[j % 2]
                P.act(sgt[:], pg[:, 0:T], AF.Silu, [rg], [r_sg[j % 2]])
                P.tt("dve", aT[:, j, :], sgt[:], pu[:, 0:T], ALU.mult, [ru, r_sg[j % 2]], [r_aT[j]])

        def down(t):
            b = t % 2
            hbt = hb[b]
            for s in range(2):
                for n in range(2):
                    pd = pdn[n]
                    for j in range(NF):
                        P.mm(pd[:], aT[:, j, s * 128:(s + 1) * 128], wdn[:, j, n * 512:(n + 1) * 512],
                             j == 0, j == NF - 1, [r_aT[j], r_wdn[j]], [r_pdn[n]])
                    P.tt("dve", hbt[:, s, n * 512:(n + 1) * 512], hbt[:, s, n * 512:(n + 1) * 512], pd[:], ALU.add,
                         [r_pdn[n], r_hb[b][s]], [r_hb[b][s]])
                rstd_from(hbt[:, s, :], 2 + s, r_hb[b][s])
                P.stt("dve", hbt[:, s, :], hbt[:, s, :], st8[:, 2 + s:3 + s], gL[:], ALU.mult, ALU.mult,
                      [r_hb[b][s], r_st[2 + s], r_gL], [r_hb[b][s]])
            final_pts.append(P.dma("sp", lambda e, t=t, hbt=hbt: e.dma_start(out=o_v[t], in_=hbt[:]),
                                   reads=[r_hb[b][0], r_hb[b][1]]))

        prologue(0)
        for t in range(NT):
            gateup(t)
            if t + 1 < NT:
                prologue(t + 1)
            down(t)
        return final_pts


def _host_consts(S):
    idx = np.arange(128)
    ident = np.eye(128, dtype=np.float32)
    tri = (idx[:, None] <= idx[None, :]).astype(np.float32)
    trim = -(idx[:, None] > idx[None, :]).astype(np.float32)
    ones = np.ones((128, 128), np.float32)
    consts = np.ascontiguousarray(np.stack([ident, tri, trim, ones], axis=1))
    inv = (np.float32(10000.0) ** (-np.arange(32, dtype=np.float32) / np.float32(32))).astype(np.float32)
    ang = np.arange(S, dtype=np.float32)[:, None] * inv[None, :]
    cos, sin = np.cos(ang).astype(np.float32), np.sin(ang).astype(np.float32)
    rope = np.ascontiguousarray(np.stack([np.concatenate([cos, cos], 1), np.concatenate([-sin, sin], 1)], axis=1))
    k = np.arange(128)[:, None, None]
    a = np.arange(4)[None, :, None]
    q = np.arange(512)[None, None, :]
    jq, ja = q // 256, a // 2
    cm = np.where(jq > ja, 1.0, np.where(jq < ja, 0.0, (q >= 128 * a + k).astype(np.float32)))
    cmask = np.ascontiguousarray(np.broadcast_to(cm, (128, 4, 512)).astype(ml_dtypes.bfloat16))
    blk = np.arange(16)[:, None]
    j = np.arange(16)[None, :]
    pastb = np.where(j < blk, 0.0, np.where(j == blk, 1e30, -1e30)).astype(np.float32)
    oneh = np.where(j == blk, -1024.0, 0.0).astype(np.float32)
    blkc = np.ascontiguousarray(np.broadcast_to(np.concatenate([pastb, oneh], 1)[None], (128, 16, 32)).astype(np.float32))
    return dict(consts=consts, rope=rope, cmask=cmask, blkc=blkc)


def make_in_maps(inp, ncores):
    f = lambda k: np.ascontiguousarray(np.asarray(inp[k], dtype=np.float32))
    x = np.asarray(inp["x"], dtype=np.float32)
    S = x.shape[1]
    common = {
        "norm_mix_g": f("norm_mix_g").reshape(1, D),
        "w_in": f("w_in")[0],
        "conv_wT": np.ascontiguousarray(f("conv_w")[0].T),
        "bias8": np.ascontiguousarray(np.concatenate([f("b_igate")[0], f("b_fgate")[0]]).reshape(1, 8)),
        "mlstm_norm_g": f("mlstm_norm_g").reshape(1, 512),
        "w_proj_a": f("w_proj_a")[0],
        "w_proj_b": f("w_proj_b")[0],
        "w_out": f("w_out")[0],
        "norm_ffn_g": f("norm_ffn_g").reshape(1, D),
        "norm_final_g": f("norm_final_g").reshape(1, D),
        "w_gate_up": f("w_gate_up")[0],
        "w_down": f("w_down")[0],
    }
    common.update(_host_consts(S))
    return [dict(common, x=np.ascontiguousarray(x[b])) for b in range(ncores)]


def kernel(x, norm_mix_g, w_in, conv_w, b_igate, b_fgate, mlstm_norm_g, w_proj_a, w_proj_b,
           w_out, norm_ffn_g, w_gate_up, w_down, norm_final_g):
    inp = dict(x=x, norm_mix_g=norm_mix_g, w_in=w_in, conv_w=conv_w, b_igate=b_igate, b_fgate=b_fgate,
               mlstm_norm_g=mlstm_norm_g, w_proj_a=w_proj_a, w_proj_b=w_proj_b, w_out=w_out,
               norm_ffn_g=norm_ffn_g, w_gate_up=w_gate_up, w_down=w_down, norm_final_g=norm_final_g)
    B, S, _ = np.asarray(x).shape
    nc = build_nc(S)
    in_maps = make_in_maps(inp, B)
    res = run_bass_kernel_spmd(nc, in_maps, core_ids=list(range(B)))
    return np.stack([np.asarray(r["out"]) for r in res.results], axis=0).astype(np.float32)
```

```python
import numpy as np
import ml_dtypes
from contextlib import ExitStack

import concourse.bass as bass
import concourse.mybir as mybir
from concourse.bass_utils import run_bass_kernel_spmd

F32 = mybir.dt.float32
BF16 = mybir.dt.bfloat16
ALU = mybir.AluOpType
AF = mybir.ActivationFunctionType
AX = mybir.AxisListType

D = 1024
DFF = 2816
EPS = 1e-6
NCORES = 8


class Res:
    __slots__ = ("name", "w", "r")

    def __init__(self, name):
        self.name = name
        self.w = None
        self.r = {}


class Prog:
    COMPUTE = ("pe", "dve", "act", "pool")

    def __init__(self, nc, stack, n_sp=12, n_pool=6):
        self.nc = nc
        self.sem = {}
        self.stack = stack
        self.epoch = 0
        self.ekey = {e: e for e in self.COMPUTE}
        for e in self.COMPUTE:
            self.sem[e] = stack.enter_context(nc.semaphore("s_" + e))
        self.dma_pool = {"sp": [], "pool": []}
        for i in range(n_sp):
            k = "dsp%d" % i
            self.sem[k] = stack.enter_context(nc.semaphore(k))
            self.dma_pool["sp"].append(k)
        for i in range(n_pool):
            k = "dpl%d" % i
            self.sem[k] = stack.enter_context(nc.semaphore(k))
            self.dma_pool["pool"].append(k)
        self.dma_rr = {"sp": 0, "pool": 0}
        self.cnt = {k: 0 for k in self.sem}
        self.ops = {e: [] for e in ("pe", "dve", "act", "pool", "sp")}
        self.know = {e: {} for e in self.ops}
        self.clock = {}
        self.pe_pending = False

    def _needs(self, reads, writes):
        need = {}
        for r in reads:
            if r.w is not None:
                k, v = r.w
                if need.get(k, 0) < v:
                    need[k] = v
        for w in writes:
            if w.w is not None:
                k, v = w.w
                if need.get(k, 0) < v:
                    need[k] = v
            for k, v in w.r.items():
                if need.get(k, 0) < v:
                    need[k] = v
        return need

    def _waits(self, eng, need):
        know = self.know[eng]
        waits = []
        for k, v in need.items():
            if eng == "pe" and k == self.ekey["pe"]:
                continue
            if know.get(k, 0) >= v:
                continue
            waits.append((k, v))
        for k, v in waits:
            ck = self.clock.get((k, v))
            if ck:
                for kk, vv in ck.items():
                    if know.get(kk, 0) < vv:
                        know[kk] = vv
            if know.get(k, 0) < v:
                know[k] = v
        return waits

    def _record(self, point, reads, writes):
        k, v = point
        for r in reads:
            if r.r.get(k, 0) < v:
                r.r[k] = v
        for w in writes:
            w.w = point
            w.r = {}

    def op(self, eng, fn, reads=(), writes=(), inc=True):
        need = self._needs(reads, writes)
        waits = self._waits(eng, need)
        sk = self.ekey[eng]
        if eng == "pe" and not inc:
            point = (sk, self.cnt[sk] + 1)
            self.pe_pending = True
            self.ops[eng].append((waits, fn, None, 0))
        else:
            self.cnt[sk] += 1
            point = (sk, self.cnt[sk])
            if eng == "pe":
                self.pe_pending = False
            self.ops[eng].append((waits, fn, sk, 1))
            ck = dict(self.know[eng])
            ck[sk] = self.cnt[sk]
            self.clock[point] = ck
        self._record(point, reads, writes)
        return point

    def new_epoch(self):
        self.epoch += 1
        for e in self.COMPUTE:
            k = "%s_%d" % (e, self.epoch)
            self.sem[k] = self.stack.enter_context(self.nc.semaphore("s_" + k))
            self.cnt[k] = 0
            self.ekey[e] = k

    def dma(self, q, fn, reads=(), writes=()):
        pool = self.dma_pool[q]
        sk = pool[self.dma_rr[q] % len(pool)]
        self.dma_rr[q] += 1
        need = self._needs(reads, writes)
        if self.cnt[sk] > 0 and need.get(sk, 0) < self.cnt[sk]:
            need[sk] = self.cnt[sk]
        waits = self._waits(q, need)
        self.cnt[sk] += 16
        point = (sk, self.cnt[sk])
        self.ops[q].append((waits, fn, sk, 16))
        ck = dict(self.know[q])
        ck[sk] = self.cnt[sk]
        self.clock[point] = ck
        self._record(point, reads, writes)
        return point

    def mm(self, out, lhsT, rhs, start, stop, reads, writes, inc=None):
        if inc is None:
            inc = stop
        return self.op("pe", lambda e: e.matmul(out, lhsT=lhsT, rhs=rhs, start=start, stop=stop),
                       reads, writes, inc)

    def tr(self, out, in_, ident, reads, writes, inc=True):
        return self.op("pe", lambda e: e.transpose(out=out, in_=in_, identity=ident), reads, writes, inc)

    def act(self, out, in_, func, reads, writes, bias=None, scale=None, accum=None):
        kw = {}
        if bias is not None:
            kw["bias"] = bias
        if scale is not None:
            kw["scale"] = scale
        if accum is not None:
            kw["accum_out"] = accum
        return self.op("act", lambda e: e.activation(out=out, in_=in_, func=func, **kw), reads, writes)

    def tt(self, eng, out, in0, in1, op, reads, writes):
        return self.op(eng, lambda e: e.tensor_tensor(out=out, in0=in0, in1=in1, op=op), reads, writes)

    def ts(self, eng, out, in0, s1, op0, reads, writes, s2=None, op1=None):
        if op1 is None:
            return self.op(eng, lambda e: e.tensor_scalar(out=out, in0=in0, scalar1=s1, scalar2=None, op0=op0),
                           reads, writes)
        return self.op(eng, lambda e: e.tensor_scalar(out=out, in0=in0, scalar1=s1, scalar2=s2, op0=op0, op1=op1),
                       reads, writes)

    def stt(self, eng, out, in0, scalar, in1, op0, op1, reads, writes):
        return self.op(eng, lambda e: e.scalar_tensor_tensor(out=out, in0=in0, scalar=scalar, in1=in1,
                                                             op0=op0, op1=op1), reads, writes)

    def cp(self, eng, out, in_, reads, writes):
        if eng == "act":
            return self.op("act", lambda e: e.copy(out=out, in_=in_), reads, writes)
        return self.op(eng, lambda e: e.tensor_copy(out=out, in_=in_), reads, writes)

    def recip(self, out, in_, reads, writes):
        return self.op("dve", lambda e: e.reciprocal(out=out, in_=in_), reads, writes)

    def memset(self, eng, ap, val, writes):
        return self.op(eng, lambda e: e.memset(ap, val), (), writes)

    def barrier(self):
        assert not self.pe_pending
        need = {k: v for k, v in self.cnt.items() if v > 0}
        for e in self.ops:
            waits = self._waits(e, dict(need))
            if waits:
                self.ops[e].append((waits, None, None, 0))

    def finish(self, final_points):
        need = {}
        for k, v in final_points:
            if need.get(k, 0) < v:
                need[k] = v
        waits = [(k, v) for k, v in need.items()]
        self.ops["sp"].append((waits, None, None, 0))

    def emit(self):
        nc = self.nc
        assert not self.pe_pending, "PE has trailing non-inc instructions"
        sem = self.sem
        ops = self.ops

        def run(e, lst):
            for waits, fn, sk, inc in lst:
                for k, v in waits:
                    e.wait_ge(sem[k], v)
                if fn is None:
                    continue
                ins = fn(e)
                if sk is not None:
                    ins.then_inc(sem[sk], inc)

        with nc.Block() as block:
            @block.tensor
            def _(e):
                run(e, ops["pe"])

            @block.vector
            def _(e):
                run(e, ops["dve"])

            @block.scalar
            def _(e):
                run(e, ops["act"])

            @block.gpsimd
            def _(e):
                run(e, ops["pool"])

            @block.sync
            def _(e):
                run(e, ops["sp"])


IN_W = 5640
A_QA, A_KA, A_VA = 0, 512, 1024
B_OFF = 1536
WB = IN_W - B_OFF
QKB, VBo, OBo, IBo, GAo, GBo = 0, 1024, 1536, 2048, 2056, 3080
LNC = float(np.log(128.0 ** -0.5))


def build_nc(S, phases=("A", "B", "C")):
    nc = bass.Bass("TRN2", target_bir_lowering=False)
    dr = {}

    def din(name, shape, dt=F32):
        dr[name] = nc.dram_tensor(name, shape, dt, kind="ExternalInput").ap()

    din("x", [S, D])
    din("norm_mix_g", [1, D])
    din("w_in", [D, IN_W])
    din("conv_wT", [D, 4])
    din("bias8", [1, 8])
    din("mlstm_norm_g", [1, 512])
    din("w_proj_a", [512, D])
    din("w_proj_b", [512, D])
    din("w_out", [D, D])
    din("norm_ffn_g", [1, D])
    din("norm_final_g", [1, D])
    din("w_gate_up", [D, 2 * DFF])
    din("w_down", [DFF, D])
    din("consts", [128, 4, 128])
    din("rope", [S, 2, 64])
    din("cmask", [128, 4, 512], BF16)
    din("blkc", [128, 16, 32])
    out_d = nc.dram_tensor("out", [S, D], F32, kind="ExternalOutput").ap()
    h1_d = nc.dram_tensor("h1_s", [S, D], F32, kind="Internal").ap()
    ya_d = nc.dram_tensor("ya_s", [8, 64, S], BF16, kind="Internal").ap()

    with ExitStack() as stack:
        P = Prog(nc, stack)
        ec = stack.enter_context
        identb = ec(nc.sbuf_tensor("identb", [128, 128], BF16))
        cst = ec(nc.sbuf_tensor("cst", [128, 4, 128], F32))
        r_c = Res("consts")
        P.dma("pool", lambda e: e.dma_start(out=identb[:], in_=dr["consts"][:, 0, :]), writes=[r_c])
        P.dma("sp", lambda e: e.dma_start(out=cst[:], in_=dr["consts"]), writes=[r_c])
        G = dict(identb=identb, cst=cst, r_c=r_c)

        r_ya = [[Res("ya_s%d_%d" % (i, h)) for h in range(8)] for i in range(S // 256)]
        r_h1 = [Res("h1_s%d" % i) for i in range(S // 256)]
        if "A" in phases:
            phase_A(nc, P, S, dr, ya_d, r_ya, G)
        else:
            with nc.sbuf_tensor("zt", [64, S], BF16) as zt:
                rz = Res("zt")
                P.memset("dve", zt[:], 0.0, [rz])
                for h in range(8):
                    P.dma("sp", lambda e, h=h: e.dma_start(out=ya_d[h], in_=zt[:]), reads=[rz],
                          writes=[r_ya[i][h] for i in range(S // 256)])
                P.barrier()
        P.barrier()
        P.new_epoch()
        if "B" in phases:
            phase_B(nc, P, S, dr, h1_d, ya_d, r_ya, r_h1, G)
            h_src = h1_d
        else:
            h_src = dr["x"]
        P.barrier()
        P.new_epoch()
        final_pts = phase_C(nc, P, S, h_src, r_h1, out_d, dr, G)
        P.finish(final_pts)
        P.emit()
    return nc


def load_norm_tile(nc, P, xt, r_x, s, gM, r_g, st8, r_st, col, epsb, r_eps, junk, r_junk, ub, r_ub):
    P.act(junk[:], xt[:, s, :], AF.Square, [r_x], [r_junk, r_st[col]], accum=st8[:, col:col + 1])
    P.act(st8[:, col:col + 1], st8[:, col:col + 1], AF.Sqrt, [r_st[col], r_eps], [r_st[col]],
          bias=epsb[:], scale=1.0 / D)
    P.recip(st8[:, col:col + 1], st8[:, col:col + 1], [r_st[col]], [r_st[col]])
    P.stt("dve", ub[:], xt[:, s, :], st8[:, col:col + 1], gM[:], ALU.mult, ALU.mult,
          [r_x, r_st[col], r_g], [r_ub])


def phase_A(nc, P, S, dr, ya_d, r_ya, G):
    T = 512
    NT = S // T
    NKT = S // 128
    identb, cst, r_c = G["identb"], G["cst"], G["r_c"]
    with ExitStack() as st:
        ec = st.enter_context

        def sbt(name, shape, dt):
            return ec(nc.sbuf_tensor(name, shape, dt))

        wA = sbt("wA", [128, 8, 1536], BF16)
        gM = sbt("gMa", [128, D], F32)
        cmk = sbt("cmk", [128, 4, 512], BF16)
        blkc = sbt("blkc_sb", [128, 16, 32], F32)
        epsb = sbt("epsb_a", [128, 1], F32)
        ropeT = [sbt("ropeT%d" % i, [128, 4, 2, 64], F32) for i in range(2)]
        kTa = sbt("kTa", [80, 8, S], BF16)
        vA = sbt("vA", [128, NKT, 8, 65], BF16)
        kmf = sbt("kmf", [64, 8], F32)
        kmT = sbt("kmT", [64, 8, 16], BF16)
        xs = [sbt("xs%d" % i, [128, 1, D], F32) for i in range(2)]
        ub = sbt("ub_a", [128, D], BF16)
        junk = sbt("junk_a", [128, D], BF16)
        st8 = sbt("st8_a", [128, 8], F32)
        uT = sbt("uTa", [128, 8, T], BF16)
        t1 = sbt("t1", [128, 8, 64], F32)
        t2 = sbt("t2", [128, 8, 64], F32)
        qtok = [sbt("qtok%d" % i, [128, 8, 80], BF16) for i in range(2)]
        ktok = [sbt("ktok%d" % i, [128, 8, 80], BF16) for i in range(2)]
        qTa = sbt("qTa", [80, 8, T], BF16)
        gsb = sbt("gsb_a", [128, 8, 16], F32)
        top8 = sbt("top8", [128, 8, 8], F32)
        PT = [sbt("PT%d" % i, [128, 512], BF16) for i in range(4)]
        oT = [sbt("oT%d" % i, [65, 512], F32) for i in range(2)]
        yo = [sbt("yo%d" % i, [64, 512], BF16) for i in range(2)]
        PB = [ec(nc.psum_tensor("PBa%d" % i, [128, 512], F32)) for i in range(8)]
        PBb = [PB[i].bitcast(BF16) for i in range(8)]

        R = lambda n: Res(n)
        r_wA = [R("wA%d" % c) for c in range(8)]
        r_k = R("constsA")
        r_rope = [R("rope%d" % i) for i in range(2)]
        r_kTa = [R("kTa%d" % i) for i in range(NKT)]
        r_vA = [R("vA%d" % i) for i in range(NKT)]
        r_kmf, r_kmT = R("kmf"), R("kmT")
        r_xs = [R("xs%d" % i) for i in range(2)]
        r_ub, r_junk = R("ub"), R("junk")
        r_st = [R("st%d" % i) for i in range(8)]
        r_uT = [R("uT%d" % s) for s in range(4)]
        r_t1, r_t2 = R("t1"), R("t2")
        r_qtok = [R("qtok%d" % i) for i in range(2)]
        r_ktok = [R("ktok%d" % i) for i in range(2)]
        r_qTa = [R("qTa%d" % s) for s in range(4)]
        r_gsb, r_top8 = R("gsb"), R("top8")
        r_PT = [R("PT%d" % i) for i in range(4)]
        r_oT = [R("oT%d" % i) for i in range(2)]
        r_yo = [R("yo%d" % i) for i in range(2)]
        r_pb = [R("pba%d" % i) for i in range(8)]

        wv = dr["w_in"].rearrange("(c p) f -> p c f", p=128)
        for c in range(8):
            P.dma("pool", lambda e, c=c: e.dma_start(out=wA[:, c, :], in_=wv[:, c, 0:1536]), writes=[r_wA[c]])
        P.dma("sp", lambda e: e.dma_start(out=gM[:], in_=dr["norm_mix_g"].to_broadcast([128, D])), writes=[r_k])
        P.dma("sp", lambda e: e.dma_start(out=cmk[:], in_=dr["cmask"]), writes=[r_k])
        P.dma("sp", lambda e: e.dma_start(out=blkc[:], in_=dr["blkc"]), writes=[r_k])
        P.memset("dve", epsb[:], EPS, [r_k])
        P.memset("dve", kmT[:], 0.0, [r_kmT])
        P.memset("dve", vA[:], 1.0, r_vA)

        xv = dr["x"].rearrange("(n p) d -> n p d", p=128)
        rope_v = dr["rope"].rearrange("(n s p) a d -> n p s a d", s=4, p=128)
        onesb = sbt("onesb", [128, 64], BF16)
        hiT = [sbt("hiT%d" % i, [65, 512], BF16) for i in range(2)]
        loT = [sbt("loT%d" % i, [65, 512], BF16) for i in range(2)]
        qTb = sbt("qTa2", [80, 8, T], BF16)
        qTas = [qTa, qTb]
        r_qTas = [r_qTa, [R("qTb%d" % s) for s in range(4)]]
        r_hl = [R("hl%d" % i) for i in range(2)]
        P.memset("dve", onesb[:], 1.0, [r_k])
        P.dma("sp", lambda e: e.dma_start(out=xs[0][:, 0, :], in_=xv[0]), writes=[r_xs[0]])

        def rope_ops(pz, rpz, rb, s, dst, rdst):
            z3 = pz.rearrange("p (h d) -> p h d", h=8)
            C2 = ropeT[rb][:, s, 0, :].unsqueeze(1).to_broadcast([128, 8, 64])
            Sa = ropeT[rb][:, s, 1, 0:32].unsqueeze(1).to_broadcast([128, 8, 32])
            Sb = ropeT[rb][:, s, 1, 32:64].unsqueeze(1).to_broadcast([128, 8, 32])
            P.tt("dve", t1[:], z3, C2, ALU.mult, [rpz, r_rope[rb]], [r_t1])
            P.tt("dve", t2[:, :, 0:32], z3[:, :, 32:64], Sa, ALU.mult, [rpz, r_rope[rb]], [r_t2])
            P.tt("dve", t2[:, :, 32:64], z3[:, :, 0:32], Sb, ALU.mult, [rpz, r_rope[rb]], [r_t2])
            P.tt("dve", dst[:, :, 0:64], t1[:], t2[:], ALU.add, [r_t1, r_t2], [rdst])

        def tr_group(src_fn, rsrc, rows, dst_fn, rdst, prow=None):
            for g4 in range(2):
                for j in range(4):
                    P.tr(PBb[g4][0:rows, j * 128:(j + 1) * 128], src_fn(g4 * 4 + j), identb[:], [rsrc, r_c],
                         [r_pb[g4]], inc=(j == 3))
            for g4 in range(2):
                lo_, hi_ = prow if prow else (0, rows)
                P.cp("dve", dst_fn(g4, lo_, hi_), PBb[g4][lo_:hi_, 0:512].rearrange("p (j t) -> p j t", j=4),
                     [r_pb[g4]], [rdst])

        def prologue(t):
            rb = t % 2
            qT_ = qTas[t % 2]
            rq_ = r_qTas[t % 2]
            P.dma("sp", lambda e, t=t, rb=rb: e.dma_start(out=ropeT[rb][:], in_=rope_v[t]), writes=[r_rope[rb]])
            for s in range(4):
                n = t * 4 + s
                sub = slice(s * 128, (s + 1) * 128)
                ksub = slice(n * 128, (n + 1) * 128)
                blk = n // 2
                if n + 1 < NKT:
                    P.dma("sp", lambda e, n=n: e.dma_start(out=xs[(n + 1) % 2][:, 0, :], in_=xv[n + 1]),
                          writes=[r_xs[(n + 1) % 2]])
                xt = xs[n % 2]
                load_norm_tile(nc, P, xt, r_xs[n % 2], 0, gM, r_k, st8, r_st, s, epsb, r_k, junk, r_junk, ub, r_ub)
                yield
                for g4 in range(2):
                    for j in range(4):
                        c = g4 * 4 + j
                        P.tr(PBb[g4][:, j * 128:(j + 1) * 128], ub[:, c * 128:(c + 1) * 128], identb[:],
                             [r_ub, r_c], [r_pb[g4]], inc=(j == 3))
                for g4 in range(2):
                    P.cp("dve", uT[:, g4 * 4:(g4 + 1) * 4, sub],
                         PBb[g4][:, 0:512].rearrange("p (j t) -> p j t", j=4), [r_pb[g4]], [r_uT[s]])
                yield
                kb, rkb = ktok[s % 2], r_ktok[s % 2]
                pz = PB[5][:]
                for c in range(8):
                    P.mm(pz, uT[:, c, sub], wA[:, c, A_KA:A_KA + 512], c == 0, c == 7, [r_uT[s], r_wA[c]], [r_pb[5]])
                rope_ops(pz, r_pb[5], rb, s, kb, rkb)
                P.cp("dve", kb[:, :, 64:80], blkc[:, blk, 16:32].unsqueeze(1).to_broadcast([128, 8, 16]),
                     [r_k], [rkb])
                yield
                tr_group(lambda h: kb[:, h, :], rkb, 80,
                         lambda g4, lo_, hi_: kTa[lo_:hi_, g4 * 4:(g4 + 1) * 4, ksub], r_kTa[n])
                if s % 2 == 1:
                    P.op("dve", lambda e, blk=blk: e.tensor_reduce(
                        out=kmf[:], in_=kTa[0:64, :, blk * 256:(blk + 1) * 256], axis=AX.X, op=ALU.add),
                        [r_kTa[n - 1], r_kTa[n]], [r_kmf])
                    P.ts("dve", kmT[:, :, blk], kmf[:], 1.0 / 256, ALU.mult, [r_kmf], [r_kmT])
                yield
                for c in range(8):
                    P.mm(pz, uT[:, c, sub], wA[:, c, A_VA:A_VA + 512], c == 0, c == 7, [r_uT[s], r_wA[c]], [r_pb[5]])
                P.cp("dve", vA[:, n, :, 0:64], pz.rearrange("p (h d) -> p h d", h=8), [r_pb[5]], [r_vA[n]])
                yield
                qb, rqb = qtok[s % 2], r_qtok[s % 2]
                for c in range(8):
                    P.mm(pz, uT[:, c, sub], wA[:, c, A_QA:A_QA + 512], c == 0, c == 7, [r_uT[s], r_wA[c]], [r_pb[5]])
                rope_ops(pz, r_pb[5], rb, s, qb, rqb)
                yield
                tr_group(lambda h: qb[:, h, 0:64], rqb, 64,
                         lambda g4, lo_, hi_: qT_[lo_:hi_, g4 * 4:(g4 + 1) * 4, sub], rq_[s])
                yield
                pgt = PB[5][:, 0:128].rearrange("p (h j) -> p h j", h=8)
                for h in range(8):
                    P.mm(pgt[:, h, :], qT_[0:64, h, sub], kmT[:, h, :], True, True, [rq_[s], r_kmT], [r_pb[5]],
                         inc=(h == 7))
                P.tt("dve", gsb[:], pgt, blkc[:, blk, 0:16].unsqueeze(1).to_broadcast([128, 8, 16]), ALU.add,
                     [r_pb[5], r_k], [r_gsb])
                for h in range(8):
                    P.op("dve", lambda e, h=h: e.max(out=top8[:, h, :], in_=gsb[:, h, :]), [r_gsb], [r_top8])
                P.tt("dve", qb[:, :, 64:80], gsb[:], top8[:, :, 3:4].to_broadcast([128, 8, 16]), ALU.is_lt,
                     [r_gsb, r_top8], [rqb])
                yield
                tr_group(lambda h: qb[:, h, :], rqb, 80,
                         lambda g4, lo_, hi_: qT_[lo_:hi_, g4 * 4:(g4 + 1) * 4, sub], rq_[s], prow=(64, 80))
                yield

        def attention(t):
            qT_ = qTas[t % 2]
            rq_ = r_qTas[t % 2]
            nkt = 4 * t + 4
            items = [(h, kt) for h in range(8) for kt in range(nkt)]
            LA = 2
            deferred = {}

            def emit_S(i):
                h, kt = items[i]
                bi = 2 + i % 3
                q0 = 128 * max(0, kt - 4 * t)
                P.mm(PB[bi][:, q0:512], kTa[:, h, kt * 128:(kt + 1) * 128], qT_[:, h, q0:512], True, True,
                     [r_kTa[kt]] + rq_, [r_pb[bi]])

            def emit_PV(i):
                h, kt = items[i]
                bi = 2 + i % 3
                pi = i % 4
                po = PB[6 + h % 2]
                rpo = r_pb[6 + h % 2]
                q0 = 128 * max(0, kt - 4 * t)
                P.act(PT[pi][:, q0:512], PB[bi][:, q0:512], AF.Exp, [r_pb[bi]], [r_PT[pi]], scale=0.125)
                if kt >= 4 * t:
                    P.tt("pool", PT[pi][:, q0:512], PT[pi][:, q0:512], cmk[:, kt - 4 * t, q0:512], ALU.mult,
                         [r_PT[pi], r_k], [r_PT[pi]])
                P.mm(po[0:65, q0:512], vA[:, kt, h, :], PT[pi][:, q0:512], kt == 0, kt == nkt - 1,
                     [r_vA[kt], r_PT[pi]], [rpo])
                if kt == nkt - 1:
                    ob, rob = oT[h % 2], r_oT[h % 2]
                    P.cp("dve", ob[:], po[0:65, :], [rpo], [rob])
                    P.recip(ob[64:65, :], ob[64:65, :], [rob], [rob])
                    P.cp("dve", hiT[h % 2][64:65, :], ob[64:65, :], [rob], [r_hl[h % 2]])
                    P.tt("dve", loT[h % 2][64:65, :], ob[64:65, :], hiT[h % 2][64:65, :], ALU.subtract,
                         [rob, r_hl[h % 2]], [r_hl[h % 2]])
                    deferred.setdefault(min(i + 4, len(items) - 1), []).append(h)

            def emit_epi(h):
                ob, rob = oT[h % 2], r_oT[h % 2]
                P.mm(PB[5][0:64, :], onesb[64:65, :], hiT[h % 2][64:65, :], True, False, [r_k, r_hl[h % 2]],
                     [r_pb[5]], inc=False)
                P.mm(PB[5][0:64, :], onesb[64:65, :], loT[h % 2][64:65, :], False, True, [r_k, r_hl[h % 2]],
                     [r_pb[5]])
                yb_ = yo[h % 2]
                P.tt("dve", yb_[:], ob[0:64, :], PB[5][0:64, :], ALU.mult, [rob, r_pb[5]], [r_yo[h % 2]])
                P.dma("sp", lambda e, h=h, t=t, yb_=yb_: e.dma_start(out=ya_d[h, :, t * T:(t + 1) * T], in_=yb_[:]),
                      reads=[r_yo[h % 2]], writes=[r_ya[2 * t][h], r_ya[2 * t + 1][h]])

            for i in range(min(LA, len(items))):
                emit_S(i)
            for i in range(len(items)):
                if i + LA < len(items):
                    emit_S(i + LA)
                emit_PV(i)
                for hh in deferred.pop(i, []):
                    emit_epi(hh)
                yield
            assert not deferred

        for _ in prologue(0):
            pass
        for t in range(NT):
            n_main = 8 * (4 * t + 4)
            side = prologue(t + 1) if t + 1 < NT else iter(())
            n_side = 36
            done = 0
            for i, _ in enumerate(attention(t)):
                want = ((i + 1) * n_side) // n_main
                while done < want:
                    next(side, None)
                    done += 1
            for _ in side:
                pass


def phase_B(nc, P, S, dr, h1_d, ya_d, r_ya, r_h1, G):
    T = 256
    NT = S // T
    identb, cst, r_c = G["identb"], G["cst"], G["r_c"]
    idf, tri, trim, onesf = cst[:, 0, :], cst[:, 1, :], cst[:, 2, :], cst[:, 3, :]
    with ExitStack() as st:
        ec = st.enter_context

        def sbt(name, shape, dt):
            return ec(nc.sbuf_tensor(name, shape, dt))

        wB = sbt("wB", [128, 8, WB], BF16)
        wpa = sbt("wpa", [128, 4, D], BF16)
        wpb = sbt("wpb", [128, 4, D], BF16)
        wo = sbt("wo", [128, 8, D], BF16)
        gM = sbt("gMb", [128, D], F32)
        gN = sbt("gN", [128, 512], F32)
        bias8 = sbt("bias8_sb", [128, 8], F32)
        cw = sbt("cw", [128, 8, 4], F32)
        trib = sbt("trib", [128, 128], BF16)
        epsb = sbt("epsb_b", [128, 1], F32)
        oneb = sbt("oneb", [128, 1], F32)
        lncb = sbt("lncb", [128, 1], F32)
        lnhb = sbt("lnhb", [128, 1], F32)
        xb = [sbt("xb%d" % i, [128, 2, D], F32) for i in range(3)]
        yaT = [sbt("yaT%d" % i, [128, 4, T], BF16) for i in range(3)]
        uT = [sbt("uTb%d" % i, [128, 8, T], BF16) for i in range(2)]
        ub2 = [sbt("ub_b%d" % i, [128, D], BF16) for i in range(2)]
        junk = sbt("junk_b", [128, D], BF16)
        st8 = sbt("st8_b", [128, 8], F32)
        gsb = sbt("gsb", [128, 2, 8], F32)
        e1 = sbt("e1", [128, 2, 4], F32)
        nlf = sbt("nlf", [128, 2, 4], F32)
        arg = sbt("arg", [128, 2, 4], F32)
        ksc = sbt("ksc", [128, 2, 4], F32)
        wc = sbt("wc", [128, 2, 4], F32)
        EBt = sbt("EBt", [128, 4, T], F32)
        EAt = sbt("EAt", [128, 4, T], F32)
        xc = sbt("xc", [128, 8, T + 3], F32)
        yc = [sbt("yc%d" % i, [128, T], F32) for i in range(2)]
        slq = [sbt("slq%d" % i, [128, T], BF16) for i in range(2)]
        kT0 = sbt("kT0", [128, 4, T], BF16)
        qT = sbt("qTb", [128, 4, T], BF16)
        kT = sbt("kTb", [128, 4, T], BF16)
        vaug = sbt("vaug", [128, 2, 4, 129], BF16)
        sgo = sbt("sgo", [128, 2, 512], F32)
        kS = sbt("kS", [128, 4, 128], BF16)
        scT = sbt("scT", [128, 4, 128], BF16)
        C32 = sbt("C32", [128, 4, 129], F32)
        Cbf = sbt("Cbf", [128, 4, 129], BF16)
        numS = [sbt("numS%d" % i, [128, 4, 129], F32) for i in range(2)]
        r_numS = [[Res("numS%d_%d" % (i, h)) for h in range(4)] for i in range(2)]
        rr = sbt("rr", [128, 4], F32)
        ssq = sbt("ssq", [128, 4], F32)
        t4 = sbt("t4", [128, 4], F32)
        sc4 = sbt("sc4", [128, 4], F32)
        ybf = sbt("ybf", [128, 512], F32)
        ybb2 = [sbt("ybb%d" % i, [128, 512], BF16) for i in range(2)]
        ybT = sbt("ybT", [128, 4, T], BF16)
        sga = [sbt("sga%d" % i, [128, T], F32) for i in range(1)]
        sgb = [sbt("sgb%d" % i, [128, T], F32) for i in range(1)]
        m1 = [sbt("m1_%d" % i, [128, T], F32) for i in range(1)]
        m2 = [sbt("m2_%d" % i, [128, T], F32) for i in range(1)]
        mT = sbt("mT", [128, 8, T], BF16)
        PB = [ec(nc.psum_tensor("PBb%d" % i, [128, 512], F32)) for i in range(8)]

        R = lambda n: Res(n)
        r_wB = [R("wB%d" % c) for c in range(8)]
        r_wpa, r_wpb, r_wo = R("wpa"), R("wpb"), [R("wo%d" % c) for c in range(8)]
        r_k = R("constsB")
        r_xb = [[R("xb%d_%d" % (i, s)) for s in range(2)] for i in range(3)]
        r_yaT = [R("yaT%d" % i) for i in range(3)]
        r_uT = [[R("uTb%d_%d" % (i, s)) for s in range(2)] for i in range(2)]
        r_ub2, r_junk = [R("ub0"), R("ub1")], R("junk")
        r_st = [R("st%d" % i) for i in range(8)]
        r_gs = [R("gsb%d" % s) for s in range(2)]
        r_e1 = [R("e1%d" % s) for s in range(2)]
        r_nlf = [R("nlf%d" % s) for s in range(2)]
        r_arg = [R("arg%d" % s) for s in range(2)]
        r_ksc = [R("ksc%d" % s) for s in range(2)]
        r_wc = [R("wc%d" % s) for s in range(2)]
        r_EB = [[R("EB%d_%d" % (h, s)) for s in range(2)] for h in range(4)]
        r_EA = [[R("EA%d_%d" % (h, s)) for s in range(2)] for h in range(4)]
        r_xc = [R("xc%d" % m) for m in range(8)]
        r_yc = [R("yc%d" % i) for i in range(2)]
        r_slq = [R("slq%d" % i) for i in range(2)]
        r_kT0 = [R("kT0%d" % h) for h in range(4)]
        r_qT = [R("qT%d" % h) for h in range(4)]
        r_kT = [R("kT%d" % h) for h in range(4)]
        r_va = [R("vaug%d" % s) for s in range(2)]
        r_sgo = [R("sgo%d" % s) for s in range(2)]
        r_kS = [R("kS%d" % h) for h in range(4)]
        r_scT = [R("scT%d" % h) for h in range(4)]
        r_C32 = [R("C32%d" % h) for h in range(4)]
        r_Cbf = [R("Cbf%d" % h) for h in range(4)]
        r_rr, r_ssq, r_t4, r_sc4 = R("rr"), R("ssq"), R("t4"), R("sc4")
        r_ybf, r_ybb2 = R("ybf"), [R("ybb0"), R("ybb1")]
        r_ybT = [R("ybT%d" % s) for s in range(2)]
        r_sga = [R("sga%d" % i) for i in range(2)]
        r_sgb = [R("sgb%d" % i) for i in range(2)]
        r_m1 = [R("m1%d" % i) for i in range(2)]
        r_m2 = [R("m2%d" % i) for i in range(2)]
        r_mT = [R("mT%d" % f) for f in range(8)]
        r_pb = [R("pb%d" % i) for i in range(8)]
        PBb = [PB[i].bitcast(BF16) for i in range(8)]

        wv = dr["w_in"].rearrange("(c p) f -> p c f", p=128)
        r_wB2 = [R("wB2_%d" % c) for c in range(8)]
        for c in range(8):
            P.dma("pool", lambda e, c=c: e.dma_start(out=wB[:, c, 0:GAo], in_=wv[:, c, B_OFF:B_OFF + GAo]),
                  writes=[r_wB[c]])
        for c in range(8):
            P.dma("pool", lambda e, c=c: e.dma_start(out=wB[:, c, GAo:WB], in_=wv[:, c, B_OFF + GAo:IN_W]),
                  writes=[r_wB2[c]])
        P.dma("pool", lambda e: e.dma_start(out=wpa[:], in_=dr["w_proj_a"].rearrange("(c p) f -> p c f", p=128)),
              writes=[r_wpa])
        P.dma("pool", lambda e: e.dma_start(out=wpb[:], in_=dr["w_proj_b"].rearrange("(c p) f -> p c f", p=128)),
              writes=[r_wpb])
        wov = dr["w_out"].rearrange("(c p) f -> p c f", p=128)
        for c in range(8):
            P.dma("pool", lambda e, c=c: e.dma_start(out=wo[:, c, :], in_=wov[:, c, :]), writes=[r_wo[c]])
        P.dma("pool", lambda e: e.dma_start(out=trib[:], in_=dr["consts"][:, 1, :]), writes=[r_k])
        P.dma("sp", lambda e: e.dma_start(out=gM[:], in_=dr["norm_mix_g"].to_broadcast([128, D])), writes=[r_k])
        P.dma("sp", lambda e: e.dma_start(out=gN[:], in_=dr["mlstm_norm_g"].to_broadcast([128, 512])), writes=[r_k])
        P.dma("sp", lambda e: e.dma_start(out=bias8[:], in_=dr["bias8"].to_broadcast([128, 8])), writes=[r_k])
        P.dma("sp", lambda e: e.dma_start(out=cw[:], in_=dr["conv_wT"].rearrange("(m p) j -> p m j", p=128)),
              writes=[r_k])
        P.ts("dve", gN[:], gN[:], 0.5, ALU.mult, [r_k], [r_k])
        P.memset("dve", epsb[:], EPS, [r_k])
        P.memset("dve", oneb[:], 1.0, [r_k])
        P.memset("dve", lncb[:], LNC + float(np.log(0.5)), [r_k])
        P.memset("dve", lnhb[:], float(np.log(0.5)), [r_k])
        P.memset("dve", C32[:], 0.0, r_C32)
        P.memset("dve", Cbf[:], 0.0, r_Cbf)
        P.memset("dve", vaug[:], 1.0, r_va)
        P.memset("dve", xc[:], 0.0, r_xc)

        xv = dr["x"].rearrange("(t s p) d -> t p s d", s=2, p=128)
        hv = h1_d.rearrange("(t s p) d -> t p s d", s=2, p=128)
        ya_v = ya_d.rearrange("h d s -> (h d) s").rearrange("(c p) s -> p c s", p=128)

        ybT2 = sbt("ybT2", [128, 4, T], BF16)
        ybTs = [ybT, ybT2]
        r_ybTs = [r_ybT, [R("ybTb%d" % s) for s in range(2)]]

        def stage_A(t):
            b = t % 2
            xt = xb[t % 3]
            uTt = uT[b]
            for s in range(2):
                load_norm_tile(nc, P, xt, r_xb[t % 3][s], s, gM, r_k, st8, r_st, s, epsb, r_k, junk, r_junk,
                               ub2[s], r_ub2[s])
            yield
            for s in range(2):
                sub = slice(s * 128, (s + 1) * 128)
                ub, r_ub = ub2[s], r_ub2[s]
                for g4 in range(2):
                    for j in range(4):
                        c = g4 * 4 + j
                        P.tr(PBb[2 + g4][:, j * 128:(j + 1) * 128], ub[:, c * 128:(c + 1) * 128],
                             identb[:], [r_ub, r_c], [r_pb[2 + g4]], inc=(j == 3))
                for g4 in range(2):
                    P.cp("act", uTt[:, g4 * 4:(g4 + 1) * 4, sub],
                         PBb[2 + g4][:, 0:512].rearrange("p (j t) -> p j t", j=4),
                         [r_pb[2 + g4]], [r_uT[b][s]])
                yield
                pg = PB[4][:, 0:8]
                for c in range(8):
                    P.mm(pg, uTt[:, c, sub], wB[:, c, IBo:IBo + 8], c == 0, c == 7,
                         [r_uT[b][s], r_wB[c]], [r_pb[4]])
                P.tt("dve", gsb[:, s, :], pg, bias8[:], ALU.add, [r_pb[4], r_k], [r_gs[s]])
                P.act(e1[:, s, :], gsb[:, s, 4:8], AF.Exp, [r_gs[s]], [r_e1[s]], scale=-1.0)
                P.act(nlf[:, s, :], e1[:, s, :], AF.Ln, [r_e1[s], r_k], [r_nlf[s]], bias=oneb[:])
                pv = PB[2][:]
                for c in range(8):
                    P.mm(pv, uTt[:, c, sub], wB[:, c, VBo:VBo + 512], c == 0, c == 7,
                         [r_uT[b][s], r_wB[c]], [r_pb[2]])
                P.cp("act", vaug[:, s, :, 0:128], pv.rearrange("p (h e) -> p h e", h=4), [r_pb[2]], [r_va[s]])
                yield
                po = PB[3][:]
                for c in range(8):
                    P.mm(po, uTt[:, c, sub], wB[:, c, OBo:OBo + 512], c == 0, c == 7,
                         [r_uT[b][s], r_wB[c]], [r_pb[3]])
                P.act(sgo[:, s, :], po, AF.Tanh, [r_pb[3]], [r_sgo[s]], scale=0.5)
                P.mm(PB[4][:, 8:12], trim, nlf[:, s, :], True, True, [r_c, r_nlf[s]], [r_pb[4]], inc=False)
                P.mm(PB[4][:, 12:16], onesf, nlf[:, s, :], True, True, [r_c, r_nlf[s]], [r_pb[4]])
                P.tt("dve", arg[:, s, :], PB[4][:, 8:12], gsb[:, s, 0:4], ALU.add, [r_pb[4], r_gs[s]], [r_arg[s]])
                P.act(ksc[:, s, :], arg[:, s, :], AF.Exp, [r_arg[s], r_k], [r_ksc[s]], bias=lncb[:])
                P.act(wc[:, s, :], PB[4][:, 12:16], AF.Exp, [r_pb[4]], [r_wc[s]], scale=-1.0)
                yield
                for h in range(4):
                    i0 = (2 * h) % 3
                    i1 = (2 * h + 1) % 3
                    pe_b = PB[5 + i0][:, 0:128]
                    pe_a = PB[5 + i1][:, 0:128]
                    nb_l = nlf[:, s, h:h + 1].to_broadcast([128, 128])
                    ip_l = gsb[:, s, h:h + 1].to_broadcast([128, 128])
                    P.mm(pe_b, nb_l, tri, True, True, [r_nlf[s], r_c], [r_pb[5 + i0]])
                    P.mm(pe_a, ip_l, idf, True, False, [r_gs[s], r_c], [r_pb[5 + i1]], inc=False)
                    P.mm(pe_a, nb_l, tri, False, True, [r_nlf[s], r_c], [r_pb[5 + i1]])
                    P.act(EBt[:, h, sub], pe_b, AF.Exp, [r_pb[5 + i0], r_k], [r_EB[h][s]], scale=-1.0, bias=lnhb[:])
                    P.act(EAt[:, h, sub], pe_a, AF.Exp, [r_pb[5 + i1], r_k], [r_EA[h][s]], bias=lncb[:])
                    if h % 2 == 1:
                        yield
            for m in range(8):
                hh = m % 4
                pq = PB[2 + m % 2][:, 0:256]
                for c in range(8):
                    P.mm(pq, wB[:, c, QKB + m * 128:QKB + (m + 1) * 128], uTt[:, c, :], c == 0, c == 7,
                         [r_wB[c]] + r_uT[b], [r_pb[2 + m % 2]])
                P.cp("act", xc[:, m, 3:3 + T], pq, [r_pb[2 + m % 2]], [r_xc[m]])
                y = yc[m % 2]
                ry = r_yc[m % 2]
                P.ts("dve", y[:], xc[:, m, 0:T], cw[:, m, 0:1], ALU.mult, [r_xc[m], r_k], [ry])
                for j in range(1, 4):
                    P.stt("dve", y[:], xc[:, m, j:j + T], cw[:, m, j:j + 1], y[:], ALU.mult, ALU.add,
                          [r_xc[m], r_k, ry], [ry])
                P.cp("dve", xc[:, m, 0:3], xc[:, m, T:T + 3], [r_xc[m]], [r_xc[m]])
                if m < 4:
                    sl = slq[m % 2]
                    P.act(sl[:], y[:], AF.Tanh, [ry], [r_slq[m % 2]], scale=0.5)
                    P.stt("dve", y[:], sl[:], 1.0, y[:], ALU.add, ALU.mult, [r_slq[m % 2], ry], [ry])
                    P.tt("dve", qT[:, hh, :], y[:], EBt[:, hh, :], ALU.mult, [ry] + r_EB[hh], [r_qT[hh]])
                else:
                    sl = slq[m % 2]
                    P.act(sl[:], y[:], AF.Tanh, [ry], [r_slq[m % 2]], scale=0.5)
                    P.stt("dve", kT0[:, hh, :], sl[:], 1.0, y[:], ALU.add, ALU.mult, [r_slq[m % 2], ry], [r_kT0[hh]])
                    P.tt("dve", kT[:, hh, :], kT0[:, hh, :], EAt[:, hh, :], ALU.mult,
                         [r_kT0[hh]] + r_EA[hh], [r_kT[hh]])
                yield
            for s in range(2):
                sub = slice(s * 128, (s + 1) * 128)
                pnum = [PB[6][:, 0:129], PB[7][:, 0:129], PB[6][:, 256:385], PB[7][:, 256:385]]
                r_pn = [r_pb[6], r_pb[7], r_pb[6], r_pb[7]]
                pdc = [PB[2][:, 256:385], PB[3][:, 256:385], PB[2][:, 256:385], PB[3][:, 256:385]]
                r_pd = [r_pb[2], r_pb[3], r_pb[2], r_pb[3]]
                for h in range(4):
                    ptk = PBb[2 + h % 2][:, 0:128]
                    P.tr(ptk, kT0[:, h, sub], identb[:], [r_kT0[h], r_c], [r_pb[2 + h % 2]])
                    pS = PB[4 + h % 2][:, 0:128]
                    P.mm(pS, kT[:, h, sub], qT[:, h, sub], True, True, [r_kT[h], r_qT[h]], [r_pb[4 + h % 2]])
                    P.ts("dve", kS[:, h, :], ptk, ksc[:, s, h:h + 1], ALU.mult, [r_pb[2 + h % 2], r_ksc[s]],
                         [r_kS[h]])
                    P.tt("dve", scT[:, h, :], pS, trib[:], ALU.mult, [r_pb[4 + h % 2], r_k], [r_scT[h]])
                    yield
                    P.mm(pnum[h], qT[:, h, sub], Cbf[:, h, :], True, False, [r_qT[h], r_Cbf[h]], [r_pn[h]],
                         inc=False)
                    P.mm(pnum[h], scT[:, h, :], vaug[:, s, h, :], False, True, [r_scT[h], r_va[s]], [r_pn[h]])
                    P.cp("act", numS[s][:, h, :], pnum[h], [r_pn[h]], [r_numS[s][h]])
                    P.mm(pdc[h], kS[:, h, :], vaug[:, s, h, :], True, True, [r_kS[h], r_va[s]], [r_pd[h]])
                    P.stt("dve", C32[:, h, :], C32[:, h, :], wc[:, s, h:h + 1], pdc[h], ALU.mult, ALU.add,
                          [r_C32[h], r_wc[s], r_pd[h]], [r_C32[h]])
                    P.cp("act", Cbf[:, h, :], C32[:, h, :], [r_C32[h]], [r_Cbf[h]])
                    yield
                for h in range(4):
                    P.act(junk[:, 0:128], numS[s][:, h, 0:128], AF.Square, [r_numS[s][h]], [r_junk, r_ssq],
                          accum=ssq[:, h:h + 1])
                P.stt("dve", rr[:], numS[s][:, :, 128], -1.0, numS[s][:, :, 128], ALU.mult, ALU.max,
                      r_numS[s], [r_rr])
                P.ts("dve", rr[:], rr[:], 1.0, ALU.max, [r_rr], [r_rr])
                P.recip(rr[:], rr[:], [r_rr], [r_rr])
                P.tt("dve", t4[:], rr[:], rr[:], ALU.mult, [r_rr], [r_t4])
                P.tt("dve", t4[:], t4[:], ssq[:], ALU.mult, [r_t4, r_ssq], [r_t4])
                P.act(t4[:], t4[:], AF.Sqrt, [r_t4, r_k], [r_t4], bias=epsb[:], scale=1.0 / 128)
                P.recip(t4[:], t4[:], [r_t4], [r_t4])
                P.tt("dve", sc4[:], t4[:], rr[:], ALU.mult, [r_t4, r_rr], [r_sc4])
                yield
                for h in range(4):
                    P.stt("dve", ybf[:, h * 128:(h + 1) * 128], numS[s][:, h, 0:128], sc4[:, h:h + 1],
                          gN[:, h * 128:(h + 1) * 128], ALU.mult, ALU.mult, [r_numS[s][h], r_sc4, r_k], [r_ybf])
                P.stt("dve", ybb2[s][:], sgo[:, s, :], 1.0, ybf[:], ALU.add, ALU.mult, [r_ybf, r_sgo[s]], [r_ybb2[s]])
                yield
                if s == 1:
                    yb_transposes(t, 0)
                    yield

        def yb_transposes(t, s):
            b = t % 2
            sub = slice(s * 128, (s + 1) * 128)
            for h in range(4):
                P.tr(PBb[3][:, h * 128:(h + 1) * 128], ybb2[s][:, h * 128:(h + 1) * 128], identb[:],
                     [r_ybb2[s], r_c], [r_pb[3]], inc=(h == 3))
            P.cp("act", ybTs[b][:, :, sub], PBb[3][:, 0:512].rearrange("p (j t) -> p j t", j=4),
                 [r_pb[3]], [r_ybTs[b][s]])

        def stage_M(t):
            b = t % 2
            xt = xb[t % 3]
            uTt = uT[b]
            for f in range(8):
                fs = slice(f * 128, (f + 1) * 128)
                pga = PB[0][:, 0:256]
                pgb = PB[0][:, 256:512]
                ppa = PB[1][:, 0:256]
                ppb = PB[1][:, 256:512]
                for c in range(8):
                    P.mm(pga, wB[:, c, GAo + f * 128:GAo + (f + 1) * 128], uTt[:, c, :], c == 0, c == 7,
                         [r_wB2[c]] + r_uT[b], [r_pb[0]])
                for c in range(8):
                    P.mm(pgb, wB[:, c, GBo + f * 128:GBo + (f + 1) * 128], uTt[:, c, :], c == 0, c == 7,
                         [r_wB2[c]] + r_uT[b], [r_pb[0]])
                k2 = 0
                P.act(sga[k2][:], pga, AF.Tanh, [r_pb[0]], [r_sga[k2]], scale=0.5)
                P.act(sgb[k2][:], pgb, AF.Tanh, [r_pb[0]], [r_sgb[k2]], scale=0.5)
                for pr in range(4):
                    P.mm(ppa, wpa[:, pr, fs], yaT[t % 3][:, pr, :], pr == 0, pr == 3, [r_wpa, r_yaT[t % 3]], [r_pb[1]])
                if f == 0:
                    yb_transposes(t, 1)
                for pr in range(4):
                    P.mm(ppb, wpb[:, pr, fs], ybTs[b][:, pr, :], pr == 0, pr == 3, [r_wpb] + r_ybTs[b], [r_pb[1]])
                P.stt("dve", m1[k2][:], sga[k2][:], 1.0, ppa, ALU.add, ALU.mult, [r_sga[k2], r_pb[1]], [r_m1[k2]])
                P.stt("dve", m2[k2][:], sgb[k2][:], 1.0, ppb, ALU.add, ALU.mult, [r_sgb[k2], r_pb[1]], [r_m2[k2]])
                P.tt("dve", mT[:, f, :], m1[k2][:], m2[k2][:], ALU.add, [r_m1[k2], r_m2[k2]], [r_mT[f]])
                yield
            for s in range(2):
                sub = slice(s * 128, (s + 1) * 128)
                for n in range(2):
                    po2 = PB[n][:]
                    rpo = [r_pb[n]]
                    for c in range(8):
                        P.mm(po2, mT[:, c, sub], wo[:, c, n * 512:(n + 1) * 512], c == 0, c == 7,
                             [r_mT[c], r_wo[c]], rpo)
                    P.stt("dve", xt[:, s, n * 512:(n + 1) * 512], po2, 0.5, xt[:, s, n * 512:(n + 1) * 512],
                          ALU.mult, ALU.add, rpo + [r_xb[t % 3][s]], [r_xb[t % 3][s]])
                    yield
            P.dma("sp", lambda e, t=t, xt=xt: e.dma_start(out=hv[t], in_=xt[:]), reads=r_xb[t % 3], writes=[r_h1[t]])

        def prefetch(tt_):
            if tt_ < NT:
                P.dma("sp", lambda e: e.dma_start(out=xb[tt_ % 3][:], in_=xv[tt_]), writes=r_xb[tt_ % 3])
                P.dma("sp", lambda e: e.dma_start(out=yaT[tt_ % 3][:], in_=ya_v[:, :, tt_ * T:(tt_ + 1) * T]),
                      reads=r_ya[tt_], writes=[r_yaT[tt_ % 3]])

        prefetch(0)
        prefetch(1)
        for _ in stage_A(0):
            pass
        for t in range(NT):
            if t + 1 < NT:
                prefetch(t + 2)
                side = stage_M(t)
                n_side = 12
                done = 0
                for i, _ in enumerate(stage_A(t + 1)):
                    want = ((i + 1) * n_side) // 46
                    while done < want:
                        next(side, None)
                        done += 1
                for _ in side:
                    pass
            else:
                for _ in stage_M(t):
                    pass


def phase_C(nc, P, S, h_d, r_h1, out_d, dr, G):
    T = 256
    NT = S // T
    NF = DFF // 128
    identb, r_ident = G["identb"], G["r_c"]
    g_ffn_d, g_fin_d, wgu_d, wdn_d = dr["norm_ffn_g"], dr["norm_final_g"], dr["w_gate_up"], dr["w_down"]
    with ExitStack() as st:
        ec = st.enter_context
        wgu = ec(nc.sbuf_tensor("wgu", [128, 8, 2 * DFF], BF16))
        wdn = ec(nc.sbuf_tensor("wdn", [128, NF, D], BF16))
        gF = ec(nc.sbuf_tensor("gF", [128, D], F32))
        gL = ec(nc.sbuf_tensor("gL", [128, D], F32))
        hb = [ec(nc.sbuf_tensor("hb%d" % i, [128, 2, D], F32)) for i in range(2)]
        ub = ec(nc.sbuf_tensor("ub", [128, D], BF16))
        uT = [ec(nc.sbuf_tensor("uT%d" % i, [128, 8, T], BF16)) for i in range(2)]
        aT = ec(nc.sbuf_tensor("aT", [128, NF, T], BF16))
        sg = [ec(nc.sbuf_tensor("sg%d" % i, [128, T], BF16)) for i in range(2)]
        junk = ec(nc.sbuf_tensor("junk", [128, D], BF16))
        st8 = ec(nc.sbuf_tensor("st8", [128, 8], F32))
        epsb = ec(nc.sbuf_tensor("epsb", [128, 1], F32))
        r_eps = Res("eps")
        P.op("dve", lambda e: e.memset(epsb[:], EPS), writes=[r_eps])
        ptp = [ec(nc.psum_tensor("ptp%d" % i, [128, 4, 128], BF16)) for i in range(2)]
        pgu = [ec(nc.psum_tensor("pgu%d" % i, [128, 512], F32)) for i in range(4)]
        pdn = [ec(nc.psum_tensor("pdn%d" % i, [128, 512], F32)) for i in range(2)]

        r_wdn = [Res("wdn%d" % c) for c in range(NF)]
        r_gF, r_gL = Res("gF"), Res("gL")
        r_hb = [[Res("hb%d_%d" % (i, s)) for s in range(2)] for i in range(2)]
        r_ub = Res("ub")
        r_uT = [Res("uT%d" % i) for i in range(2)]
        r_aT = [Res("aT%d" % j) for j in range(NF)]
        r_sg = [Res("sg%d" % i) for i in range(2)]
        r_junk = Res("junk")
        r_st = [Res("st%d" % i) for i in range(8)]
        r_ptp = [Res("ptp%d" % i) for i in range(2)]
        r_pgu = [Res("pgu%d" % i) for i in range(4)]
        r_pdn = [Res("pdn%d" % i) for i in range(2)]

        P.dma("pool", lambda e: e.dma_start(out=gF[:], in_=g_ffn_d.to_broadcast([128, D])), writes=[r_gF])
        P.dma("pool", lambda e: e.dma_start(out=gL[:], in_=g_fin_d.to_broadcast([128, D])), writes=[r_gL])
        wgu_v = wgu_d.rearrange("(c p) f -> p c f", p=128)
        wdn_v = wdn_d.rearrange("(c p) f -> p c f", p=128)
        NBLK = 11
        r_wgu = [Res("wgu%d" % i) for i in range(NBLK)]
        for i in range(NBLK):
            for hlf in range(2):
                lo = hlf * DFF + i * 256
                P.dma("pool", lambda e, lo=lo: e.dma_start(out=wgu[:, :, lo:lo + 256], in_=wgu_v[:, :, lo:lo + 256]),
                      writes=[r_wgu[i]])
        for c in range(NF):
            P.dma("pool", lambda e, c=c: e.dma_start(out=wdn[:, c, :], in_=wdn_v[:, c, :]),
                  writes=[r_wdn[c]])
        h_v = h_d.rearrange("(t s p) d -> t p s d", s=2, p=128)
        o_v = out_d.rearrange("(t s p) d -> t p s d", s=2, p=128)
        final_pts = []

        def rstd_from(h_ap, col, r_h):
            P.op("act", lambda e: e.activation(out=junk[:], in_=h_ap, func=AF.Square,
                                               accum_out=st8[:, col:col + 1]),
                 reads=[r_h], writes=[r_junk, r_st[col]])
            P.op("act", lambda e: e.activation(out=st8[:, col:col + 1], in_=st8[:, col:col + 1], func=AF.Sqrt,
                                               scale=1.0 / D, bias=epsb[:]),
                 reads=[r_st[col], r_eps], writes=[r_st[col]])
            P.op("dve", lambda e: e.reciprocal(out=st8[:, col:col + 1], in_=st8[:, col:col + 1]),
                 reads=[r_st[col]], writes=[r_st[col]])

        ubc = [ub, ec(nc.sbuf_tensor("ub_c2", [128, D], BF16))]
        r_ubc = [r_ub, Res("ub_c2")]

        def prologue_norm(t):
            b = t % 2
            hbt = hb[b]
            P.dma("sp", lambda e, t=t, hbt=hbt: e.dma_start(out=hbt[:], in_=h_v[t]),
                  reads=[r_h1[t]], writes=[r_hb[b][0], r_hb[b][1]])
            for s in range(2):
                rstd_from(hbt[:, s, :], s, r_hb[b][s])
                P.stt("dve", ubc[s][:], hbt[:, s, :], st8[:, s:s + 1], gF[:], ALU.mult, ALU.mult,
                      [r_hb[b][s], r_st[s], r_gF], [r_ubc[s]])

        def prologue_tr(t):
            b = t % 2
            for s in range(2):
                for g4 in range(2):
                    pt = ptp[g4]
                    for j in range(4):
                        c = g4 * 4 + j
                        P.tr(pt[:, j, :], ubc[s][:, c * 128:(c + 1) * 128], identb[:], [r_ubc[s], r_ident],
                             [r_ptp[g4]], inc=(j == 3))
                    P.cp("act", uT[b][:, g4 * 4:(g4 + 1) * 4, s * 128:(s + 1) * 128], pt[:], [r_ptp[g4]], [r_uT[b]])

        def gateup(t):
            b = t % 2
            for j in range(NF):
                if j == NF // 2 and t + 1 < NT:
                    prologue_norm(t + 1)
                pg = pgu[(2 * j) % 4]
                pu = pgu[(2 * j + 1) % 4]
                rg = r_pgu[(2 * j) % 4]
                ru = r_pgu[(2 * j + 1) % 4]
                for c in range(8):
                    P.mm(pg[:, 0:T], wgu[:, c, j * 128:(j + 1) * 128], uT[b][:, c, :], c == 0, c == 7,
                         [r_wgu[j // 2], r_uT[b]], [rg])
                for c in range(8):
                    P.mm(pu[:, 0:T], wgu[:, c, DFF + j * 128:DFF + (j + 1) * 128], uT[b][:, c, :], c == 0, c == 7,
                         [r_wgu[j // 2], r_uT[b]], [ru])
                sgt = sg[j % 2]
                P.act(sgt[:], pg[:, 0:T], AF.Silu, [rg], [r_sg[j % 2]])
                P.tt("dve", aT[:, j, :], sgt[:], pu[:, 0:T], ALU.mult, [ru, r_sg[j % 2]], [r_aT[j]])

        def down(t):
            b = t % 2
            hbt = hb[b]
            for s in range(2):
                for n in range(2):
                    pd = pdn[n]
                    for j in range(NF):
                        P.mm(pd[:], aT[:, j, s * 128:(s + 1) * 128], wdn[:, j, n * 512:(n + 1) * 512],
                             j == 0, j == NF - 1, [r_aT[j], r_wdn[j]], [r_pdn[n]])
                    P.tt("dve", hbt[:, s, n * 512:(n + 1) * 512], hbt[:, s, n * 512:(n + 1) * 512], pd[:], ALU.add,
                         [r_pdn[n], r_hb[b][s]], [r_hb[b][s]])
                rstd_from(hbt[:, s, :], 2 + s, r_hb[b][s])
                P.stt("dve", hbt[:, s, :], hbt[:, s, :], st8[:, 2 + s:3 + s], gL[:], ALU.mult, ALU.mult,
                      [r_hb[b][s], r_st[2 + s], r_gL], [r_hb[b][s]])
            final_pts.append(P.dma("sp", lambda e, t=t, hbt=hbt: e.dma_start(out=o_v[t], in_=hbt[:]),
                                   reads=[r_hb[b][0], r_hb[b][1]]))

        prologue_norm(0)
        prologue_tr(0)
        for t in range(NT):
            gateup(t)
            if t + 1 < NT:
                prologue_tr(t + 1)
            down(t)
        return final_pts


def _host_consts(S):
    idx = np.arange(128)
    ident = np.eye(128, dtype=np.float32)
    tri = (idx[:, None] <= idx[None, :]).astype(np.float32)
    trim = -(idx[:, None] > idx[None, :]).astype(np.float32)
    ones = np.ones((128, 128), np.float32)
    consts = np.ascontiguousarray(np.stack([ident, tri, trim, ones], axis=1))
    inv = (np.float32(10000.0) ** (-np.arange(32, dtype=np.float32) / np.float32(32))).astype(np.float32)
    ang = np.arange(S, dtype=np.float32)[:, None] * inv[None, :]
    cos, sin = np.cos(ang).astype(np.float32), np.sin(ang).astype(np.float32)
    rope = np.ascontiguousarray(np.stack([np.concatenate([cos, cos], 1), np.concatenate([-sin, sin], 1)], axis=1))
    k = np.arange(128)[:, None, None]
    a = np.arange(4)[None, :, None]
    q = np.arange(512)[None, None, :]
    jq, ja = q // 256, a // 2
    cm = np.where(jq > ja, 1.0, np.where(jq < ja, 0.0, (q >= 128 * a + k).astype(np.float32)))
    cmask = np.ascontiguousarray(np.broadcast_to(cm, (128, 4, 512)).astype(ml_dtypes.bfloat16))
    blk = np.arange(16)[:, None]
    j = np.arange(16)[None, :]
    pastb = np.where(j < blk, 0.0, np.where(j == blk, 1e30, -1e30)).astype(np.float32)
    oneh = np.where(j == blk, -1024.0, 0.0).astype(np.float32)
    blkc = np.ascontiguousarray(np.broadcast_to(np.concatenate([pastb, oneh], 1)[None], (128, 16, 32)).astype(np.float32))
    return dict(consts=consts, rope=rope, cmask=cmask, blkc=blkc)


def make_in_maps(inp, ncores):
    f = lambda k: np.ascontiguousarray(np.asarray(inp[k], dtype=np.float32))
    x = np.asarray(inp["x"], dtype=np.float32)
    S = x.shape[1]
    common = {
        "norm_mix_g": f("norm_mix_g").reshape(1, D),
        "w_in": f("w_in")[0],
        "conv_wT": np.ascontiguousarray(f("conv_w")[0].T),
        "bias8": np.ascontiguousarray(np.concatenate([f("b_igate")[0], f("b_fgate")[0]]).reshape(1, 8)),
        "mlstm_norm_g": f("mlstm_norm_g").reshape(1, 512),
        "w_proj_a": f("w_proj_a")[0],
        "w_proj_b": f("w_proj_b")[0],
        "w_out": f("w_out")[0],
        "norm_ffn_g": f("norm_ffn_g").reshape(1, D),
        "norm_final_g": f("norm_final_g").reshape(1, D),
        "w_gate_up": f("w_gate_up")[0],
        "w_down": f("w_down")[0],
    }
    common.update(_host_consts(S))
    return [dict(common, x=np.ascontiguousarray(x[b])) for b in range(ncores)]


def kernel(x, norm_mix_g, w_in, conv_w, b_igate, b_fgate, mlstm_norm_g, w_proj_a, w_proj_b,
           w_out, norm_ffn_g, w_gate_up, w_down, norm_final_g):
    inp = dict(x=x, norm_mix_g=norm_mix_g, w_in=w_in, conv_w=conv_w, b_igate=b_igate, b_fgate=b_fgate,
               mlstm_norm_g=mlstm_norm_g, w_proj_a=w_proj_a, w_proj_b=w_proj_b, w_out=w_out,
               norm_ffn_g=norm_ffn_g, w_gate_up=w_gate_up, w_down=w_down, norm_final_g=norm_final_g)
    B, S, _ = np.asarray(x).shape
    nc = build_nc(S)
    in_maps = make_in_maps(inp, B)
    res = run_bass_kernel_spmd(nc, in_maps, core_ids=list(range(B)))
    return np.stack([np.asarray(r["out"]) for r in res.results], axis=0).astype(np.float32)
```
